# Optimizing a Trainium2 kernel written in Bass

```python
import jax, jax.numpy as jnp
from jax import lax
import numpy as np

D_MODEL = 2048
BATCH = 4
SEQ = 2048
DEPTH = 4
DEC_BATCH = 32
DEC_SEQ = 8
PAST_LEN = 16384
PAGE_SIZE = 128

N_MIXERS = 3
RMS_EPS = 1e-6
A_HEADS = 32
A_KV_HEADS = 4
A_HEAD_DIM = 64
A_GROUP = A_HEADS // A_KV_HEADS
A_Q_DIM = A_HEADS * A_HEAD_DIM
A_KV_DIM = A_KV_HEADS * A_HEAD_DIM
A_IN_DIM = 2 * A_Q_DIM + 2 * A_KV_DIM
WINDOW = 128
ATTN_BLOCK = 128
ROT_DIM = A_HEAD_DIM // 4
ROPE_THETA = 500000.0
CONV_WIDTH = 3
B_WIDTH = D_MODEL
B_IN_DIM = 4 * B_WIDTH
C_HEADS = 16
C_KEY_DIM = 128
C_VAL_DIM = D_MODEL // C_HEADS
C_F_DIM = C_HEADS * C_KEY_DIM
C_I_DIM = C_HEADS * C_VAL_DIM
C_IN_DIM = 2 * C_F_DIM + C_I_DIM + D_MODEL
C_CHUNK = 32

N_A_LAYERS = (DEPTH + 2) // 3
N_B_LAYERS = (DEPTH + 1) // 3
N_C_LAYERS = DEPTH // 3

kernel_name = "hybrid_swa_conv_hgrn2_step"


def rmsnorm(x, g):
    xf = x.astype(jnp.float32)
    y = xf * lax.rsqrt(jnp.mean(xf * xf, axis=-1, keepdims=True) + RMS_EPS)
    return (y * g.astype(jnp.float32)).astype(x.dtype)


def partial_rope(x, pos):
    half = ROT_DIM // 2
    inv = ROPE_THETA ** (-jnp.arange(half, dtype=jnp.float32) * 2.0 / ROT_DIM)
    ang = pos.astype(jnp.float32)[:, None] * inv[None, :]
    cos = jnp.cos(ang)[None, :, None, :]
    sin = jnp.sin(ang)[None, :, None, :]
    xf = x.astype(jnp.float32)
    x1 = xf[..., :half]
    x2 = xf[..., half:ROT_DIM]
    out = jnp.concatenate([x1 * cos - x2 * sin, x2 * cos + x1 * sin, xf[..., ROT_DIM:]], axis=-1)
    return out.astype(x.dtype)


def sink_attend(q, k, v, mask, sinks):
    s = jnp.einsum("...qhgd,...khd->...hgqk", q.astype(jnp.float32), k.astype(jnp.float32)) * (A_HEAD_DIM ** -0.5)
    s = jnp.where(mask, s, -jnp.inf)
    sk = sinks.astype(jnp.float32).reshape(A_KV_HEADS, A_GROUP, 1, 1)
    m = jnp.maximum(jnp.max(s, axis=-1, keepdims=True), sk)
    p = jnp.exp(s - m)
    denom = jnp.sum(p, axis=-1, keepdims=True) + jnp.exp(sk - m)
    o = jnp.einsum("...hgqk,...khd->...qhgd", p / denom, v.astype(jnp.float32))
    return o.astype(q.dtype)


def window_attn_prompt(q, k, v, sinks):
    bsz, s = q.shape[:2]
    nb = s // ATTN_BLOCK
    qb = q.reshape(bsz, nb, ATTN_BLOCK, A_KV_HEADS, A_GROUP, A_HEAD_DIM)
    kb = k.reshape(bsz, nb, ATTN_BLOCK, A_KV_HEADS, A_HEAD_DIM)
    vb = v.reshape(bsz, nb, ATTN_BLOCK, A_KV_HEADS, A_HEAD_DIM)
    prev = lambda t: jnp.concatenate([jnp.zeros_like(t[:, :1]), t[:, :-1]], axis=1)
    kk = jnp.concatenate([prev(kb), kb], axis=2)
    vv = jnp.concatenate([prev(vb), vb], axis=2)
    blk = jnp.arange(nb)[:, None]
    qpos = blk * ATTN_BLOCK + jnp.arange(ATTN_BLOCK)[None, :]
    kpos = (blk - 1) * ATTN_BLOCK + jnp.arange(2 * ATTN_BLOCK)[None, :]
    diff = qpos[:, :, None] - kpos[:, None, :]
    mask = (diff >= 0) & (diff < WINDOW) & (kpos[:, None, :] >= 0)
    o = sink_attend(qb, kk, vv, mask[:, None, None], sinks)
    return o.reshape(bsz, s, A_HEADS, A_HEAD_DIM)


def window_attn_sample(q, k_all, v_all, sinks):
    bsz, t = q.shape[:2]
    w = k_all.shape[1] - t
    qpos = w + jnp.arange(t)
    kpos = jnp.arange(w + t)
    diff = qpos[:, None] - kpos[None, :]
    mask = (diff >= 0) & (diff < WINDOW)
    qg = q.reshape(bsz, t, A_KV_HEADS, A_GROUP, A_HEAD_DIM)
    o = sink_attend(qg, k_all, v_all, mask, sinks)
    return o.reshape(bsz, t, A_HEADS, A_HEAD_DIM)


def attn_mixer(h, w_in, w_out, sinks, pos, past_k, past_v):
    bsz, t, _ = h.shape
    proj = h @ w_in
    q, k, v, z = jnp.split(proj, [A_Q_DIM, A_Q_DIM + A_KV_DIM, A_Q_DIM + 2 * A_KV_DIM], axis=-1)
    q = partial_rope(q.reshape(bsz, t, A_HEADS, A_HEAD_DIM), pos)
    k = partial_rope(k.reshape(bsz, t, A_KV_HEADS, A_HEAD_DIM), pos)
    v = v.reshape(bsz, t, A_KV_HEADS, A_HEAD_DIM)
    if past_k is None:
        o = window_attn_prompt(q, k, v, sinks)
        k_all, v_all = k, v
    else:
        k_all = jnp.concatenate([past_k.astype(k.dtype), k], axis=1)
        v_all = jnp.concatenate([past_v.astype(v.dtype), v], axis=1)
        o = window_attn_sample(q, k_all, v_all, sinks)
    w = min(WINDOW, k_all.shape[1])
    y = (jax.nn.silu(z) * o.reshape(bsz, t, A_Q_DIM)) @ w_out
    return y, k_all[:, -w:], v_all[:, -w:]


def conv_mixer(h, w_in, conv_w, w_out, past_u):
    bsz, t, _ = h.shape
    proj = h @ w_in
    b_gate, c_gate, xin, z = jnp.split(proj, 4, axis=-1)
    u = c_gate * xin
    if past_u is None:
        past_u = jnp.zeros((bsz, CONV_WIDTH - 1, B_WIDTH), u.dtype)
    u_pad = jnp.concatenate([past_u.astype(u.dtype), u], axis=1)
    conv = conv_w[0] * u_pad[:, 0:t]
    for j in range(1, CONV_WIDTH):
        conv = conv + conv_w[j] * u_pad[:, j:j + t]
    y = (jax.nn.silu(z) * b_gate * conv) @ w_out
    return y, u_pad[:, -(CONV_WIDTH - 1):]


def gla_chunked(q, k, v, log_f, s0):
    bsz, t, nh, dk = q.shape
    n = -(-t // C_CHUNK)
    pad = n * C_CHUNK - t
    padf = lambda a: jnp.pad(a, ((0, 0), (0, pad), (0, 0), (0, 0)))
    to_chunks = lambda a: jnp.moveaxis(a.reshape(bsz, n, C_CHUNK, nh, a.shape[-1]), 1, 0)
    xs = tuple(to_chunks(padf(a)) for a in (q, k, v, log_f))
    causal = jnp.tril(jnp.ones((C_CHUNK, C_CHUNK), dtype=bool))

    def step(S, inp):
        qc, kc, vc, lc = inp
        b = jnp.cumsum(lc, axis=1)
        b_last = b[:, -1]
        q_t = qc * jnp.exp(b)
        k_intra = kc * jnp.exp(-b)
        k_state = kc * jnp.exp(b_last[:, None] - b)
        o_inter = jnp.einsum("bchk,bhkv->bchv", q_t, S)
        att = jnp.where(causal, jnp.einsum("bthk,bshk->bhts", q_t, k_intra), 0.0)
        o_intra = jnp.einsum("bhts,bshv->bthv", att, vc)
        S_new = jnp.exp(b_last)[..., None] * S + jnp.einsum("bchk,bchv->bhkv", k_state, vc)
        return S_new, o_inter + o_intra

    s_t, o = lax.scan(step, s0.astype(jnp.float32), xs)
    o = jnp.moveaxis(o, 0, 1).reshape(bsz, n * C_CHUNK, nh, v.shape[-1])[:, :t]
    return o, s_t


def hgrn2_mixer(h, w_in, norm_g, w_out, lb, s0):
    bsz, t, _ = h.shape
    proj = h @ w_in
    q, fl, i, z = jnp.split(proj, [C_F_DIM, 2 * C_F_DIM, 2 * C_F_DIM + C_I_DIM], axis=-1)
    lbf = lb.astype(jnp.float32)
    f = lbf + (1.0 - lbf) * jax.nn.sigmoid(fl.astype(jnp.float32))
    log_f = jnp.log(f)
    k = 1.0 - f
    q = jax.nn.silu(q.astype(jnp.float32))
    shp = (bsz, t, C_HEADS, C_KEY_DIM)
    if s0 is None:
        s0 = jnp.zeros((bsz, C_HEADS, C_KEY_DIM, C_VAL_DIM), jnp.float32)
    o, s_t = gla_chunked(q.reshape(shp), k.reshape(shp),
                         i.astype(jnp.float32).reshape(bsz, t, C_HEADS, C_VAL_DIM),
                         log_f.reshape(shp), s0)
    o = rmsnorm(o, norm_g).reshape(bsz, t, C_I_DIM).astype(h.dtype)
    y = (jax.nn.silu(z) * o) @ w_out
    return y, s_t


def trunk(x, pos, cache_k, cache_v, state_conv, state_hgrn, ln_g, final_g,
          a_w_in, a_w_out, a_sinks, b_w_in, b_conv_w, b_w_out,
          c_w_in, c_norm_g, c_w_out, c_lb_logits):
    lb_cum = jnp.cumsum(jax.nn.softmax(c_lb_logits.astype(jnp.float32), axis=0), axis=0)
    lb_all = lb_cum - lb_cum[0]
    ks, vs, us, ss = [], [], [], []
    for li in range(DEPTH):
        h = rmsnorm(x, ln_g[li])
        kind = li % N_MIXERS
        j = li // N_MIXERS
        if kind == 0:
            pk = None if cache_k is None else cache_k[j]
            pv = None if cache_v is None else cache_v[j]
            y, nk, nv = attn_mixer(h, a_w_in[j], a_w_out[j], a_sinks[j], pos, pk, pv)
            ks.append(nk)
            vs.append(nv)
        elif kind == 1:
            pu = None if state_conv is None else state_conv[j]
            y, nu = conv_mixer(h, b_w_in[j], b_conv_w[j], b_w_out[j], pu)
            us.append(nu)
        else:
            ps = None if state_hgrn is None else state_hgrn[j]
            y, ns = hgrn2_mixer(h, c_w_in[j], c_norm_g[j], c_w_out[j], lb_all[li], ps)
            ss.append(ns)
        x = x + y
    return rmsnorm(x, final_g), jnp.stack(ks), jnp.stack(vs), jnp.stack(us), jnp.stack(ss)


def setup_inputs(seed: int = 0) -> dict:
    key = jax.random.key(seed)
    ks = jax.random.split(key, 20)
    nrm = lambda k, shape, scale: jax.random.normal(k, shape, jnp.float32) * scale
    win_rows = min(WINDOW, PAST_LEN)
    return {
        "x_prompt": nrm(ks[0], (BATCH, SEQ, D_MODEL), 1.0),
        "x_sample": nrm(ks[1], (DEC_BATCH, DEC_SEQ, D_MODEL), 1.0),
        "cache_k": nrm(ks[2], (N_A_LAYERS, DEC_BATCH, win_rows, A_KV_HEADS, A_HEAD_DIM), 1.0),
        "cache_v": nrm(ks[3], (N_A_LAYERS, DEC_BATCH, win_rows, A_KV_HEADS, A_HEAD_DIM), 1.0),
        "state_conv": nrm(ks[4], (N_B_LAYERS, DEC_BATCH, CONV_WIDTH - 1, B_WIDTH), 1.0),
        "state_hgrn": nrm(ks[5], (N_C_LAYERS, DEC_BATCH, C_HEADS, C_KEY_DIM, C_VAL_DIM), 0.4),
        "ln_g": 1.0 + nrm(ks[6], (DEPTH, D_MODEL), 0.02),
        "final_g": 1.0 + nrm(ks[7], (D_MODEL,), 0.02),
        "a_w_in": nrm(ks[8], (N_A_LAYERS, D_MODEL, A_IN_DIM), D_MODEL ** -0.5),
        "a_w_out": nrm(ks[9], (N_A_LAYERS, A_Q_DIM, D_MODEL), A_Q_DIM ** -0.5),
        "a_sinks": nrm(ks[10], (N_A_LAYERS, A_HEADS), 0.5),
        "b_w_in": nrm(ks[11], (N_B_LAYERS, D_MODEL, B_IN_DIM), D_MODEL ** -0.5),
        "b_conv_w": nrm(ks[12], (N_B_LAYERS, CONV_WIDTH, B_WIDTH), CONV_WIDTH ** -0.5),
        "b_w_out": nrm(ks[13], (N_B_LAYERS, B_WIDTH, D_MODEL), B_WIDTH ** -0.5),
        "c_w_in": nrm(ks[14], (N_C_LAYERS, D_MODEL, C_IN_DIM), D_MODEL ** -0.5),
        "c_norm_g": 1.0 + nrm(ks[15], (N_C_LAYERS, C_VAL_DIM), 0.02),
        "c_w_out": nrm(ks[16], (N_C_LAYERS, C_I_DIM, D_MODEL), C_I_DIM ** -0.5),
        "c_lb_logits": nrm(ks[17], (DEPTH, C_F_DIM), 0.1),
    }


def reference(x_prompt, x_sample, cache_k, cache_v, state_conv, state_hgrn, ln_g, final_g,
              a_w_in, a_w_out, a_sinks, b_w_in, b_conv_w, b_w_out,
              c_w_in, c_norm_g, c_w_out, c_lb_logits):
    pos_prompt = jnp.arange(x_prompt.shape[1], dtype=jnp.int32)
    pos_sample = PAST_LEN + jnp.arange(x_sample.shape[1], dtype=jnp.int32)
    y_prompt, k_p, v_p, conv_p, hgrn_p = trunk(
        x_prompt, pos_prompt, None, None, None, None, ln_g, final_g,
        a_w_in, a_w_out, a_sinks, b_w_in, b_conv_w, b_w_out,
        c_w_in, c_norm_g, c_w_out, c_lb_logits)
    y_sample, k_s, v_s, conv_s, hgrn_s = trunk(
        x_sample, pos_sample, cache_k, cache_v, state_conv, state_hgrn, ln_g, final_g,
        a_w_in, a_w_out, a_sinks, b_w_in, b_conv_w, b_w_out,
        c_w_in, c_norm_g, c_w_out, c_lb_logits)
    return (y_prompt, y_sample, k_p, v_p, k_s, v_s, conv_p, conv_s, hgrn_p, hgrn_s)
```

```python
import contextlib
import os
import numpy as np
import concourse.bass as bass
import concourse.mybir as mybir
from concourse.bass_utils import run_bass_kernel_spmd

F32 = mybir.dt.float32
BF16 = mybir.dt.bfloat16
AF = mybir.ActivationFunctionType
ALU = mybir.AluOpType

D = 2048
NB = 9
NTOK = 1056
EPS = 1e-6
PAIRS = [[0, 1], [2, 3], [4, 5], [6, 7]]
COMPUTE = ("pe", "act", "dve", "pool")


class Sched:
    def __init__(self, nc):
        self.nc = nc
        self.ops = []
        self.res_w = {}
        self.res_r = {}
        self.dma_last = {}
        self.streams = {e: [] for e in ("pe", "act", "dve", "pool", "sp")}

    def _deps(self, reads, writes, idx, key):
        raw, war = set(), set()
        for r in reads:
            w = self.res_w.get(r)
            if w is not None:
                raw.add(w)
            if isinstance(r, tuple) and r[0] == "ps":
                for k2, rd in self.res_r.get(r, {}).items():
                    if k2 != key:
                        raw.add(rd)
        for r in writes:
            w = self.res_w.get(r)
            if w is not None:
                war.add(w)
            for rd in self.res_r.get(r, {}).values():
                war.add(rd)
        for r in writes:
            self.res_w[r] = idx
            self.res_r[r] = {}
        for r in reads:
            self.res_r.setdefault(r, {})[key if key is not None else ("dma", idx)] = idx
        raw.discard(idx)
        war.discard(idx)
        return raw, war

    def op(self, eng, fn, reads=(), writes=()):
        idx = len(self.ops)
        raw, war = self._deps(reads, writes, idx, eng)
        self.ops.append(dict(eng=eng, fn=fn, raw=raw, war=war, dma=None, idx=idx))
        self.streams[eng].append(idx)
        return idx

    def dma(self, queue, slot, fn, n=1, inc=16, reads=(), writes=()):
        idx = len(self.ops)
        raw, war = self._deps(reads, writes, idx, None)
        last = self.dma_last.get(slot)
        if last is not None:
            raw.add(last)
        self.dma_last[slot] = idx
        self.ops.append(dict(eng=queue, fn=fn, raw=raw, war=war, dma=slot, idx=idx, n=n, inc=inc))
        self.streams[queue].append(idx)
        return idx

    def barrier(self):
        last = set()
        for e, st in self.streams.items():
            if st:
                last.add(st[-1])
        for v in self.dma_last.values():
            last.add(v)
        for e in COMPUTE + ("sp",):
            idx = len(self.ops)
            self.ops.append(dict(eng=e, fn=None, raw=set(last), war=set(), dma=None, idx=idx))
            self.streams[e].append(idx)

    def emit(self):
        nc, ops = self.nc, self.ops

        def same_eng(p, o):
            return p["dma"] is None and o["dma"] is None and p["eng"] == "pe" and o["eng"] == "pe"

        needed = set()
        for o in ops:
            needed |= o["raw"]
            for d in o["war"]:
                if not same_eng(ops[d], o):
                    needed.add(d)
        with contextlib.ExitStack() as es:
            sem_eng = {e: es.enter_context(nc.semaphore(f"s_{e}")) for e in COMPUTE}
            slots = list(self.dma_last.keys())
            sem_dma = {k: es.enter_context(nc.semaphore(f"d_{i}")) for i, k in enumerate(slots)}
            cnt = {e: 0 for e in COMPUTE}
            dcnt = {k: 0 for k in slots}
            ev = {}
            for o in ops:
                if o["dma"] is not None:
                    dcnt[o["dma"]] += o["inc"] * o["n"]
                    ev[o["idx"]] = (sem_dma[o["dma"]], dcnt[o["dma"]])
                elif o["idx"] in needed and o["fn"] is not None:
                    cnt[o["eng"]] += 1
                    ev[o["idx"]] = (sem_eng[o["eng"]], cnt[o["eng"]])
            self.stats = (dict(cnt), {str(k): v for k, v in dcnt.items()}, {e: len(v) for e, v in self.streams.items()})
            blk = es.enter_context(nc.Block())

            def replay(engname):
                def body(eng):
                    waited = {}
                    for idx in self.streams[engname]:
                        o = ops[idx]
                        deps = set(o["raw"])
                        for d in o["war"]:
                            if not same_eng(ops[d], o):
                                deps.add(d)
                        wl = {}
                        for d in deps:
                            if d not in ev:
                                continue
                            s, v = ev[d]
                            key = id(s)
                            if waited.get(key, 0) >= v:
                                continue
                            if key not in wl or wl[key][1] < v:
                                wl[key] = (s, v)
                        for key, (s, v) in wl.items():
                            eng.wait_ge(s, v)
                            waited[key] = v
                        if o["fn"] is None:
                            continue
                        r = o["fn"](eng)
                        if o["dma"] is not None:
                            s, _ = ev[idx]
                            assert len(r) == o["n"], (len(r), o["n"])
                            for ins in r:
                                ins.then_inc(s, o["inc"])
                        elif idx in ev:
                            ins = r[-1] if isinstance(r, (list, tuple)) else r
                            ins.then_inc(ev[idx][0], 1)
                    if engname == "sp":
                        for k, s in sem_dma.items():
                            if dcnt[k]:
                                eng.wait_ge(s, dcnt[k])
                return body

            blk.tensor(replay("pe"))
            blk.scalar(replay("act"))
            blk.vector(replay("dve"))
            blk.gpsimd(replay("pool"))
            blk.sync(replay("sp"))


def build(n_layers=4, dbg=False, stop=99, small=False):
    nc = bass.Bass("TRN2", target_bir_lowering=False)
    din = lambda name, shape, dt=F32: nc.dram_tensor(name, list(shape), dt, kind="ExternalInput").ap()
    dout = lambda name, shape, dt=F32: nc.dram_tensor(name, list(shape), dt, kind="ExternalOutput").ap()
    dint = lambda name, shape, dt=F32: nc.dram_tensor(name, list(shape), dt, kind="Internal").ap()

    xp = din("xp", [1024, D])
    xsm = din("xsm", [32, D])
    ck = din("ck", [2, 4, 128, 256])
    cv = din("cv", [2, 4, 128, 256])
    sconv = din("sconv", [8, D])
    shg = din("shg", [4, 16, 128, 128])
    ln_g = din("ln_g", [4, D])
    final_g = din("final_g", [D])
    na = 2 if (n_layers >= 4 or not small) else 1
    a_w_in = din("a_w_in", [na, D, 4608])
    a_w_out = din("a_w_out", [na, D, D])
    a_sinks = din("a_sinks", [2, 32])
    b_w_in = din("b_w_in", [1, D, 8192] if (n_layers >= 2 or not small) else [1, 1, 8192])
    b_conv_w = din("b_conv_w", [1, 3, D])
    b_w_out = din("b_w_out", [1, D, D] if (n_layers >= 2 or not small) else [1, 1, D])
    c_w_in = din("c_w_in", [1, D, 8192] if (n_layers >= 3 or not small) else [1, 1, 8192])
    c_norm_g = din("c_norm_g", [1, 128])
    c_w_out = din("c_w_out", [1, D, D] if (n_layers >= 3 or not small) else [1, 1, D])
    c_lb = din("c_lb", [4, D])
    ropec = din("ropec", [128, NB, 8])
    ropes = din("ropes", [128, NB, 8])
    flag_d = din("flag", [128, 1])

    y = dout("y", [NTOK, D])
    kp = dout("kp", [2, 128, 256])
    vp = dout("vp", [2, 128, 256])
    ks = dout("ks", [2, 4, 128, 256])
    vs = dout("vs", [2, 4, 128, 256])
    convp = dout("convp", [2, D])
    convs = dout("convs", [8, D])
    hgp = dout("hgp", [16, 128, 128])
    hgs = dout("hgs", [4, 16, 128, 128])

    cc_a_in = [dint(f"cc_a_in{j}", [128, 512], BF16) for j in range(2)]
    cc_a_out = [dint(f"cc_a_out{j}", [256, 512], BF16) for j in range(2)]
    cc_b_in = dint("cc_b_in", [128, 32])
    cc_b_out = dint("cc_b_out", [256, 32])
    cc_h_in = [dint(f"cc_h_in{h}", [128, 128]) for h in range(16)]
    cc_h_out = [dint(f"cc_h_out{h}", [256, 128]) for h in range(16)]

    S = Sched(nc)
    with contextlib.ExitStack() as es:
        uniq = [0]

        def T(name, shape, dt, stack=es):
            uniq[0] += 1
            return stack.enter_context(nc.sbuf_tensor(f"{name}_{uniq[0]}", list(shape), dt))

        xs = T("xs", [128, NB, D], F32)
        hT = T("hT", [128, 16, NTOK], BF16)
        goT = T("goT", [128, 16, NTOK], BF16)
        wt = [T(f"wt{i}", [128, 16, 256], BF16) for i in range(2)]
        rt = [None] * 4
        ident = T("ident", [128, 128], BF16)
        identf = T("identf", [128, 128], F32)
        onesb = T("onesb", [128, 128], BF16)
        onesf = T("onesf", [128, 128], F32)
        mprev = T("mprev", [128, 2, 128], BF16)
        mown = T("mown", [128, 2, 128], BF16)
        ones4 = T("ones4", [128, 2, 128], BF16)
        flag = T("flag_sb", [128, 1], F32)
        ss = T("ss", [128, NB], F32)
        rstd = T("rstd", [128, NB], F32)
        cosT = T("cosT", [128, NB, 8], F32)
        sinT = T("sinT", [128, NB, 8], F32)
        ps = [es.enter_context(nc.psum_tensor(f"ps{i}", [128, 512], F32)) for i in range(8)]
        psb = [p.bitcast(BF16) for p in ps]

        PS = lambda i: ("ps", i)

        S.op("pool", lambda e: e.memset(onesb[:], 1.0), writes=["onesb"])
        S.op("pool", lambda e: e.memset(onesf[:], 1.0), writes=["onesf"])
        S.op("pool", lambda e: e.memset(ones4[:], 1.0), writes=["ones4"])
        S.op("pool", lambda e: e.affine_select(out=ident[:], in_=onesb[:], pattern=[[-1, 128]], compare_op=ALU.is_equal,
                                               fill=0.0, base=0, channel_multiplier=1), reads=["onesb"], writes=["ident"])
        S.op("pool", lambda e: e.affine_select(out=identf[:], in_=onesf[:], pattern=[[-1, 128]], compare_op=ALU.is_equal,
                                               fill=0.0, base=0, channel_multiplier=1), reads=["onesf"], writes=["identf"])
        S.op("pool", lambda e: e.affine_select(out=mprev[:], in_=ones4[:], pattern=[[0, 2], [-1, 128]], compare_op=ALU.is_ge,
                                               fill=0.0, base=-1, channel_multiplier=1), reads=["ones4"], writes=["mprev"])
        S.op("pool", lambda e: e.affine_select(out=mown[:], in_=ones4[:], pattern=[[0, 2], [1, 128]], compare_op=ALU.is_ge,
                                               fill=0.0, base=0, channel_multiplier=-1), reads=["ones4"], writes=["mown"])
        S.op("pool", lambda e: e.memset(xs[:, 8, :], 0.0), writes=[("x", 8)])
        S.dma("sp", "misc", lambda e: [e.dma_start(out=flag[:], in_=flag_d[:, :]),
                                       e.dma_start(out=cosT[:], in_=ropec[:, :, :]),
                                       e.dma_start(out=sinT[:], in_=ropes[:, :, :])], n=3, writes=["flag", "rope"])
        for b in range(8):
            S.dma("sp", ("xin", b % 4), lambda e, b=b: [e.dma_start(out=xs[:, b, :], in_=xp[b * 128:(b + 1) * 128, :])],
                  writes=[("x", b)])
        S.dma("sp", ("xin", 0), lambda e: [e.dma_start(out=xs[0:32, 8, :], in_=xsm[:, :])], writes=[("x", 8)])

        tiles = []

        def wsrc(w2d, c0, width=256):
            return [(w2d[:, c0:c0 + width], 0)]

        def group_src(w2d, j, hh):
            return [(w2d[:, g * 2048 + j * 128: g * 2048 + (j + 1) * 128], (g % 2) * 128) for g in (2 * hh, 2 * hh + 1)]

        layer_kinds = ["a", "b", "c", "a"][:n_layers]
        for li, kind in enumerate(layer_kinds):
            jj = li // 3
            if kind == "a":
                w_in, w_out = a_w_in[jj], a_w_out[jj]
                tiles.append(wsrc(w_in, 2048))
                tiles.append(wsrc(w_in, 2304))
                for GG in range(8):
                    tiles.append(wsrc(w_in, GG * 256))
                    tiles.append(wsrc(w_in, 2560 + GG * 256))
            elif kind == "b":
                w_in, w_out = b_w_in[0], b_w_out[0]
                for j in range(16):
                    tiles.append(group_src(w_in, j, 0))
                    tiles.append(group_src(w_in, j, 1))
            else:
                w_in, w_out = c_w_in[0], c_w_out[0]
                for j in range(16):
                    tiles.append(group_src(w_in, j, 0))
                    tiles.append(group_src(w_in, j, 1))
            for t in range(8):
                tiles.append(wsrc(w_out, t * 256))
        tstate = dict(next_load=0, next_use=0)

        def issue_load():
            i = tstate["next_load"]
            if i >= len(tiles):
                return
            tstate["next_load"] += 1
            slot = i % 2
            srcs = tiles[i]

            def fn(e, srcs=srcs, slot=slot):
                return [e.dma_start(out=wt[slot][:, :, off:off + a.shape[1]],
                                    in_=a.rearrange("(k p) n -> p k n", p=128)) for a, off in srcs]
            S.dma("pool", ("wt", slot), fn, n=len(srcs), writes=[("wt", slot)])

        def next_tile(prefetch=True):
            i = tstate["next_use"]
            tstate["next_use"] += 1
            while tstate["next_load"] <= (min(i + 1, len(tiles) - 1) if prefetch else i):
                issue_load()
            return wt[i % 2], ("wt", i % 2)

        issue_load()

        def blkP(b):
            return 128 if b < 8 else 32

        def tok0(b):
            return b * 128

        rr = dict(pt=0, ev=0)

        def pt_bank():
            rr["pt"] ^= 1
            return rr["pt"]

        def evac_eng():
            rr["ev"] ^= 1
            return "act" if rr["ev"] else "dve"

        def copy(engname, out, in_, reads, writes):
            if engname == "act":
                S.op("act", lambda e: e.copy(out=out, in_=in_), reads=reads, writes=writes)
            else:
                S.op(engname, lambda e: e.tensor_copy(out=out, in_=in_), reads=reads, writes=writes)

        def transpose_to(src_tile, src_res, ncols_blocks, P, dst_fn, dst_res_fn):
            for g0 in range(0, ncols_blocks, 4):
                rr["tr"] = rr.get("tr", 0) ^ 1
                bank = 4 + rr["tr"]
                n = min(4, ncols_blocks - g0)
                for j in range(n):
                    c = g0 + j
                    S.op("pe", lambda e, c=c, j=j, bank=bank: e.transpose(
                        out=psb[bank][:, j * 128:j * 128 + P], in_=src_tile[0:P, c * 128:(c + 1) * 128],
                        identity=ident[0:P, 0:P]), reads=[src_res, "ident"], writes=[PS(bank)])
                ee = evac_eng()
                for j in range(n):
                    c = g0 + j
                    copy(ee, dst_fn(c), psb[bank][:, j * 128:j * 128 + P], [PS(bank)], [dst_res_fn(c)])

        def norm_phase(g_row, final=False):
            with contextlib.ExitStack() as ph:
                gbc = T("gbc", [128, D], F32, ph)
                junk = T("junk", [128, D], BF16, ph)
                if final:
                    yb = [T(f"yb{i}", [128, D], F32, ph) for i in range(2)]
                else:
                    hb = [T(f"hb{i}", [128, D], BF16, ph) for i in range(2)]
                S.dma("sp", "gbc", lambda e: [e.dma_start(out=gbc[:], in_=g_row.partition_broadcast(128))], writes=["gbc"])
                for b in range(NB):
                    P = blkP(b)
                    S.op("act", lambda e, b=b, P=P: e.activation(out=junk[0:P, :], in_=xs[0:P, b, :], func=AF.Square,
                                                                 accum_out=ss[0:P, b:b + 1]),
                         reads=[("x", b)], writes=["junk", ("ss", b)])
                    S.op("act", lambda e, b=b, P=P: e.activation(out=rstd[0:P, b:b + 1], in_=ss[0:P, b:b + 1], func=AF.Sqrt,
                                                                 scale=1.0 / D, bias=EPS), reads=[("ss", b)], writes=[("rstd", b)])
                    S.op("dve", lambda e, b=b, P=P: e.reciprocal(out=rstd[0:P, b:b + 1], in_=rstd[0:P, b:b + 1]),
                         reads=[("rstd", b)], writes=[("rstd", b)])
                    if final:
                        ybt = yb[b % 2]
                        S.op("dve", lambda e, b=b, P=P, ybt=ybt: e.scalar_tensor_tensor(
                            out=ybt[0:P, :], in0=xs[0:P, b, :], scalar=rstd[0:P, b:b + 1], in1=gbc[0:P, :],
                            op0=ALU.mult, op1=ALU.mult), reads=[("x", b), ("rstd", b), "gbc"], writes=[("yb", b % 2)])
                        S.dma("sp", ("yout", b % 2), lambda e, b=b, P=P, ybt=ybt: [
                            e.dma_start(out=y[b * 128:b * 128 + P, :], in_=ybt[0:P, :])], reads=[("yb", b % 2)])
                    else:
                        hbt = hb[b % 2]
                        S.op("dve", lambda e, b=b, P=P, hbt=hbt: e.scalar_tensor_tensor(
                            out=hbt[0:P, :], in0=xs[0:P, b, :], scalar=rstd[0:P, b:b + 1], in1=gbc[0:P, :],
                            op0=ALU.mult, op1=ALU.mult), reads=[("x", b), ("rstd", b), "gbc"], writes=[("hb", b % 2)])
                        transpose_to(hbt, ("hb", b % 2), 16, P,
                                     lambda c, b=b, P=P: hT[:, c, tok0(b):tok0(b) + P], lambda c, b=b: ("hT", b))
                S.barrier()

        def wout_phase():
            for t in range(8):
                w, wres = next_tile()
                for b in range(NB):
                    P = blkP(b)
                    bank = pt_bank()
                    for k in range(16):
                        S.op("pe", lambda e, k=k, b=b, P=P, bank=bank, w=w: e.matmul(
                            ps[bank][0:P, 0:256], lhsT=goT[:, k, tok0(b):tok0(b) + P], rhs=w[:, k, :],
                            start=(k == 0), stop=(k == 15)), reads=[wres, ("goT", b)], writes=[PS(bank)])
                    S.op("dve", lambda e, b=b, P=P, bank=bank, t=t: e.tensor_tensor(
                        out=xs[0:P, b, t * 256:(t + 1) * 256], in0=xs[0:P, b, t * 256:(t + 1) * 256],
                        in1=ps[bank][0:P, 0:256], op=ALU.add), reads=[PS(bank), ("x", b)], writes=[("x", b)])
            S.barrier()

        def proj_tm(w, wres, b, bank, ncols=256):
            P = blkP(b)
            for k in range(16):
                S.op("pe", lambda e, k=k, b=b, P=P, bank=bank, w=w: e.matmul(
                    ps[bank][0:P, 0:ncols], lhsT=hT[:, k, tok0(b):tok0(b) + P], rhs=w[:, k, 0:ncols],
                    start=(k == 0), stop=(k == 15)), reads=[wres, ("hT", b)], writes=[PS(bank)])

        def rope_tm(src3, dst1, dst2, P, b, nh, rd, wr):
            x1, x2 = src3[:, :, 0:8], src3[:, :, 8:16]
            cosb = cosT[0:P, b, :].unsqueeze(1).to_broadcast([P, nh, 8])
            sinb = sinT[0:P, b, :].unsqueeze(1).to_broadcast([P, nh, 8])
            r = [t_[0:P, 0:nh, :] for t_ in rt]
            for i, (xa, tb_) in enumerate(((x1, cosb), (x2, sinb), (x2, cosb), (x1, sinb))):
                S.op("dve", lambda e, i=i, xa=xa, tb_=tb_, r=r: e.tensor_tensor(out=r[i], in0=xa, in1=tb_, op=ALU.mult),
                     reads=rd + ["rope"], writes=[f"rt{i}"])
            S.op("dve", lambda e, r=r: e.tensor_tensor(out=dst1, in0=r[0], in1=r[1], op=ALU.subtract),
                 reads=["rt0", "rt1"], writes=wr)
            S.op("dve", lambda e, r=r: e.tensor_tensor(out=dst2, in0=r[2], in1=r[3], op=ALU.add),
                 reads=["rt2", "rt3"], writes=wr)

        def attn_phase(jl):
            with contextlib.ExitStack() as ph:
                kT2 = T("kT2", [128, 4, 1152], BF16, ph)
                vall = T("vall", [128, 9, 4, 66], BF16, ph)
                kT2s = T("kT2s", [128, 4, 4 * 136], BF16, ph)
                vSc = T("vSc", [128, 4, 4, 66], BF16, ph)
                vSn = T("vSn", [8, 4, 4, 66], BF16, ph)
                vS32 = T("vS32", [32, 4, 64], BF16, ph)
                kcb = T("kcb", [128, 4, 2, 64], BF16, ph)
                kc32 = T("kc32", [128, 256], BF16, ph)
                ktm = T("ktm", [128, NB, 256], F32, ph) if False else None
                ktm1 = T("ktm1", [128, 256], F32, ph)
                k7 = T("k7", [128, 256], F32, ph)
                vtm = T("vtm", [128, 256], F32, ph)
                kb = T("kb", [128, 4, 2, 64], BF16, ph)
                halo = T("halo", [128, 512], BF16, ph)
                halo_in = T("halo_in", [128, 512], BF16, ph)
                rt = [T(f"rt{i}", [128, 4, 8], F32, ph) for i in range(4)]
                qf = [T(f"qf{i}", [128, 256], F32, ph) for i in range(3)]
                qb = [T(f"qb{i}", [128, 256], BF16, ph) for i in range(3)]
                qT = T("qT", [128, 2, 128], BF16, ph)
                E = [T(f"E{i}", [128, 2, 128], BF16, ph) for i in range(8)]
                den = T("den", [128, 4], F32, ph)
                ob = T("ob", [128, 4, 64], BF16, ph)
                obs = T("obs", [8, 4, 64], BF16, ph)
                oball = T("oball", [128, NB, 256], BF16, ph)
                sz = [T(f"sz{i}", [128, 256], BF16, ph) for i in range(2)]
                gg = [T(f"gg{i}", [128, 256], BF16, ph) for i in range(2)]
                esink = T("esink", [128, 32], F32, ph)
                attn_body(jl, locals())
            S.barrier()

        def attn_body(jl, L):
            kT2, vall, kT2s, vSc, vSn, vS32, kcb, kc32 = (L[k] for k in "kT2 vall kT2s vSc vSn vS32 kcb kc32".split())
            ktm1, k7, vtm, kb, halo, halo_in, qf, qb, qT, E, den, ob, obs, oball, sz, gg, esink = (
                L[k] for k in "ktm1 k7 vtm kb halo halo_in qf qb qT E den ob obs oball sz gg esink".split())
            nonlocal_rt = L["rt"]
            rt[:] = nonlocal_rt

            S.dma("sp", "misc", lambda e: [e.dma_start(out=esink[:], in_=a_sinks[jl].partition_broadcast(128))],
                  writes=["esink"])
            S.op("act", lambda e: e.activation(out=esink[:], in_=esink[:], func=AF.Exp), reads=["esink"], writes=["esink"])
            S.op("pool", lambda e: e.memset(ob[:], 0.0), writes=["ob"])
            S.op("pool", lambda e: e.memset(vall[:, :, :, 64:65], 1.0), writes=["vall_ones"])
            S.op("pool", lambda e: e.memset(vSc[:, :, :, 64:65], 1.0), writes=["vSc_ones"])
            S.op("pool", lambda e: e.memset(vSn[:, :, :, 64:65], 1.0), writes=["vSn_ones"])
            S.dma("pool", "cachev", lambda e: [e.dma_start(
                out=vSc[:, s, :, 0:64], in_=cv[jl, s].rearrange("r (h d) -> r h d", d=64)) for s in range(4)], n=4,
                reads=["vSc_ones"], writes=["vSc"])
            for s in range(4):
                S.dma("pool", "cachek", lambda e, s=s: [e.dma_start(out=kc32[:], in_=ck[jl, s])], writes=["kc32"])
                for half in range(2):
                    copy("dve" if half else "act", kcb[:, :, half, :], kc32[:].rearrange("r (h d) -> r h d", d=64),
                         ["kc32"], [("kcb", half)])
                bank = pt_bank()
                for h in range(4):
                    S.op("pe", lambda e, h=h, bank=bank: e.transpose(
                        out=psb[bank][:, h * 128:(h + 1) * 128], in_=kcb[:, h].rearrange("r t d -> r (t d)"), identity=ident[:]),
                        reads=[("kcb", 0), ("kcb", 1), "ident"], writes=[PS(bank)])
                copy(evac_eng(), kT2s[:, :, s * 136:s * 136 + 128], psb[bank][:, 0:512].rearrange("p (h t) -> p h t", t=128),
                     [PS(bank)], [("kT2s", s)])
            S.dma("sp", "cachecp", lambda e: [e.dma_start(out=ks[jl, :, 0:120, :], in_=ck[jl, :, 8:128, :]),
                                              e.dma_start(out=vs[jl, :, 0:120, :], in_=cv[jl, :, 8:128, :])], n=2)

            border = [7, 6, 5, 4, 3, 2, 1, 0, 8]
            if stop <= 1:
                return
            w, wres = next_tile()
            for b in border:
                P = blkP(b)
                bank = pt_bank()
                proj_tm(w, wres, b, bank)
                kt = k7 if b >= 7 else ktm1
                kres = "k7" if b >= 7 else "ktm1"
                copy("act", kt[0:P, :], ps[bank][0:P, 0:256], [PS(bank)], [kres])
                kv3 = kt[0:P, :].rearrange("p (h d) -> p h d", d=64)
                rope_tm(kv3, kv3[:, :, 0:8], kv3[:, :, 8:16], P, b, 4, [kres], [kres])
                for half in range(2):
                    copy("act" if half else "dve", kb[0:P, :, half, :], kv3, [kres], [("kb", half)])
                tb = pt_bank()
                for h in range(4):
                    S.op("pe", lambda e, h=h, P=P, tb=tb: e.transpose(
                        out=psb[tb][:, h * 128:h * 128 + P], in_=kb[0:P, h].rearrange("p t d -> p (t d)"),
                        identity=ident[0:P, 0:P]), reads=[("kb", 0), ("kb", 1), "ident"], writes=[PS(tb)])
                pv3 = psb[tb][:, 0:512].rearrange("p (h t) -> p h t", t=128)
                if b < 8:
                    copy(evac_eng(), kT2[:, :, 128 + b * 128:256 + b * 128], pv3, [PS(tb)], [("kT2", 1 + b)])
                else:
                    for s in range(4):
                        copy(evac_eng(), kT2s[:, :, s * 136 + 128:s * 136 + 136], pv3[:, :, s * 8:(s + 1) * 8],
                             [PS(tb)], [("kT2s", s)])
                if b == 7:
                    S.dma("sp", "kvout", lambda e: [e.dma_start(out=kp[jl], in_=k7[:, :])], reads=["k7"])
                    copy("dve", halo[:, 0:256], k7[:, :], ["k7"], ["halo"])
                if b == 8:
                    S.dma("sp", "kvout", lambda e: [e.dma_start(out=ks[jl, s, 120:128, :], in_=k7[s * 8:(s + 1) * 8, :])
                                                    for s in range(4)], n=4, reads=["k7"])
            if stop <= 2:
                return
            w, wres = next_tile()
            for b in border:
                P = blkP(b)
                bank = pt_bank()
                proj_tm(w, wres, b, bank)
                if stop < 2.1:
                    continue
                if b < 8:
                    copy("dve", vall[:, 1 + b, :, 0:64], ps[bank][:, 0:256].rearrange("p (h d) -> p h d", d=64),
                         [PS(bank), "vall_ones"], [("vall", 1 + b)])
                else:
                    copy("dve", vS32[:, :, :], ps[bank][0:32, 0:256].rearrange("p (h d) -> p h d", d=64), [PS(bank)], ["vS32"])
                    if stop >= 2.3:
                        S.dma("sp", "vsn", lambda e: [e.dma_start(out=vSn[:, s, :, 0:64], in_=vS32[s * 8:(s + 1) * 8, :, :])
                                                      for s in range(4)], n=4, reads=["vS32", "vSn_ones"], writes=["vSn"])
                if stop < 2.15:
                    continue
                if b >= 7:
                    copy("act", vtm[0:P, :], ps[bank][0:P, 0:256], [PS(bank), ("vall", 1 + b) if b < 8 else "vS32"], ["vtm"])
                if b == 8:
                    S.dma("sp", "kvout", lambda e: [e.dma_start(out=vs[jl, s, 120:128, :], in_=vtm[s * 8:(s + 1) * 8, :])
                                                    for s in range(4)], n=4, reads=["vtm"])
                if b == 7:
                    S.dma("sp", "kvout", lambda e: [e.dma_start(out=vp[jl], in_=vtm[:, :])], reads=["vtm"])
                    if stop < 2.2:
                        continue
                    copy("dve", halo[:, 256:512], vtm[:, :], ["vtm"], ["halo"])
                    S.dma("sp", "halo", lambda e: [e.dma_start(out=cc_a_in[jl][:, :], in_=halo[:, :])],
                          reads=["halo"], writes=["cc_a_in"])
                    if stop >= 2.6:
                      S.dma("pool", "cc_a", lambda e: [e.collective_compute(
                        "AllGather", ALU.bypass, replica_groups=PAIRS, ins=[cc_a_in[jl].opt()],
                        outs=[cc_a_out[jl].opt()])], inc=1, reads=["cc_a_in"], writes=["cc_a_out"])
                    S.dma("sp", "halo", lambda e: [e.dma_start(out=halo_in[:, :], in_=cc_a_out[jl][0:128, :])],
                          reads=["cc_a_out"], writes=["halo_in"])
            if stop <= 3:
                return
            for half in range(2):
                copy("dve", kb[:, :, half, :], halo_in[:, 0:256].rearrange("p (h d) -> p h d", d=64),
                     ["halo_in"], [("kb", half)])
            copy("act", vall[:, 0, :, 0:64], halo_in[:, 256:512].rearrange("p (h d) -> p h d", d=64),
                 ["halo_in", "vall_ones"], [("vall", 0)])
            tb = pt_bank()
            for h in range(4):
                S.op("pe", lambda e, h=h, tb=tb: e.transpose(
                    out=psb[tb][:, h * 128:(h + 1) * 128], in_=kb[:, h].rearrange("p t d -> p (t d)"),
                    identity=ident[:]), reads=[("kb", 0), ("kb", 1), "ident"], writes=[PS(tb)])
            copy("dve", kT2[:, :, 0:128], psb[tb][:, 0:512].rearrange("p (h t) -> p h t", t=128), [PS(tb)], [("kT2", 0)])

            if stop <= 4:
                return
            for GG in range(8 if stop >= 10 else stop - 4):
                G = GG // 2
                wq, wqres = next_tile()

                def stP(b, wq=wq, wqres=wqres):
                    P = blkP(b)
                    i3 = b % 3
                    bank = pt_bank()
                    proj_tm(wq, wqres, b, bank)
                    copy("act", qf[i3][0:P, :], ps[bank][0:P, 0:256], [PS(bank)], [("qf", i3)])
                    q3 = qf[i3][0:P, :].rearrange("p (h d) -> p h d", d=64)
                    rope_tm(q3, q3[:, :, 0:8], q3[:, :, 8:16], P, b, 4, [("qf", i3)], [("qf", i3)])
                    copy("act", qb[i3][0:P, :], qf[i3][0:P, :], [("qf", i3)], [("qb", i3)])

                def stA(b, part, G=G, GG=GG):
                    P = blkP(b)
                    i3 = b % 3
                    e0 = (b % 2) * 4
                    if part == 1:
                        transpose_to(qb[i3], ("qb", i3), 2, P, lambda c, P=P: qT[:, c, 0:P], lambda c: "qT")
                    seqs = [None] if b < 8 else list(range(4))
                    if b == 8 and part == 2:
                        return
                    for s in seqs:
                        if s is None:
                            NQ, qc = 128, slice(0, 128)
                            kprev = kT2[:, G, b * 128:(b + 1) * 128]
                            kown = kT2[:, G, (b + 1) * 128:(b + 2) * 128]
                            vprev, vown = vall[:, b, G, 0:65], vall[:, b + 1, G, 0:65]
                            KO = 128
                            rdk = [("kT2", b), ("kT2", b + 1)]
                            rdv = [("vall", b), ("vall", b + 1)]
                        else:
                            NQ, qc = 8, slice(s * 8, s * 8 + 8)
                            kprev = kT2s[:, G, s * 136:s * 136 + 128]
                            kown = kT2s[:, G, s * 136 + 128:s * 136 + 136]
                            vprev, vown = vSc[:, s, G, 0:65], vSn[0:8, s, G, 0:65]
                            KO = 8
                            rdk = [("kT2s", s)]
                            rdv = ["vSc", "vSn"]
                        if part == 1:
                            for hl in range(4):
                                c, half = hl // 2, hl % 2
                                r0 = half * 64
                                S.op("pe", lambda e, c=c, r0=r0, half=half, kprev=kprev, qc=qc, NQ=NQ: e.matmul(
                                    ps[2 + half][:, c * 128:c * 128 + NQ], lhsT=kprev[r0:r0 + 64, :], rhs=qT[r0:r0 + 64, c, qc],
                                    start=True, stop=True), reads=rdk + ["qT"], writes=[PS(2 + half)])
                                S.op("pe", lambda e, c=c, r0=r0, half=half, kown=kown, qc=qc, NQ=NQ, KO=KO: e.matmul(
                                    ps[2 + half][0:KO, 256 + c * 128:256 + c * 128 + NQ], lhsT=kown[r0:r0 + 64, :], rhs=qT[r0:r0 + 64, c, qc],
                                    start=True, stop=True), reads=rdk + ["qT"], writes=[PS(2 + half)])
                            for i, (bnk, off) in enumerate(((2, 0), (3, 0), (2, 256), (3, 256))):
                                KP = 128 if i < 2 else KO
                                Ev = E[e0 + i][0:KP, :, 0:NQ]
                                S.op("act", lambda e, Ev=Ev, bnk=bnk, off=off, KP=KP, NQ=NQ: e.activation(
                                    out=Ev, in_=ps[bnk][0:KP, off:off + 256].rearrange("p (h q) -> p h q", q=128)[:, :, 0:NQ],
                                    func=AF.Exp, scale=0.125), reads=[PS(bnk)], writes=[("E", e0 + i)])
                                m = (mprev if i < 2 else mown)[0:KP, :, 0:NQ]
                                if i < 2 and b == 0:
                                    S.op("dve", lambda e, Ev=Ev, m=m: e.scalar_tensor_tensor(
                                        out=Ev, in0=Ev, scalar=flag[:, 0:1], in1=m, op0=ALU.mult, op1=ALU.mult),
                                        reads=[("E", e0 + i), "flag", "mprev", "mown"], writes=[("E", e0 + i)])
                                else:
                                    S.op("dve", lambda e, Ev=Ev, m=m: e.tensor_tensor(out=Ev, in0=Ev, in1=m, op=ALU.mult),
                                         reads=[("E", e0 + i), "mprev", "mown"], writes=[("E", e0 + i)])
                            if s is None:
                                continue
                        bnk = 6 + ((b if s is None else s) % 2)
                        for hl in range(4):
                            c, half = hl // 2, hl % 2
                            S.op("pe", lambda e, c=c, half=half, hl=hl, bnk=bnk, vprev=vprev, NQ=NQ: e.matmul(
                                ps[bnk][0:NQ, hl * 65:(hl + 1) * 65], lhsT=E[e0 + half][:, c, 0:NQ], rhs=vprev,
                                start=True, stop=False), reads=[("E", e0 + half)] + rdv, writes=[PS(bnk)])
                            S.op("pe", lambda e, c=c, half=half, hl=hl, bnk=bnk, vown=vown, NQ=NQ, KO=KO: e.matmul(
                                ps[bnk][0:NQ, hl * 65:(hl + 1) * 65], lhsT=E[e0 + 2 + half][0:KO, c, 0:NQ], rhs=vown,
                                start=False, stop=True), reads=[("E", e0 + 2 + half)] + rdv, writes=[PS(bnk)])
                        pv = ps[bnk][0:NQ, 0:260].rearrange("p (h e) -> p h e", e=65)
                        es_ = esink[0:NQ, GG * 4:(GG + 1) * 4]
                        dn = den[0:NQ, 0:4]
                        S.op("dve", lambda e, pv=pv, es_=es_, dn=dn: e.tensor_tensor(
                            out=dn, in0=pv[:, :, 64], in1=es_, op=ALU.add), reads=[PS(bnk), "esink"], writes=["den"])
                        S.op("dve", lambda e, dn=dn: e.reciprocal(out=dn, in_=dn), reads=["den"], writes=["den"])
                        if s is None:
                            obv = oball[0:NQ, b, :].rearrange("p (h d) -> p h d", d=64)
                            wr = [("oball", b)]
                        else:
                            obv = obs[0:NQ]
                            wr = ["obs"]
                        S.op("dve", lambda e, pv=pv, dn=dn, obv=obv, NQ=NQ: e.tensor_tensor(
                            out=obv, in0=pv[:, :, 0:64], in1=dn.unsqueeze(2).to_broadcast([NQ, 4, 64]), op=ALU.mult),
                            reads=[PS(bnk), "den"], writes=wr)
                        if s is not None:
                            S.dma("sp", "obs", lambda e, s=s, b=b: [e.dma_start(
                                out=oball[s * 8:(s + 1) * 8, b, :], in_=obs[:, :, :].rearrange("p h d -> p (h d)"))],
                                reads=["obs"], writes=[("oball", b)])

                stP(0)
                stP(1)
                for b in range(8):
                    stA(b, 1)
                    if b + 2 <= 8:
                        stP(b + 2)
                    stA(b, 2)
                stA(8, 1)

                wz, wzres = next_tile()

                def stZP(b, wz=wz, wzres=wzres):
                    P = blkP(b)
                    i2 = b % 2
                    bank = pt_bank()
                    proj_tm(wz, wzres, b, bank)
                    S.op("act", lambda e, P=P, bank=bank, i2=i2: e.activation(out=sz[i2][0:P, :], in_=ps[bank][0:P, 0:256], func=AF.Silu),
                         reads=[PS(bank)], writes=[("sz", i2)])
                    S.op("dve", lambda e, P=P, b=b, i2=i2: e.tensor_tensor(out=gg[i2][0:P, :], in0=sz[i2][0:P, :], in1=oball[0:P, b, :], op=ALU.mult),
                         reads=[("sz", i2), ("oball", b)], writes=[("gg", i2)])

                def stZT(b, GG=GG):
                    P = blkP(b)
                    i2 = b % 2
                    transpose_to(gg[i2], ("gg", i2), 2, P, lambda c, b=b, P=P, GG=GG: goT[:, 2 * GG + c, tok0(b):tok0(b) + P],
                                 lambda c, b=b: ("goT", b))

                stZP(0)
                for b in range(NB):
                    if b + 1 < NB:
                        stZP(b + 1)
                    stZT(b)

        TT = [(0, 512), (512, 512), (1024, 32)]

        def hT_res(T0, N):
            return [("hT", b) for b in range(T0 // 128, (T0 + N + 127) // 128)]

        def goT_res(T0, N):
            return [("goT", b) for b in range(T0 // 128, (T0 + N + 127) // 128)]

        def proj_fm(w, wres, col0, T0, N, bank):
            for k in range(16):
                S.op("pe", lambda e, k=k, w=w, col0=col0, T0=T0, N=N, bank=bank: e.matmul(
                    ps[bank][:, 0:N], lhsT=w[:, k, col0:col0 + 128], rhs=hT[:, k, T0:T0 + N],
                    start=(k == 0), stop=(k == 15)), reads=[wres] + hT_res(T0, N), writes=[PS(bank)])

        def tm_to_fm(src, src_res, R, dst, dst_res):
            bank = pt_bank()
            for j in range(16):
                S.op("pe", lambda e, j=j, bank=bank: e.transpose(out=ps[bank][:, j * R:(j + 1) * R], in_=src[0:R, j * 128:(j + 1) * 128],
                                                             identity=identf[0:R, 0:R]), reads=[src_res, "identf"], writes=[PS(bank)])
            copy("dve", dst, ps[bank][:, 0:16 * R].rearrange("p (j r) -> p j r", r=R), [PS(bank)], [dst_res])

        def fm_to_tm_out(src, src_res, R, stage, dram_out, slot):
            for g0 in range(0, 16, 4):
                bank = pt_bank()
                for jj_ in range(4):
                    j = g0 + jj_
                    S.op("pe", lambda e, j=j, jj_=jj_, bank=bank: e.transpose(
                        out=ps[bank][0:R, jj_ * 128:(jj_ + 1) * 128], in_=src[:, j, :], identity=identf[:]),
                        reads=[src_res, "identf"], writes=[PS(bank)])
                copy("act", stage[0:R, g0 * 128:(g0 + 4) * 128], ps[bank][0:R, 0:512], [PS(bank)], [("stage", slot)])
            S.dma("sp", slot, lambda e: [e.dma_start(out=dram_out, in_=stage[0:R, :])], reads=[("stage", slot)])

        def conv_phase():
            with contextlib.ExitStack() as ph:
                bsb = T("bsb", [128, NTOK], F32, ph)
                csb = T("csb", [128, NTOK], F32, ph)
                uT = T("uT", [128, 1026], F32, ph)
                usT = T("usT", [128, 4, 10], F32, ph)
                szb = T("szb", [128, 512], F32, ph)
                acc = T("acc", [128, 512], F32, ph)
                cwt = T("bufA", [8, D], F32, ph)
                cw = T("cw", [128, 16, 3], F32, ph)
                sct0 = T("bufB", [8, D], F32, ph)
                sct = T("sct", [128, 16, 8], F32, ph)
                ulast = T("ulast", [128, 16, 2], F32, ph)
                uls = T("uls", [128, 16, 8], F32, ph)
                bz01 = T("bz01", [128, 16, 2], F32, ph)
                cv01 = T("cv01", [128, 16, 2], F32, ph)
                uh = T("uh", [128, 16, 2], F32, ph)
                tmp2 = [T(f"tmp2{i}", [128, 16], F32, ph) for i in range(3)]
                stage, stage2 = cwt, sct0

                S.dma("sp", "misc", lambda e: [e.dma_start(out=cwt[0:3, :], in_=b_conv_w[0]),
                                               e.dma_start(out=sct0[:, :], in_=sconv[:, :])], n=2, writes=[("stage", "cvo1"), ("stage", "cvo2")])
                tm_to_fm(cwt, ("stage", "cvo1"), 3, cw[:], "cw")
                tm_to_fm(sct0, ("stage", "cvo2"), 8, sct[:], "sct")
                S.op("pool", lambda e: e.memset(uT[:, 0:2], 0.0), writes=["uT"])
                for j in range(16):
                    w0_, w0res = next_tile()
                    for (T0, N) in TT:
                        bank = pt_bank()
                        proj_fm(w0_, w0res, 0, T0, N, bank)
                        copy("act", bsb[:, T0:T0 + N], ps[bank][:, 0:N], [PS(bank)], [("bsb", T0)])
                        bank = pt_bank()
                        proj_fm(w0_, w0res, 128, T0, N, bank)
                        copy("act", csb[:, T0:T0 + N], ps[bank][:, 0:N], [PS(bank)], [("csb", T0)])
                    w1_, w1res = next_tile()
                    copy("dve", usT[:, :, 0:2], sct[:, j, :].rearrange("p (s i) -> p s i", i=2), ["sct"], ["usT"])
                    for (T0, N) in TT:
                        bank = pt_bank()
                        proj_fm(w1_, w1res, 0, T0, N, bank)
                        if T0 < 1024:
                            S.op("dve", lambda e, T0=T0, N=N, bank=bank: e.tensor_tensor(
                                out=uT[:, 2 + T0:2 + T0 + N], in0=csb[:, T0:T0 + N], in1=ps[bank][:, 0:N], op=ALU.mult),
                                reads=[PS(bank), ("csb", T0)], writes=["uT"])
                        else:
                            S.op("dve", lambda e, T0=T0, N=N, bank=bank: e.tensor_tensor(
                                out=usT[:, :, 2:10], in0=csb[:, T0:T0 + N].rearrange("p (s t) -> p s t", t=8),
                                in1=ps[bank][:, 0:N].rearrange("p (s t) -> p s t", t=8), op=ALU.mult),
                                reads=[PS(bank), ("csb", T0)], writes=["usT"])
                        bank = pt_bank()
                        proj_fm(w1_, w1res, 128, T0, N, bank)
                        S.op("act", lambda e, N=N, bank=bank: e.activation(out=szb[:, 0:N], in_=ps[bank][:, 0:N], func=AF.Silu),
                             reads=[PS(bank)], writes=["szb"])
                        S.op("dve", lambda e, T0=T0, N=N: e.tensor_tensor(out=szb[:, 0:N], in0=szb[:, 0:N], in1=bsb[:, T0:T0 + N], op=ALU.mult),
                             reads=["szb", ("bsb", T0)], writes=["szb"])
                        if T0 < 1024:
                            u0, u1, u2 = uT[:, T0:T0 + N], uT[:, T0 + 1:T0 + 1 + N], uT[:, T0 + 2:T0 + 2 + N]
                            a_, ures = acc[:, 0:N], "uT"
                            sz_ = szb[:, 0:N]
                            gout = goT[:, j, T0:T0 + N]
                        else:
                            u0, u1, u2 = usT[:, :, 0:8], usT[:, :, 1:9], usT[:, :, 2:10]
                            a_, ures = acc[:, 0:32].rearrange("p (s t) -> p s t", t=8), "usT"
                            sz_ = szb[:, 0:32].rearrange("p (s t) -> p s t", t=8)
                            gout = goT[:, j, T0:T0 + N].rearrange("p (s t) -> p s t", t=8)
                        S.op("dve", lambda e, j=j, u0=u0, a_=a_: e.tensor_scalar(out=a_, in0=u0, scalar1=cw[:, j, 0:1], scalar2=None, op0=ALU.mult),
                             reads=[ures, "cw"], writes=["acc"])
                        S.op("dve", lambda e, j=j, u1=u1, a_=a_: e.scalar_tensor_tensor(out=a_, in0=u1, scalar=cw[:, j, 1:2], in1=a_, op0=ALU.mult, op1=ALU.add),
                             reads=[ures, "cw", "acc"], writes=["acc"])
                        S.op("dve", lambda e, j=j, u2=u2, a_=a_: e.scalar_tensor_tensor(out=a_, in0=u2, scalar=cw[:, j, 2:3], in1=a_, op0=ALU.mult, op1=ALU.add),
                             reads=[ures, "cw", "acc"], writes=["acc"])
                        if T0 == 0:
                            copy("act", bz01[:, j, :], szb[:, 0:2], ["szb"], ["bz01"])
                            copy("act", cv01[:, j, :], acc[:, 0:2], ["acc"], ["cv01"])
                        S.op("dve", lambda e, a_=a_, sz_=sz_, gout=gout: e.tensor_tensor(out=gout, in0=a_, in1=sz_, op=ALU.mult),
                             reads=["acc", "szb"], writes=goT_res(T0, N))
                    copy("act", ulast[:, j, :], uT[:, 1024:1026], ["uT"], ["ulast"])
                    copy("act", uls[:, j, :].rearrange("p (s i) -> p s i", i=2), usT[:, :, 8:10], ["usT"], ["uls"])
                S.dma("sp", "cvx", lambda e: [e.dma_start(out=cc_b_in[:, :], in_=ulast[:].rearrange("p j i -> p (j i)"))],
                      reads=["ulast"], writes=["cc_b_in"])
                S.dma("pool", "cc_b", lambda e: [e.collective_compute(
                    "AllGather", ALU.bypass, replica_groups=PAIRS, ins=[cc_b_in.opt()], outs=[cc_b_out.opt()])],
                    inc=1, reads=["cc_b_in"], writes=["cc_b_out"])
                S.dma("sp", "cvx", lambda e: [e.dma_start(out=uh[:].rearrange("p j i -> p (j i)"), in_=cc_b_out[0:128, :])],
                      reads=["cc_b_out"], writes=["uh"])
                S.op("dve", lambda e: e.tensor_scalar(out=uh[:], in0=uh[:], scalar1=flag[:, 0:1], scalar2=None, op0=ALU.mult),
                     reads=["uh", "flag"], writes=["uh"])
                t0_, t1_, t2_ = tmp2[0][:], tmp2[1][:], tmp2[2][:]
                S.op("dve", lambda e: e.tensor_tensor(out=t0_, in0=cw[:, :, 0], in1=uh[:, :, 0], op=ALU.mult), reads=["cw", "uh"], writes=["t0"])
                S.op("dve", lambda e: e.tensor_tensor(out=t1_, in0=cw[:, :, 1], in1=uh[:, :, 1], op=ALU.mult), reads=["cw", "uh"], writes=["t1"])
                S.op("dve", lambda e: e.tensor_tensor(out=t2_, in0=cw[:, :, 0], in1=uh[:, :, 1], op=ALU.mult), reads=["cw", "uh"], writes=["t2"])
                S.op("dve", lambda e: e.tensor_tensor(out=t0_, in0=t0_, in1=t1_, op=ALU.add), reads=["t0", "t1"], writes=["t0"])
                S.op("dve", lambda e: e.tensor_tensor(out=cv01[:, :, 0], in0=cv01[:, :, 0], in1=t0_, op=ALU.add), reads=["t0", "cv01"], writes=["cv01"])
                S.op("dve", lambda e: e.tensor_tensor(out=cv01[:, :, 1], in0=cv01[:, :, 1], in1=t2_, op=ALU.add), reads=["t2", "cv01"], writes=["cv01"])
                S.op("dve", lambda e: e.tensor_tensor(out=goT[:, :, 0:2], in0=cv01[:], in1=bz01[:], op=ALU.mult),
                     reads=["cv01", "bz01"], writes=[("goT", 0)])
                fm_to_tm_out(ulast, "ulast", 2, stage, convp[:, :], "cvo1")
                fm_to_tm_out(uls, "uls", 8, stage2, convs[:, :], "cvo2")
            S.barrier()

        def hgrn_phase(li):
            with contextlib.ExitStack() as ph:
                clb0 = T("clb0", [64, 128], F32, ph)
                clbf = T("clbf", [128, 4, 16], F32, ph)
                lbt = T("lbt", [128, 16], F32, ph)
                omlt = T("omlt", [128, 16], F32, ph)
                dent = T("dent", [128, 16], F32, ph)
                ng = T("ng", [128, 1], F32, ph)
                m64 = T("m64", [128, 512], BF16, ph)
                m8 = T("m8", [128, 32], BF16, ph)
                mone = T("mone", [128, 512], BF16, ph)
                t_sq = T("t_sq", [128, NTOK], F32, ph)
                t_fg = T("t_fg", [128, NTOK], F32, ph)
                t_lf = T("t_lf", [128, NTOK], F32, ph)
                t_b = T("t_b", [128, NTOK], F32, ph)
                t_bg = T("t_bg", [128, NTOK], F32, ph)
                t_sq2 = T("t_sq2", [128, NTOK], BF16, ph)
                qtT = T("qtT", [128, NTOK], BF16, ph)
                kiT = T("kiT", [128, NTOK], BF16, ph)
                ksT = T("ksT", [128, NTOK], BF16, ph)
                qgT = T("qgT", [128, 1024], BF16, ph)
                vT = T("vT", [128, NTOK], BF16, ph)
                szT = T("szT", [128, NTOK], BF16, ph)
                oTs = t_sq
                ebl = T("ebl", [128, 20], F32, ph)
                ebt = T("ebt", [128, 1], F32, ph)
                bgl = T("bgl", [128, 1], F32, ph)
                Sloc = T("Sloc", [128, 16, 128], F32, ph)
                Sball = t_lf[:].bitcast(BF16)[:, 0:1920].rearrange("p (c d) -> p c d", d=128)
                SA = T("SA", [128, 128], F32, ph)
                SAb = T("SAb", [128, 128], BF16, ph)
                Sf = T("Sf", [128, 128], F32, ph)
                SfP = [Sf, T("Sf1", [128, 128], F32, ph)]
                SbP = [SAb, T("SAb1", [128, 128], BF16, ph)]
                kvtm = [T(f"kvtm{i}", [64, 256], BF16, ph) for i in range(2)]
                att = [T(f"att{i}", [64, 64], BF16, ph) for i in range(2)]

                S.dma("sp", "misc", lambda e: [e.dma_start(out=clb0[:, :], in_=c_lb.rearrange("r (j p) -> (r j) p", p=128)),
                                               e.dma_start(out=ng[:, :], in_=c_norm_g[0].rearrange("(p o) -> p o", o=1))],
                      n=2, writes=["clb0", "ng"])
                bank = pt_bank()
                S.op("pe", lambda e, bank=bank: e.transpose(out=ps[bank][:, 0:64], in_=clb0[:, :], identity=identf[0:64, 0:64]),
                     reads=["clb0", "identf"], writes=[PS(bank)])
                S.op("act", lambda e, bank=bank: e.activation(out=clbf[:].rearrange("p r j -> p (r j)"), in_=ps[bank][:, 0:64], func=AF.Exp),
                     reads=[PS(bank)], writes=["clbf"])
                assert li == 2
                S.op("dve", lambda e: e.tensor_tensor(out=dent[:], in0=clbf[:, 0, :], in1=clbf[:, 1, :], op=ALU.add), reads=["clbf"], writes=["dent"])
                S.op("dve", lambda e: e.tensor_tensor(out=lbt[:], in0=clbf[:, 2, :], in1=clbf[:, 3, :], op=ALU.add), reads=["clbf"], writes=["lbt"])
                S.op("dve", lambda e: e.tensor_tensor(out=dent[:], in0=dent[:], in1=lbt[:], op=ALU.add), reads=["dent", "lbt"], writes=["dent"])
                S.op("dve", lambda e: e.reciprocal(out=dent[:], in_=dent[:]), reads=["dent"], writes=["dent"])
                S.op("dve", lambda e: e.tensor_tensor(out=lbt[:], in0=clbf[:, 1, :], in1=clbf[:, 2, :], op=ALU.add), reads=["clbf", "lbt"], writes=["lbt"])
                S.op("dve", lambda e: e.tensor_tensor(out=lbt[:], in0=lbt[:], in1=dent[:], op=ALU.mult), reads=["lbt", "dent"], writes=["lbt"])
                S.op("dve", lambda e: e.tensor_scalar(out=omlt[:], in0=lbt[:], scalar1=-1.0, scalar2=1.0, op0=ALU.mult, op1=ALU.add),
                     reads=["lbt"], writes=["omlt"])
                S.op("pool", lambda e: e.memset(mone[:], 1.0), writes=["mone"])
                S.op("pool", lambda e: e.memset(m64[:], 1.0), writes=["m64"])
                S.op("pool", lambda e: e.memset(m64[:].rearrange("p (c t) -> p c t", t=64)[:, :, 0:1], 0.0), reads=["m64"], writes=["m64"])
                S.op("pool", lambda e: e.memset(m8[:], 1.0), writes=["m8"])
                S.op("pool", lambda e: e.memset(m8[:].rearrange("p (c t) -> p c t", t=8)[:, :, 0:1], 0.0), reads=["m8"], writes=["m8"])

                def fq_proj():
                    w0_, w0res = next_tile(prefetch=False)
                    issue_load()
                    for ti, (T0, N) in enumerate(TT):
                        proj_fm(w0_, w0res, 128, T0, N, 3)
                        S.op("act", lambda e, T0=T0, N=N: e.activation(out=t_fg[:, T0:T0 + N], in_=ps[3][:, 0:N], func=AF.Sigmoid),
                             reads=[PS(3)], writes=[("t_fg", ti)])
                    evs = []
                    for ti, (T0, N) in enumerate(TT):
                        bank = 4 + ti
                        proj_fm(w0_, w0res, 0, T0, N, bank)
                        evs.append(lambda ti=ti, T0=T0, N=N, bank=bank: S.op(
                            "act", lambda e: e.activation(out=t_sq[:, T0:T0 + N], in_=ps[bank][:, 0:N], func=AF.Silu),
                            reads=[PS(bank)], writes=[("t_sq", ti)]))
                    return evs

                for j in range(16):
                    def hb_bank():
                        rr["hb"] = (rr.get("hb", -1) + 1) % 7
                        return rr["hb"]
                    if j == 0:
                        for ev_ in fq_proj():
                            ev_()
                    w1_, w1res = next_tile(prefetch=False)
                    issue_load()

                    def chain(ti, T0, N, j=j):
                        C = 64 if T0 < 1024 else 8
                        nch = N // C
                        c0 = T0 // 64
                        msk = m64 if T0 < 1024 else m8
                        sl = slice(T0, T0 + N)
                        fg, lf, bb, bg, sq_ = t_fg[:, sl], t_lf[:, sl], t_b[:, sl], t_bg[:, sl], t_sq[:, sl]
                        R = lambda n: (n, ti)
                        st = []
                        st.append(lambda: S.op("dve", lambda e: e.tensor_scalar(out=fg, in0=fg, scalar1=omlt[:, j:j + 1], scalar2=lbt[:, j:j + 1],
                                                                                op0=ALU.mult, op1=ALU.add),
                                               reads=[R("t_fg"), "omlt", "lbt"], writes=[R("t_fg")]))
                        st.append(lambda: S.op("act", lambda e: e.activation(out=lf, in_=fg, func=AF.Ln), reads=[R("t_fg")], writes=[R("t_lf")]))
                        st.append(lambda: S.op("dve", lambda e: e.tensor_scalar(out=fg, in0=fg, scalar1=-1.0, scalar2=1.0, op0=ALU.mult, op1=ALU.add),
                                               reads=[R("t_fg"), R("t_lf")], writes=[R("t_fg")]))
                        st.append(lambda: S.op("dve", lambda e: e.tensor_tensor_scan(out=bb, data0=msk[:, 0:N], data1=lf, initial=0.0,
                                                                                     op0=ALU.mult, op1=ALU.add),
                                               reads=[R("t_lf"), "m64", "m8"], writes=[R("t_b")]))
                        if T0 < 1024:
                            init = 0.0 if ti == 0 else bgl[:, 0:1]

                            def gsc():
                                S.op("dve", lambda e: e.tensor_tensor_scan(out=bg, data0=mone[:, 0:N], data1=lf, initial=init,
                                                                           op0=ALU.mult, op1=ALU.add),
                                     reads=[R("t_lf"), "mone", "bgl"], writes=[R("t_bg")])
                                copy("dve", bgl[:, 0:1], t_bg[:, T0 + N - 1:T0 + N], [R("t_bg")], ["bgl"])
                                if ti == 1:
                                    S.op("act", lambda e: e.activation(out=ebt[:, 0:1], in_=bgl[:, 0:1], func=AF.Exp), reads=["bgl"], writes=["ebt"])
                            st.append(gsc)
                        else:
                            st.append(lambda: None)
                        st.append(lambda: S.op("act", lambda e: e.activation(out=lf, in_=bb, func=AF.Exp), reads=[R("t_b"), R("t_bg")], writes=[R("t_lf")]))
                        st.append(lambda: S.op("dve", lambda e: e.tensor_tensor(out=qtT[:, sl], in0=sq_, in1=lf, op=ALU.mult),
                                               reads=[R("t_sq"), R("t_lf")], writes=[("qtT", ti)]))
                        st.append(lambda: S.op("act", lambda e: e.activation(out=lf, in_=bb, func=AF.Exp, scale=-1.0),
                                               reads=[R("t_b"), ("qtT", ti)], writes=[R("t_lf")]))
                        st.append(lambda: S.op("dve", lambda e: e.tensor_tensor(out=fg, in0=fg, in1=lf, op=ALU.mult),
                                               reads=[R("t_fg"), R("t_lf")], writes=[R("t_fg")]))
                        st.append(lambda: S.op("act", lambda e: e.activation(
                            out=ebl[:, c0:c0 + nch], in_=bb.rearrange("p (c t) -> p c t", t=C)[:, :, C - 1], func=AF.Exp),
                            reads=[R("t_b")], writes=[("ebl", ti)]))
                        st.append(lambda: copy("pool", kiT[:, sl], fg, [R("t_fg")], [("kiT", ti)]))
                        st.append(lambda: S.op("dve", lambda e: e.tensor_tensor(
                            out=ksT[:, sl].rearrange("p (c t) -> p c t", t=C), in0=fg.rearrange("p (c t) -> p c t", t=C),
                            in1=ebl[:, c0:c0 + nch].unsqueeze(2).to_broadcast([128, nch, C]), op=ALU.mult),
                            reads=[R("t_fg"), ("ebl", ti)], writes=[("ksT", ti)]))
                        if T0 < 1024:
                            st.append(lambda: S.op("act", lambda e: e.activation(out=bg, in_=bg, func=AF.Exp), reads=[R("t_bg")], writes=[R("t_bg")]))
                            st.append(lambda: S.op("dve", lambda e: e.tensor_tensor(out=qgT[:, sl], in0=sq_, in1=bg, op=ALU.mult),
                                                   reads=[R("t_sq"), R("t_bg")], writes=[("qgT", ti)]))
                        return st

                    chains = [chain(ti, T0, N) for ti, (T0, N) in enumerate(TT)]
                    for k in range(max(len(c_) for c_ in chains)):
                        for c_ in chains:
                            if k < len(c_):
                                c_[k]()

                    for ti, (T0, N) in enumerate(TT):
                        bank = hb_bank()
                        proj_fm(w1_, w1res, 128, T0, N, bank)
                        S.op("act", lambda e, N=N, T0=T0, bank=bank: e.activation(out=szT[:, T0:T0 + N], in_=ps[bank][:, 0:N], func=AF.Silu),
                             reads=[PS(bank)], writes=[("szT", ti)])
                    for ti, (T0, N) in enumerate(TT):
                        bank = hb_bank()
                        proj_fm(w1_, w1res, 0, T0, N, bank)
                        copy("act", vT[:, T0:T0 + N], ps[bank][:, 0:N], [PS(bank)], [("vT", ti)])

                    def chunk_step(t0, C, ci, Sf32, Sbf, sres, par):
                        ti = 0 if t0 < 512 else (1 if t0 < 1024 else 2)
                        kv = kvtm[par]
                        at = att[par]
                        tb = 7
                        S.op("pe", lambda e: e.transpose(out=psb[tb][0:C, 0:128], in_=ksT[:, t0:t0 + C], identity=ident[:]),
                             reads=[("ksT", ti), "ident"], writes=[PS(tb)])
                        S.op("pe", lambda e: e.transpose(out=psb[tb][0:C, 128:256], in_=vT[:, t0:t0 + C], identity=ident[:]),
                             reads=[("vT", ti), "ident"], writes=[PS(tb)])
                        copy("act", kv[0:C, :], psb[tb][0:C, 0:256], [PS(tb)], [("kvtm", par)])
                        S.op("pe", lambda e: e.matmul(ps[2][0:C, 0:C], lhsT=kiT[:, t0:t0 + C], rhs=qtT[:, t0:t0 + C], start=True, stop=True),
                             reads=[("kiT", ti), ("qtT", ti)], writes=[PS(2)])
                        S.op("dve", lambda e: e.tensor_tensor(out=at[0:C, 0:C], in0=ps[2][0:C, 0:C], in1=mown[0:C, 0, 0:C], op=ALU.mult),
                             reads=[PS(2), "mown"], writes=[("att", par)])
                        ob_ = par
                        S.op("pe", lambda e: e.matmul(ps[ob_][:, 0:C], lhsT=Sbf[:, :], rhs=qtT[:, t0:t0 + C], start=True, stop=False),
                             reads=[sres + "b", ("qtT", ti)], writes=[PS(ob_)])
                        S.op("pe", lambda e: e.matmul(ps[ob_][:, 0:C], lhsT=kv[0:C, 128:256], rhs=at[0:C, 0:C], start=False, stop=True),
                             reads=[("kvtm", par), ("att", par)], writes=[PS(ob_)])
                        copy("act", oTs[:, t0:t0 + C], ps[ob_][:, 0:C], [PS(ob_)], [("t_sq", ti)])
                        sb_ = 7
                        S.op("pe", lambda e: e.matmul(ps[sb_][:, 0:128], lhsT=kv[0:C, 0:128], rhs=kv[0:C, 128:256], start=True, stop=True),
                             reads=[("kvtm", par)], writes=[PS(sb_)])
                        S.op("dve", lambda e: e.scalar_tensor_tensor(out=Sf32[:, :], in0=Sf32[:, :], scalar=ebl[:, ci:ci + 1], in1=ps[sb_][:, 0:128],
                                                                     op0=ALU.mult, op1=ALU.add),
                             reads=[PS(sb_), sres, ("ebl", ti)], writes=[sres])
                        copy("act", Sbf[:, :], Sf32[:, :], [sres], [sres + "b"])

                    TLR = [("t_lf", 0), ("t_lf", 1), ("t_lf", 2)]

                    def l_tr(c):
                        t0, ti, par = c * 64, (0 if c < 8 else 1), c % 2
                        kv = kvtm[par]
                        S.op("pe", lambda e: e.transpose(out=psb[7][0:64, 0:128], in_=ksT[:, t0:t0 + 64], identity=ident[:]),
                             reads=[("ksT", ti), "ident"], writes=[PS(7)])
                        S.op("pe", lambda e: e.transpose(out=psb[7][0:64, 128:256], in_=vT[:, t0:t0 + 64], identity=ident[:]),
                             reads=[("vT", ti), "ident"], writes=[PS(7)])
                        copy("act", kv[0:64, :], psb[7][0:64, 0:256], [PS(7)], [("kvtm", par)])

                    def l_att(c):
                        t0, ti, par = c * 64, (0 if c < 8 else 1), c % 2
                        at = att[par]
                        S.op("pe", lambda e: e.matmul(ps[2][0:64, 0:64], lhsT=kiT[:, t0:t0 + 64], rhs=qtT[:, t0:t0 + 64], start=True, stop=True),
                             reads=[("kiT", ti), ("qtT", ti)], writes=[PS(2)])
                        S.op("dve", lambda e: e.tensor_tensor(out=at[0:64, 0:64], in0=ps[2][0:64, 0:64], in1=mown[0:64, 0, 0:64], op=ALU.mult),
                             reads=[PS(2), "mown"], writes=[("att", par)])

                    def q_half(half):
                        bank = 3 + half
                        for c in range(8 * half, 8 * half + 8):
                            if c == 0:
                                continue
                            col = (c % 8) * 64
                            S.op("pe", lambda e, c=c, col=col: e.matmul(
                                ps[bank][:, col:col + 64], lhsT=Sball[:, c - 1, :], rhs=qtT[:, c * 64:(c + 1) * 64], start=True, stop=True),
                                reads=[("Sball", (c - 1) // 8), ("qtT", half)] + TLR, writes=[PS(bank)])
                        lo = 64 if half == 0 else 0
                        T0_ = half * 512
                        S.op("dve", lambda e: e.tensor_tensor(
                            out=oTs[:, T0_ + lo:T0_ + 512], in0=oTs[:, T0_ + lo:T0_ + 512], in1=ps[bank][:, lo:512], op=ALU.add),
                            reads=[PS(bank), ("t_sq", half)], writes=[("t_sq", half)])

                    def l_mm(c):
                        ti, par = (0 if c < 8 else 1), c % 2
                        kv, at = kvtm[par], att[par]
                        ob_ = 3 + c // 8
                        col = (c % 8) * 64
                        S.op("pe", lambda e: e.matmul(ps[ob_][:, col:col + 64], lhsT=kv[0:64, 128:256], rhs=at[0:64, 0:64], start=True, stop=True),
                             reads=[("kvtm", par), ("att", par)], writes=[PS(ob_)])
                        sb_ = 5 + (c // 4) % 2
                        scol = (c % 4) * 128
                        S.op("pe", lambda e: e.matmul(ps[sb_][:, scol:scol + 128], lhsT=kv[0:64, 0:128], rhs=kv[0:64, 128:256], start=True, stop=True),
                             reads=[("kvtm", par)], writes=[PS(sb_)])
                        if c % 4 == 3:
                            copy("dve", Sloc[:, c - 3:c + 1, :], ps[sb_][:, 0:512].rearrange("p (c d) -> p c d", d=128),
                                 [PS(sb_)], [("Sl", cc_) for cc_ in range(c - 3, c + 1)])
                        if c % 8 == 7:
                            T0_ = (c // 8) * 512
                            copy("act", oTs[:, T0_:T0_ + 512], ps[ob_][:, 0:512], [PS(ob_)], [("t_sq", ti)])
                        if c % 4 == 3:
                            for c2 in range(max(c - 3, 1), c + 1):
                                ti2 = 0 if c2 < 8 else 1
                                S.op("dve", lambda e, c2=c2: e.scalar_tensor_tensor(out=Sloc[:, c2, :], in0=Sloc[:, c2 - 1, :], scalar=ebl[:, c2:c2 + 1],
                                                                                  in1=Sloc[:, c2, :], op0=ALU.mult, op1=ALU.add),
                                     reads=[("Sl", c2), ("Sl", c2 - 1), ("ebl", ti2)], writes=[("Sl", c2)])
                        if c == 7 or c == 15:
                            half = c // 8
                            lo_c, hi_c = (0, 8) if half == 0 else (8, 15)
                            copy("act", Sball[:, lo_c:hi_c, :], Sloc[:, lo_c:hi_c, :], [("Sl", cc_) for cc_ in range(lo_c, hi_c)],
                                 [("Sball", half)] + TLR)

                    l_tr(0)
                    l_att(0)
                    for c in range(16):
                        if c + 1 < 16:
                            l_tr(c + 1)
                            l_att(c + 1)
                        l_mm(c)
                        if c == 11:
                            q_half(0)
                    q_half(1)
                    S.dma("sp", "hgx", lambda e, j=j: [e.dma_start(out=cc_h_in[j][:, :], in_=Sloc[:, 15, :])], reads=[("Sl", 15)], writes=["cc_h_in"])
                    S.dma("pool", "cc_h", lambda e, j=j: [e.collective_compute(
                        "AllGather", ALU.bypass, replica_groups=PAIRS, ins=[cc_h_in[j].opt()], outs=[cc_h_out[j].opt()])],
                        inc=1, reads=["cc_h_in"], writes=["cc_h_out"])
                    q_evs = fq_proj() if j + 1 < 16 else []
                    for s_ in range(4):
                        pq = s_ % 2
                        Sfq, Sbq, nmq = SfP[pq], SbP[pq], f"Sf{pq}"
                        S.dma("sp", ("hgs_in", pq), lambda e, s_=s_, j=j, Sfq=Sfq: [e.dma_start(out=Sfq[:, :], in_=shg[s_, j])], writes=[nmq])
                        copy("act", Sbq[:, :], Sfq[:, :], [nmq], [nmq + "b"])
                        chunk_step(1024 + s_ * 8, 8, 16 + s_, Sfq, Sbq, nmq, pq)
                        S.dma("sp", ("hgs_out", pq), lambda e, s_=s_, j=j, Sfq=Sfq: [e.dma_start(out=hgs[s_, j], in_=Sfq[:, :])], reads=[nmq])
                    S.dma("sp", "hgx", lambda e, j=j: [e.dma_start(out=SA[:, :], in_=cc_h_out[j][0:128, :])], reads=["cc_h_out"], writes=["SA"])
                    S.op("dve", lambda e: e.tensor_scalar(out=SA[:, :], in0=SA[:, :], scalar1=flag[:, 0:1], scalar2=None, op0=ALU.mult),
                         reads=["SA", "flag"], writes=["SA"])
                    copy("act", SAb[:, :], SA[:, :], ["SA", "Sf0b"], ["SAb", "Sf0b"])
                    for ti, (T0, N) in enumerate(TT[:2]):
                        bank = pt_bank()
                        S.op("pe", lambda e, T0=T0, N=N, bank=bank: e.matmul(ps[bank][:, 0:N], lhsT=SAb[:, :], rhs=qgT[:, T0:T0 + N],
                                                                           start=True, stop=True),
                             reads=["SAb", ("qgT", ti)], writes=[PS(bank)])
                        S.op("dve", lambda e, T0=T0, N=N, bank=bank: e.tensor_tensor(out=oTs[:, T0:T0 + N], in0=oTs[:, T0:T0 + N],
                                                                                  in1=ps[bank][:, 0:N], op=ALU.add),
                             reads=[PS(bank), ("t_sq", ti)], writes=[("t_sq", ti)])
                    S.op("dve", lambda e: e.scalar_tensor_tensor(out=Sf[:, :], in0=SA[:, :], scalar=ebt[:, 0:1], in1=Sloc[:, 15, :],
                                                                 op0=ALU.mult, op1=ALU.add), reads=["SA", "ebt", ("Sl", 15), "Sf0"], writes=["Sf0"])
                    S.dma("sp", "hgp_out", lambda e, j=j: [e.dma_start(out=hgp[j], in_=Sf[:, :])], reads=["Sf0"])
                    def tail(ti, T0, N, j=j):
                        sl = slice(T0, T0 + N)
                        bank = (0, 1, 7)[ti]
                        R = lambda n: (n, ti)
                        return [
                            lambda: S.op("pool", lambda e: e.tensor_tensor(out=t_sq2[:, sl], in0=oTs[:, sl], in1=oTs[:, sl], op=ALU.mult),
                                         reads=[("t_sq", ti)], writes=[R("t_sq2")]),
                            lambda: S.op("pe", lambda e: e.matmul(ps[bank][:, 0:N], lhsT=onesb[:, :], rhs=t_sq2[:, sl], start=True, stop=True),
                                         reads=[R("t_sq2"), "onesb"], writes=[PS(bank)]),
                            lambda: S.op("act", lambda e: e.activation(out=t_b[:, sl], in_=ps[bank][:, 0:N], func=AF.Ln,
                                                                       scale=1.0 / 128.0, bias=EPS), reads=[PS(bank)], writes=[R("t_b")]),
                            lambda: S.op("act", lambda e: e.activation(out=t_b[:, sl], in_=t_b[:, sl], func=AF.Exp, scale=-0.5),
                                         reads=[R("t_b")], writes=[R("t_b")]),
                            lambda: S.op("dve", lambda e: e.tensor_tensor(out=t_b[:, sl], in0=t_b[:, sl], in1=oTs[:, sl], op=ALU.mult),
                                         reads=[R("t_b"), ("t_sq", ti)], writes=[R("t_b")]),
                            lambda: S.op("dve", lambda e: e.scalar_tensor_tensor(out=goT[:, j, sl], in0=t_b[:, sl], scalar=ng[:, 0:1],
                                                                                 in1=szT[:, sl], op0=ALU.mult, op1=ALU.mult),
                                         reads=[R("t_b"), "ng", ("szT", ti)], writes=goT_res(T0, N)),
                        ]
                    tails = [tail(ti, T0, N) for ti, (T0, N) in enumerate(TT)]
                    for k in range(6):
                        for t_ in tails:
                            t_[k]()
                    for ev_ in q_evs:
                        ev_()
            S.barrier()

        if True:
            for li, kind in enumerate(layer_kinds):
                norm_phase(ln_g[li])
                if kind == "a":
                    attn_phase(li // 3)
                elif kind == "b":
                    conv_phase()
                else:
                    hgrn_phase(li)
                wout_phase()
            norm_phase(final_g if not dbg else final_g, final=True)
        S.emit()
    nc._sched_stats = S.stats
    return nc


def rope_tables(hf):
    half = 8
    inv = 500000.0 ** (-np.arange(half, dtype=np.float64) * 2.0 / 16.0)
    pos = np.zeros((128, NB), np.float64)
    for b in range(8):
        pos[:, b] = hf * 1024 + b * 128 + np.arange(128)
    pos[:, 8] = 16384 + (np.arange(128) % 8)
    ang = pos[:, :, None].astype(np.float32).astype(np.float64) * inv.astype(np.float32).astype(np.float64)[None, None, :]
    ang = ang.astype(np.float32).astype(np.float64)
    return np.cos(ang).astype(np.float32), np.sin(ang).astype(np.float32)


_NC_CACHE = {}


def make_in_maps(inp):
    f = lambda a: np.ascontiguousarray(np.asarray(a, dtype=np.float32))
    shared = dict(
        ln_g=f(inp["ln_g"]), final_g=f(inp["final_g"]), a_w_in=f(inp["a_w_in"]), a_w_out=f(inp["a_w_out"]),
        a_sinks=f(inp["a_sinks"]), b_w_in=f(inp["b_w_in"]), b_conv_w=f(inp["b_conv_w"]), b_w_out=f(inp["b_w_out"]),
        c_w_in=f(inp["c_w_in"]), c_norm_g=f(inp["c_norm_g"]), c_w_out=f(inp["c_w_out"]), c_lb=f(inp["c_lb_logits"]))
    x_prompt, x_sample = f(inp["x_prompt"]), f(inp["x_sample"])
    cache_k, cache_v = f(inp["cache_k"]), f(inp["cache_v"])
    state_conv, state_hgrn = f(inp["state_conv"]), f(inp["state_hgrn"])
    maps = []
    for c in range(8):
        p, hf = c // 2, c % 2
        cs, sn = rope_tables(hf)
        m = dict(shared)
        m.update(
            xp=np.ascontiguousarray(x_prompt[p, hf * 1024:(hf + 1) * 1024]),
            xsm=np.ascontiguousarray(x_sample[4 * c:4 * c + 4].reshape(32, D)),
            ck=np.ascontiguousarray(cache_k[:, 4 * c:4 * c + 4].reshape(2, 4, 128, 256)),
            cv=np.ascontiguousarray(cache_v[:, 4 * c:4 * c + 4].reshape(2, 4, 128, 256)),
            sconv=np.ascontiguousarray(state_conv[0, 4 * c:4 * c + 4].reshape(8, D)),
            shg=np.ascontiguousarray(state_hgrn[0, 4 * c:4 * c + 4]),
            ropec=cs, ropes=sn, flag=np.full((128, 1), float(hf), np.float32))
        maps.append(m)
    return maps


def assemble(res):
    y_prompt = np.zeros((4, 2048, D), np.float32)
    y_sample = np.zeros((32, 8, D), np.float32)
    kpo = np.zeros((2, 4, 128, 4, 64), np.float32)
    vpo = np.zeros_like(kpo)
    kso = np.zeros((2, 32, 128, 4, 64), np.float32)
    vso = np.zeros_like(kso)
    cpo = np.zeros((1, 4, 2, D), np.float32)
    cso = np.zeros((1, 32, 2, D), np.float32)
    hpo = np.zeros((1, 4, 16, 128, 128), np.float32)
    hso = np.zeros((1, 32, 16, 128, 128), np.float32)
    for c in range(8):
        r = res[c]
        p, hf = c // 2, c % 2
        y_prompt[p, hf * 1024:(hf + 1) * 1024] = r["y"][:1024]
        y_sample[4 * c:4 * c + 4] = r["y"][1024:1056].reshape(4, 8, D)
        kso[:, 4 * c:4 * c + 4] = r["ks"].reshape(2, 4, 128, 4, 64)
        vso[:, 4 * c:4 * c + 4] = r["vs"].reshape(2, 4, 128, 4, 64)
        cso[0, 4 * c:4 * c + 4] = r["convs"].reshape(4, 2, D)
        hso[0, 4 * c:4 * c + 4] = r["hgs"]
        if hf == 1:
            kpo[:, p] = r["kp"].reshape(2, 128, 4, 64)
            vpo[:, p] = r["vp"].reshape(2, 128, 4, 64)
            cpo[0, p] = r["convp"]
            hpo[0, p] = r["hgp"]
    return (y_prompt, y_sample, kpo, vpo, kso, vso, cpo, cso, hpo, hso)


def kernel(**inputs):
    if "nc" not in _NC_CACHE:
        _NC_CACHE["nc"] = build()
    nc = _NC_CACHE["nc"]
    maps = make_in_maps(inputs)
    res = run_bass_kernel_spmd(nc, maps, core_ids=list(range(8)))
    return assemble(res.results)
```

```python
import contextlib
import os
import numpy as np
import concourse.bass as bass
import concourse.mybir as mybir
from concourse.bass_utils import run_bass_kernel_spmd

F32 = mybir.dt.float32
BF16 = mybir.dt.bfloat16
AF = mybir.ActivationFunctionType
ALU = mybir.AluOpType

D = 2048
NB = 9
NTOK = 1056
EPS = 1e-6
PAIRS = [[0, 1], [2, 3], [4, 5], [6, 7]]
COMPUTE = ("pe", "act", "dve", "pool")


class Sched:
    def __init__(self, nc):
        self.nc = nc
        self.ops = []
        self.res_w = {}
        self.res_r = {}
        self.dma_last = {}
        self.streams = {e: [] for e in ("pe", "act", "dve", "pool", "sp")}

    def _deps(self, reads, writes, idx, key):
        raw, war = set(), set()
        for r in reads:
            w = self.res_w.get(r)
            if w is not None:
                raw.add(w)
            if isinstance(r, tuple) and r[0] == "ps":
                for k2, rd in self.res_r.get(r, {}).items():
                    if k2 != key:
                        raw.add(rd)
        for r in writes:
            w = self.res_w.get(r)
            if w is not None:
                war.add(w)
            for rd in self.res_r.get(r, {}).values():
                war.add(rd)
        for r in writes:
            self.res_w[r] = idx
            self.res_r[r] = {}
        for r in reads:
            self.res_r.setdefault(r, {})[key if key is not None else ("dma", idx)] = idx
        raw.discard(idx)
        war.discard(idx)
        return raw, war

    def op(self, eng, fn, reads=(), writes=()):
        idx = len(self.ops)
        raw, war = self._deps(reads, writes, idx, eng)
        self.ops.append(dict(eng=eng, fn=fn, raw=raw, war=war, dma=None, idx=idx))
        self.streams[eng].append(idx)
        return idx

    def dma(self, queue, slot, fn, n=1, inc=16, reads=(), writes=()):
        idx = len(self.ops)
        raw, war = self._deps(reads, writes, idx, None)
        last = self.dma_last.get(slot)
        if last is not None:
            raw.add(last)
        self.dma_last[slot] = idx
        self.ops.append(dict(eng=queue, fn=fn, raw=raw, war=war, dma=slot, idx=idx, n=n, inc=inc))
        self.streams[queue].append(idx)
        return idx

    def barrier(self):
        last = set()
        for e, st in self.streams.items():
            if st:
                last.add(st[-1])
        for v in self.dma_last.values():
            last.add(v)
        for e in COMPUTE + ("sp",):
            idx = len(self.ops)
            self.ops.append(dict(eng=e, fn=None, raw=set(last), war=set(), dma=None, idx=idx))
            self.streams[e].append(idx)

    def emit(self):
        nc, ops = self.nc, self.ops

        def same_eng(p, o):
            return p["dma"] is None and o["dma"] is None and p["eng"] == "pe" and o["eng"] == "pe"

        needed = set()
        for o in ops:
            needed |= o["raw"]
            for d in o["war"]:
                if not same_eng(ops[d], o):
                    needed.add(d)
        with contextlib.ExitStack() as es:
            sem_eng = {e: es.enter_context(nc.semaphore(f"s_{e}")) for e in COMPUTE}
            slots = list(self.dma_last.keys())
            sem_dma = {k: es.enter_context(nc.semaphore(f"d_{i}")) for i, k in enumerate(slots)}
            cnt = {e: 0 for e in COMPUTE}
            dcnt = {k: 0 for k in slots}
            ev = {}
            for o in ops:
                if o["dma"] is not None:
                    dcnt[o["dma"]] += o["inc"] * o["n"]
                    ev[o["idx"]] = (sem_dma[o["dma"]], dcnt[o["dma"]])
                elif o["idx"] in needed and o["fn"] is not None:
                    cnt[o["eng"]] += 1
                    ev[o["idx"]] = (sem_eng[o["eng"]], cnt[o["eng"]])
            self.stats = (dict(cnt), {str(k): v for k, v in dcnt.items()}, {e: len(v) for e, v in self.streams.items()})
            blk = es.enter_context(nc.Block())

            def replay(engname):
                def body(eng):
                    waited = {}
                    for idx in self.streams[engname]:
                        o = ops[idx]
                        deps = set(o["raw"])
                        for d in o["war"]:
                            if not same_eng(ops[d], o):
                                deps.add(d)
                        wl = {}
                        for d in deps:
                            if d not in ev:
                                continue
                            s, v = ev[d]
                            key = id(s)
                            if waited.get(key, 0) >= v:
                                continue
                            if key not in wl or wl[key][1] < v:
                                wl[key] = (s, v)
                        for key, (s, v) in wl.items():
                            eng.wait_ge(s, v)
                            waited[key] = v
                        if o["fn"] is None:
                            continue
                        r = o["fn"](eng)
                        if o["dma"] is not None:
                            s, _ = ev[idx]
                            assert len(r) == o["n"], (len(r), o["n"])
                            for ins in r:
                                ins.then_inc(s, o["inc"])
                        elif idx in ev:
                            ins = r[-1] if isinstance(r, (list, tuple)) else r
                            ins.then_inc(ev[idx][0], 1)
                    if engname == "sp":
                        for k, s in sem_dma.items():
                            if dcnt[k]:
                                eng.wait_ge(s, dcnt[k])
                return body

            blk.tensor(replay("pe"))
            blk.scalar(replay("act"))
            blk.vector(replay("dve"))
            blk.gpsimd(replay("pool"))
            blk.sync(replay("sp"))


def build(n_layers=4, dbg=False, stop=99, small=False):
    nc = bass.Bass("TRN2", target_bir_lowering=False)
    din = lambda name, shape, dt=F32: nc.dram_tensor(name, list(shape), dt, kind="ExternalInput").ap()
    dout = lambda name, shape, dt=F32: nc.dram_tensor(name, list(shape), dt, kind="ExternalOutput").ap()
    dint = lambda name, shape, dt=F32: nc.dram_tensor(name, list(shape), dt, kind="Internal").ap()

    xp = din("xp", [1024, D])
    xsm = din("xsm", [32, D])
    ck = din("ck", [2, 4, 128, 256])
    cv = din("cv", [2, 4, 128, 256])
    sconv = din("sconv", [8, D])
    shg = din("shg", [4, 16, 128, 128])
    ln_g = din("ln_g", [4, D])
    final_g = din("final_g", [D])
    na = 2 if (n_layers >= 4 or not small) else 1
    a_w_in = din("a_w_in", [na, D, 4608])
    a_w_out = din("a_w_out", [na, D, D])
    a_sinks = din("a_sinks", [2, 32])
    b_w_in = din("b_w_in", [1, D, 8192] if (n_layers >= 2 or not small) else [1, 1, 8192])
    b_conv_w = din("b_conv_w", [1, 3, D])
    b_w_out = din("b_w_out", [1, D, D] if (n_layers >= 2 or not small) else [1, 1, D])
    c_w_in = din("c_w_in", [1, D, 8192] if (n_layers >= 3 or not small) else [1, 1, 8192])
    c_norm_g = din("c_norm_g", [1, 128])
    c_w_out = din("c_w_out", [1, D, D] if (n_layers >= 3 or not small) else [1, 1, D])
    c_lb = din("c_lb", [4, D])
    ropec = din("ropec", [128, NB, 8])
    ropes = din("ropes", [128, NB, 8])
    flag_d = din("flag", [128, 1])

    y = dout("y", [NTOK, D])
    kp = dout("kp", [2, 128, 256])
    vp = dout("vp", [2, 128, 256])
    ks = dout("ks", [2, 4, 128, 256])
    vs = dout("vs", [2, 4, 128, 256])
    convp = dout("convp", [2, D])
    convs = dout("convs", [8, D])
    hgp = dout("hgp", [16, 128, 128])
    hgs = dout("hgs", [4, 16, 128, 128])

    cc_a_in = [dint(f"cc_a_in{j}", [128, 512], BF16) for j in range(2)]
    cc_a_out = [dint(f"cc_a_out{j}", [256, 512], BF16) for j in range(2)]
    cc_b_in = dint("cc_b_in", [128, 32])
    cc_b_out = dint("cc_b_out", [256, 32])
    cc_h_in = [dint(f"cc_h_in{h}", [128, 128]) for h in range(16)]
    cc_h_out = [dint(f"cc_h_out{h}", [256, 128]) for h in range(16)]

    S = Sched(nc)
    with contextlib.ExitStack() as es:
        uniq = [0]

        def T(name, shape, dt, stack=es):
            uniq[0] += 1
            return stack.enter_context(nc.sbuf_tensor(f"{name}_{uniq[0]}", list(shape), dt))

        xs = T("xs", [128, NB, D], F32)
        hT = T("hT", [128, 16, NTOK], BF16)
        goT = T("goT", [128, 16, NTOK], BF16)
        wt = [T(f"wt{i}", [128, 16, 256], BF16) for i in range(2)]
        rt = [None] * 4
        ident = T("ident", [128, 128], BF16)
        identf = T("identf", [128, 128], F32)
        onesb = T("onesb", [128, 128], BF16)
        onesf = T("onesf", [128, 128], F32)
        mprev = T("mprev", [128, 2, 128], BF16)
        mown = T("mown", [128, 2, 128], BF16)
        ones4 = T("ones4", [128, 2, 128], BF16)
        flag = T("flag_sb", [128, 1], F32)
        ss = T("ss", [128, NB], F32)
        rstd = T("rstd", [128, NB], F32)
        cosT = T("cosT", [128, NB, 8], F32)
        sinT = T("sinT", [128, NB, 8], F32)
        ps = [es.enter_context(nc.psum_tensor(f"ps{i}", [128, 512], F32)) for i in range(8)]
        psb = [p.bitcast(BF16) for p in ps]

        PS = lambda i: ("ps", i)

        S.op("pool", lambda e: e.memset(onesb[:], 1.0), writes=["onesb"])
        S.op("pool", lambda e: e.memset(onesf[:], 1.0), writes=["onesf"])
        S.op("pool", lambda e: e.memset(ones4[:], 1.0), writes=["ones4"])
        S.op("pool", lambda e: e.affine_select(out=ident[:], in_=onesb[:], pattern=[[-1, 128]], compare_op=ALU.is_equal,
                                               fill=0.0, base=0, channel_multiplier=1), reads=["onesb"], writes=["ident"])
        S.op("pool", lambda e: e.affine_select(out=identf[:], in_=onesf[:], pattern=[[-1, 128]], compare_op=ALU.is_equal,
                                               fill=0.0, base=0, channel_multiplier=1), reads=["onesf"], writes=["identf"])
        S.op("pool", lambda e: e.affine_select(out=mprev[:], in_=ones4[:], pattern=[[0, 2], [-1, 128]], compare_op=ALU.is_ge,
                                               fill=0.0, base=-1, channel_multiplier=1), reads=["ones4"], writes=["mprev"])
        S.op("pool", lambda e: e.affine_select(out=mown[:], in_=ones4[:], pattern=[[0, 2], [1, 128]], compare_op=ALU.is_ge,
                                               fill=0.0, base=0, channel_multiplier=-1), reads=["ones4"], writes=["mown"])
        S.op("pool", lambda e: e.memset(xs[:, 8, :], 0.0), writes=[("x", 8)])
        S.dma("sp", "misc", lambda e: [e.dma_start(out=flag[:], in_=flag_d[:, :]),
                                       e.dma_start(out=cosT[:], in_=ropec[:, :, :]),
                                       e.dma_start(out=sinT[:], in_=ropes[:, :, :])], n=3, writes=["flag", "rope"])
        for b in range(8):
            S.dma("sp", ("xin", b % 4), lambda e, b=b: [e.dma_start(out=xs[:, b, :], in_=xp[b * 128:(b + 1) * 128, :])],
                  writes=[("x", b)])
        S.dma("sp", ("xin", 0), lambda e: [e.dma_start(out=xs[0:32, 8, :], in_=xsm[:, :])], writes=[("x", 8)])

        tiles = []

        def wsrc(w2d, c0, width=256):
            return [(w2d[:, c0:c0 + width], 0)]

        def group_src(w2d, j, hh):
            return [(w2d[:, g * 2048 + j * 128: g * 2048 + (j + 1) * 128], (g % 2) * 128) for g in (2 * hh, 2 * hh + 1)]

        layer_kinds = ["a", "b", "c", "a"][:n_layers]
        for li, kind in enumerate(layer_kinds):
            jj = li // 3
            if kind == "a":
                w_in, w_out = a_w_in[jj], a_w_out[jj]
                tiles.append(wsrc(w_in, 2048))
                tiles.append(wsrc(w_in, 2304))
                for GG in range(8):
                    tiles.append(wsrc(w_in, GG * 256))
                    tiles.append(wsrc(w_in, 2560 + GG * 256))
            elif kind == "b":
                w_in, w_out = b_w_in[0], b_w_out[0]
                for j in range(16):
                    tiles.append(group_src(w_in, j, 0))
                    tiles.append(group_src(w_in, j, 1))
            else:
                w_in, w_out = c_w_in[0], c_w_out[0]
                for j in range(16):
                    tiles.append(group_src(w_in, j, 0))
                    tiles.append(group_src(w_in, j, 1))
            for t in range(8):
                tiles.append(wsrc(w_out, t * 256))
        tstate = dict(next_load=0, next_use=0)

        def issue_load():
            i = tstate["next_load"]
            if i >= len(tiles):
                return
            tstate["next_load"] += 1
            slot = i % 2
            srcs = tiles[i]

            def fn(e, srcs=srcs, slot=slot):
                return [e.dma_start(out=wt[slot][:, :, off:off + a.shape[1]],
                                    in_=a.rearrange("(k p) n -> p k n", p=128)) for a, off in srcs]
            S.dma("pool", ("wt", slot), fn, n=len(srcs), writes=[("wt", slot)])

        def next_tile(prefetch=True):
            i = tstate["next_use"]
            tstate["next_use"] += 1
            while tstate["next_load"] <= (min(i + 1, len(tiles) - 1) if prefetch else i):
                issue_load()
            return wt[i % 2], ("wt", i % 2)

        issue_load()

        def blkP(b):
            return 128 if b < 8 else 32

        def tok0(b):
            return b * 128

        rr = dict(pt=0, ev=0)

        def pt_bank():
            rr["pt"] ^= 1
            return rr["pt"]

        def evac_eng():
            rr["ev"] ^= 1
            return "act" if rr["ev"] else "dve"

        def copy(engname, out, in_, reads, writes):
            if engname == "act":
                S.op("act", lambda e: e.copy(out=out, in_=in_), reads=reads, writes=writes)
            else:
                S.op(engname, lambda e: e.tensor_copy(out=out, in_=in_), reads=reads, writes=writes)

        def transpose_to(src_tile, src_res, ncols_blocks, P, dst_fn, dst_res_fn):
            for g0 in range(0, ncols_blocks, 4):
                rr["tr"] = rr.get("tr", 0) ^ 1
                bank = 4 + rr["tr"]
                n = min(4, ncols_blocks - g0)
                for j in range(n):
                    c = g0 + j
                    S.op("pe", lambda e, c=c, j=j, bank=bank: e.transpose(
                        out=psb[bank][:, j * 128:j * 128 + P], in_=src_tile[0:P, c * 128:(c + 1) * 128],
                        identity=ident[0:P, 0:P]), reads=[src_res, "ident"], writes=[PS(bank)])
                ee = evac_eng()
                for j in range(n):
                    c = g0 + j
                    copy(ee, dst_fn(c), psb[bank][:, j * 128:j * 128 + P], [PS(bank)], [dst_res_fn(c)])

        def norm_phase(g_row, final=False):
            with contextlib.ExitStack() as ph:
                gbc = T("gbc", [128, D], F32, ph)
                junk = T("junk", [128, D], BF16, ph)
                if final:
                    yb = [T(f"yb{i}", [128, D], F32, ph) for i in range(2)]
                else:
                    hb = [T(f"hb{i}", [128, D], BF16, ph) for i in range(2)]
                S.dma("sp", "gbc", lambda e: [e.dma_start(out=gbc[:], in_=g_row.partition_broadcast(128))], writes=["gbc"])
                for b in range(NB):
                    P = blkP(b)
                    S.op("act", lambda e, b=b, P=P: e.activation(out=junk[0:P, :], in_=xs[0:P, b, :], func=AF.Square,
                                                                 accum_out=ss[0:P, b:b + 1]),
                         reads=[("x", b)], writes=["junk", ("ss", b)])
                    S.op("act", lambda e, b=b, P=P: e.activation(out=rstd[0:P, b:b + 1], in_=ss[0:P, b:b + 1], func=AF.Sqrt,
                                                                 scale=1.0 / D, bias=EPS), reads=[("ss", b)], writes=[("rstd", b)])
                    S.op("dve", lambda e, b=b, P=P: e.reciprocal(out=rstd[0:P, b:b + 1], in_=rstd[0:P, b:b + 1]),
                         reads=[("rstd", b)], writes=[("rstd", b)])
                    if final:
                        ybt = yb[b % 2]
                        S.op("dve", lambda e, b=b, P=P, ybt=ybt: e.scalar_tensor_tensor(
                            out=ybt[0:P, :], in0=xs[0:P, b, :], scalar=rstd[0:P, b:b + 1], in1=gbc[0:P, :],
                            op0=ALU.mult, op1=ALU.mult), reads=[("x", b), ("rstd", b), "gbc"], writes=[("yb", b % 2)])
                        S.dma("sp", ("yout", b % 2), lambda e, b=b, P=P, ybt=ybt: [
                            e.dma_start(out=y[b * 128:b * 128 + P, :], in_=ybt[0:P, :])], reads=[("yb", b % 2)])
                    else:
                        hbt = hb[b % 2]
                        S.op("dve", lambda e, b=b, P=P, hbt=hbt: e.scalar_tensor_tensor(
                            out=hbt[0:P, :], in0=xs[0:P, b, :], scalar=rstd[0:P, b:b + 1], in1=gbc[0:P, :],
                            op0=ALU.mult, op1=ALU.mult), reads=[("x", b), ("rstd", b), "gbc"], writes=[("hb", b % 2)])
                        transpose_to(hbt, ("hb", b % 2), 16, P,
                                     lambda c, b=b, P=P: hT[:, c, tok0(b):tok0(b) + P], lambda c, b=b: ("hT", b))
                S.barrier()

        def wout_phase():
            for t in range(8):
                w, wres = next_tile()
                for b in range(NB):
                    P = blkP(b)
                    bank = pt_bank()
                    for k in range(16):
                        S.op("pe", lambda e, k=k, b=b, P=P, bank=bank, w=w: e.matmul(
                            ps[bank][0:P, 0:256], lhsT=goT[:, k, tok0(b):tok0(b) + P], rhs=w[:, k, :],
                            start=(k == 0), stop=(k == 15)), reads=[wres, ("goT", b)], writes=[PS(bank)])
                    S.op("dve", lambda e, b=b, P=P, bank=bank, t=t: e.tensor_tensor(
                        out=xs[0:P, b, t * 256:(t + 1) * 256], in0=xs[0:P, b, t * 256:(t + 1) * 256],
                        in1=ps[bank][0:P, 0:256], op=ALU.add), reads=[PS(bank), ("x", b)], writes=[("x", b)])
            S.barrier()

        def proj_tm(w, wres, b, bank, ncols=256):
            P = blkP(b)
            for k in range(16):
                S.op("pe", lambda e, k=k, b=b, P=P, bank=bank, w=w: e.matmul(
                    ps[bank][0:P, 0:ncols], lhsT=hT[:, k, tok0(b):tok0(b) + P], rhs=w[:, k, 0:ncols],
                    start=(k == 0), stop=(k == 15)), reads=[wres, ("hT", b)], writes=[PS(bank)])

        def rope_tm(src3, dst1, dst2, P, b, nh, rd, wr):
            x1, x2 = src3[:, :, 0:8], src3[:, :, 8:16]
            cosb = cosT[0:P, b, :].unsqueeze(1).to_broadcast([P, nh, 8])
            sinb = sinT[0:P, b, :].unsqueeze(1).to_broadcast([P, nh, 8])
            r = [t_[0:P, 0:nh, :] for t_ in rt]
            for i, (xa, tb_) in enumerate(((x1, cosb), (x2, sinb), (x2, cosb), (x1, sinb))):
                S.op("dve", lambda e, i=i, xa=xa, tb_=tb_, r=r: e.tensor_tensor(out=r[i], in0=xa, in1=tb_, op=ALU.mult),
                     reads=rd + ["rope"], writes=[f"rt{i}"])
            S.op("dve", lambda e, r=r: e.tensor_tensor(out=dst1, in0=r[0], in1=r[1], op=ALU.subtract),
                 reads=["rt0", "rt1"], writes=wr)
            S.op("dve", lambda e, r=r: e.tensor_tensor(out=dst2, in0=r[2], in1=r[3], op=ALU.add),
                 reads=["rt2", "rt3"], writes=wr)

        def attn_phase(jl):
            with contextlib.ExitStack() as ph:
                kT2 = T("kT2", [128, 4, 1152], BF16, ph)
                vall = T("vall", [128, 9, 4, 66], BF16, ph)
                kT2s = T("kT2s", [128, 4, 4 * 136], BF16, ph)
                vSc = T("vSc", [128, 4, 4, 66], BF16, ph)
                vSn = T("vSn", [8, 4, 4, 66], BF16, ph)
                vS32 = T("vS32", [32, 4, 64], BF16, ph)
                kcb = T("kcb", [128, 4, 2, 64], BF16, ph)
                kc32 = T("kc32", [128, 256], BF16, ph)
                ktm = T("ktm", [128, NB, 256], F32, ph) if False else None
                ktm1 = T("ktm1", [128, 256], F32, ph)
                k7 = T("k7", [128, 256], F32, ph)
                vtm = T("vtm", [128, 256], F32, ph)
                kb = T("kb", [128, 4, 2, 64], BF16, ph)
                halo = T("halo", [128, 512], BF16, ph)
                halo_in = T("halo_in", [128, 512], BF16, ph)
                rt = [T(f"rt{i}", [128, 4, 8], F32, ph) for i in range(4)]
                qf = [T(f"qf{i}", [128, 256], F32, ph) for i in range(3)]
                qb = [T(f"qb{i}", [128, 256], BF16, ph) for i in range(3)]
                qT = T("qT", [128, 2, 128], BF16, ph)
                E = [T(f"E{i}", [128, 2, 128], BF16, ph) for i in range(8)]
                den = T("den", [128, 4], F32, ph)
                ob = T("ob", [128, 4, 64], BF16, ph)
                obs = T("obs", [8, 4, 64], BF16, ph)
                oball = T("oball", [128, NB, 256], BF16, ph)
                sz = [T(f"sz{i}", [128, 256], BF16, ph) for i in range(2)]
                gg = [T(f"gg{i}", [128, 256], BF16, ph) for i in range(2)]
                esink = T("esink", [128, 32], F32, ph)
                attn_body(jl, locals())
            S.barrier()

        def attn_body(jl, L):
            kT2, vall, kT2s, vSc, vSn, vS32, kcb, kc32 = (L[k] for k in "kT2 vall kT2s vSc vSn vS32 kcb kc32".split())
            ktm1, k7, vtm, kb, halo, halo_in, qf, qb, qT, E, den, ob, obs, oball, sz, gg, esink = (
                L[k] for k in "ktm1 k7 vtm kb halo halo_in qf qb qT E den ob obs oball sz gg esink".split())
            nonlocal_rt = L["rt"]
            rt[:] = nonlocal_rt

            S.dma("sp", "misc", lambda e: [e.dma_start(out=esink[:], in_=a_sinks[jl].partition_broadcast(128))],
                  writes=["esink"])
            S.op("act", lambda e: e.activation(out=esink[:], in_=esink[:], func=AF.Exp), reads=["esink"], writes=["esink"])
            S.op("pool", lambda e: e.memset(ob[:], 0.0), writes=["ob"])
            S.op("pool", lambda e: e.memset(vall[:, :, :, 64:65], 1.0), writes=["vall_ones"])
            S.op("pool", lambda e: e.memset(vSc[:, :, :, 64:65], 1.0), writes=["vSc_ones"])
            S.op("pool", lambda e: e.memset(vSn[:, :, :, 64:65], 1.0), writes=["vSn_ones"])
            S.dma("pool", "cachev", lambda e: [e.dma_start(
                out=vSc[:, s, :, 0:64], in_=cv[jl, s].rearrange("r (h d) -> r h d", d=64)) for s in range(4)], n=4,
                reads=["vSc_ones"], writes=["vSc"])
            for s in range(4):
                S.dma("pool", "cachek", lambda e, s=s: [e.dma_start(out=kc32[:], in_=ck[jl, s])], writes=["kc32"])
                for half in range(2):
                    copy("dve" if half else "act", kcb[:, :, half, :], kc32[:].rearrange("r (h d) -> r h d", d=64),
                         ["kc32"], [("kcb", half)])
                bank = pt_bank()
                for h in range(4):
                    S.op("pe", lambda e, h=h, bank=bank: e.transpose(
                        out=psb[bank][:, h * 128:(h + 1) * 128], in_=kcb[:, h].rearrange("r t d -> r (t d)"), identity=ident[:]),
                        reads=[("kcb", 0), ("kcb", 1), "ident"], writes=[PS(bank)])
                copy(evac_eng(), kT2s[:, :, s * 136:s * 136 + 128], psb[bank][:, 0:512].rearrange("p (h t) -> p h t", t=128),
                     [PS(bank)], [("kT2s", s)])
            S.dma("sp", "cachecp", lambda e: [e.dma_start(out=ks[jl, :, 0:120, :], in_=ck[jl, :, 8:128, :]),
                                              e.dma_start(out=vs[jl, :, 0:120, :], in_=cv[jl, :, 8:128, :])], n=2)

            border = [7, 6, 5, 4, 3, 2, 1, 0, 8]
            if stop <= 1:
                return
            w, wres = next_tile()
            for b in border:
                P = blkP(b)
                bank = pt_bank()
                proj_tm(w, wres, b, bank)
                kt = k7 if b >= 7 else ktm1
                kres = "k7" if b >= 7 else "ktm1"
                copy("act", kt[0:P, :], ps[bank][0:P, 0:256], [PS(bank)], [kres])
                kv3 = kt[0:P, :].rearrange("p (h d) -> p h d", d=64)
                rope_tm(kv3, kv3[:, :, 0:8], kv3[:, :, 8:16], P, b, 4, [kres], [kres])
                for half in range(2):
                    copy("act" if half else "dve", kb[0:P, :, half, :], kv3, [kres], [("kb", half)])
                tb = pt_bank()
                for h in range(4):
                    S.op("pe", lambda e, h=h, P=P, tb=tb: e.transpose(
                        out=psb[tb][:, h * 128:h * 128 + P], in_=kb[0:P, h].rearrange("p t d -> p (t d)"),
                        identity=ident[0:P, 0:P]), reads=[("kb", 0), ("kb", 1), "ident"], writes=[PS(tb)])
                pv3 = psb[tb][:, 0:512].rearrange("p (h t) -> p h t", t=128)
                if b < 8:
                    copy(evac_eng(), kT2[:, :, 128 + b * 128:256 + b * 128], pv3, [PS(tb)], [("kT2", 1 + b)])
                else:
                    for s in range(4):
                        copy(evac_eng(), kT2s[:, :, s * 136 + 128:s * 136 + 136], pv3[:, :, s * 8:(s + 1) * 8],
                             [PS(tb)], [("kT2s", s)])
                if b == 7:
                    S.dma("sp", "kvout", lambda e: [e.dma_start(out=kp[jl], in_=k7[:, :])], reads=["k7"])
                    copy("dve", halo[:, 0:256], k7[:, :], ["k7"], ["halo"])
                if b == 8:
                    S.dma("sp", "kvout", lambda e: [e.dma_start(out=ks[jl, s, 120:128, :], in_=k7[s * 8:(s + 1) * 8, :])
                                                    for s in range(4)], n=4, reads=["k7"])
            if stop <= 2:
                return
            w, wres = next_tile()
            for b in border:
                P = blkP(b)
                bank = pt_bank()
                proj_tm(w, wres, b, bank)
                if stop < 2.1:
                    continue
                if b < 8:
                    copy("dve", vall[:, 1 + b, :, 0:64], ps[bank][:, 0:256].rearrange("p (h d) -> p h d", d=64),
                         [PS(bank), "vall_ones"], [("vall", 1 + b)])
                else:
                    copy("dve", vS32[:, :, :], ps[bank][0:32, 0:256].rearrange("p (h d) -> p h d", d=64), [PS(bank)], ["vS32"])
                    if stop >= 2.3:
                        S.dma("sp", "vsn", lambda e: [e.dma_start(out=vSn[:, s, :, 0:64], in_=vS32[s * 8:(s + 1) * 8, :, :])
                                                      for s in range(4)], n=4, reads=["vS32", "vSn_ones"], writes=["vSn"])
                if stop < 2.15:
                    continue
                if b >= 7:
                    copy("act", vtm[0:P, :], ps[bank][0:P, 0:256], [PS(bank), ("vall", 1 + b) if b < 8 else "vS32"], ["vtm"])
                if b == 8:
                    S.dma("sp", "kvout", lambda e: [e.dma_start(out=vs[jl, s, 120:128, :], in_=vtm[s * 8:(s + 1) * 8, :])
                                                    for s in range(4)], n=4, reads=["vtm"])
                if b == 7:
                    S.dma("sp", "kvout", lambda e: [e.dma_start(out=vp[jl], in_=vtm[:, :])], reads=["vtm"])
                    if stop < 2.2:
                        continue
                    copy("dve", halo[:, 256:512], vtm[:, :], ["vtm"], ["halo"])
                    S.dma("sp", "halo", lambda e: [e.dma_start(out=cc_a_in[jl][:, :], in_=halo[:, :])],
                          reads=["halo"], writes=["cc_a_in"])
                    if stop >= 2.6:
                      S.dma("pool", "cc_a", lambda e: [e.collective_compute(
                        "AllGather", ALU.bypass, replica_groups=PAIRS, ins=[cc_a_in[jl].opt()],
                        outs=[cc_a_out[jl].opt()])], inc=1, reads=["cc_a_in"], writes=["cc_a_out"])
                    S.dma("sp", "halo", lambda e: [e.dma_start(out=halo_in[:, :], in_=cc_a_out[jl][0:128, :])],
                          reads=["cc_a_out"], writes=["halo_in"])
            if stop <= 3:
                return
            for half in range(2):
                copy("dve", kb[:, :, half, :], halo_in[:, 0:256].rearrange("p (h d) -> p h d", d=64),
                     ["halo_in"], [("kb", half)])
            copy("act", vall[:, 0, :, 0:64], halo_in[:, 256:512].rearrange("p (h d) -> p h d", d=64),
                 ["halo_in", "vall_ones"], [("vall", 0)])
            tb = pt_bank()
            for h in range(4):
                S.op("pe", lambda e, h=h, tb=tb: e.transpose(
                    out=psb[tb][:, h * 128:(h + 1) * 128], in_=kb[:, h].rearrange("p t d -> p (t d)"),
                    identity=ident[:]), reads=[("kb", 0), ("kb", 1), "ident"], writes=[PS(tb)])
            copy("dve", kT2[:, :, 0:128], psb[tb][:, 0:512].rearrange("p (h t) -> p h t", t=128), [PS(tb)], [("kT2", 0)])

            if stop <= 4:
                return
            for GG in range(8 if stop >= 10 else stop - 4):
                G = GG // 2
                wq, wqres = next_tile()

                def stP(b, wq=wq, wqres=wqres):
                    P = blkP(b)
                    i3 = b % 3
                    bank = pt_bank()
                    proj_tm(wq, wqres, b, bank)
                    copy("act", qf[i3][0:P, :], ps[bank][0:P, 0:256], [PS(bank)], [("qf", i3)])
                    q3 = qf[i3][0:P, :].rearrange("p (h d) -> p h d", d=64)
                    rope_tm(q3, q3[:, :, 0:8], q3[:, :, 8:16], P, b, 4, [("qf", i3)], [("qf", i3)])
                    copy("act", qb[i3][0:P, :], qf[i3][0:P, :], [("qf", i3)], [("qb", i3)])

                def stA(b, part, G=G, GG=GG):
                    P = blkP(b)
                    i3 = b % 3
                    e0 = (b % 2) * 4
                    if part == 1:
                        transpose_to(qb[i3], ("qb", i3), 2, P, lambda c, P=P: qT[:, c, 0:P], lambda c: "qT")
                    seqs = [None] if b < 8 else list(range(4))
                    if b == 8 and part == 2:
                        return
                    for s in seqs:
                        if s is None:
                            NQ, qc = 128, slice(0, 128)
                            kprev = kT2[:, G, b * 128:(b + 1) * 128]
                            kown = kT2[:, G, (b + 1) * 128:(b + 2) * 128]
                            vprev, vown = vall[:, b, G, 0:65], vall[:, b + 1, G, 0:65]
                            KO = 128
                            rdk = [("kT2", b), ("kT2", b + 1)]
                            rdv = [("vall", b), ("vall", b + 1)]
                        else:
                            NQ, qc = 8, slice(s * 8, s * 8 + 8)
                            kprev = kT2s[:, G, s * 136:s * 136 + 128]
                            kown = kT2s[:, G, s * 136 + 128:s * 136 + 136]
                            vprev, vown = vSc[:, s, G, 0:65], vSn[0:8, s, G, 0:65]
                            KO = 8
                            rdk = [("kT2s", s)]
                            rdv = ["vSc", "vSn"]
                        if part == 1:
                            for hl in range(4):
                                c, half = hl // 2, hl % 2
                                r0 = half * 64
                                S.op("pe", lambda e, c=c, r0=r0, half=half, kprev=kprev, qc=qc, NQ=NQ: e.matmul(
                                    ps[2 + half][:, c * 128:c * 128 + NQ], lhsT=kprev[r0:r0 + 64, :], rhs=qT[r0:r0 + 64, c, qc],
                                    start=True, stop=True), reads=rdk + ["qT"], writes=[PS(2 + half)])
                                S.op("pe", lambda e, c=c, r0=r0, half=half, kown=kown, qc=qc, NQ=NQ, KO=KO: e.matmul(
                                    ps[2 + half][0:KO, 256 + c * 128:256 + c * 128 + NQ], lhsT=kown[r0:r0 + 64, :], rhs=qT[r0:r0 + 64, c, qc],
                                    start=True, stop=True), reads=rdk + ["qT"], writes=[PS(2 + half)])
                            for i, (bnk, off) in enumerate(((2, 0), (3, 0), (2, 256), (3, 256))):
                                KP = 128 if i < 2 else KO
                                Ev = E[e0 + i][0:KP, :, 0:NQ]
                                S.op("act", lambda e, Ev=Ev, bnk=bnk, off=off, KP=KP, NQ=NQ: e.activation(
                                    out=Ev, in_=ps[bnk][0:KP, off:off + 256].rearrange("p (h q) -> p h q", q=128)[:, :, 0:NQ],
                                    func=AF.Exp, scale=0.125), reads=[PS(bnk)], writes=[("E", e0 + i)])
                                m = (mprev if i < 2 else mown)[0:KP, :, 0:NQ]
                                if i < 2 and b == 0:
                                    S.op("dve", lambda e, Ev=Ev, m=m: e.scalar_tensor_tensor(
                                        out=Ev, in0=Ev, scalar=flag[:, 0:1], in1=m, op0=ALU.mult, op1=ALU.mult),
                                        reads=[("E", e0 + i), "flag", "mprev", "mown"], writes=[("E", e0 + i)])
                                else:
                                    S.op("dve", lambda e, Ev=Ev, m=m: e.tensor_tensor(out=Ev, in0=Ev, in1=m, op=ALU.mult),
                                         reads=[("E", e0 + i), "mprev", "mown"], writes=[("E", e0 + i)])
                            if s is None:
                                continue
                        bnk = 6 + ((b if s is None else s) % 2)
                        for hl in range(4):
                            c, half = hl // 2, hl % 2
                            S.op("pe", lambda e, c=c, half=half, hl=hl, bnk=bnk, vprev=vprev, NQ=NQ: e.matmul(
                                ps[bnk][0:NQ, hl * 65:(hl + 1) * 65], lhsT=E[e0 + half][:, c, 0:NQ], rhs=vprev,
                                start=True, stop=False), reads=[("E", e0 + half)] + rdv, writes=[PS(bnk)])
                            S.op("pe", lambda e, c=c, half=half, hl=hl, bnk=bnk, vown=vown, NQ=NQ, KO=KO: e.matmul(
                                ps[bnk][0:NQ, hl * 65:(hl + 1) * 65], lhsT=E[e0 + 2 + half][0:KO, c, 0:NQ], rhs=vown,
                                start=False, stop=True), reads=[("E", e0 + 2 + half)] + rdv, writes=[PS(bnk)])
                        pv = ps[bnk][0:NQ, 0:260].rearrange("p (h e) -> p h e", e=65)
                        es_ = esink[0:NQ, GG * 4:(GG + 1) * 4]
                        dn = den[0:NQ, 0:4]
                        S.op("dve", lambda e, pv=pv, es_=es_, dn=dn: e.tensor_tensor(
                            out=dn, in0=pv[:, :, 64], in1=es_, op=ALU.add), reads=[PS(bnk), "esink"], writes=["den"])
                        S.op("dve", lambda e, dn=dn: e.reciprocal(out=dn, in_=dn), reads=["den"], writes=["den"])
                        if s is None:
                            obv = oball[0:NQ, b, :].rearrange("p (h d) -> p h d", d=64)
                            wr = [("oball", b)]
                        else:
                            obv = obs[0:NQ]
                            wr = ["obs"]
                        S.op("dve", lambda e, pv=pv, dn=dn, obv=obv, NQ=NQ: e.tensor_tensor(
                            out=obv, in0=pv[:, :, 0:64], in1=dn.unsqueeze(2).to_broadcast([NQ, 4, 64]), op=ALU.mult),
                            reads=[PS(bnk), "den"], writes=wr)
                        if s is not None:
                            S.dma("sp", "obs", lambda e, s=s, b=b: [e.dma_start(
                                out=oball[s * 8:(s + 1) * 8, b, :], in_=obs[:, :, :].rearrange("p h d -> p (h d)"))],
                                reads=["obs"], writes=[("oball", b)])

                stP(0)
                stP(1)
                for b in range(8):
                    stA(b, 1)
                    if b + 2 <= 8:
                        stP(b + 2)
                    stA(b, 2)
                stA(8, 1)

                wz, wzres = next_tile()

                def stZP(b, wz=wz, wzres=wzres):
                    P = blkP(b)
                    i2 = b % 2
                    bank = pt_bank()
                    proj_tm(wz, wzres, b, bank)
                    S.op("act", lambda e, P=P, bank=bank, i2=i2: e.activation(out=sz[i2][0:P, :], in_=ps[bank][0:P, 0:256], func=AF.Silu),
                         reads=[PS(bank)], writes=[("sz", i2)])
                    S.op("dve", lambda e, P=P, b=b, i2=i2: e.tensor_tensor(out=gg[i2][0:P, :], in0=sz[i2][0:P, :], in1=oball[0:P, b, :], op=ALU.mult),
                         reads=[("sz", i2), ("oball", b)], writes=[("gg", i2)])

                def stZT(b, GG=GG):
                    P = blkP(b)
                    i2 = b % 2
                    transpose_to(gg[i2], ("gg", i2), 2, P, lambda c, b=b, P=P, GG=GG: goT[:, 2 * GG + c, tok0(b):tok0(b) + P],
                                 lambda c, b=b: ("goT", b))

                stZP(0)
                for b in range(NB):
                    if b + 1 < NB:
                        stZP(b + 1)
                    stZT(b)

        TT = [(0, 512), (512, 512), (1024, 32)]

        def hT_res(T0, N):
            return [("hT", b) for b in range(T0 // 128, (T0 + N + 127) // 128)]

        def goT_res(T0, N):
            return [("goT", b) for b in range(T0 // 128, (T0 + N + 127) // 128)]

        def proj_fm(w, wres, col0, T0, N, bank):
            for k in range(16):
                S.op("pe", lambda e, k=k, w=w, col0=col0, T0=T0, N=N, bank=bank: e.matmul(
                    ps[bank][:, 0:N], lhsT=w[:, k, col0:col0 + 128], rhs=hT[:, k, T0:T0 + N],
                    start=(k == 0), stop=(k == 15)), reads=[wres] + hT_res(T0, N), writes=[PS(bank)])

        def tm_to_fm(src, src_res, R, dst, dst_res):
            bank = pt_bank()
            for j in range(16):
                S.op("pe", lambda e, j=j, bank=bank: e.transpose(out=ps[bank][:, j * R:(j + 1) * R], in_=src[0:R, j * 128:(j + 1) * 128],
                                                             identity=identf[0:R, 0:R]), reads=[src_res, "identf"], writes=[PS(bank)])
            copy("dve", dst, ps[bank][:, 0:16 * R].rearrange("p (j r) -> p j r", r=R), [PS(bank)], [dst_res])

        def fm_to_tm_out(src, src_res, R, stage, dram_out, slot):
            for g0 in range(0, 16, 4):
                bank = pt_bank()
                for jj_ in range(4):
                    j = g0 + jj_
                    S.op("pe", lambda e, j=j, jj_=jj_, bank=bank: e.transpose(
                        out=ps[bank][0:R, jj_ * 128:(jj_ + 1) * 128], in_=src[:, j, :], identity=identf[:]),
                        reads=[src_res, "identf"], writes=[PS(bank)])
                copy("act", stage[0:R, g0 * 128:(g0 + 4) * 128], ps[bank][0:R, 0:512], [PS(bank)], [("stage", slot)])
            S.dma("sp", slot, lambda e: [e.dma_start(out=dram_out, in_=stage[0:R, :])], reads=[("stage", slot)])

        def conv_phase():
            with contextlib.ExitStack() as ph:
                bsb = T("bsb", [128, NTOK], F32, ph)
                csb = T("csb", [128, NTOK], F32, ph)
                uT = T("uT", [128, 1026], F32, ph)
                usT = T("usT", [128, 4, 10], F32, ph)
                szb = T("szb", [128, 512], F32, ph)
                acc = T("acc", [128, 512], F32, ph)
                cwt = T("bufA", [8, D], F32, ph)
                cw = T("cw", [128, 16, 3], F32, ph)
                sct0 = T("bufB", [8, D], F32, ph)
                sct = T("sct", [128, 16, 8], F32, ph)
                ulast = T("ulast", [128, 16, 2], F32, ph)
                uls = T("uls", [128, 16, 8], F32, ph)
                bz01 = T("bz01", [128, 16, 2], F32, ph)
                cv01 = T("cv01", [128, 16, 2], F32, ph)
                uh = T("uh", [128, 16, 2], F32, ph)
                tmp2 = [T(f"tmp2{i}", [128, 16], F32, ph) for i in range(3)]
                stage, stage2 = cwt, sct0

                S.dma("sp", "misc", lambda e: [e.dma_start(out=cwt[0:3, :], in_=b_conv_w[0]),
                                               e.dma_start(out=sct0[:, :], in_=sconv[:, :])], n=2, writes=[("stage", "cvo1"), ("stage", "cvo2")])
                tm_to_fm(cwt, ("stage", "cvo1"), 3, cw[:], "cw")
                tm_to_fm(sct0, ("stage", "cvo2"), 8, sct[:], "sct")
                S.op("pool", lambda e: e.memset(uT[:, 0:2], 0.0), writes=["uT"])
                for j in range(16):
                    w0_, w0res = next_tile()
                    for (T0, N) in TT:
                        bank = pt_bank()
                        proj_fm(w0_, w0res, 0, T0, N, bank)
                        copy("act", bsb[:, T0:T0 + N], ps[bank][:, 0:N], [PS(bank)], [("bsb", T0)])
                        bank = pt_bank()
                        proj_fm(w0_, w0res, 128, T0, N, bank)
                        copy("act", csb[:, T0:T0 + N], ps[bank][:, 0:N], [PS(bank)], [("csb", T0)])
                    w1_, w1res = next_tile()
                    copy("dve", usT[:, :, 0:2], sct[:, j, :].rearrange("p (s i) -> p s i", i=2), ["sct"], ["usT"])
                    for (T0, N) in TT:
                        bank = pt_bank()
                        proj_fm(w1_, w1res, 0, T0, N, bank)
                        if T0 < 1024:
                            S.op("dve", lambda e, T0=T0, N=N, bank=bank: e.tensor_tensor(
                                out=uT[:, 2 + T0:2 + T0 + N], in0=csb[:, T0:T0 + N], in1=ps[bank][:, 0:N], op=ALU.mult),
                                reads=[PS(bank), ("csb", T0)], writes=["uT"])
                        else:
                            S.op("dve", lambda e, T0=T0, N=N, bank=bank: e.tensor_tensor(
                                out=usT[:, :, 2:10], in0=csb[:, T0:T0 + N].rearrange("p (s t) -> p s t", t=8),
                                in1=ps[bank][:, 0:N].rearrange("p (s t) -> p s t", t=8), op=ALU.mult),
                                reads=[PS(bank), ("csb", T0)], writes=["usT"])
                        bank = pt_bank()
                        proj_fm(w1_, w1res, 128, T0, N, bank)
                        S.op("act", lambda e, N=N, bank=bank: e.activation(out=szb[:, 0:N], in_=ps[bank][:, 0:N], func=AF.Silu),
                             reads=[PS(bank)], writes=["szb"])
                        S.op("dve", lambda e, T0=T0, N=N: e.tensor_tensor(out=szb[:, 0:N], in0=szb[:, 0:N], in1=bsb[:, T0:T0 + N], op=ALU.mult),
                             reads=["szb", ("bsb", T0)], writes=["szb"])
                        if T0 < 1024:
                            u0, u1, u2 = uT[:, T0:T0 + N], uT[:, T0 + 1:T0 + 1 + N], uT[:, T0 + 2:T0 + 2 + N]
                            a_, ures = acc[:, 0:N], "uT"
                            sz_ = szb[:, 0:N]
                            gout = goT[:, j, T0:T0 + N]
                        else:
                            u0, u1, u2 = usT[:, :, 0:8], usT[:, :, 1:9], usT[:, :, 2:10]
                            a_, ures = acc[:, 0:32].rearrange("p (s t) -> p s t", t=8), "usT"
                            sz_ = szb[:, 0:32].rearrange("p (s t) -> p s t", t=8)
                            gout = goT[:, j, T0:T0 + N].rearrange("p (s t) -> p s t", t=8)
                        S.op("dve", lambda e, j=j, u0=u0, a_=a_: e.tensor_scalar(out=a_, in0=u0, scalar1=cw[:, j, 0:1], scalar2=None, op0=ALU.mult),
                             reads=[ures, "cw"], writes=["acc"])
                        S.op("dve", lambda e, j=j, u1=u1, a_=a_: e.scalar_tensor_tensor(out=a_, in0=u1, scalar=cw[:, j, 1:2], in1=a_, op0=ALU.mult, op1=ALU.add),
                             reads=[ures, "cw", "acc"], writes=["acc"])
                        S.op("dve", lambda e, j=j, u2=u2, a_=a_: e.scalar_tensor_tensor(out=a_, in0=u2, scalar=cw[:, j, 2:3], in1=a_, op0=ALU.mult, op1=ALU.add),
                             reads=[ures, "cw", "acc"], writes=["acc"])
                        if T0 == 0:
                            copy("act", bz01[:, j, :], szb[:, 0:2], ["szb"], ["bz01"])
                            copy("act", cv01[:, j, :], acc[:, 0:2], ["acc"], ["cv01"])
                        S.op("dve", lambda e, a_=a_, sz_=sz_, gout=gout: e.tensor_tensor(out=gout, in0=a_, in1=sz_, op=ALU.mult),
                             reads=["acc", "szb"], writes=goT_res(T0, N))
                    copy("act", ulast[:, j, :], uT[:, 1024:1026], ["uT"], ["ulast"])
                    copy("act", uls[:, j, :].rearrange("p (s i) -> p s i", i=2), usT[:, :, 8:10], ["usT"], ["uls"])
                S.dma("sp", "cvx", lambda e: [e.dma_start(out=cc_b_in[:, :], in_=ulast[:].rearrange("p j i -> p (j i)"))],
                      reads=["ulast"], writes=["cc_b_in"])
                S.dma("pool", "cc_b", lambda e: [e.collective_compute(
                    "AllGather", ALU.bypass, replica_groups=PAIRS, ins=[cc_b_in.opt()], outs=[cc_b_out.opt()])],
                    inc=1, reads=["cc_b_in"], writes=["cc_b_out"])
                S.dma("sp", "cvx", lambda e: [e.dma_start(out=uh[:].rearrange("p j i -> p (j i)"), in_=cc_b_out[0:128, :])],
                      reads=["cc_b_out"], writes=["uh"])
                S.op("dve", lambda e: e.tensor_scalar(out=uh[:], in0=uh[:], scalar1=flag[:, 0:1], scalar2=None, op0=ALU.mult),
                     reads=["uh", "flag"], writes=["uh"])
                t0_, t1_, t2_ = tmp2[0][:], tmp2[1][:], tmp2[2][:]
                S.op("dve", lambda e: e.tensor_tensor(out=t0_, in0=cw[:, :, 0], in1=uh[:, :, 0], op=ALU.mult), reads=["cw", "uh"], writes=["t0"])
                S.op("dve", lambda e: e.tensor_tensor(out=t1_, in0=cw[:, :, 1], in1=uh[:, :, 1], op=ALU.mult), reads=["cw", "uh"], writes=["t1"])
                S.op("dve", lambda e: e.tensor_tensor(out=t2_, in0=cw[:, :, 0], in1=uh[:, :, 1], op=ALU.mult), reads=["cw", "uh"], writes=["t2"])
                S.op("dve", lambda e: e.tensor_tensor(out=t0_, in0=t0_, in1=t1_, op=ALU.add), reads=["t0", "t1"], writes=["t0"])
                S.op("dve", lambda e: e.tensor_tensor(out=cv01[:, :, 0], in0=cv01[:, :, 0], in1=t0_, op=ALU.add), reads=["t0", "cv01"], writes=["cv01"])
                S.op("dve", lambda e: e.tensor_tensor(out=cv01[:, :, 1], in0=cv01[:, :, 1], in1=t2_, op=ALU.add), reads=["t2", "cv01"], writes=["cv01"])
                S.op("dve", lambda e: e.tensor_tensor(out=goT[:, :, 0:2], in0=cv01[:], in1=bz01[:], op=ALU.mult),
                     reads=["cv01", "bz01"], writes=[("goT", 0)])
                fm_to_tm_out(ulast, "ulast", 2, stage, convp[:, :], "cvo1")
                fm_to_tm_out(uls, "uls", 8, stage2, convs[:, :], "cvo2")
            S.barrier()

        def hgrn_phase(li):
            with contextlib.ExitStack() as ph:
                clb0 = T("clb0", [64, 128], F32, ph)
                clbf = T("clbf", [128, 4, 16], F32, ph)
                lbt = T("lbt", [128, 16], F32, ph)
                omlt = T("omlt", [128, 16], F32, ph)
                dent = T("dent", [128, 16], F32, ph)
                ng = T("ng", [128, 1], F32, ph)
                m64 = T("m64", [128, 512], BF16, ph)
                m8 = T("m8", [128, 32], BF16, ph)
                mone = T("mone", [128, 512], BF16, ph)
                t_sq = T("t_sq", [128, NTOK], F32, ph)
                t_fg = T("t_fg", [128, NTOK], F32, ph)
                t_lf = T("t_lf", [128, NTOK], F32, ph)
                t_b = T("t_b", [128, NTOK], F32, ph)
                t_bg = T("t_bg", [128, NTOK], F32, ph)
                t_sq2 = T("t_sq2", [128, NTOK], BF16, ph)
                qtT = T("qtT", [128, NTOK], BF16, ph)
                kiT = T("kiT", [128, NTOK], BF16, ph)
                ksT = T("ksT", [128, NTOK], BF16, ph)
                qgT = T("qgT", [128, 1024], BF16, ph)
                vT = T("vT", [128, NTOK], BF16, ph)
                szT = T("szT", [128, NTOK], BF16, ph)
                oTs = t_sq
                ebl = T("ebl", [128, 20], F32, ph)
                ebt = T("ebt", [128, 1], F32, ph)
                bgl = T("bgl", [128, 1], F32, ph)
                Sloc = T("Sloc", [128, 16, 128], F32, ph)
                Sball = t_lf[:].bitcast(BF16)[:, 0:1920].rearrange("p (c d) -> p c d", d=128)
                SA = T("SA", [128, 128], F32, ph)
                SAb = T("SAb", [128, 128], BF16, ph)
                Sf = T("Sf", [128, 128], F32, ph)
                SfP = [Sf, T("Sf1", [128, 128], F32, ph)]
                SbP = [SAb, T("SAb1", [128, 128], BF16, ph)]
                kvtm = [T(f"kvtm{i}", [64, 256], BF16, ph) for i in range(2)]
                att = [T(f"att{i}", [64, 64], BF16, ph) for i in range(2)]

                S.dma("sp", "misc", lambda e: [e.dma_start(out=clb0[:, :], in_=c_lb.rearrange("r (j p) -> (r j) p", p=128)),
                                               e.dma_start(out=ng[:, :], in_=c_norm_g[0].rearrange("(p o) -> p o", o=1))],
                      n=2, writes=["clb0", "ng"])
                bank = pt_bank()
                S.op("pe", lambda e, bank=bank: e.transpose(out=ps[bank][:, 0:64], in_=clb0[:, :], identity=identf[0:64, 0:64]),
                     reads=["clb0", "identf"], writes=[PS(bank)])
                S.op("act", lambda e, bank=bank: e.activation(out=clbf[:].rearrange("p r j -> p (r j)"), in_=ps[bank][:, 0:64], func=AF.Exp),
                     reads=[PS(bank)], writes=["clbf"])
                assert li == 2
                S.op("dve", lambda e: e.tensor_tensor(out=dent[:], in0=clbf[:, 0, :], in1=clbf[:, 1, :], op=ALU.add), reads=["clbf"], writes=["dent"])
                S.op("dve", lambda e: e.tensor_tensor(out=lbt[:], in0=clbf[:, 2, :], in1=clbf[:, 3, :], op=ALU.add), reads=["clbf"], writes=["lbt"])
                S.op("dve", lambda e: e.tensor_tensor(out=dent[:], in0=dent[:], in1=lbt[:], op=ALU.add), reads=["dent", "lbt"], writes=["dent"])
                S.op("dve", lambda e: e.reciprocal(out=dent[:], in_=dent[:]), reads=["dent"], writes=["dent"])
                S.op("dve", lambda e: e.tensor_tensor(out=lbt[:], in0=clbf[:, 1, :], in1=clbf[:, 2, :], op=ALU.add), reads=["clbf", "lbt"], writes=["lbt"])
                S.op("dve", lambda e: e.tensor_tensor(out=lbt[:], in0=lbt[:], in1=dent[:], op=ALU.mult), reads=["lbt", "dent"], writes=["lbt"])
                S.op("dve", lambda e: e.tensor_scalar(out=omlt[:], in0=lbt[:], scalar1=-1.0, scalar2=1.0, op0=ALU.mult, op1=ALU.add),
                     reads=["lbt"], writes=["omlt"])
                S.op("pool", lambda e: e.memset(mone[:], 1.0), writes=["mone"])
                S.op("pool", lambda e: e.memset(m64[:], 1.0), writes=["m64"])
                S.op("pool", lambda e: e.memset(m64[:].rearrange("p (c t) -> p c t", t=64)[:, :, 0:1], 0.0), reads=["m64"], writes=["m64"])
                S.op("pool", lambda e: e.memset(m8[:], 1.0), writes=["m8"])
                S.op("pool", lambda e: e.memset(m8[:].rearrange("p (c t) -> p c t", t=8)[:, :, 0:1], 0.0), reads=["m8"], writes=["m8"])

                def fq_park():
                    w0_, w0res = next_tile(prefetch=False)
                    issue_load()
                    plan = [(128, 0, 3), (128, 1, 4), (0, 0, 5), (0, 1, 6)]
                    groups = [(lambda col=col, ti=ti, bank=bank: proj_fm(w0_, w0res, col, TT[ti][0], TT[ti][1], bank))
                              for col, ti, bank in plan]

                    def post():
                        for ti in (0, 1):
                            T0, N = TT[ti]
                            S.op("act", lambda e, T0=T0, N=N, ti=ti: e.activation(out=t_fg[:, T0:T0 + N], in_=ps[3 + ti][:, 0:N], func=AF.Sigmoid),
                                 reads=[PS(3 + ti)], writes=[("t_fg", ti)])
                        T0, N = TT[2]
                        proj_fm(w0_, w0res, 128, T0, N, 0)
                        S.op("act", lambda e, T0=T0, N=N: e.activation(out=t_fg[:, T0:T0 + N], in_=ps[0][:, 0:N], func=AF.Sigmoid),
                             reads=[PS(0)], writes=[("t_fg", 2)])
                        for ti in (0, 1):
                            T0, N = TT[ti]
                            S.op("act", lambda e, T0=T0, N=N, ti=ti: e.activation(out=t_sq[:, T0:T0 + N], in_=ps[5 + ti][:, 0:N], func=AF.Silu),
                                 reads=[PS(5 + ti)], writes=[("t_sq", ti)])
                        T0, N = TT[2]
                        proj_fm(w0_, w0res, 0, T0, N, 1)
                        S.op("act", lambda e, T0=T0, N=N: e.activation(out=t_sq[:, T0:T0 + N], in_=ps[1][:, 0:N], func=AF.Silu),
                             reads=[PS(1)], writes=[("t_sq", 2)])
                    return groups, post

                for j in range(16):
                    def hb_bank():
                        rr["hb"] = (rr.get("hb", -1) + 1) % 7
                        return rr["hb"]
                    if j == 0:
                        g0_, post0_ = fq_park()
                        for x_ in g0_:
                            x_()
                        post0_()
                    w1_, w1res = next_tile(prefetch=False)
                    issue_load()

                    def chain(ti, T0, N, j=j):
                        C = 64 if T0 < 1024 else 8
                        nch = N // C
                        c0 = T0 // 64
                        msk = m64 if T0 < 1024 else m8
                        sl = slice(T0, T0 + N)
                        fg, lf, bb, bg, sq_ = t_fg[:, sl], t_lf[:, sl], t_b[:, sl], t_bg[:, sl], t_sq[:, sl]
                        R = lambda n: (n, ti)
                        st = []
                        st.append(lambda: S.op("dve", lambda e: e.tensor_scalar(out=fg, in0=fg, scalar1=omlt[:, j:j + 1], scalar2=lbt[:, j:j + 1],
                                                                                op0=ALU.mult, op1=ALU.add),
                                               reads=[R("t_fg"), "omlt", "lbt"], writes=[R("t_fg")]))
                        st.append(lambda: S.op("act", lambda e: e.activation(out=lf, in_=fg, func=AF.Ln), reads=[R("t_fg")], writes=[R("t_lf")]))
                        st.append(lambda: S.op("dve", lambda e: e.tensor_scalar(out=fg, in0=fg, scalar1=-1.0, scalar2=1.0, op0=ALU.mult, op1=ALU.add),
                                               reads=[R("t_fg"), R("t_lf")], writes=[R("t_fg")]))
                        st.append(lambda: S.op("dve", lambda e: e.tensor_tensor_scan(out=bb, data0=msk[:, 0:N], data1=lf, initial=0.0,
                                                                                     op0=ALU.mult, op1=ALU.add),
                                               reads=[R("t_lf"), "m64", "m8"], writes=[R("t_b")]))
                        if T0 < 1024:
                            init = 0.0 if ti == 0 else bgl[:, 0:1]

                            def gsc():
                                S.op("dve", lambda e: e.tensor_tensor_scan(out=bg, data0=mone[:, 0:N], data1=lf, initial=init,
                                                                           op0=ALU.mult, op1=ALU.add),
                                     reads=[R("t_lf"), "mone", "bgl"], writes=[R("t_bg")])
                                copy("dve", bgl[:, 0:1], t_bg[:, T0 + N - 1:T0 + N], [R("t_bg")], ["bgl"])
                                if ti == 1:
                                    S.op("act", lambda e: e.activation(out=ebt[:, 0:1], in_=bgl[:, 0:1], func=AF.Exp), reads=["bgl"], writes=["ebt"])
                            st.append(gsc)
                        else:
                            st.append(lambda: None)
                        st.append(lambda: S.op("act", lambda e: e.activation(out=lf, in_=bb, func=AF.Exp), reads=[R("t_b"), R("t_bg")], writes=[R("t_lf")]))
                        st.append(lambda: S.op("dve", lambda e: e.tensor_tensor(out=qtT[:, sl], in0=sq_, in1=lf, op=ALU.mult),
                                               reads=[R("t_sq"), R("t_lf")], writes=[("qtT", ti)]))
                        st.append(lambda: S.op("act", lambda e: e.activation(out=lf, in_=bb, func=AF.Exp, scale=-1.0),
                                               reads=[R("t_b"), ("qtT", ti)], writes=[R("t_lf")]))
                        st.append(lambda: S.op("dve", lambda e: e.tensor_tensor(out=fg, in0=fg, in1=lf, op=ALU.mult),
                                               reads=[R("t_fg"), R("t_lf")], writes=[R("t_fg")]))
                        st.append(lambda: S.op("act", lambda e: e.activation(
                            out=ebl[:, c0:c0 + nch], in_=bb.rearrange("p (c t) -> p c t", t=C)[:, :, C - 1], func=AF.Exp),
                            reads=[R("t_b")], writes=[("ebl", ti)]))
                        st.append(lambda: copy("pool", kiT[:, sl], fg, [R("t_fg")], [("kiT", ti)]))
                        st.append(lambda: S.op("dve", lambda e: e.tensor_tensor(
                            out=ksT[:, sl].rearrange("p (c t) -> p c t", t=C), in0=fg.rearrange("p (c t) -> p c t", t=C),
                            in1=ebl[:, c0:c0 + nch].unsqueeze(2).to_broadcast([128, nch, C]), op=ALU.mult),
                            reads=[R("t_fg"), ("ebl", ti)], writes=[("ksT", ti)]))
                        if T0 < 1024:
                            st.append(lambda: S.op("act", lambda e: e.activation(out=bg, in_=bg, func=AF.Exp), reads=[R("t_bg")], writes=[R("t_bg")]))
                            st.append(lambda: S.op("dve", lambda e: e.tensor_tensor(out=qgT[:, sl], in0=sq_, in1=bg, op=ALU.mult),
                                                   reads=[R("t_sq"), R("t_bg")], writes=[("qgT", ti)]))
                        return st

                    chains = [chain(ti, T0, N) for ti, (T0, N) in enumerate(TT)]
                    for k in range(max(len(c_) for c_ in chains)):
                        for c_ in chains:
                            if k < len(c_):
                                c_[k]()

                    for ti, (T0, N) in enumerate(TT):
                        bank = hb_bank()
                        proj_fm(w1_, w1res, 128, T0, N, bank)
                        S.op("act", lambda e, N=N, T0=T0, bank=bank: e.activation(out=szT[:, T0:T0 + N], in_=ps[bank][:, 0:N], func=AF.Silu),
                             reads=[PS(bank)], writes=[("szT", ti)])
                    for ti, (T0, N) in enumerate(TT):
                        bank = hb_bank()
                        proj_fm(w1_, w1res, 0, T0, N, bank)
                        copy("act", vT[:, T0:T0 + N], ps[bank][:, 0:N], [PS(bank)], [("vT", ti)])

                    def chunk_step(t0, C, ci, Sf32, Sbf, sres, par):
                        ti = 0 if t0 < 512 else (1 if t0 < 1024 else 2)
                        kv = kvtm[par]
                        at = att[par]
                        tb = 7
                        S.op("pe", lambda e: e.transpose(out=psb[tb][0:C, 0:128], in_=ksT[:, t0:t0 + C], identity=ident[:]),
                             reads=[("ksT", ti), "ident"], writes=[PS(tb)])
                        S.op("pe", lambda e: e.transpose(out=psb[tb][0:C, 128:256], in_=vT[:, t0:t0 + C], identity=ident[:]),
                             reads=[("vT", ti), "ident"], writes=[PS(tb)])
                        copy("act", kv[0:C, :], psb[tb][0:C, 0:256], [PS(tb)], [("kvtm", par)])
                        S.op("pe", lambda e: e.matmul(ps[2][0:C, 0:C], lhsT=kiT[:, t0:t0 + C], rhs=qtT[:, t0:t0 + C], start=True, stop=True),
                             reads=[("kiT", ti), ("qtT", ti)], writes=[PS(2)])
                        S.op("dve", lambda e: e.tensor_tensor(out=at[0:C, 0:C], in0=ps[2][0:C, 0:C], in1=mown[0:C, 0, 0:C], op=ALU.mult),
                             reads=[PS(2), "mown"], writes=[("att", par)])
                        ob_ = par
                        S.op("pe", lambda e: e.matmul(ps[ob_][:, 0:C], lhsT=Sbf[:, :], rhs=qtT[:, t0:t0 + C], start=True, stop=False),
                             reads=[sres + "b", ("qtT", ti)], writes=[PS(ob_)])
                        S.op("pe", lambda e: e.matmul(ps[ob_][:, 0:C], lhsT=kv[0:C, 128:256], rhs=at[0:C, 0:C], start=False, stop=True),
                             reads=[("kvtm", par), ("att", par)], writes=[PS(ob_)])
                        copy("act", oTs[:, t0:t0 + C], ps[ob_][:, 0:C], [PS(ob_)], [("t_sq", ti)])
                        sb_ = 7
                        S.op("pe", lambda e: e.matmul(ps[sb_][:, 0:128], lhsT=kv[0:C, 0:128], rhs=kv[0:C, 128:256], start=True, stop=True),
                             reads=[("kvtm", par)], writes=[PS(sb_)])
                        S.op("dve", lambda e: e.scalar_tensor_tensor(out=Sf32[:, :], in0=Sf32[:, :], scalar=ebl[:, ci:ci + 1], in1=ps[sb_][:, 0:128],
                                                                     op0=ALU.mult, op1=ALU.add),
                             reads=[PS(sb_), sres, ("ebl", ti)], writes=[sres])
                        copy("act", Sbf[:, :], Sf32[:, :], [sres], [sres + "b"])

                    TLR = [("t_lf", 0), ("t_lf", 1), ("t_lf", 2)]

                    def l_tr(c):
                        t0, ti, par = c * 64, (0 if c < 8 else 1), c % 2
                        kv = kvtm[par]
                        S.op("pe", lambda e: e.transpose(out=psb[7][0:64, 0:128], in_=ksT[:, t0:t0 + 64], identity=ident[:]),
                             reads=[("ksT", ti), "ident"], writes=[PS(7)])
                        S.op("pe", lambda e: e.transpose(out=psb[7][0:64, 128:256], in_=vT[:, t0:t0 + 64], identity=ident[:]),
                             reads=[("vT", ti), "ident"], writes=[PS(7)])
                        copy("act", kv[0:64, :], psb[7][0:64, 0:256], [PS(7)], [("kvtm", par)])

                    def l_att(c):
                        t0, ti, par = c * 64, (0 if c < 8 else 1), c % 2
                        at = att[par]
                        S.op("pe", lambda e: e.matmul(ps[2][0:64, 0:64], lhsT=kiT[:, t0:t0 + 64], rhs=qtT[:, t0:t0 + 64], start=True, stop=True),
                             reads=[("kiT", ti), ("qtT", ti)], writes=[PS(2)])
                        S.op("dve", lambda e: e.tensor_tensor(out=at[0:64, 0:64], in0=ps[2][0:64, 0:64], in1=mown[0:64, 0, 0:64], op=ALU.mult),
                             reads=[PS(2), "mown"], writes=[("att", par)])

                    def q_half(half):
                        bank = 3 + half
                        for c in range(8 * half, 8 * half + 8):
                            if c == 0:
                                continue
                            col = (c % 8) * 64
                            S.op("pe", lambda e, c=c, col=col: e.matmul(
                                ps[bank][:, col:col + 64], lhsT=Sball[:, c - 1, :], rhs=qtT[:, c * 64:(c + 1) * 64], start=True, stop=True),
                                reads=[("Sball", (c - 1) // 8), ("qtT", half)] + TLR, writes=[PS(bank)])
                        lo = 64 if half == 0 else 0
                        T0_ = half * 512
                        S.op("dve", lambda e: e.tensor_tensor(
                            out=oTs[:, T0_ + lo:T0_ + 512], in0=oTs[:, T0_ + lo:T0_ + 512], in1=ps[bank][:, lo:512], op=ALU.add),
                            reads=[PS(bank), ("t_sq", half)], writes=[("t_sq", half)])

                    def l_mm(c):
                        ti, par = (0 if c < 8 else 1), c % 2
                        kv, at = kvtm[par], att[par]
                        ob_ = 3 + c // 8
                        col = (c % 8) * 64
                        S.op("pe", lambda e: e.matmul(ps[ob_][:, col:col + 64], lhsT=kv[0:64, 128:256], rhs=at[0:64, 0:64], start=True, stop=True),
                             reads=[("kvtm", par), ("att", par)], writes=[PS(ob_)])
                        sb_ = 5 + (c // 4) % 2
                        scol = (c % 4) * 128
                        S.op("pe", lambda e: e.matmul(ps[sb_][:, scol:scol + 128], lhsT=kv[0:64, 0:128], rhs=kv[0:64, 128:256], start=True, stop=True),
                             reads=[("kvtm", par)], writes=[PS(sb_)])
                        if c % 4 == 3:
                            copy("dve", Sloc[:, c - 3:c + 1, :], ps[sb_][:, 0:512].rearrange("p (c d) -> p c d", d=128),
                                 [PS(sb_)], [("Sl", cc_) for cc_ in range(c - 3, c + 1)])
                        if c % 8 == 7:
                            T0_ = (c // 8) * 512
                            copy("act", oTs[:, T0_:T0_ + 512], ps[ob_][:, 0:512], [PS(ob_)], [("t_sq", ti)])
                        if c % 4 == 3:
                            for c2 in range(max(c - 3, 1), c + 1):
                                ti2 = 0 if c2 < 8 else 1
                                S.op("dve", lambda e, c2=c2: e.scalar_tensor_tensor(out=Sloc[:, c2, :], in0=Sloc[:, c2 - 1, :], scalar=ebl[:, c2:c2 + 1],
                                                                                  in1=Sloc[:, c2, :], op0=ALU.mult, op1=ALU.add),
                                     reads=[("Sl", c2), ("Sl", c2 - 1), ("ebl", ti2)], writes=[("Sl", c2)])
                        if c == 7 or c == 15:
                            half = c // 8
                            lo_c, hi_c = (0, 8) if half == 0 else (8, 15)
                            copy("act", Sball[:, lo_c:hi_c, :], Sloc[:, lo_c:hi_c, :], [("Sl", cc_) for cc_ in range(lo_c, hi_c)],
                                 [("Sball", half)] + TLR)

                    l_tr(0)
                    l_att(0)
                    for c in range(16):
                        if c + 1 < 16:
                            l_tr(c + 1)
                            l_att(c + 1)
                        l_mm(c)
                        if c == 11:
                            q_half(0)
                    q_half(1)
                    S.dma("sp", "hgx", lambda e, j=j: [e.dma_start(out=cc_h_in[j][:, :], in_=Sloc[:, 15, :])], reads=[("Sl", 15)], writes=["cc_h_in"])
                    S.dma("pool", "cc_h", lambda e, j=j: [e.collective_compute(
                        "AllGather", ALU.bypass, replica_groups=PAIRS, ins=[cc_h_in[j].opt()], outs=[cc_h_out[j].opt()])],
                        inc=1, reads=["cc_h_in"], writes=["cc_h_out"])
                    pk_g, pk_post = fq_park() if j + 1 < 16 else ([], None)
                    for s_ in range(4):
                        pq = s_ % 2
                        Sfq, Sbq, nmq = SfP[pq], SbP[pq], f"Sf{pq}"
                        S.dma("sp", ("hgs_in", pq), lambda e, s_=s_, j=j, Sfq=Sfq: [e.dma_start(out=Sfq[:, :], in_=shg[s_, j])], writes=[nmq])
                        copy("act", Sbq[:, :], Sfq[:, :], [nmq], [nmq + "b"])
                        chunk_step(1024 + s_ * 8, 8, 16 + s_, Sfq, Sbq, nmq, pq)
                        S.dma("sp", ("hgs_out", pq), lambda e, s_=s_, j=j, Sfq=Sfq: [e.dma_start(out=hgs[s_, j], in_=Sfq[:, :])], reads=[nmq])
                        if pk_g and s_ in (0, 2):
                            pk_g[s_ // 2]()
                    S.dma("sp", "hgx", lambda e, j=j: [e.dma_start(out=SA[:, :], in_=cc_h_out[j][0:128, :])], reads=["cc_h_out"], writes=["SA"])
                    S.op("dve", lambda e: e.tensor_scalar(out=SA[:, :], in0=SA[:, :], scalar1=flag[:, 0:1], scalar2=None, op0=ALU.mult),
                         reads=["SA", "flag"], writes=["SA"])
                    copy("act", SAb[:, :], SA[:, :], ["SA", "Sf0b"], ["SAb", "Sf0b"])
                    for ti, (T0, N) in enumerate(TT[:2]):
                        bank = pt_bank()
                        S.op("pe", lambda e, T0=T0, N=N, bank=bank: e.matmul(ps[bank][:, 0:N], lhsT=SAb[:, :], rhs=qgT[:, T0:T0 + N],
                                                                           start=True, stop=True),
                             reads=["SAb", ("qgT", ti)], writes=[PS(bank)])
                        S.op("dve", lambda e, T0=T0, N=N, bank=bank: e.tensor_tensor(out=oTs[:, T0:T0 + N], in0=oTs[:, T0:T0 + N],
                                                                                  in1=ps[bank][:, 0:N], op=ALU.add),
                             reads=[PS(bank), ("t_sq", ti)], writes=[("t_sq", ti)])
                    S.op("dve", lambda e: e.scalar_tensor_tensor(out=Sf[:, :], in0=SA[:, :], scalar=ebt[:, 0:1], in1=Sloc[:, 15, :],
                                                                 op0=ALU.mult, op1=ALU.add), reads=["SA", "ebt", ("Sl", 15), "Sf0"], writes=["Sf0"])
                    S.dma("sp", "hgp_out", lambda e, j=j: [e.dma_start(out=hgp[j], in_=Sf[:, :])], reads=["Sf0"])
                    if pk_g:
                        pk_g[2]()
                    def tail(ti, T0, N, j=j):
                        sl = slice(T0, T0 + N)
                        bank = (0, 1, 7)[ti]
                        R = lambda n: (n, ti)
                        return [
                            lambda: S.op("pool", lambda e: e.tensor_tensor(out=t_sq2[:, sl], in0=oTs[:, sl], in1=oTs[:, sl], op=ALU.mult),
                                         reads=[("t_sq", ti)], writes=[R("t_sq2")]),
                            lambda: S.op("pe", lambda e: e.matmul(ps[bank][:, 0:N], lhsT=onesb[:, :], rhs=t_sq2[:, sl], start=True, stop=True),
                                         reads=[R("t_sq2"), "onesb"], writes=[PS(bank)]),
                            lambda: S.op("act", lambda e: e.activation(out=t_b[:, sl], in_=ps[bank][:, 0:N], func=AF.Ln,
                                                                       scale=1.0 / 128.0, bias=EPS), reads=[PS(bank)], writes=[R("t_b")]),
                            lambda: S.op("act", lambda e: e.activation(out=t_b[:, sl], in_=t_b[:, sl], func=AF.Exp, scale=-0.5),
                                         reads=[R("t_b")], writes=[R("t_b")]),
                            lambda: S.op("dve", lambda e: e.tensor_tensor(out=t_b[:, sl], in0=t_b[:, sl], in1=oTs[:, sl], op=ALU.mult),
                                         reads=[R("t_b"), ("t_sq", ti)], writes=[R("t_b")]),
                            lambda: S.op("dve", lambda e: e.scalar_tensor_tensor(out=goT[:, j, sl], in0=t_b[:, sl], scalar=ng[:, 0:1],
                                                                                 in1=szT[:, sl], op0=ALU.mult, op1=ALU.mult),
                                         reads=[R("t_b"), "ng", ("szT", ti)], writes=goT_res(T0, N)),
                        ]
                    tails = [tail(ti, T0, N) for ti, (T0, N) in enumerate(TT)]
                    for k in range(6):
                        for t_ in tails:
                            t_[k]()
                        if k == 0 and pk_g:
                            pk_g[3]()
                    if pk_post is not None:
                        pk_post()
            S.barrier()

        if True:
            for li, kind in enumerate(layer_kinds):
                norm_phase(ln_g[li])
                if kind == "a":
                    attn_phase(li // 3)
                elif kind == "b":
                    conv_phase()
                else:
                    hgrn_phase(li)
                wout_phase()
            norm_phase(final_g if not dbg else final_g, final=True)
        S.emit()
    nc._sched_stats = S.stats
    return nc


def rope_tables(hf):
    half = 8
    inv = 500000.0 ** (-np.arange(half, dtype=np.float64) * 2.0 / 16.0)
    pos = np.zeros((128, NB), np.float64)
    for b in range(8):
        pos[:, b] = hf * 1024 + b * 128 + np.arange(128)
    pos[:, 8] = 16384 + (np.arange(128) % 8)
    ang = pos[:, :, None].astype(np.float32).astype(np.float64) * inv.astype(np.float32).astype(np.float64)[None, None, :]
    ang = ang.astype(np.float32).astype(np.float64)
    return np.cos(ang).astype(np.float32), np.sin(ang).astype(np.float32)


_NC_CACHE = {}


def make_in_maps(inp):
    f = lambda a: np.ascontiguousarray(np.asarray(a, dtype=np.float32))
    shared = dict(
        ln_g=f(inp["ln_g"]), final_g=f(inp["final_g"]), a_w_in=f(inp["a_w_in"]), a_w_out=f(inp["a_w_out"]),
        a_sinks=f(inp["a_sinks"]), b_w_in=f(inp["b_w_in"]), b_conv_w=f(inp["b_conv_w"]), b_w_out=f(inp["b_w_out"]),
        c_w_in=f(inp["c_w_in"]), c_norm_g=f(inp["c_norm_g"]), c_w_out=f(inp["c_w_out"]), c_lb=f(inp["c_lb_logits"]))
    x_prompt, x_sample = f(inp["x_prompt"]), f(inp["x_sample"])
    cache_k, cache_v = f(inp["cache_k"]), f(inp["cache_v"])
    state_conv, state_hgrn = f(inp["state_conv"]), f(inp["state_hgrn"])
    maps = []
    for c in range(8):
        p, hf = c // 2, c % 2
        cs, sn = rope_tables(hf)
        m = dict(shared)
        m.update(
            xp=np.ascontiguousarray(x_prompt[p, hf * 1024:(hf + 1) * 1024]),
            xsm=np.ascontiguousarray(x_sample[4 * c:4 * c + 4].reshape(32, D)),
            ck=np.ascontiguousarray(cache_k[:, 4 * c:4 * c + 4].reshape(2, 4, 128, 256)),
            cv=np.ascontiguousarray(cache_v[:, 4 * c:4 * c + 4].reshape(2, 4, 128, 256)),
            sconv=np.ascontiguousarray(state_conv[0, 4 * c:4 * c + 4].reshape(8, D)),
            shg=np.ascontiguousarray(state_hgrn[0, 4 * c:4 * c + 4]),
            ropec=cs, ropes=sn, flag=np.full((128, 1), float(hf), np.float32))
        maps.append(m)
    return maps


def assemble(res):
    y_prompt = np.zeros((4, 2048, D), np.float32)
    y_sample = np.zeros((32, 8, D), np.float32)
    kpo = np.zeros((2, 4, 128, 4, 64), np.float32)
    vpo = np.zeros_like(kpo)
    kso = np.zeros((2, 32, 128, 4, 64), np.float32)
    vso = np.zeros_like(kso)
    cpo = np.zeros((1, 4, 2, D), np.float32)
    cso = np.zeros((1, 32, 2, D), np.float32)
    hpo = np.zeros((1, 4, 16, 128, 128), np.float32)
    hso = np.zeros((1, 32, 16, 128, 128), np.float32)
    for c in range(8):
        r = res[c]
        p, hf = c // 2, c % 2
        y_prompt[p, hf * 1024:(hf + 1) * 1024] = r["y"][:1024]
        y_sample[4 * c:4 * c + 4] = r["y"][1024:1056].reshape(4, 8, D)
        kso[:, 4 * c:4 * c + 4] = r["ks"].reshape(2, 4, 128, 4, 64)
        vso[:, 4 * c:4 * c + 4] = r["vs"].reshape(2, 4, 128, 4, 64)
        cso[0, 4 * c:4 * c + 4] = r["convs"].reshape(4, 2, D)
        hso[0, 4 * c:4 * c + 4] = r["hgs"]
        if hf == 1:
            kpo[:, p] = r["kp"].reshape(2, 128, 4, 64)
            vpo[:, p] = r["vp"].reshape(2, 128, 4, 64)
            cpo[0, p] = r["convp"]
            hpo[0, p] = r["hgp"]
    return (y_prompt, y_sample, kpo, vpo, kso, vso, cpo, cso, hpo, hso)


def kernel(**inputs):
    if "nc" not in _NC_CACHE:
        _NC_CACHE["nc"] = build()
    nc = _NC_CACHE["nc"]
    maps = make_in_maps(inputs)
    res = run_bass_kernel_spmd(nc, maps, core_ids=list(range(8)))
    return assemble(res.results)
```

```python
import contextlib
import os
import numpy as np
import concourse.bass as bass
import concourse.mybir as mybir
from concourse.bass_utils import run_bass_kernel_spmd

F32 = mybir.dt.float32
BF16 = mybir.dt.bfloat16
AF = mybir.ActivationFunctionType
ALU = mybir.AluOpType

D = 2048
NB = 9
NTOK = 1056
EPS = 1e-6
PAIRS = [[0, 1], [2, 3], [4, 5], [6, 7]]
COMPUTE = ("pe", "act", "dve", "pool")


class Sched:
    def __init__(self, nc):
        self.nc = nc
        self.ops = []
        self.res_w = {}
        self.res_r = {}
        self.dma_last = {}
        self.streams = {e: [] for e in ("pe", "act", "dve", "pool", "sp")}

    def _deps(self, reads, writes, idx, key):
        raw, war = set(), set()
        for r in reads:
            w = self.res_w.get(r)
            if w is not None:
                raw.add(w)
            if isinstance(r, tuple) and r[0] == "ps":
                for k2, rd in self.res_r.get(r, {}).items():
                    if k2 != key:
                        raw.add(rd)
        for r in writes:
            w = self.res_w.get(r)
            if w is not None:
                war.add(w)
            for rd in self.res_r.get(r, {}).values():
                war.add(rd)
        for r in writes:
            self.res_w[r] = idx
            self.res_r[r] = {}
        for r in reads:
            self.res_r.setdefault(r, {})[key if key is not None else ("dma", idx)] = idx
        raw.discard(idx)
        war.discard(idx)
        return raw, war

    def op(self, eng, fn, reads=(), writes=()):
        idx = len(self.ops)
        raw, war = self._deps(reads, writes, idx, eng)
        self.ops.append(dict(eng=eng, fn=fn, raw=raw, war=war, dma=None, idx=idx))
        self.streams[eng].append(idx)
        return idx

    def dma(self, queue, slot, fn, n=1, inc=16, reads=(), writes=()):
        idx = len(self.ops)
        raw, war = self._deps(reads, writes, idx, None)
        last = self.dma_last.get(slot)
        if last is not None:
            raw.add(last)
        self.dma_last[slot] = idx
        self.ops.append(dict(eng=queue, fn=fn, raw=raw, war=war, dma=slot, idx=idx, n=n, inc=inc))
        self.streams[queue].append(idx)
        return idx

    def barrier(self):
        last = set()
        for e, st in self.streams.items():
            if st:
                last.add(st[-1])
        for v in self.dma_last.values():
            last.add(v)
        for e in COMPUTE + ("sp",):
            idx = len(self.ops)
            self.ops.append(dict(eng=e, fn=None, raw=set(last), war=set(), dma=None, idx=idx))
            self.streams[e].append(idx)

    def emit(self):
        nc, ops = self.nc, self.ops

        def same_eng(p, o):
            return p["dma"] is None and o["dma"] is None and p["eng"] == "pe" and o["eng"] == "pe"

        needed = set()
        for o in ops:
            needed |= o["raw"]
            for d in o["war"]:
                if not same_eng(ops[d], o):
                    needed.add(d)
        with contextlib.ExitStack() as es:
            sem_eng = {e: es.enter_context(nc.semaphore(f"s_{e}")) for e in COMPUTE}
            slots = list(self.dma_last.keys())
            sem_dma = {k: es.enter_context(nc.semaphore(f"d_{i}")) for i, k in enumerate(slots)}
            cnt = {e: 0 for e in COMPUTE}
            dcnt = {k: 0 for k in slots}
            ev = {}
            for o in ops:
                if o["dma"] is not None:
                    dcnt[o["dma"]] += o["inc"] * o["n"]
                    ev[o["idx"]] = (sem_dma[o["dma"]], dcnt[o["dma"]])
                elif o["idx"] in needed and o["fn"] is not None:
                    cnt[o["eng"]] += 1
                    ev[o["idx"]] = (sem_eng[o["eng"]], cnt[o["eng"]])
            self.stats = (dict(cnt), {str(k): v for k, v in dcnt.items()}, {e: len(v) for e, v in self.streams.items()})
            blk = es.enter_context(nc.Block())

            def replay(engname):
                def body(eng):
                    waited = {}
                    for idx in self.streams[engname]:
                        o = ops[idx]
                        deps = set(o["raw"])
                        for d in o["war"]:
                            if not same_eng(ops[d], o):
                                deps.add(d)
                        wl = {}
                        for d in deps:
                            if d not in ev:
                                continue
                            s, v = ev[d]
                            key = id(s)
                            if waited.get(key, 0) >= v:
                                continue
                            if key not in wl or wl[key][1] < v:
                                wl[key] = (s, v)
                        for key, (s, v) in wl.items():
                            eng.wait_ge(s, v)
                            waited[key] = v
                        if o["fn"] is None:
                            continue
                        r = o["fn"](eng)
                        if o["dma"] is not None:
                            s, _ = ev[idx]
                            assert len(r) == o["n"], (len(r), o["n"])
                            for ins in r:
                                ins.then_inc(s, o["inc"])
                        elif idx in ev:
                            ins = r[-1] if isinstance(r, (list, tuple)) else r
                            ins.then_inc(ev[idx][0], 1)
                    if engname == "sp":
                        for k, s in sem_dma.items():
                            if dcnt[k]:
                                eng.wait_ge(s, dcnt[k])
                return body

            blk.tensor(replay("pe"))
            blk.scalar(replay("act"))
            blk.vector(replay("dve"))
            blk.gpsimd(replay("pool"))
            blk.sync(replay("sp"))


def build(n_layers=4, dbg=False, stop=99, small=False):
    nc = bass.Bass("TRN2", target_bir_lowering=False)
    din = lambda name, shape, dt=F32: nc.dram_tensor(name, list(shape), dt, kind="ExternalInput").ap()
    dout = lambda name, shape, dt=F32: nc.dram_tensor(name, list(shape), dt, kind="ExternalOutput").ap()
    dint = lambda name, shape, dt=F32: nc.dram_tensor(name, list(shape), dt, kind="Internal").ap()

    xp = din("xp", [1024, D])
    xsm = din("xsm", [32, D])
    ck = din("ck", [2, 4, 128, 256])
    cv = din("cv", [2, 4, 128, 256])
    sconv = din("sconv", [8, D])
    shg = din("shg", [4, 16, 128, 128])
    ln_g = din("ln_g", [4, D])
    final_g = din("final_g", [D])
    na = 2 if (n_layers >= 4 or not small) else 1
    a_w_in = din("a_w_in", [na, D, 4608])
    a_w_out = din("a_w_out", [na, D, D])
    a_sinks = din("a_sinks", [2, 32])
    b_w_in = din("b_w_in", [1, D, 8192] if (n_layers >= 2 or not small) else [1, 1, 8192])
    b_conv_w = din("b_conv_w", [1, 3, D])
    b_w_out = din("b_w_out", [1, D, D] if (n_layers >= 2 or not small) else [1, 1, D])
    c_w_in = din("c_w_in", [1, D, 8192] if (n_layers >= 3 or not small) else [1, 1, 8192])
    c_norm_g = din("c_norm_g", [1, 128])
    c_w_out = din("c_w_out", [1, D, D] if (n_layers >= 3 or not small) else [1, 1, D])
    c_lb = din("c_lb", [4, D])
    ropec = din("ropec", [128, NB, 8])
    ropes = din("ropes", [128, NB, 8])
    flag_d = din("flag", [128, 1])

    y = dout("y", [NTOK, D])
    kp = dout("kp", [2, 128, 256])
    vp = dout("vp", [2, 128, 256])
    ks = dout("ks", [2, 4, 128, 256])
    vs = dout("vs", [2, 4, 128, 256])
    convp = dout("convp", [2, D])
    convs = dout("convs", [8, D])
    hgp = dout("hgp", [16, 128, 128])
    hgs = dout("hgs", [4, 16, 128, 128])

    cc_a_in = [dint(f"cc_a_in{j}", [128, 512], BF16) for j in range(2)]
    cc_a_out = [dint(f"cc_a_out{j}", [256, 512], BF16) for j in range(2)]
    cc_b_in = dint("cc_b_in", [128, 32])
    cc_b_out = dint("cc_b_out", [256, 32])
    cc_h_in = [dint(f"cc_h_in{h}", [128, 128]) for h in range(16)]
    cc_h_out = [dint(f"cc_h_out{h}", [256, 128]) for h in range(16)]

    S = Sched(nc)
    with contextlib.ExitStack() as es:
        uniq = [0]

        def T(name, shape, dt, stack=es):
            uniq[0] += 1
            return stack.enter_context(nc.sbuf_tensor(f"{name}_{uniq[0]}", list(shape), dt))

        xs = T("xs", [128, NB, D], F32)
        hT = T("hT", [128, 16, NTOK], BF16)
        goT = T("goT", [128, 16, NTOK], BF16)
        wt = [T(f"wt{i}", [128, 16, 256], BF16) for i in range(2)]
        rt = [None] * 4
        ident = T("ident", [128, 128], BF16)
        identf = T("identf", [128, 128], F32)
        onesb = T("onesb", [128, 128], BF16)
        onesf = T("onesf", [128, 128], F32)
        mprev = T("mprev", [128, 2, 128], BF16)
        mown = T("mown", [128, 2, 128], BF16)
        ones4 = T("ones4", [128, 2, 128], BF16)
        flag = T("flag_sb", [128, 1], F32)
        ss = T("ss", [128, NB], F32)
        rstd = T("rstd", [128, NB], F32)
        cosT = T("cosT", [128, NB, 8], F32)
        sinT = T("sinT", [128, NB, 8], F32)
        ps = [es.enter_context(nc.psum_tensor(f"ps{i}", [128, 512], F32)) for i in range(8)]
        psb = [p.bitcast(BF16) for p in ps]

        PS = lambda i: ("ps", i)

        S.op("pool", lambda e: e.memset(onesb[:], 1.0), writes=["onesb"])
        S.op("pool", lambda e: e.memset(onesf[:], 1.0), writes=["onesf"])
        S.op("pool", lambda e: e.memset(ones4[:], 1.0), writes=["ones4"])
        S.op("pool", lambda e: e.affine_select(out=ident[:], in_=onesb[:], pattern=[[-1, 128]], compare_op=ALU.is_equal,
                                               fill=0.0, base=0, channel_multiplier=1), reads=["onesb"], writes=["ident"])
        S.op("pool", lambda e: e.affine_select(out=identf[:], in_=onesf[:], pattern=[[-1, 128]], compare_op=ALU.is_equal,
                                               fill=0.0, base=0, channel_multiplier=1), reads=["onesf"], writes=["identf"])
        S.op("pool", lambda e: e.affine_select(out=mprev[:], in_=ones4[:], pattern=[[0, 2], [-1, 128]], compare_op=ALU.is_ge,
                                               fill=0.0, base=-1, channel_multiplier=1), reads=["ones4"], writes=["mprev"])
        S.op("pool", lambda e: e.affine_select(out=mown[:], in_=ones4[:], pattern=[[0, 2], [1, 128]], compare_op=ALU.is_ge,
                                               fill=0.0, base=0, channel_multiplier=-1), reads=["ones4"], writes=["mown"])
        S.op("pool", lambda e: e.memset(xs[:, 8, :], 0.0), writes=[("x", 8)])
        S.dma("sp", "misc", lambda e: [e.dma_start(out=flag[:], in_=flag_d[:, :]),
                                       e.dma_start(out=cosT[:], in_=ropec[:, :, :]),
                                       e.dma_start(out=sinT[:], in_=ropes[:, :, :])], n=3, writes=["flag", "rope"])
        for b in range(8):
            S.dma("sp", ("xin", b % 4), lambda e, b=b: [e.dma_start(out=xs[:, b, :], in_=xp[b * 128:(b + 1) * 128, :])],
                  writes=[("x", b)])
        S.dma("sp", ("xin", 0), lambda e: [e.dma_start(out=xs[0:32, 8, :], in_=xsm[:, :])], writes=[("x", 8)])

        tiles = []

        def wsrc(w2d, c0, width=256):
            return [(w2d[:, c0:c0 + width], 0)]

        def group_src(w2d, j, hh):
            return [(w2d[:, g * 2048 + j * 128: g * 2048 + (j + 1) * 128], (g % 2) * 128) for g in (2 * hh, 2 * hh + 1)]

        layer_kinds = ["a", "b", "c", "a"][:n_layers]
        for li, kind in enumerate(layer_kinds):
            jj = li // 3
            if kind == "a":
                w_in, w_out = a_w_in[jj], a_w_out[jj]
                tiles.append(wsrc(w_in, 2048))
                tiles.append(wsrc(w_in, 2304))
                for GG in range(8):
                    tiles.append(wsrc(w_in, GG * 256))
                    tiles.append(wsrc(w_in, 2560 + GG * 256))
            elif kind == "b":
                w_in, w_out = b_w_in[0], b_w_out[0]
                for j in range(16):
                    tiles.append(group_src(w_in, j, 0))
                    tiles.append(group_src(w_in, j, 1))
            else:
                w_in, w_out = c_w_in[0], c_w_out[0]
                for j in range(16):
                    tiles.append(group_src(w_in, j, 0))
                    tiles.append(group_src(w_in, j, 1))
            for t in range(8):
                tiles.append(wsrc(w_out, t * 256))
        tstate = dict(next_load=0, next_use=0)

        def issue_load():
            i = tstate["next_load"]
            if i >= len(tiles):
                return
            tstate["next_load"] += 1
            slot = i % 2
            srcs = tiles[i]

            def fn(e, srcs=srcs, slot=slot):
                return [e.dma_start(out=wt[slot][:, :, off:off + a.shape[1]],
                                    in_=a.rearrange("(k p) n -> p k n", p=128)) for a, off in srcs]
            S.dma("pool", ("wt", slot), fn, n=len(srcs), writes=[("wt", slot)])

        def next_tile(prefetch=True):
            i = tstate["next_use"]
            tstate["next_use"] += 1
            while tstate["next_load"] <= (min(i + 1, len(tiles) - 1) if prefetch else i):
                issue_load()
            return wt[i % 2], ("wt", i % 2)

        issue_load()

        def blkP(b):
            return 128 if b < 8 else 32

        def tok0(b):
            return b * 128

        rr = dict(pt=0, ev=0)

        def pt_bank():
            rr["pt"] ^= 1
            return rr["pt"]

        def evac_eng():
            rr["ev"] ^= 1
            return "act" if rr["ev"] else "dve"

        def copy(engname, out, in_, reads, writes):
            if engname == "act":
                S.op("act", lambda e: e.copy(out=out, in_=in_), reads=reads, writes=writes)
            else:
                S.op(engname, lambda e: e.tensor_copy(out=out, in_=in_), reads=reads, writes=writes)

        def transpose_to(src_tile, src_res, ncols_blocks, P, dst_fn, dst_res_fn):
            for g0 in range(0, ncols_blocks, 4):
                rr["tr"] = rr.get("tr", 0) ^ 1
                bank = 4 + rr["tr"]
                n = min(4, ncols_blocks - g0)
                for j in range(n):
                    c = g0 + j
                    S.op("pe", lambda e, c=c, j=j, bank=bank: e.transpose(
                        out=psb[bank][:, j * 128:j * 128 + P], in_=src_tile[0:P, c * 128:(c + 1) * 128],
                        identity=ident[0:P, 0:P]), reads=[src_res, "ident"], writes=[PS(bank)])
                ee = evac_eng()
                for j in range(n):
                    c = g0 + j
                    copy(ee, dst_fn(c), psb[bank][:, j * 128:j * 128 + P], [PS(bank)], [dst_res_fn(c)])

        def norm_phase(g_row, final=False):
            with contextlib.ExitStack() as ph:
                gbc = T("gbc", [128, D], F32, ph)
                junk = T("junk", [128, D], BF16, ph)
                if final:
                    yb = [T(f"yb{i}", [128, D], F32, ph) for i in range(2)]
                else:
                    hb = [T(f"hb{i}", [128, D], BF16, ph) for i in range(2)]
                S.dma("sp", "gbc", lambda e: [e.dma_start(out=gbc[:], in_=g_row.partition_broadcast(128))], writes=["gbc"])
                for b in range(NB):
                    P = blkP(b)
                    S.op("act", lambda e, b=b, P=P: e.activation(out=junk[0:P, :], in_=xs[0:P, b, :], func=AF.Square,
                                                                 accum_out=ss[0:P, b:b + 1]),
                         reads=[("x", b)], writes=["junk", ("ss", b)])
                    S.op("act", lambda e, b=b, P=P: e.activation(out=rstd[0:P, b:b + 1], in_=ss[0:P, b:b + 1], func=AF.Sqrt,
                                                                 scale=1.0 / D, bias=EPS), reads=[("ss", b)], writes=[("rstd", b)])
                    S.op("dve", lambda e, b=b, P=P: e.reciprocal(out=rstd[0:P, b:b + 1], in_=rstd[0:P, b:b + 1]),
                         reads=[("rstd", b)], writes=[("rstd", b)])
                    if final:
                        ybt = yb[b % 2]
                        S.op("dve", lambda e, b=b, P=P, ybt=ybt: e.scalar_tensor_tensor(
                            out=ybt[0:P, :], in0=xs[0:P, b, :], scalar=rstd[0:P, b:b + 1], in1=gbc[0:P, :],
                            op0=ALU.mult, op1=ALU.mult), reads=[("x", b), ("rstd", b), "gbc"], writes=[("yb", b % 2)])
                        S.dma("sp", ("yout", b % 2), lambda e, b=b, P=P, ybt=ybt: [
                            e.dma_start(out=y[b * 128:b * 128 + P, :], in_=ybt[0:P, :])], reads=[("yb", b % 2)])
                    else:
                        hbt = hb[b % 2]
                        S.op("dve", lambda e, b=b, P=P, hbt=hbt: e.scalar_tensor_tensor(
                            out=hbt[0:P, :], in0=xs[0:P, b, :], scalar=rstd[0:P, b:b + 1], in1=gbc[0:P, :],
                            op0=ALU.mult, op1=ALU.mult), reads=[("x", b), ("rstd", b), "gbc"], writes=[("hb", b % 2)])
                        transpose_to(hbt, ("hb", b % 2), 16, P,
                                     lambda c, b=b, P=P: hT[:, c, tok0(b):tok0(b) + P], lambda c, b=b: ("hT", b))
                S.barrier()

        def wout_phase():
            for t in range(8):
                w, wres = next_tile()
                for b in range(NB):
                    P = blkP(b)
                    bank = pt_bank()
                    for k in range(16):
                        S.op("pe", lambda e, k=k, b=b, P=P, bank=bank, w=w: e.matmul(
                            ps[bank][0:P, 0:256], lhsT=goT[:, k, tok0(b):tok0(b) + P], rhs=w[:, k, :],
                            start=(k == 0), stop=(k == 15)), reads=[wres, ("goT", b)], writes=[PS(bank)])
                    S.op("dve", lambda e, b=b, P=P, bank=bank, t=t: e.tensor_tensor(
                        out=xs[0:P, b, t * 256:(t + 1) * 256], in0=xs[0:P, b, t * 256:(t + 1) * 256],
                        in1=ps[bank][0:P, 0:256], op=ALU.add), reads=[PS(bank), ("x", b)], writes=[("x", b)])

        def proj_tm(w, wres, b, bank, ncols=256):
            P = blkP(b)
            for k in range(16):
                S.op("pe", lambda e, k=k, b=b, P=P, bank=bank, w=w: e.matmul(
                    ps[bank][0:P, 0:ncols], lhsT=hT[:, k, tok0(b):tok0(b) + P], rhs=w[:, k, 0:ncols],
                    start=(k == 0), stop=(k == 15)), reads=[wres, ("hT", b)], writes=[PS(bank)])

        def rope_tm(src3, dst1, dst2, P, b, nh, rd, wr):
            x1, x2 = src3[:, :, 0:8], src3[:, :, 8:16]
            cosb = cosT[0:P, b, :].unsqueeze(1).to_broadcast([P, nh, 8])
            sinb = sinT[0:P, b, :].unsqueeze(1).to_broadcast([P, nh, 8])
            r = [t_[0:P, 0:nh, :] for t_ in rt]
            for i, (xa, tb_) in enumerate(((x1, cosb), (x2, sinb), (x2, cosb), (x1, sinb))):
                S.op("dve", lambda e, i=i, xa=xa, tb_=tb_, r=r: e.tensor_tensor(out=r[i], in0=xa, in1=tb_, op=ALU.mult),
                     reads=rd + ["rope"], writes=[f"rt{i}"])
            S.op("dve", lambda e, r=r: e.tensor_tensor(out=dst1, in0=r[0], in1=r[1], op=ALU.subtract),
                 reads=["rt0", "rt1"], writes=wr)
            S.op("dve", lambda e, r=r: e.tensor_tensor(out=dst2, in0=r[2], in1=r[3], op=ALU.add),
                 reads=["rt2", "rt3"], writes=wr)

        def attn_phase(jl):
            with contextlib.ExitStack() as ph:
                kT2 = T("kT2", [128, 4, 1152], BF16, ph)
                vall = T("vall", [128, 9, 4, 66], BF16, ph)
                kT2s = T("kT2s", [128, 4, 4 * 136], BF16, ph)
                vSc = T("vSc", [128, 4, 4, 66], BF16, ph)
                vSn = T("vSn", [8, 4, 4, 66], BF16, ph)
                vS32 = T("vS32", [32, 4, 64], BF16, ph)
                kcb = T("kcb", [128, 4, 2, 64], BF16, ph)
                kc32 = T("kc32", [128, 256], BF16, ph)
                ktm = T("ktm", [128, NB, 256], F32, ph) if False else None
                ktm1 = T("ktm1", [128, 256], F32, ph)
                k7 = T("k7", [128, 256], F32, ph)
                vtm = T("vtm", [128, 256], F32, ph)
                kb = T("kb", [128, 4, 2, 64], BF16, ph)
                halo = T("halo", [128, 512], BF16, ph)
                halo_in = T("halo_in", [128, 512], BF16, ph)
                rt = [T(f"rt{i}", [128, 4, 8], F32, ph) for i in range(4)]
                qf = [T(f"qf{i}", [128, 256], F32, ph) for i in range(3)]
                qb = [T(f"qb{i}", [128, 256], BF16, ph) for i in range(3)]
                qT = T("qT", [128, 2, 128], BF16, ph)
                E = [T(f"E{i}", [128, 2, 128], BF16, ph) for i in range(8)]
                den = T("den", [128, 4], F32, ph)
                ob = T("ob", [128, 4, 64], BF16, ph)
                obs = T("obs", [8, 4, 64], BF16, ph)
                oball = T("oball", [128, NB, 256], BF16, ph)
                sz = [T(f"sz{i}", [128, 256], BF16, ph) for i in range(2)]
                gg = [T(f"gg{i}", [128, 256], BF16, ph) for i in range(2)]
                esink = T("esink", [128, 32], F32, ph)
                attn_body(jl, locals())
            S.barrier()

        def attn_body(jl, L):
            kT2, vall, kT2s, vSc, vSn, vS32, kcb, kc32 = (L[k] for k in "kT2 vall kT2s vSc vSn vS32 kcb kc32".split())
            ktm1, k7, vtm, kb, halo, halo_in, qf, qb, qT, E, den, ob, obs, oball, sz, gg, esink = (
                L[k] for k in "ktm1 k7 vtm kb halo halo_in qf qb qT E den ob obs oball sz gg esink".split())
            nonlocal_rt = L["rt"]
            rt[:] = nonlocal_rt

            S.dma("sp", "misc", lambda e: [e.dma_start(out=esink[:], in_=a_sinks[jl].partition_broadcast(128))],
                  writes=["esink"])
            S.op("act", lambda e: e.activation(out=esink[:], in_=esink[:], func=AF.Exp), reads=["esink"], writes=["esink"])
            S.op("pool", lambda e: e.memset(ob[:], 0.0), writes=["ob"])
            S.op("pool", lambda e: e.memset(vall[:, :, :, 64:65], 1.0), writes=["vall_ones"])
            S.op("pool", lambda e: e.memset(vSc[:, :, :, 64:65], 1.0), writes=["vSc_ones"])
            S.op("pool", lambda e: e.memset(vSn[:, :, :, 64:65], 1.0), writes=["vSn_ones"])
            S.dma("pool", "cachev", lambda e: [e.dma_start(
                out=vSc[:, s, :, 0:64], in_=cv[jl, s].rearrange("r (h d) -> r h d", d=64)) for s in range(4)], n=4,
                reads=["vSc_ones"], writes=["vSc"])
            for s in range(4):
                S.dma("pool", "cachek", lambda e, s=s: [e.dma_start(out=kc32[:], in_=ck[jl, s])], writes=["kc32"])
                for half in range(2):
                    copy("dve" if half else "act", kcb[:, :, half, :], kc32[:].rearrange("r (h d) -> r h d", d=64),
                         ["kc32"], [("kcb", half)])
                bank = pt_bank()
                for h in range(4):
                    S.op("pe", lambda e, h=h, bank=bank: e.transpose(
                        out=psb[bank][:, h * 128:(h + 1) * 128], in_=kcb[:, h].rearrange("r t d -> r (t d)"), identity=ident[:]),
                        reads=[("kcb", 0), ("kcb", 1), "ident"], writes=[PS(bank)])
                copy(evac_eng(), kT2s[:, :, s * 136:s * 136 + 128], psb[bank][:, 0:512].rearrange("p (h t) -> p h t", t=128),
                     [PS(bank)], [("kT2s", s)])
            S.dma("sp", "cachecp", lambda e: [e.dma_start(out=ks[jl, :, 0:120, :], in_=ck[jl, :, 8:128, :]),
                                              e.dma_start(out=vs[jl, :, 0:120, :], in_=cv[jl, :, 8:128, :])], n=2)

            border = [7, 6, 5, 4, 3, 2, 1, 0, 8]
            if stop <= 1:
                return
            w, wres = next_tile()
            for b in border:
                P = blkP(b)
                bank = pt_bank()
                proj_tm(w, wres, b, bank)
                kt = k7 if b >= 7 else ktm1
                kres = "k7" if b >= 7 else "ktm1"
                copy("act", kt[0:P, :], ps[bank][0:P, 0:256], [PS(bank)], [kres])
                kv3 = kt[0:P, :].rearrange("p (h d) -> p h d", d=64)
                rope_tm(kv3, kv3[:, :, 0:8], kv3[:, :, 8:16], P, b, 4, [kres], [kres])
                for half in range(2):
                    copy("act" if half else "dve", kb[0:P, :, half, :], kv3, [kres], [("kb", half)])
                tb = pt_bank()
                for h in range(4):
                    S.op("pe", lambda e, h=h, P=P, tb=tb: e.transpose(
                        out=psb[tb][:, h * 128:h * 128 + P], in_=kb[0:P, h].rearrange("p t d -> p (t d)"),
                        identity=ident[0:P, 0:P]), reads=[("kb", 0), ("kb", 1), "ident"], writes=[PS(tb)])
                pv3 = psb[tb][:, 0:512].rearrange("p (h t) -> p h t", t=128)
                if b < 8:
                    copy(evac_eng(), kT2[:, :, 128 + b * 128:256 + b * 128], pv3, [PS(tb)], [("kT2", 1 + b)])
                else:
                    for s in range(4):
                        copy(evac_eng(), kT2s[:, :, s * 136 + 128:s * 136 + 136], pv3[:, :, s * 8:(s + 1) * 8],
                             [PS(tb)], [("kT2s", s)])
                if b == 7:
                    S.dma("sp", "kvout", lambda e: [e.dma_start(out=kp[jl], in_=k7[:, :])], reads=["k7"])
                    copy("dve", halo[:, 0:256], k7[:, :], ["k7"], ["halo"])
                if b == 8:
                    S.dma("sp", "kvout", lambda e: [e.dma_start(out=ks[jl, s, 120:128, :], in_=k7[s * 8:(s + 1) * 8, :])
                                                    for s in range(4)], n=4, reads=["k7"])
            if stop <= 2:
                return
            w, wres = next_tile()
            for b in border:
                P = blkP(b)
                bank = pt_bank()
                proj_tm(w, wres, b, bank)
                if stop < 2.1:
                    continue
                if b < 8:
                    copy("dve", vall[:, 1 + b, :, 0:64], ps[bank][:, 0:256].rearrange("p (h d) -> p h d", d=64),
                         [PS(bank), "vall_ones"], [("vall", 1 + b)])
                else:
                    copy("dve", vS32[:, :, :], ps[bank][0:32, 0:256].rearrange("p (h d) -> p h d", d=64), [PS(bank)], ["vS32"])
                    if stop >= 2.3:
                        S.dma("sp", "vsn", lambda e: [e.dma_start(out=vSn[:, s, :, 0:64], in_=vS32[s * 8:(s + 1) * 8, :, :])
                                                      for s in range(4)], n=4, reads=["vS32", "vSn_ones"], writes=["vSn"])
                if stop < 2.15:
                    continue
                if b >= 7:
                    copy("act", vtm[0:P, :], ps[bank][0:P, 0:256], [PS(bank), ("vall", 1 + b) if b < 8 else "vS32"], ["vtm"])
                if b == 8:
                    S.dma("sp", "kvout", lambda e: [e.dma_start(out=vs[jl, s, 120:128, :], in_=vtm[s * 8:(s + 1) * 8, :])
                                                    for s in range(4)], n=4, reads=["vtm"])
                if b == 7:
                    S.dma("sp", "kvout", lambda e: [e.dma_start(out=vp[jl], in_=vtm[:, :])], reads=["vtm"])
                    if stop < 2.2:
                        continue
                    copy("dve", halo[:, 256:512], vtm[:, :], ["vtm"], ["halo"])
                    S.dma("sp", "halo", lambda e: [e.dma_start(out=cc_a_in[jl][:, :], in_=halo[:, :])],
                          reads=["halo"], writes=["cc_a_in"])
                    if stop >= 2.6:
                      S.dma("pool", "cc_a", lambda e: [e.collective_compute(
                        "AllGather", ALU.bypass, replica_groups=PAIRS, ins=[cc_a_in[jl].opt()],
                        outs=[cc_a_out[jl].opt()])], inc=1, reads=["cc_a_in"], writes=["cc_a_out"])
                    S.dma("sp", "halo", lambda e: [e.dma_start(out=halo_in[:, :], in_=cc_a_out[jl][0:128, :])],
                          reads=["cc_a_out"], writes=["halo_in"])
            if stop <= 3:
                return
            for half in range(2):
                copy("dve", kb[:, :, half, :], halo_in[:, 0:256].rearrange("p (h d) -> p h d", d=64),
                     ["halo_in"], [("kb", half)])
            copy("act", vall[:, 0, :, 0:64], halo_in[:, 256:512].rearrange("p (h d) -> p h d", d=64),
                 ["halo_in", "vall_ones"], [("vall", 0)])
            tb = pt_bank()
            for h in range(4):
                S.op("pe", lambda e, h=h, tb=tb: e.transpose(
                    out=psb[tb][:, h * 128:(h + 1) * 128], in_=kb[:, h].rearrange("p t d -> p (t d)"),
                    identity=ident[:]), reads=[("kb", 0), ("kb", 1), "ident"], writes=[PS(tb)])
            copy("dve", kT2[:, :, 0:128], psb[tb][:, 0:512].rearrange("p (h t) -> p h t", t=128), [PS(tb)], [("kT2", 0)])

            if stop <= 4:
                return
            for GG in range(8 if stop >= 10 else stop - 4):
                G = GG // 2
                wq, wqres = next_tile()

                def stP(b, wq=wq, wqres=wqres):
                    P = blkP(b)
                    i3 = b % 3
                    bank = pt_bank()
                    proj_tm(wq, wqres, b, bank)
                    copy("act", qf[i3][0:P, :], ps[bank][0:P, 0:256], [PS(bank)], [("qf", i3)])
                    q3 = qf[i3][0:P, :].rearrange("p (h d) -> p h d", d=64)
                    rope_tm(q3, q3[:, :, 0:8], q3[:, :, 8:16], P, b, 4, [("qf", i3)], [("qf", i3)])
                    copy("act", qb[i3][0:P, :], qf[i3][0:P, :], [("qf", i3)], [("qb", i3)])

                def stA(b, part, G=G, GG=GG):
                    P = blkP(b)
                    i3 = b % 3
                    e0 = (b % 2) * 4
                    if part == 1:
                        transpose_to(qb[i3], ("qb", i3), 2, P, lambda c, P=P: qT[:, c, 0:P], lambda c: "qT")
                    seqs = [None] if b < 8 else list(range(4))
                    if b == 8 and part == 2:
                        return
                    for s in seqs:
                        if s is None:
                            NQ, qc = 128, slice(0, 128)
                            kprev = kT2[:, G, b * 128:(b + 1) * 128]
                            kown = kT2[:, G, (b + 1) * 128:(b + 2) * 128]
                            vprev, vown = vall[:, b, G, 0:65], vall[:, b + 1, G, 0:65]
                            KO = 128
                            rdk = [("kT2", b), ("kT2", b + 1)]
                            rdv = [("vall", b), ("vall", b + 1)]
                        else:
                            NQ, qc = 8, slice(s * 8, s * 8 + 8)
                            kprev = kT2s[:, G, s * 136:s * 136 + 128]
                            kown = kT2s[:, G, s * 136 + 128:s * 136 + 136]
                            vprev, vown = vSc[:, s, G, 0:65], vSn[0:8, s, G, 0:65]
                            KO = 8
                            rdk = [("kT2s", s)]
                            rdv = ["vSc", "vSn"]
                        if part == 1:
                            for hl in range(4):
                                c, half = hl // 2, hl % 2
                                r0 = half * 64
                                S.op("pe", lambda e, c=c, r0=r0, half=half, kprev=kprev, qc=qc, NQ=NQ: e.matmul(
                                    ps[2 + half][:, c * 128:c * 128 + NQ], lhsT=kprev[r0:r0 + 64, :], rhs=qT[r0:r0 + 64, c, qc],
                                    start=True, stop=True), reads=rdk + ["qT"], writes=[PS(2 + half)])
                                S.op("pe", lambda e, c=c, r0=r0, half=half, kown=kown, qc=qc, NQ=NQ, KO=KO: e.matmul(
                                    ps[2 + half][0:KO, 256 + c * 128:256 + c * 128 + NQ], lhsT=kown[r0:r0 + 64, :], rhs=qT[r0:r0 + 64, c, qc],
                                    start=True, stop=True), reads=rdk + ["qT"], writes=[PS(2 + half)])
                            for i, (bnk, off) in enumerate(((2, 0), (3, 0), (2, 256), (3, 256))):
                                KP = 128 if i < 2 else KO
                                Ev = E[e0 + i][0:KP, :, 0:NQ]
                                S.op("act", lambda e, Ev=Ev, bnk=bnk, off=off, KP=KP, NQ=NQ: e.activation(
                                    out=Ev, in_=ps[bnk][0:KP, off:off + 256].rearrange("p (h q) -> p h q", q=128)[:, :, 0:NQ],
                                    func=AF.Exp, scale=0.125), reads=[PS(bnk)], writes=[("E", e0 + i)])
                                m = (mprev if i < 2 else mown)[0:KP, :, 0:NQ]
                                if i < 2 and b == 0:
                                    S.op("dve", lambda e, Ev=Ev, m=m: e.scalar_tensor_tensor(
                                        out=Ev, in0=Ev, scalar=flag[:, 0:1], in1=m, op0=ALU.mult, op1=ALU.mult),
                                        reads=[("E", e0 + i), "flag", "mprev", "mown"], writes=[("E", e0 + i)])
                                else:
                                    S.op("dve", lambda e, Ev=Ev, m=m: e.tensor_tensor(out=Ev, in0=Ev, in1=m, op=ALU.mult),
                                         reads=[("E", e0 + i), "mprev", "mown"], writes=[("E", e0 + i)])
                            if s is None:
                                continue
                        bnk = 6 + ((b if s is None else s) % 2)
                        for hl in range(4):
                            c, half = hl // 2, hl % 2
                            S.op("pe", lambda e, c=c, half=half, hl=hl, bnk=bnk, vprev=vprev, NQ=NQ: e.matmul(
                                ps[bnk][0:NQ, hl * 65:(hl + 1) * 65], lhsT=E[e0 + half][:, c, 0:NQ], rhs=vprev,
                                start=True, stop=False), reads=[("E", e0 + half)] + rdv, writes=[PS(bnk)])
                            S.op("pe", lambda e, c=c, half=half, hl=hl, bnk=bnk, vown=vown, NQ=NQ, KO=KO: e.matmul(
                                ps[bnk][0:NQ, hl * 65:(hl + 1) * 65], lhsT=E[e0 + 2 + half][0:KO, c, 0:NQ], rhs=vown,
                                start=False, stop=True), reads=[("E", e0 + 2 + half)] + rdv, writes=[PS(bnk)])
                        pv = ps[bnk][0:NQ, 0:260].rearrange("p (h e) -> p h e", e=65)
                        es_ = esink[0:NQ, GG * 4:(GG + 1) * 4]
                        dn = den[0:NQ, 0:4]
                        S.op("dve", lambda e, pv=pv, es_=es_, dn=dn: e.tensor_tensor(
                            out=dn, in0=pv[:, :, 64], in1=es_, op=ALU.add), reads=[PS(bnk), "esink"], writes=["den"])
                        S.op("dve", lambda e, dn=dn: e.reciprocal(out=dn, in_=dn), reads=["den"], writes=["den"])
                        if s is None:
                            obv = oball[0:NQ, b, :].rearrange("p (h d) -> p h d", d=64)
                            wr = [("oball", b)]
                        else:
                            obv = obs[0:NQ]
                            wr = ["obs"]
                        S.op("dve", lambda e, pv=pv, dn=dn, obv=obv, NQ=NQ: e.tensor_tensor(
                            out=obv, in0=pv[:, :, 0:64], in1=dn.unsqueeze(2).to_broadcast([NQ, 4, 64]), op=ALU.mult),
                            reads=[PS(bnk), "den"], writes=wr)
                        if s is not None:
                            S.dma("sp", "obs", lambda e, s=s, b=b: [e.dma_start(
                                out=oball[s * 8:(s + 1) * 8, b, :], in_=obs[:, :, :].rearrange("p h d -> p (h d)"))],
                                reads=["obs"], writes=[("oball", b)])

                stP(0)
                stP(1)
                for b in range(8):
                    stA(b, 1)
                    if b + 2 <= 8:
                        stP(b + 2)
                    stA(b, 2)
                stA(8, 1)

                wz, wzres = next_tile()

                def stZP(b, wz=wz, wzres=wzres):
                    P = blkP(b)
                    i2 = b % 2
                    bank = pt_bank()
                    proj_tm(wz, wzres, b, bank)
                    S.op("act", lambda e, P=P, bank=bank, i2=i2: e.activation(out=sz[i2][0:P, :], in_=ps[bank][0:P, 0:256], func=AF.Silu),
                         reads=[PS(bank)], writes=[("sz", i2)])
                    S.op("dve", lambda e, P=P, b=b, i2=i2: e.tensor_tensor(out=gg[i2][0:P, :], in0=sz[i2][0:P, :], in1=oball[0:P, b, :], op=ALU.mult),
                         reads=[("sz", i2), ("oball", b)], writes=[("gg", i2)])

                def stZT(b, GG=GG):
                    P = blkP(b)
                    i2 = b % 2
                    transpose_to(gg[i2], ("gg", i2), 2, P, lambda c, b=b, P=P, GG=GG: goT[:, 2 * GG + c, tok0(b):tok0(b) + P],
                                 lambda c, b=b: ("goT", b))

                stZP(0)
                for b in range(NB):
                    if b + 1 < NB:
                        stZP(b + 1)
                    stZT(b)

        TT = [(0, 512), (512, 512), (1024, 32)]

        def hT_res(T0, N):
            return [("hT", b) for b in range(T0 // 128, (T0 + N + 127) // 128)]

        def goT_res(T0, N):
            return [("goT", b) for b in range(T0 // 128, (T0 + N + 127) // 128)]

        def proj_fm(w, wres, col0, T0, N, bank):
            for k in range(16):
                S.op("pe", lambda e, k=k, w=w, col0=col0, T0=T0, N=N, bank=bank: e.matmul(
                    ps[bank][:, 0:N], lhsT=w[:, k, col0:col0 + 128], rhs=hT[:, k, T0:T0 + N],
                    start=(k == 0), stop=(k == 15)), reads=[wres] + hT_res(T0, N), writes=[PS(bank)])

        def tm_to_fm(src, src_res, R, dst, dst_res):
            bank = pt_bank()
            for j in range(16):
                S.op("pe", lambda e, j=j, bank=bank: e.transpose(out=ps[bank][:, j * R:(j + 1) * R], in_=src[0:R, j * 128:(j + 1) * 128],
                                                             identity=identf[0:R, 0:R]), reads=[src_res, "identf"], writes=[PS(bank)])
            copy("dve", dst, ps[bank][:, 0:16 * R].rearrange("p (j r) -> p j r", r=R), [PS(bank)], [dst_res])

        def fm_to_tm_out(src, src_res, R, stage, dram_out, slot):
            for g0 in range(0, 16, 4):
                bank = pt_bank()
                for jj_ in range(4):
                    j = g0 + jj_
                    S.op("pe", lambda e, j=j, jj_=jj_, bank=bank: e.transpose(
                        out=ps[bank][0:R, jj_ * 128:(jj_ + 1) * 128], in_=src[:, j, :], identity=identf[:]),
                        reads=[src_res, "identf"], writes=[PS(bank)])
                copy("act", stage[0:R, g0 * 128:(g0 + 4) * 128], ps[bank][0:R, 0:512], [PS(bank)], [("stage", slot)])
            S.dma("sp", slot, lambda e: [e.dma_start(out=dram_out, in_=stage[0:R, :])], reads=[("stage", slot)])

        def conv_phase():
            with contextlib.ExitStack() as ph:
                bsb = T("bsb", [128, NTOK], F32, ph)
                csb = T("csb", [128, NTOK], F32, ph)
                uT = T("uT", [128, 1026], F32, ph)
                usT = T("usT", [128, 4, 10], F32, ph)
                szb = T("szb", [128, 512], F32, ph)
                acc = T("acc", [128, 512], F32, ph)
                cwt = T("bufA", [8, D], F32, ph)
                cw = T("cw", [128, 16, 3], F32, ph)
                sct0 = T("bufB", [8, D], F32, ph)
                sct = T("sct", [128, 16, 8], F32, ph)
                ulast = T("ulast", [128, 16, 2], F32, ph)
                uls = T("uls", [128, 16, 8], F32, ph)
                bz01 = T("bz01", [128, 16, 2], F32, ph)
                cv01 = T("cv01", [128, 16, 2], F32, ph)
                uh = T("uh", [128, 16, 2], F32, ph)
                tmp2 = [T(f"tmp2{i}", [128, 16], F32, ph) for i in range(3)]
                stage, stage2 = cwt, sct0

                S.dma("sp", "misc", lambda e: [e.dma_start(out=cwt[0:3, :], in_=b_conv_w[0]),
                                               e.dma_start(out=sct0[:, :], in_=sconv[:, :])], n=2, writes=[("stage", "cvo1"), ("stage", "cvo2")])
                tm_to_fm(cwt, ("stage", "cvo1"), 3, cw[:], "cw")
                tm_to_fm(sct0, ("stage", "cvo2"), 8, sct[:], "sct")
                S.op("pool", lambda e: e.memset(uT[:, 0:2], 0.0), writes=["uT"])
                for j in range(16):
                    w0_, w0res = next_tile()
                    for (T0, N) in TT:
                        bank = pt_bank()
                        proj_fm(w0_, w0res, 0, T0, N, bank)
                        copy("act", bsb[:, T0:T0 + N], ps[bank][:, 0:N], [PS(bank)], [("bsb", T0)])
                        bank = pt_bank()
                        proj_fm(w0_, w0res, 128, T0, N, bank)
                        copy("act", csb[:, T0:T0 + N], ps[bank][:, 0:N], [PS(bank)], [("csb", T0)])
                    w1_, w1res = next_tile()
                    copy("dve", usT[:, :, 0:2], sct[:, j, :].rearrange("p (s i) -> p s i", i=2), ["sct"], ["usT"])
                    for (T0, N) in TT:
                        bank = pt_bank()
                        proj_fm(w1_, w1res, 0, T0, N, bank)
                        if T0 < 1024:
                            S.op("dve", lambda e, T0=T0, N=N, bank=bank: e.tensor_tensor(
                                out=uT[:, 2 + T0:2 + T0 + N], in0=csb[:, T0:T0 + N], in1=ps[bank][:, 0:N], op=ALU.mult),
                                reads=[PS(bank), ("csb", T0)], writes=["uT"])
                        else:
                            S.op("dve", lambda e, T0=T0, N=N, bank=bank: e.tensor_tensor(
                                out=usT[:, :, 2:10], in0=csb[:, T0:T0 + N].rearrange("p (s t) -> p s t", t=8),
                                in1=ps[bank][:, 0:N].rearrange("p (s t) -> p s t", t=8), op=ALU.mult),
                                reads=[PS(bank), ("csb", T0)], writes=["usT"])
                        bank = pt_bank()
                        proj_fm(w1_, w1res, 128, T0, N, bank)
                        S.op("act", lambda e, N=N, bank=bank: e.activation(out=szb[:, 0:N], in_=ps[bank][:, 0:N], func=AF.Silu),
                             reads=[PS(bank)], writes=["szb"])
                        S.op("dve", lambda e, T0=T0, N=N: e.tensor_tensor(out=szb[:, 0:N], in0=szb[:, 0:N], in1=bsb[:, T0:T0 + N], op=ALU.mult),
                             reads=["szb", ("bsb", T0)], writes=["szb"])
                        if T0 < 1024:
                            u0, u1, u2 = uT[:, T0:T0 + N], uT[:, T0 + 1:T0 + 1 + N], uT[:, T0 + 2:T0 + 2 + N]
                            a_, ures = acc[:, 0:N], "uT"
                            sz_ = szb[:, 0:N]
                            gout = goT[:, j, T0:T0 + N]
                        else:
                            u0, u1, u2 = usT[:, :, 0:8], usT[:, :, 1:9], usT[:, :, 2:10]
                            a_, ures = acc[:, 0:32].rearrange("p (s t) -> p s t", t=8), "usT"
                            sz_ = szb[:, 0:32].rearrange("p (s t) -> p s t", t=8)
                            gout = goT[:, j, T0:T0 + N].rearrange("p (s t) -> p s t", t=8)
                        S.op("dve", lambda e, j=j, u0=u0, a_=a_: e.tensor_scalar(out=a_, in0=u0, scalar1=cw[:, j, 0:1], scalar2=None, op0=ALU.mult),
                             reads=[ures, "cw"], writes=["acc"])
                        S.op("dve", lambda e, j=j, u1=u1, a_=a_: e.scalar_tensor_tensor(out=a_, in0=u1, scalar=cw[:, j, 1:2], in1=a_, op0=ALU.mult, op1=ALU.add),
                             reads=[ures, "cw", "acc"], writes=["acc"])
                        S.op("dve", lambda e, j=j, u2=u2, a_=a_: e.scalar_tensor_tensor(out=a_, in0=u2, scalar=cw[:, j, 2:3], in1=a_, op0=ALU.mult, op1=ALU.add),
                             reads=[ures, "cw", "acc"], writes=["acc"])
                        if T0 == 0:
                            copy("act", bz01[:, j, :], szb[:, 0:2], ["szb"], ["bz01"])
                            copy("act", cv01[:, j, :], acc[:, 0:2], ["acc"], ["cv01"])
                        S.op("dve", lambda e, a_=a_, sz_=sz_, gout=gout: e.tensor_tensor(out=gout, in0=a_, in1=sz_, op=ALU.mult),
                             reads=["acc", "szb"], writes=goT_res(T0, N))
                    copy("act", ulast[:, j, :], uT[:, 1024:1026], ["uT"], ["ulast"])
                    copy("act", uls[:, j, :].rearrange("p (s i) -> p s i", i=2), usT[:, :, 8:10], ["usT"], ["uls"])
                S.dma("sp", "cvx", lambda e: [e.dma_start(out=cc_b_in[:, :], in_=ulast[:].rearrange("p j i -> p (j i)"))],
                      reads=["ulast"], writes=["cc_b_in"])
                S.dma("pool", "cc_b", lambda e: [e.collective_compute(
                    "AllGather", ALU.bypass, replica_groups=PAIRS, ins=[cc_b_in.opt()], outs=[cc_b_out.opt()])],
                    inc=1, reads=["cc_b_in"], writes=["cc_b_out"])
                S.dma("sp", "cvx", lambda e: [e.dma_start(out=uh[:].rearrange("p j i -> p (j i)"), in_=cc_b_out[0:128, :])],
                      reads=["cc_b_out"], writes=["uh"])
                S.op("dve", lambda e: e.tensor_scalar(out=uh[:], in0=uh[:], scalar1=flag[:, 0:1], scalar2=None, op0=ALU.mult),
                     reads=["uh", "flag"], writes=["uh"])
                t0_, t1_, t2_ = tmp2[0][:], tmp2[1][:], tmp2[2][:]
                S.op("dve", lambda e: e.tensor_tensor(out=t0_, in0=cw[:, :, 0], in1=uh[:, :, 0], op=ALU.mult), reads=["cw", "uh"], writes=["t0"])
                S.op("dve", lambda e: e.tensor_tensor(out=t1_, in0=cw[:, :, 1], in1=uh[:, :, 1], op=ALU.mult), reads=["cw", "uh"], writes=["t1"])
                S.op("dve", lambda e: e.tensor_tensor(out=t2_, in0=cw[:, :, 0], in1=uh[:, :, 1], op=ALU.mult), reads=["cw", "uh"], writes=["t2"])
                S.op("dve", lambda e: e.tensor_tensor(out=t0_, in0=t0_, in1=t1_, op=ALU.add), reads=["t0", "t1"], writes=["t0"])
                S.op("dve", lambda e: e.tensor_tensor(out=cv01[:, :, 0], in0=cv01[:, :, 0], in1=t0_, op=ALU.add), reads=["t0", "cv01"], writes=["cv01"])
                S.op("dve", lambda e: e.tensor_tensor(out=cv01[:, :, 1], in0=cv01[:, :, 1], in1=t2_, op=ALU.add), reads=["t2", "cv01"], writes=["cv01"])
                S.op("dve", lambda e: e.tensor_tensor(out=goT[:, :, 0:2], in0=cv01[:], in1=bz01[:], op=ALU.mult),
                     reads=["cv01", "bz01"], writes=[("goT", 0)])
                fm_to_tm_out(ulast, "ulast", 2, stage, convp[:, :], "cvo1")
                fm_to_tm_out(uls, "uls", 8, stage2, convs[:, :], "cvo2")
            S.barrier()

        def hgrn_phase(li):
            with contextlib.ExitStack() as ph:
                clb0 = T("clb0", [64, 128], F32, ph)
                clbf = T("clbf", [128, 4, 16], F32, ph)
                lbt = T("lbt", [128, 16], F32, ph)
                omlt = T("omlt", [128, 16], F32, ph)
                dent = T("dent", [128, 16], F32, ph)
                ng = T("ng", [128, 1], F32, ph)
                m64 = T("m64", [128, 512], BF16, ph)
                m8 = T("m8", [128, 32], BF16, ph)
                mone = T("mone", [128, 512], BF16, ph)
                t_sq = T("t_sq", [128, NTOK], F32, ph)
                t_fg = T("t_fg", [128, NTOK], F32, ph)
                t_lf = T("t_lf", [128, NTOK], F32, ph)
                t_b = T("t_b", [128, NTOK], F32, ph)
                t_bg = T("t_bg", [128, NTOK], F32, ph)
                t_sq2 = T("t_sq2", [128, NTOK], BF16, ph)
                qtT = T("qtT", [128, NTOK], BF16, ph)
                kiT = T("kiT", [128, NTOK], BF16, ph)
                ksT = T("ksT", [128, NTOK], BF16, ph)
                qgT = T("qgT", [128, 1024], BF16, ph)
                vT = T("vT", [128, NTOK], BF16, ph)
                szT = T("szT", [128, NTOK], BF16, ph)
                oTs = t_sq
                ebl = T("ebl", [128, 20], F32, ph)
                ebt = T("ebt", [128, 1], F32, ph)
                bgl = T("bgl", [128, 1], F32, ph)
                Sloc = T("Sloc", [128, 16, 128], F32, ph)
                Sball = t_lf[:].bitcast(BF16)[:, 0:1920].rearrange("p (c d) -> p c d", d=128)
                SA = T("SA", [128, 128], F32, ph)
                SAb = T("SAb", [128, 128], BF16, ph)
                Sf = T("Sf", [128, 128], F32, ph)
                SfP = [Sf, T("Sf1", [128, 128], F32, ph)]
                SbP = [SAb, T("SAb1", [128, 128], BF16, ph)]
                kvtm = [T(f"kvtm{i}", [64, 256], BF16, ph) for i in range(2)]
                att = [T(f"att{i}", [64, 64], BF16, ph) for i in range(2)]

                S.dma("sp", "misc", lambda e: [e.dma_start(out=clb0[:, :], in_=c_lb.rearrange("r (j p) -> (r j) p", p=128)),
                                               e.dma_start(out=ng[:, :], in_=c_norm_g[0].rearrange("(p o) -> p o", o=1))],
                      n=2, writes=["clb0", "ng"])
                bank = pt_bank()
                S.op("pe", lambda e, bank=bank: e.transpose(out=ps[bank][:, 0:64], in_=clb0[:, :], identity=identf[0:64, 0:64]),
                     reads=["clb0", "identf"], writes=[PS(bank)])
                S.op("act", lambda e, bank=bank: e.activation(out=clbf[:].rearrange("p r j -> p (r j)"), in_=ps[bank][:, 0:64], func=AF.Exp),
                     reads=[PS(bank)], writes=["clbf"])
                assert li == 2
                S.op("dve", lambda e: e.tensor_tensor(out=dent[:], in0=clbf[:, 0, :], in1=clbf[:, 1, :], op=ALU.add), reads=["clbf"], writes=["dent"])
                S.op("dve", lambda e: e.tensor_tensor(out=lbt[:], in0=clbf[:, 2, :], in1=clbf[:, 3, :], op=ALU.add), reads=["clbf"], writes=["lbt"])
                S.op("dve", lambda e: e.tensor_tensor(out=dent[:], in0=dent[:], in1=lbt[:], op=ALU.add), reads=["dent", "lbt"], writes=["dent"])
                S.op("dve", lambda e: e.reciprocal(out=dent[:], in_=dent[:]), reads=["dent"], writes=["dent"])
                S.op("dve", lambda e: e.tensor_tensor(out=lbt[:], in0=clbf[:, 1, :], in1=clbf[:, 2, :], op=ALU.add), reads=["clbf", "lbt"], writes=["lbt"])
                S.op("dve", lambda e: e.tensor_tensor(out=lbt[:], in0=lbt[:], in1=dent[:], op=ALU.mult), reads=["lbt", "dent"], writes=["lbt"])
                S.op("dve", lambda e: e.tensor_scalar(out=omlt[:], in0=lbt[:], scalar1=-1.0, scalar2=1.0, op0=ALU.mult, op1=ALU.add),
                     reads=["lbt"], writes=["omlt"])
                S.op("pool", lambda e: e.memset(mone[:], 1.0), writes=["mone"])
                S.op("pool", lambda e: e.memset(m64[:], 1.0), writes=["m64"])
                S.op("pool", lambda e: e.memset(m64[:].rearrange("p (c t) -> p c t", t=64)[:, :, 0:1], 0.0), reads=["m64"], writes=["m64"])
                S.op("pool", lambda e: e.memset(m8[:], 1.0), writes=["m8"])
                S.op("pool", lambda e: e.memset(m8[:].rearrange("p (c t) -> p c t", t=8)[:, :, 0:1], 0.0), reads=["m8"], writes=["m8"])

                for j in range(16):
                    def hb_bank():
                        rr["hb"] = (rr.get("hb", -1) + 1) % 7
                        return rr["hb"]
                    w0_, w0res = next_tile(prefetch=False)
                    issue_load()
                    for ti, (T0, N) in enumerate(TT):
                        bank = hb_bank()
                        proj_fm(w0_, w0res, 128, T0, N, bank)
                        S.op("act", lambda e, T0=T0, N=N, bank=bank: e.activation(out=t_fg[:, T0:T0 + N], in_=ps[bank][:, 0:N], func=AF.Sigmoid),
                             reads=[PS(bank)], writes=[("t_fg", ti)])
                    for ti, (T0, N) in enumerate(TT):
                        bank = hb_bank()
                        proj_fm(w0_, w0res, 0, T0, N, bank)
                        S.op("act", lambda e, T0=T0, N=N, bank=bank: e.activation(out=t_sq[:, T0:T0 + N], in_=ps[bank][:, 0:N], func=AF.Silu),
                             reads=[PS(bank)], writes=[("t_sq", ti)])
                    w1_, w1res = next_tile(prefetch=False)
                    issue_load()

                    def chain(ti, T0, N, j=j):
                        C = 64 if T0 < 1024 else 8
                        nch = N // C
                        c0 = T0 // 64
                        msk = m64 if T0 < 1024 else m8
                        sl = slice(T0, T0 + N)
                        fg, lf, bb, bg, sq_ = t_fg[:, sl], t_lf[:, sl], t_b[:, sl], t_bg[:, sl], t_sq[:, sl]
                        R = lambda n: (n, ti)
                        st = []
                        st.append(lambda: S.op("dve", lambda e: e.tensor_scalar(out=fg, in0=fg, scalar1=omlt[:, j:j + 1], scalar2=lbt[:, j:j + 1],
                                                                                op0=ALU.mult, op1=ALU.add),
                                               reads=[R("t_fg"), "omlt", "lbt"], writes=[R("t_fg")]))
                        st.append(lambda: S.op("act", lambda e: e.activation(out=lf, in_=fg, func=AF.Ln), reads=[R("t_fg")], writes=[R("t_lf")]))
                        st.append(lambda: S.op("dve", lambda e: e.tensor_scalar(out=fg, in0=fg, scalar1=-1.0, scalar2=1.0, op0=ALU.mult, op1=ALU.add),
                                               reads=[R("t_fg"), R("t_lf")], writes=[R("t_fg")]))
                        st.append(lambda: S.op("dve", lambda e: e.tensor_tensor_scan(out=bb, data0=msk[:, 0:N], data1=lf, initial=0.0,
                                                                                     op0=ALU.mult, op1=ALU.add),
                                               reads=[R("t_lf"), "m64", "m8"], writes=[R("t_b")]))
                        if T0 < 1024:
                            init = 0.0 if ti == 0 else bgl[:, 0:1]

                            def gsc():
                                S.op("dve", lambda e: e.tensor_tensor_scan(out=bg, data0=mone[:, 0:N], data1=lf, initial=init,
                                                                           op0=ALU.mult, op1=ALU.add),
                                     reads=[R("t_lf"), "mone", "bgl"], writes=[R("t_bg")])
                                copy("dve", bgl[:, 0:1], t_bg[:, T0 + N - 1:T0 + N], [R("t_bg")], ["bgl"])
                                if ti == 1:
                                    S.op("act", lambda e: e.activation(out=ebt[:, 0:1], in_=bgl[:, 0:1], func=AF.Exp), reads=["bgl"], writes=["ebt"])
                            st.append(gsc)
                        else:
                            st.append(lambda: None)
                        st.append(lambda: S.op("act", lambda e: e.activation(out=lf, in_=bb, func=AF.Exp), reads=[R("t_b"), R("t_bg")], writes=[R("t_lf")]))
                        st.append(lambda: S.op("dve", lambda e: e.tensor_tensor(out=qtT[:, sl], in0=sq_, in1=lf, op=ALU.mult),
                                               reads=[R("t_sq"), R("t_lf")], writes=[("qtT", ti)]))
                        st.append(lambda: S.op("act", lambda e: e.activation(out=lf, in_=bb, func=AF.Exp, scale=-1.0),
                                               reads=[R("t_b"), ("qtT", ti)], writes=[R("t_lf")]))
                        st.append(lambda: S.op("dve", lambda e: e.tensor_tensor(out=fg, in0=fg, in1=lf, op=ALU.mult),
                                               reads=[R("t_fg"), R("t_lf")], writes=[R("t_fg")]))
                        st.append(lambda: S.op("act", lambda e: e.activation(
                            out=ebl[:, c0:c0 + nch], in_=bb.rearrange("p (c t) -> p c t", t=C)[:, :, C - 1], func=AF.Exp),
                            reads=[R("t_b")], writes=[("ebl", ti)]))
                        st.append(lambda: copy("pool", kiT[:, sl], fg, [R("t_fg")], [("kiT", ti)]))
                        st.append(lambda: S.op("dve", lambda e: e.tensor_tensor(
                            out=ksT[:, sl].rearrange("p (c t) -> p c t", t=C), in0=fg.rearrange("p (c t) -> p c t", t=C),
                            in1=ebl[:, c0:c0 + nch].unsqueeze(2).to_broadcast([128, nch, C]), op=ALU.mult),
                            reads=[R("t_fg"), ("ebl", ti)], writes=[("ksT", ti)]))
                        if T0 < 1024:
                            st.append(lambda: S.op("act", lambda e: e.activation(out=bg, in_=bg, func=AF.Exp), reads=[R("t_bg")], writes=[R("t_bg")]))
                            st.append(lambda: S.op("dve", lambda e: e.tensor_tensor(out=qgT[:, sl], in0=sq_, in1=bg, op=ALU.mult),
                                                   reads=[R("t_sq"), R("t_bg")], writes=[("qgT", ti)]))
                        return st

                    chains = [chain(ti, T0, N) for ti, (T0, N) in enumerate(TT)]
                    for k in range(max(len(c_) for c_ in chains)):
                        for c_ in chains:
                            if k < len(c_):
                                c_[k]()

                    for ti, (T0, N) in enumerate(TT):
                        bank = hb_bank()
                        proj_fm(w1_, w1res, 128, T0, N, bank)
                        S.op("act", lambda e, N=N, T0=T0, bank=bank: e.activation(out=szT[:, T0:T0 + N], in_=ps[bank][:, 0:N], func=AF.Silu),
                             reads=[PS(bank)], writes=[("szT", ti)])
                    for ti, (T0, N) in enumerate(TT):
                        bank = hb_bank()
                        proj_fm(w1_, w1res, 0, T0, N, bank)
                        copy("act", vT[:, T0:T0 + N], ps[bank][:, 0:N], [PS(bank)], [("vT", ti)])

                    def chunk_step(t0, C, ci, Sf32, Sbf, sres, par):
                        ti = 0 if t0 < 512 else (1 if t0 < 1024 else 2)
                        kv = kvtm[par]
                        at = att[par]
                        tb = pt_bank()
                        S.op("pe", lambda e: e.transpose(out=psb[tb][0:C, 0:128], in_=ksT[:, t0:t0 + C], identity=ident[:]),
                             reads=[("ksT", ti), "ident"], writes=[PS(tb)])
                        S.op("pe", lambda e: e.transpose(out=psb[tb][0:C, 128:256], in_=vT[:, t0:t0 + C], identity=ident[:]),
                             reads=[("vT", ti), "ident"], writes=[PS(tb)])
                        copy("act", kv[0:C, :], psb[tb][0:C, 0:256], [PS(tb)], [("kvtm", par)])
                        S.op("pe", lambda e: e.matmul(ps[2][0:C, 0:C], lhsT=kiT[:, t0:t0 + C], rhs=qtT[:, t0:t0 + C], start=True, stop=True),
                             reads=[("kiT", ti), ("qtT", ti)], writes=[PS(2)])
                        S.op("dve", lambda e: e.tensor_tensor(out=at[0:C, 0:C], in0=ps[2][0:C, 0:C], in1=mown[0:C, 0, 0:C], op=ALU.mult),
                             reads=[PS(2), "mown"], writes=[("att", par)])
                        ob_ = 3 + par
                        S.op("pe", lambda e: e.matmul(ps[ob_][:, 0:C], lhsT=Sbf[:, :], rhs=qtT[:, t0:t0 + C], start=True, stop=False),
                             reads=[sres + "b", ("qtT", ti)], writes=[PS(ob_)])
                        S.op("pe", lambda e: e.matmul(ps[ob_][:, 0:C], lhsT=kv[0:C, 128:256], rhs=at[0:C, 0:C], start=False, stop=True),
                             reads=[("kvtm", par), ("att", par)], writes=[PS(ob_)])
                        copy("act", oTs[:, t0:t0 + C], ps[ob_][:, 0:C], [PS(ob_)], [("t_sq", ti)])
                        sb_ = 5 + par
                        S.op("pe", lambda e: e.matmul(ps[sb_][:, 0:128], lhsT=kv[0:C, 0:128], rhs=kv[0:C, 128:256], start=True, stop=True),
                             reads=[("kvtm", par)], writes=[PS(sb_)])
                        S.op("dve", lambda e: e.scalar_tensor_tensor(out=Sf32[:, :], in0=Sf32[:, :], scalar=ebl[:, ci:ci + 1], in1=ps[sb_][:, 0:128],
                                                                     op0=ALU.mult, op1=ALU.add),
                             reads=[PS(sb_), sres, ("ebl", ti)], writes=[sres])
                        copy("act", Sbf[:, :], Sf32[:, :], [sres], [sres + "b"])

                    TLR = [("t_lf", 0), ("t_lf", 1), ("t_lf", 2)]

                    def l_tr(c):
                        t0, ti, par = c * 64, (0 if c < 8 else 1), c % 2
                        kv = kvtm[par]
                        S.op("pe", lambda e: e.transpose(out=psb[7][0:64, 0:128], in_=ksT[:, t0:t0 + 64], identity=ident[:]),
                             reads=[("ksT", ti), "ident"], writes=[PS(7)])
                        S.op("pe", lambda e: e.transpose(out=psb[7][0:64, 128:256], in_=vT[:, t0:t0 + 64], identity=ident[:]),
                             reads=[("vT", ti), "ident"], writes=[PS(7)])
                        copy("act", kv[0:64, :], psb[7][0:64, 0:256], [PS(7)], [("kvtm", par)])

                    def l_att(c):
                        t0, ti, par = c * 64, (0 if c < 8 else 1), c % 2
                        at = att[par]
                        S.op("pe", lambda e: e.matmul(ps[2][0:64, 0:64], lhsT=kiT[:, t0:t0 + 64], rhs=qtT[:, t0:t0 + 64], start=True, stop=True),
                             reads=[("kiT", ti), ("qtT", ti)], writes=[PS(2)])
                        S.op("dve", lambda e: e.tensor_tensor(out=at[0:64, 0:64], in0=ps[2][0:64, 0:64], in1=mown[0:64, 0, 0:64], op=ALU.mult),
                             reads=[PS(2), "mown"], writes=[("att", par)])

                    def q_half(half):
                        bank = 3 + half
                        for c in range(8 * half, 8 * half + 8):
                            if c == 0:
                                continue
                            col = (c % 8) * 64
                            S.op("pe", lambda e, c=c, col=col: e.matmul(
                                ps[bank][:, col:col + 64], lhsT=Sball[:, c - 1, :], rhs=qtT[:, c * 64:(c + 1) * 64], start=True, stop=True),
                                reads=[("Sball", (c - 1) // 8), ("qtT", half)] + TLR, writes=[PS(bank)])
                        lo = 64 if half == 0 else 0
                        T0_ = half * 512
                        S.op("dve", lambda e: e.tensor_tensor(
                            out=oTs[:, T0_ + lo:T0_ + 512], in0=oTs[:, T0_ + lo:T0_ + 512], in1=ps[bank][:, lo:512], op=ALU.add),
                            reads=[PS(bank), ("t_sq", half)], writes=[("t_sq", half)])

                    def l_mm(c):
                        ti, par = (0 if c < 8 else 1), c % 2
                        kv, at = kvtm[par], att[par]
                        ob_ = 3 + c // 8
                        col = (c % 8) * 64
                        S.op("pe", lambda e: e.matmul(ps[ob_][:, col:col + 64], lhsT=kv[0:64, 128:256], rhs=at[0:64, 0:64], start=True, stop=True),
                             reads=[("kvtm", par), ("att", par)], writes=[PS(ob_)])
                        sb_ = 5 + (c // 4) % 2
                        scol = (c % 4) * 128
                        S.op("pe", lambda e: e.matmul(ps[sb_][:, scol:scol + 128], lhsT=kv[0:64, 0:128], rhs=kv[0:64, 128:256], start=True, stop=True),
                             reads=[("kvtm", par)], writes=[PS(sb_)])
                        if c % 4 == 3:
                            copy("dve", Sloc[:, c - 3:c + 1, :], ps[sb_][:, 0:512].rearrange("p (c d) -> p c d", d=128),
                                 [PS(sb_)], [("Sl", cc_) for cc_ in range(c - 3, c + 1)])
                        if c % 8 == 7:
                            T0_ = (c // 8) * 512
                            copy("act", oTs[:, T0_:T0_ + 512], ps[ob_][:, 0:512], [PS(ob_)], [("t_sq", ti)])
                        if c % 4 == 3:
                            for c2 in range(max(c - 3, 1), c + 1):
                                ti2 = 0 if c2 < 8 else 1
                                S.op("dve", lambda e, c2=c2: e.scalar_tensor_tensor(out=Sloc[:, c2, :], in0=Sloc[:, c2 - 1, :], scalar=ebl[:, c2:c2 + 1],
                                                                                  in1=Sloc[:, c2, :], op0=ALU.mult, op1=ALU.add),
                                     reads=[("Sl", c2), ("Sl", c2 - 1), ("ebl", ti2)], writes=[("Sl", c2)])
                        if c == 7 or c == 15:
                            half = c // 8
                            lo_c, hi_c = (0, 8) if half == 0 else (8, 15)
                            copy("act", Sball[:, lo_c:hi_c, :], Sloc[:, lo_c:hi_c, :], [("Sl", cc_) for cc_ in range(lo_c, hi_c)],
                                 [("Sball", half)] + TLR)

                    l_tr(0)
                    l_att(0)
                    for c in range(16):
                        if c + 1 < 16:
                            l_tr(c + 1)
                            l_att(c + 1)
                        l_mm(c)
                        if c == 11:
                            q_half(0)
                    q_half(1)
                    S.dma("sp", "hgx", lambda e, j=j: [e.dma_start(out=cc_h_in[j][:, :], in_=Sloc[:, 15, :])], reads=[("Sl", 15)], writes=["cc_h_in"])
                    S.dma("pool", "cc_h", lambda e, j=j: [e.collective_compute(
                        "AllGather", ALU.bypass, replica_groups=PAIRS, ins=[cc_h_in[j].opt()], outs=[cc_h_out[j].opt()])],
                        inc=1, reads=["cc_h_in"], writes=["cc_h_out"])
                    for s_ in range(4):
                        pq = s_ % 2
                        Sfq, Sbq, nmq = SfP[pq], SbP[pq], f"Sf{pq}"
                        S.dma("sp", ("hgs_in", pq), lambda e, s_=s_, j=j, Sfq=Sfq: [e.dma_start(out=Sfq[:, :], in_=shg[s_, j])], writes=[nmq])
                        copy("act", Sbq[:, :], Sfq[:, :], [nmq], [nmq + "b"])
                        chunk_step(1024 + s_ * 8, 8, 16 + s_, Sfq, Sbq, nmq, pq)
                        S.dma("sp", ("hgs_out", pq), lambda e, s_=s_, j=j, Sfq=Sfq: [e.dma_start(out=hgs[s_, j], in_=Sfq[:, :])], reads=[nmq])
                    S.dma("sp", "hgx", lambda e, j=j: [e.dma_start(out=SA[:, :], in_=cc_h_out[j][0:128, :])], reads=["cc_h_out"], writes=["SA"])
                    S.op("dve", lambda e: e.tensor_scalar(out=SA[:, :], in0=SA[:, :], scalar1=flag[:, 0:1], scalar2=None, op0=ALU.mult),
                         reads=["SA", "flag"], writes=["SA"])
                    copy("act", SAb[:, :], SA[:, :], ["SA", "Sf0b"], ["SAb", "Sf0b"])
                    for ti, (T0, N) in enumerate(TT[:2]):
                        bank = pt_bank()
                        S.op("pe", lambda e, T0=T0, N=N, bank=bank: e.matmul(ps[bank][:, 0:N], lhsT=SAb[:, :], rhs=qgT[:, T0:T0 + N],
                                                                           start=True, stop=True),
                             reads=["SAb", ("qgT", ti)], writes=[PS(bank)])
                        S.op("dve", lambda e, T0=T0, N=N, bank=bank: e.tensor_tensor(out=oTs[:, T0:T0 + N], in0=oTs[:, T0:T0 + N],
                                                                                  in1=ps[bank][:, 0:N], op=ALU.add),
                             reads=[PS(bank), ("t_sq", ti)], writes=[("t_sq", ti)])
                    S.op("dve", lambda e: e.scalar_tensor_tensor(out=Sf[:, :], in0=SA[:, :], scalar=ebt[:, 0:1], in1=Sloc[:, 15, :],
                                                                 op0=ALU.mult, op1=ALU.add), reads=["SA", "ebt", ("Sl", 15), "Sf0"], writes=["Sf0"])
                    S.dma("sp", "hgp_out", lambda e, j=j: [e.dma_start(out=hgp[j], in_=Sf[:, :])], reads=["Sf0"])
                    def tail(ti, T0, N, j=j):
                        sl = slice(T0, T0 + N)
                        bank = (0, 1, 7)[ti]
                        R = lambda n: (n, ti)
                        return [
                            lambda: S.op("pool", lambda e: e.tensor_tensor(out=t_sq2[:, sl], in0=oTs[:, sl], in1=oTs[:, sl], op=ALU.mult),
                                         reads=[("t_sq", ti)], writes=[R("t_sq2")]),
                            lambda: S.op("pe", lambda e: e.matmul(ps[bank][:, 0:N], lhsT=onesb[:, :], rhs=t_sq2[:, sl], start=True, stop=True),
                                         reads=[R("t_sq2"), "onesb"], writes=[PS(bank)]),
                            lambda: S.op("act", lambda e: e.activation(out=t_b[:, sl], in_=ps[bank][:, 0:N], func=AF.Ln,
                                                                       scale=1.0 / 128.0, bias=EPS), reads=[PS(bank)], writes=[R("t_b")]),
                            lambda: S.op("act", lambda e: e.activation(out=t_b[:, sl], in_=t_b[:, sl], func=AF.Exp, scale=-0.5),
                                         reads=[R("t_b")], writes=[R("t_b")]),
                            lambda: S.op("dve", lambda e: e.tensor_tensor(out=t_b[:, sl], in0=t_b[:, sl], in1=oTs[:, sl], op=ALU.mult),
                                         reads=[R("t_b"), ("t_sq", ti)], writes=[R("t_b")]),
                            lambda: S.op("dve", lambda e: e.scalar_tensor_tensor(out=goT[:, j, sl], in0=t_b[:, sl], scalar=ng[:, 0:1],
                                                                                 in1=szT[:, sl], op0=ALU.mult, op1=ALU.mult),
                                         reads=[R("t_b"), "ng", ("szT", ti)], writes=goT_res(T0, N)),
                        ]
                    tails = [tail(ti, T0, N) for ti, (T0, N) in enumerate(TT)]
                    for k in range(6):
                        for t_ in tails:
                            t_[k]()
            S.barrier()

        if True:
            for li, kind in enumerate(layer_kinds):
                norm_phase(ln_g[li])
                if kind == "a":
                    attn_phase(li // 3)
                elif kind == "b":
                    conv_phase()
                else:
                    hgrn_phase(li)
                wout_phase()
            norm_phase(final_g if not dbg else final_g, final=True)
        S.emit()
    nc._sched_stats = S.stats
    return nc


def rope_tables(hf):
    half = 8
    inv = 500000.0 ** (-np.arange(half, dtype=np.float64) * 2.0 / 16.0)
    pos = np.zeros((128, NB), np.float64)
    for b in range(8):
        pos[:, b] = hf * 1024 + b * 128 + np.arange(128)
    pos[:, 8] = 16384 + (np.arange(128) % 8)
    ang = pos[:, :, None].astype(np.float32).astype(np.float64) * inv.astype(np.float32).astype(np.float64)[None, None, :]
    ang = ang.astype(np.float32).astype(np.float64)
    return np.cos(ang).astype(np.float32), np.sin(ang).astype(np.float32)


_NC_CACHE = {}


def make_in_maps(inp):
    f = lambda a: np.ascontiguousarray(np.asarray(a, dtype=np.float32))
    shared = dict(
        ln_g=f(inp["ln_g"]), final_g=f(inp["final_g"]), a_w_in=f(inp["a_w_in"]), a_w_out=f(inp["a_w_out"]),
        a_sinks=f(inp["a_sinks"]), b_w_in=f(inp["b_w_in"]), b_conv_w=f(inp["b_conv_w"]), b_w_out=f(inp["b_w_out"]),
        c_w_in=f(inp["c_w_in"]), c_norm_g=f(inp["c_norm_g"]), c_w_out=f(inp["c_w_out"]), c_lb=f(inp["c_lb_logits"]))
    x_prompt, x_sample = f(inp["x_prompt"]), f(inp["x_sample"])
    cache_k, cache_v = f(inp["cache_k"]), f(inp["cache_v"])
    state_conv, state_hgrn = f(inp["state_conv"]), f(inp["state_hgrn"])
    maps = []
    for c in range(8):
        p, hf = c // 2, c % 2
        cs, sn = rope_tables(hf)
        m = dict(shared)
        m.update(
            xp=np.ascontiguousarray(x_prompt[p, hf * 1024:(hf + 1) * 1024]),
            xsm=np.ascontiguousarray(x_sample[4 * c:4 * c + 4].reshape(32, D)),
            ck=np.ascontiguousarray(cache_k[:, 4 * c:4 * c + 4].reshape(2, 4, 128, 256)),
            cv=np.ascontiguousarray(cache_v[:, 4 * c:4 * c + 4].reshape(2, 4, 128, 256)),
            sconv=np.ascontiguousarray(state_conv[0, 4 * c:4 * c + 4].reshape(8, D)),
            shg=np.ascontiguousarray(state_hgrn[0, 4 * c:4 * c + 4]),
            ropec=cs, ropes=sn, flag=np.full((128, 1), float(hf), np.float32))
        maps.append(m)
    return maps


def assemble(res):
    y_prompt = np.zeros((4, 2048, D), np.float32)
    y_sample = np.zeros((32, 8, D), np.float32)
    kpo = np.zeros((2, 4, 128, 4, 64), np.float32)
    vpo = np.zeros_like(kpo)
    kso = np.zeros((2, 32, 128, 4, 64), np.float32)
    vso = np.zeros_like(kso)
    cpo = np.zeros((1, 4, 2, D), np.float32)
    cso = np.zeros((1, 32, 2, D), np.float32)
    hpo = np.zeros((1, 4, 16, 128, 128), np.float32)
    hso = np.zeros((1, 32, 16, 128, 128), np.float32)
    for c in range(8):
        r = res[c]
        p, hf = c // 2, c % 2
        y_prompt[p, hf * 1024:(hf + 1) * 1024] = r["y"][:1024]
        y_sample[4 * c:4 * c + 4] = r["y"][1024:1056].reshape(4, 8, D)
        kso[:, 4 * c:4 * c + 4] = r["ks"].reshape(2, 4, 128, 4, 64)
        vso[:, 4 * c:4 * c + 4] = r["vs"].reshape(2, 4, 128, 4, 64)
        cso[0, 4 * c:4 * c + 4] = r["convs"].reshape(4, 2, D)
        hso[0, 4 * c:4 * c + 4] = r["hgs"]
        if hf == 1:
            kpo[:, p] = r["kp"].reshape(2, 128, 4, 64)
            vpo[:, p] = r["vp"].reshape(2, 128, 4, 64)
            cpo[0, p] = r["convp"]
            hpo[0, p] = r["hgp"]
    return (y_prompt, y_sample, kpo, vpo, kso, vso, cpo, cso, hpo, hso)


def kernel(**inputs):
    if "nc" not in _NC_CACHE:
        _NC_CACHE["nc"] = build()
    nc = _NC_CACHE["nc"]
    maps = make_in_maps(inputs)
    res = run_bass_kernel_spmd(nc, maps, core_ids=list(range(8)))
    return assemble(res.results)
```

```python
import contextlib
import os
import numpy as np
import concourse.bass as bass
import concourse.mybir as mybir
from concourse.bass_utils import run_bass_kernel_spmd

F32 = mybir.dt.float32
BF16 = mybir.dt.bfloat16
AF = mybir.ActivationFunctionType
ALU = mybir.AluOpType

D = 2048
NB = 9
NTOK = 1056
EPS = 1e-6
PAIRS = [[0, 1], [2, 3], [4, 5], [6, 7]]
COMPUTE = ("pe", "act", "dve", "pool")


class Sched:
    def __init__(self, nc):
        self.nc = nc
        self.ops = []
        self.res_w = {}
        self.res_r = {}
        self.dma_last = {}
        self.streams = {e: [] for e in ("pe", "act", "dve", "pool", "sp")}

    def _deps(self, reads, writes, idx, key):
        raw, war = set(), set()
        for r in reads:
            w = self.res_w.get(r)
            if w is not None:
                raw.add(w)
            if isinstance(r, tuple) and r[0] == "ps":
                for k2, rd in self.res_r.get(r, {}).items():
                    if k2 != key:
                        raw.add(rd)
        for r in writes:
            w = self.res_w.get(r)
            if w is not None:
                war.add(w)
            for rd in self.res_r.get(r, {}).values():
                war.add(rd)
        for r in writes:
            self.res_w[r] = idx
            self.res_r[r] = {}
        for r in reads:
            self.res_r.setdefault(r, {})[key if key is not None else ("dma", idx)] = idx
        raw.discard(idx)
        war.discard(idx)
        return raw, war

    def op(self, eng, fn, reads=(), writes=()):
        idx = len(self.ops)
        raw, war = self._deps(reads, writes, idx, eng)
        self.ops.append(dict(eng=eng, fn=fn, raw=raw, war=war, dma=None, idx=idx))
        self.streams[eng].append(idx)
        return idx

    def dma(self, queue, slot, fn, n=1, inc=16, reads=(), writes=()):
        idx = len(self.ops)
        raw, war = self._deps(reads, writes, idx, None)
        last = self.dma_last.get(slot)
        if last is not None:
            raw.add(last)
        self.dma_last[slot] = idx
        self.ops.append(dict(eng=queue, fn=fn, raw=raw, war=war, dma=slot, idx=idx, n=n, inc=inc))
        self.streams[queue].append(idx)
        return idx

    def barrier(self):
        last = set()
        for e, st in self.streams.items():
            if st:
                last.add(st[-1])
        for v in self.dma_last.values():
            last.add(v)
        for e in COMPUTE + ("sp",):
            idx = len(self.ops)
            self.ops.append(dict(eng=e, fn=None, raw=set(last), war=set(), dma=None, idx=idx))
            self.streams[e].append(idx)

    def emit(self):
        nc, ops = self.nc, self.ops

        def same_eng(p, o):
            return p["dma"] is None and o["dma"] is None and p["eng"] == "pe" and o["eng"] == "pe"

        needed = set()
        for o in ops:
            needed |= o["raw"]
            for d in o["war"]:
                if not same_eng(ops[d], o):
                    needed.add(d)
        with contextlib.ExitStack() as es:
            sem_eng = {e: es.enter_context(nc.semaphore(f"s_{e}")) for e in COMPUTE}
            slots = list(self.dma_last.keys())
            sem_dma = {k: es.enter_context(nc.semaphore(f"d_{i}")) for i, k in enumerate(slots)}
            cnt = {e: 0 for e in COMPUTE}
            dcnt = {k: 0 for k in slots}
            ev = {}
            for o in ops:
                if o["dma"] is not None:
                    dcnt[o["dma"]] += o["inc"] * o["n"]
                    ev[o["idx"]] = (sem_dma[o["dma"]], dcnt[o["dma"]])
                elif o["idx"] in needed and o["fn"] is not None:
                    cnt[o["eng"]] += 1
                    ev[o["idx"]] = (sem_eng[o["eng"]], cnt[o["eng"]])
            self.stats = (dict(cnt), {str(k): v for k, v in dcnt.items()}, {e: len(v) for e, v in self.streams.items()})
            blk = es.enter_context(nc.Block())

            def replay(engname):
                def body(eng):
                    waited = {}
                    for idx in self.streams[engname]:
                        o = ops[idx]
                        deps = set(o["raw"])
                        for d in o["war"]:
                            if not same_eng(ops[d], o):
                                deps.add(d)
                        wl = {}
                        for d in deps:
                            if d not in ev:
                                continue
                            s, v = ev[d]
                            key = id(s)
                            if waited.get(key, 0) >= v:
                                continue
                            if key not in wl or wl[key][1] < v:
                                wl[key] = (s, v)
                        for key, (s, v) in wl.items():
                            eng.wait_ge(s, v)
                            waited[key] = v
                        if o["fn"] is None:
                            continue
                        r = o["fn"](eng)
                        if o["dma"] is not None:
                            s, _ = ev[idx]
                            assert len(r) == o["n"], (len(r), o["n"])
                            for ins in r:
                                ins.then_inc(s, o["inc"])
                        elif idx in ev:
                            ins = r[-1] if isinstance(r, (list, tuple)) else r
                            ins.then_inc(ev[idx][0], 1)
                    if engname == "sp":
                        for k, s in sem_dma.items():
                            if dcnt[k]:
                                eng.wait_ge(s, dcnt[k])
                return body

            blk.tensor(replay("pe"))
            blk.scalar(replay("act"))
            blk.vector(replay("dve"))
            blk.gpsimd(replay("pool"))
            blk.sync(replay("sp"))


def build(n_layers=4, dbg=False, stop=99, small=False):
    nc = bass.Bass("TRN2", target_bir_lowering=False)
    din = lambda name, shape, dt=F32: nc.dram_tensor(name, list(shape), dt, kind="ExternalInput").ap()
    dout = lambda name, shape, dt=F32: nc.dram_tensor(name, list(shape), dt, kind="ExternalOutput").ap()
    dint = lambda name, shape, dt=F32: nc.dram_tensor(name, list(shape), dt, kind="Internal").ap()

    xp = din("xp", [1024, D])
    xsm = din("xsm", [32, D])
    ck = din("ck", [2, 4, 128, 256])
    cv = din("cv", [2, 4, 128, 256])
    sconv = din("sconv", [8, D])
    shg = din("shg", [4, 16, 128, 128])
    ln_g = din("ln_g", [4, D])
    final_g = din("final_g", [D])
    na = 2 if (n_layers >= 4 or not small) else 1
    a_w_in = din("a_w_in", [na, D, 4608])
    a_w_out = din("a_w_out", [na, D, D])
    a_sinks = din("a_sinks", [2, 32])
    b_w_in = din("b_w_in", [1, D, 8192] if (n_layers >= 2 or not small) else [1, 1, 8192])
    b_conv_w = din("b_conv_w", [1, 3, D])
    b_w_out = din("b_w_out", [1, D, D] if (n_layers >= 2 or not small) else [1, 1, D])
    c_w_in = din("c_w_in", [1, D, 8192] if (n_layers >= 3 or not small) else [1, 1, 8192])
    c_norm_g = din("c_norm_g", [1, 128])
    c_w_out = din("c_w_out", [1, D, D] if (n_layers >= 3 or not small) else [1, 1, D])
    c_lb = din("c_lb", [4, D])
    ropec = din("ropec", [128, NB, 8])
    ropes = din("ropes", [128, NB, 8])
    flag_d = din("flag", [128, 1])

    y = dout("y", [NTOK, D])
    kp = dout("kp", [2, 128, 256])
    vp = dout("vp", [2, 128, 256])
    ks = dout("ks", [2, 4, 128, 256])
    vs = dout("vs", [2, 4, 128, 256])
    convp = dout("convp", [2, D])
    convs = dout("convs", [8, D])
    hgp = dout("hgp", [16, 128, 128])
    hgs = dout("hgs", [4, 16, 128, 128])

    cc_a_in = [dint(f"cc_a_in{j}", [128, 512], BF16) for j in range(2)]
    cc_a_out = [dint(f"cc_a_out{j}", [256, 512], BF16) for j in range(2)]
    cc_b_in = dint("cc_b_in", [128, 32])
    cc_b_out = dint("cc_b_out", [256, 32])
    cc_h_in = [dint(f"cc_h_in{h}", [128, 128]) for h in range(16)]
    cc_h_out = [dint(f"cc_h_out{h}", [256, 128]) for h in range(16)]

    S = Sched(nc)
    with contextlib.ExitStack() as es:
        uniq = [0]

        def T(name, shape, dt, stack=es):
            uniq[0] += 1
            return stack.enter_context(nc.sbuf_tensor(f"{name}_{uniq[0]}", list(shape), dt))

        xs = T("xs", [128, NB, D], F32)
        hT = T("hT", [128, 16, NTOK], BF16)
        goT = T("goT", [128, 16, NTOK], BF16)
        wt = [T(f"wt{i}", [128, 16, 256], BF16) for i in range(2)]
        rt = [None] * 4
        ident = T("ident", [128, 128], BF16)
        identf = T("identf", [128, 128], F32)
        onesb = T("onesb", [128, 128], BF16)
        onesf = T("onesf", [128, 128], F32)
        mprev = T("mprev", [128, 2, 128], BF16)
        mown = T("mown", [128, 2, 128], BF16)
        ones4 = T("ones4", [128, 2, 128], BF16)
        flag = T("flag_sb", [128, 1], F32)
        ss = T("ss", [128, NB], F32)
        rstd = T("rstd", [128, NB], F32)
        cosT = T("cosT", [128, NB, 8], F32)
        sinT = T("sinT", [128, NB, 8], F32)
        ps = [es.enter_context(nc.psum_tensor(f"ps{i}", [128, 512], F32)) for i in range(8)]
        psb = [p.bitcast(BF16) for p in ps]

        PS = lambda i: ("ps", i)

        S.op("pool", lambda e: e.memset(onesb[:], 1.0), writes=["onesb"])
        S.op("pool", lambda e: e.memset(onesf[:], 1.0), writes=["onesf"])
        S.op("pool", lambda e: e.memset(ones4[:], 1.0), writes=["ones4"])
        S.op("pool", lambda e: e.affine_select(out=ident[:], in_=onesb[:], pattern=[[-1, 128]], compare_op=ALU.is_equal,
                                               fill=0.0, base=0, channel_multiplier=1), reads=["onesb"], writes=["ident"])
        S.op("pool", lambda e: e.affine_select(out=identf[:], in_=onesf[:], pattern=[[-1, 128]], compare_op=ALU.is_equal,
                                               fill=0.0, base=0, channel_multiplier=1), reads=["onesf"], writes=["identf"])
        S.op("pool", lambda e: e.affine_select(out=mprev[:], in_=ones4[:], pattern=[[0, 2], [-1, 128]], compare_op=ALU.is_ge,
                                               fill=0.0, base=-1, channel_multiplier=1), reads=["ones4"], writes=["mprev"])
        S.op("pool", lambda e: e.affine_select(out=mown[:], in_=ones4[:], pattern=[[0, 2], [1, 128]], compare_op=ALU.is_ge,
                                               fill=0.0, base=0, channel_multiplier=-1), reads=["ones4"], writes=["mown"])
        S.op("pool", lambda e: e.memset(xs[:, 8, :], 0.0), writes=[("x", 8)])
        S.dma("sp", "misc", lambda e: [e.dma_start(out=flag[:], in_=flag_d[:, :]),
                                       e.dma_start(out=cosT[:], in_=ropec[:, :, :]),
                                       e.dma_start(out=sinT[:], in_=ropes[:, :, :])], n=3, writes=["flag", "rope"])
        for b in range(8):
            S.dma("sp", ("xin", b % 4), lambda e, b=b: [e.dma_start(out=xs[:, b, :], in_=xp[b * 128:(b + 1) * 128, :])],
                  writes=[("x", b)])
        S.dma("sp", ("xin", 0), lambda e: [e.dma_start(out=xs[0:32, 8, :], in_=xsm[:, :])], writes=[("x", 8)])

        tiles = []

        def wsrc(w2d, c0, width=256):
            return [(w2d[:, c0:c0 + width], 0)]

        def group_src(w2d, j, hh):
            return [(w2d[:, g * 2048 + j * 128: g * 2048 + (j + 1) * 128], (g % 2) * 128) for g in (2 * hh, 2 * hh + 1)]

        layer_kinds = ["a", "b", "c", "a"][:n_layers]
        for li, kind in enumerate(layer_kinds):
            jj = li // 3
            if kind == "a":
                w_in, w_out = a_w_in[jj], a_w_out[jj]
                tiles.append(wsrc(w_in, 2048))
                tiles.append(wsrc(w_in, 2304))
                for GG in range(8):
                    tiles.append(wsrc(w_in, GG * 256))
                    tiles.append(wsrc(w_in, 2560 + GG * 256))
            elif kind == "b":
                w_in, w_out = b_w_in[0], b_w_out[0]
                for j in range(16):
                    tiles.append(group_src(w_in, j, 0))
                    tiles.append(group_src(w_in, j, 1))
            else:
                w_in, w_out = c_w_in[0], c_w_out[0]
                for j in range(16):
                    tiles.append(group_src(w_in, j, 0))
                    tiles.append(group_src(w_in, j, 1))
            for t in range(8):
                tiles.append(wsrc(w_out, t * 256))
        tstate = dict(next_load=0, next_use=0)

        def issue_load():
            i = tstate["next_load"]
            if i >= len(tiles):
                return
            tstate["next_load"] += 1
            slot = i % 2
            srcs = tiles[i]

            def fn(e, srcs=srcs, slot=slot):
                return [e.dma_start(out=wt[slot][:, :, off:off + a.shape[1]],
                                    in_=a.rearrange("(k p) n -> p k n", p=128)) for a, off in srcs]
            S.dma("pool", ("wt", slot), fn, n=len(srcs), writes=[("wt", slot)])

        def next_tile(prefetch=True):
            i = tstate["next_use"]
            tstate["next_use"] += 1
            while tstate["next_load"] <= (min(i + 1, len(tiles) - 1) if prefetch else i):
                issue_load()
            return wt[i % 2], ("wt", i % 2)

        issue_load()

        def blkP(b):
            return 128 if b < 8 else 32

        def tok0(b):
            return b * 128

        rr = dict(pt=0, ev=0)

        def pt_bank():
            rr["pt"] ^= 1
            return rr["pt"]

        def evac_eng():
            rr["ev"] ^= 1
            return "act" if rr["ev"] else "dve"

        def copy(engname, out, in_, reads, writes):
            if engname == "act":
                S.op("act", lambda e: e.copy(out=out, in_=in_), reads=reads, writes=writes)
            else:
                S.op(engname, lambda e: e.tensor_copy(out=out, in_=in_), reads=reads, writes=writes)

        def transpose_to(src_tile, src_res, ncols_blocks, P, dst_fn, dst_res_fn):
            for g0 in range(0, ncols_blocks, 4):
                rr["tr"] = rr.get("tr", 0) ^ 1
                bank = 4 + rr["tr"]
                n = min(4, ncols_blocks - g0)
                for j in range(n):
                    c = g0 + j
                    S.op("pe", lambda e, c=c, j=j, bank=bank: e.transpose(
                        out=psb[bank][:, j * 128:j * 128 + P], in_=src_tile[0:P, c * 128:(c + 1) * 128],
                        identity=ident[0:P, 0:P]), reads=[src_res, "ident"], writes=[PS(bank)])
                ee = evac_eng()
                for j in range(n):
                    c = g0 + j
                    copy(ee, dst_fn(c), psb[bank][:, j * 128:j * 128 + P], [PS(bank)], [dst_res_fn(c)])

        def norm_phase(g_row, final=False):
            with contextlib.ExitStack() as ph:
                gbc = T("gbc", [128, D], F32, ph)
                junk = T("junk", [128, D], BF16, ph)
                if final:
                    yb = [T(f"yb{i}", [128, D], F32, ph) for i in range(2)]
                else:
                    hb = [T(f"hb{i}", [128, D], BF16, ph) for i in range(2)]
                S.dma("sp", "gbc", lambda e: [e.dma_start(out=gbc[:], in_=g_row.partition_broadcast(128))], writes=["gbc"])
                for b in range(NB):
                    P = blkP(b)
                    S.op("act", lambda e, b=b, P=P: e.activation(out=junk[0:P, :], in_=xs[0:P, b, :], func=AF.Square,
                                                                 accum_out=ss[0:P, b:b + 1]),
                         reads=[("x", b)], writes=["junk", ("ss", b)])
                    S.op("act", lambda e, b=b, P=P: e.activation(out=rstd[0:P, b:b + 1], in_=ss[0:P, b:b + 1], func=AF.Sqrt,
                                                                 scale=1.0 / D, bias=EPS), reads=[("ss", b)], writes=[("rstd", b)])
                    S.op("dve", lambda e, b=b, P=P: e.reciprocal(out=rstd[0:P, b:b + 1], in_=rstd[0:P, b:b + 1]),
                         reads=[("rstd", b)], writes=[("rstd", b)])
                    if final:
                        ybt = yb[b % 2]
                        S.op("dve", lambda e, b=b, P=P, ybt=ybt: e.scalar_tensor_tensor(
                            out=ybt[0:P, :], in0=xs[0:P, b, :], scalar=rstd[0:P, b:b + 1], in1=gbc[0:P, :],
                            op0=ALU.mult, op1=ALU.mult), reads=[("x", b), ("rstd", b), "gbc"], writes=[("yb", b % 2)])
                        S.dma("sp", ("yout", b % 2), lambda e, b=b, P=P, ybt=ybt: [
                            e.dma_start(out=y[b * 128:b * 128 + P, :], in_=ybt[0:P, :])], reads=[("yb", b % 2)])
                    else:
                        hbt = hb[b % 2]
                        S.op("dve", lambda e, b=b, P=P, hbt=hbt: e.scalar_tensor_tensor(
                            out=hbt[0:P, :], in0=xs[0:P, b, :], scalar=rstd[0:P, b:b + 1], in1=gbc[0:P, :],
                            op0=ALU.mult, op1=ALU.mult), reads=[("x", b), ("rstd", b), "gbc"], writes=[("hb", b % 2)])
                        transpose_to(hbt, ("hb", b % 2), 16, P,
                                     lambda c, b=b, P=P: hT[:, c, tok0(b):tok0(b) + P], lambda c, b=b: ("hT", b))
                S.barrier()

        def wout_phase():
            for t in range(8):
                w, wres = next_tile()
                for b in range(NB):
                    P = blkP(b)
                    bank = pt_bank()
                    for k in range(16):
                        S.op("pe", lambda e, k=k, b=b, P=P, bank=bank, w=w: e.matmul(
                            ps[bank][0:P, 0:256], lhsT=goT[:, k, tok0(b):tok0(b) + P], rhs=w[:, k, :],
                            start=(k == 0), stop=(k == 15)), reads=[wres, ("goT", b)], writes=[PS(bank)])
                    S.op("dve", lambda e, b=b, P=P, bank=bank, t=t: e.tensor_tensor(
                        out=xs[0:P, b, t * 256:(t + 1) * 256], in0=xs[0:P, b, t * 256:(t + 1) * 256],
                        in1=ps[bank][0:P, 0:256], op=ALU.add), reads=[PS(bank), ("x", b)], writes=[("x", b)])

        def proj_tm(w, wres, b, bank, ncols=256):
            P = blkP(b)
            for k in range(16):
                S.op("pe", lambda e, k=k, b=b, P=P, bank=bank, w=w: e.matmul(
                    ps[bank][0:P, 0:ncols], lhsT=hT[:, k, tok0(b):tok0(b) + P], rhs=w[:, k, 0:ncols],
                    start=(k == 0), stop=(k == 15)), reads=[wres, ("hT", b)], writes=[PS(bank)])

        def rope_tm(src3, dst1, dst2, P, b, nh, rd, wr):
            x1, x2 = src3[:, :, 0:8], src3[:, :, 8:16]
            cosb = cosT[0:P, b, :].unsqueeze(1).to_broadcast([P, nh, 8])
            sinb = sinT[0:P, b, :].unsqueeze(1).to_broadcast([P, nh, 8])
            r = [t_[0:P, 0:nh, :] for t_ in rt]
            for i, (xa, tb_) in enumerate(((x1, cosb), (x2, sinb), (x2, cosb), (x1, sinb))):
                S.op("dve", lambda e, i=i, xa=xa, tb_=tb_, r=r: e.tensor_tensor(out=r[i], in0=xa, in1=tb_, op=ALU.mult),
                     reads=rd + ["rope"], writes=[f"rt{i}"])
            S.op("dve", lambda e, r=r: e.tensor_tensor(out=dst1, in0=r[0], in1=r[1], op=ALU.subtract),
                 reads=["rt0", "rt1"], writes=wr)
            S.op("dve", lambda e, r=r: e.tensor_tensor(out=dst2, in0=r[2], in1=r[3], op=ALU.add),
                 reads=["rt2", "rt3"], writes=wr)

        def attn_phase(jl):
            with contextlib.ExitStack() as ph:
                kT2 = T("kT2", [128, 4, 1152], BF16, ph)
                vall = T("vall", [128, 9, 4, 66], BF16, ph)
                kT2s = T("kT2s", [128, 4, 4 * 136], BF16, ph)
                vSc = T("vSc", [128, 4, 4, 66], BF16, ph)
                vSn = T("vSn", [8, 4, 4, 66], BF16, ph)
                vS32 = T("vS32", [32, 4, 64], BF16, ph)
                kcb = T("kcb", [128, 4, 2, 64], BF16, ph)
                kc32 = T("kc32", [128, 256], BF16, ph)
                ktm = T("ktm", [128, NB, 256], F32, ph) if False else None
                ktm1 = T("ktm1", [128, 256], F32, ph)
                k7 = T("k7", [128, 256], F32, ph)
                vtm = T("vtm", [128, 256], F32, ph)
                kb = T("kb", [128, 4, 2, 64], BF16, ph)
                kb2 = T("kb2", [128, 4, 2, 64], BF16, ph)
                halo = T("halo", [128, 512], BF16, ph)
                halo_in = T("halo_in", [128, 512], BF16, ph)
                rt = [T(f"rt{i}", [128, 4, 8], F32, ph) for i in range(4)]
                qf = [T(f"qf{i}", [128, 256], F32, ph) for i in range(3)]
                qb = [T(f"qb{i}", [128, 256], BF16, ph) for i in range(3)]
                qT = T("qT", [128, 2, 128], BF16, ph)
                E = [T(f"E{i}", [128, 2, 128], BF16, ph) for i in range(8)]
                den = T("den", [128, 4], F32, ph)
                ob = T("ob", [128, 4, 64], BF16, ph)
                obs = T("obs", [8, 4, 64], BF16, ph)
                oball = T("oball", [128, NB, 256], BF16, ph)
                sz = [T(f"sz{i}", [128, 256], BF16, ph) for i in range(2)]
                gg = [T(f"gg{i}", [128, 256], BF16, ph) for i in range(2)]
                esink = T("esink", [128, 32], F32, ph)
                attn_body(jl, locals())
            S.barrier()

        def attn_body(jl, L):
            kT2, vall, kT2s, vSc, vSn, vS32, kcb, kc32 = (L[k] for k in "kT2 vall kT2s vSc vSn vS32 kcb kc32".split())
            ktm1, k7, vtm, kb, halo, halo_in, qf, qb, qT, E, den, ob, obs, oball, sz, gg, esink, kb2 = (
                L[k] for k in "ktm1 k7 vtm kb halo halo_in qf qb qT E den ob obs oball sz gg esink kb2".split())
            nonlocal_rt = L["rt"]
            rt[:] = nonlocal_rt

            S.dma("sp", "misc", lambda e: [e.dma_start(out=esink[:], in_=a_sinks[jl].partition_broadcast(128))],
                  writes=["esink"])
            S.op("act", lambda e: e.activation(out=esink[:], in_=esink[:], func=AF.Exp), reads=["esink"], writes=["esink"])
            S.op("pool", lambda e: e.memset(ob[:], 0.0), writes=["ob"])
            S.op("pool", lambda e: e.memset(vall[:, :, :, 64:65], 1.0), writes=["vall_ones"])
            S.op("pool", lambda e: e.memset(vSc[:, :, :, 64:65], 1.0), writes=["vSc_ones"])
            S.op("pool", lambda e: e.memset(vSn[:, :, :, 64:65], 1.0), writes=["vSn_ones"])
            S.dma("pool", "cachev", lambda e: [e.dma_start(
                out=vSc[:, s, :, 0:64], in_=cv[jl, s].rearrange("r (h d) -> r h d", d=64)) for s in range(4)], n=4,
                reads=["vSc_ones"], writes=["vSc"])
            for s in range(4):
                S.dma("pool", "cachek", lambda e, s=s: [e.dma_start(out=kc32[:], in_=ck[jl, s])], writes=["kc32"])
                for half in range(2):
                    copy("dve" if half else "act", kcb[:, :, half, :], kc32[:].rearrange("r (h d) -> r h d", d=64),
                         ["kc32"], [("kcb", half)])
                bank = pt_bank()
                for h in range(4):
                    S.op("pe", lambda e, h=h, bank=bank: e.transpose(
                        out=psb[bank][:, h * 128:(h + 1) * 128], in_=kcb[:, h].rearrange("r t d -> r (t d)"), identity=ident[:]),
                        reads=[("kcb", 0), ("kcb", 1), "ident"], writes=[PS(bank)])
                copy(evac_eng(), kT2s[:, :, s * 136:s * 136 + 128], psb[bank][:, 0:512].rearrange("p (h t) -> p h t", t=128),
                     [PS(bank)], [("kT2s", s)])
            S.dma("sp", "cachecp", lambda e: [e.dma_start(out=ks[jl, :, 0:120, :], in_=ck[jl, :, 8:128, :]),
                                              e.dma_start(out=vs[jl, :, 0:120, :], in_=cv[jl, :, 8:128, :])], n=2)

            border = [7, 6, 5, 4, 3, 2, 1, 0, 8]
            if stop <= 1:
                return
            w, wres = next_tile()
            kbX = [kb, kb2]

            def k_stage1(i, b):
                P = blkP(b)
                kbi, kbn = kbX[i % 2], ("kb" if i % 2 == 0 else "kb1")
                bank = pt_bank()
                proj_tm(w, wres, b, bank)
                kt = k7 if b >= 7 else ktm1
                kres = "k7" if b >= 7 else "ktm1"
                copy("act", kt[0:P, :], ps[bank][0:P, 0:256], [PS(bank)], [kres])
                kv3 = kt[0:P, :].rearrange("p (h d) -> p h d", d=64)
                rope_tm(kv3, kv3[:, :, 0:8], kv3[:, :, 8:16], P, b, 4, [kres], [kres])
                for half in range(2):
                    copy("act" if half else "dve", kbi[0:P, :, half, :], kv3, [kres], [(kbn, half)])
                if b == 7:
                    S.dma("sp", "kvout", lambda e: [e.dma_start(out=kp[jl], in_=k7[:, :])], reads=["k7"])
                    copy("dve", halo[:, 0:256], k7[:, :], ["k7"], ["halo"])
                if b == 8:
                    S.dma("sp", "kvout", lambda e: [e.dma_start(out=ks[jl, s, 120:128, :], in_=k7[s * 8:(s + 1) * 8, :])
                                                    for s in range(4)], n=4, reads=["k7"])

            def k_stage2(i, b):
                P = blkP(b)
                kbi, kbn = kbX[i % 2], ("kb" if i % 2 == 0 else "kb1")
                tb = pt_bank()
                for h in range(4):
                    S.op("pe", lambda e, h=h, P=P, tb=tb, kbi=kbi: e.transpose(
                        out=psb[tb][:, h * 128:h * 128 + P], in_=kbi[0:P, h].rearrange("p t d -> p (t d)"),
                        identity=ident[0:P, 0:P]), reads=[(kbn, 0), (kbn, 1), "ident"], writes=[PS(tb)])
                pv3 = psb[tb][:, 0:512].rearrange("p (h t) -> p h t", t=128)
                if b < 8:
                    copy(evac_eng(), kT2[:, :, 128 + b * 128:256 + b * 128], pv3, [PS(tb)], [("kT2", 1 + b)])
                else:
                    for s in range(4):
                        copy(evac_eng(), kT2s[:, :, s * 136 + 128:s * 136 + 136], pv3[:, :, s * 8:(s + 1) * 8],
                             [PS(tb)], [("kT2s", s)])

            k_stage1(0, border[0])
            for i, b in enumerate(border):
                if i + 1 < len(border):
                    k_stage1(i + 1, border[i + 1])
                k_stage2(i, b)
            if stop <= 2:
                return
            w, wres = next_tile()
            for b in border:
                P = blkP(b)
                bank = pt_bank()
                proj_tm(w, wres, b, bank)
                if stop < 2.1:
                    continue
                if b < 8:
                    copy("dve", vall[:, 1 + b, :, 0:64], ps[bank][:, 0:256].rearrange("p (h d) -> p h d", d=64),
                         [PS(bank), "vall_ones"], [("vall", 1 + b)])
                else:
                    copy("dve", vS32[:, :, :], ps[bank][0:32, 0:256].rearrange("p (h d) -> p h d", d=64), [PS(bank)], ["vS32"])
                    if stop >= 2.3:
                        S.dma("sp", "vsn", lambda e: [e.dma_start(out=vSn[:, s, :, 0:64], in_=vS32[s * 8:(s + 1) * 8, :, :])
                                                      for s in range(4)], n=4, reads=["vS32", "vSn_ones"], writes=["vSn"])
                if stop < 2.15:
                    continue
                if b >= 7:
                    copy("act", vtm[0:P, :], ps[bank][0:P, 0:256], [PS(bank), ("vall", 1 + b) if b < 8 else "vS32"], ["vtm"])
                if b == 8:
                    S.dma("sp", "kvout", lambda e: [e.dma_start(out=vs[jl, s, 120:128, :], in_=vtm[s * 8:(s + 1) * 8, :])
                                                    for s in range(4)], n=4, reads=["vtm"])
                if b == 7:
                    S.dma("sp", "kvout", lambda e: [e.dma_start(out=vp[jl], in_=vtm[:, :])], reads=["vtm"])
                    if stop < 2.2:
                        continue
                    copy("dve", halo[:, 256:512], vtm[:, :], ["vtm"], ["halo"])
                    S.dma("sp", "halo", lambda e: [e.dma_start(out=cc_a_in[jl][:, :], in_=halo[:, :])],
                          reads=["halo"], writes=["cc_a_in"])
                    if stop >= 2.6:
                      S.dma("pool", "cc_a", lambda e: [e.collective_compute(
                        "AllGather", ALU.bypass, replica_groups=PAIRS, ins=[cc_a_in[jl].opt()],
                        outs=[cc_a_out[jl].opt()])], inc=1, reads=["cc_a_in"], writes=["cc_a_out"])
                    S.dma("sp", "halo", lambda e: [e.dma_start(out=halo_in[:, :], in_=cc_a_out[jl][0:128, :])],
                          reads=["cc_a_out"], writes=["halo_in"])
            if stop <= 3:
                return
            for half in range(2):
                copy("dve", kb[:, :, half, :], halo_in[:, 0:256].rearrange("p (h d) -> p h d", d=64),
                     ["halo_in"], [("kb", half)])
            copy("act", vall[:, 0, :, 0:64], halo_in[:, 256:512].rearrange("p (h d) -> p h d", d=64),
                 ["halo_in", "vall_ones"], [("vall", 0)])
            tb = pt_bank()
            for h in range(4):
                S.op("pe", lambda e, h=h, tb=tb: e.transpose(
                    out=psb[tb][:, h * 128:(h + 1) * 128], in_=kb[:, h].rearrange("p t d -> p (t d)"),
                    identity=ident[:]), reads=[("kb", 0), ("kb", 1), "ident"], writes=[PS(tb)])
            copy("dve", kT2[:, :, 0:128], psb[tb][:, 0:512].rearrange("p (h t) -> p h t", t=128), [PS(tb)], [("kT2", 0)])

            if stop <= 4:
                return
            for GG in range(8 if stop >= 10 else stop - 4):
                G = GG // 2
                wq, wqres = next_tile()

                def stP(b, wq=wq, wqres=wqres):
                    P = blkP(b)
                    i3 = b % 3
                    bank = pt_bank()
                    proj_tm(wq, wqres, b, bank)
                    copy("act", qf[i3][0:P, :], ps[bank][0:P, 0:256], [PS(bank)], [("qf", i3)])
                    q3 = qf[i3][0:P, :].rearrange("p (h d) -> p h d", d=64)
                    rope_tm(q3, q3[:, :, 0:8], q3[:, :, 8:16], P, b, 4, [("qf", i3)], [("qf", i3)])
                    copy("act", qb[i3][0:P, :], qf[i3][0:P, :], [("qf", i3)], [("qb", i3)])

                def stA(b, part, G=G, GG=GG):
                    P = blkP(b)
                    i3 = b % 3
                    e0 = (b % 2) * 4
                    if part == 1:
                        transpose_to(qb[i3], ("qb", i3), 2, P, lambda c, P=P: qT[:, c, 0:P], lambda c: "qT")
                    seqs = [None] if b < 8 else list(range(4))
                    if b == 8 and part == 2:
                        return
                    for s in seqs:
                        if s is None:
                            NQ, qc = 128, slice(0, 128)
                            kprev = kT2[:, G, b * 128:(b + 1) * 128]
                            kown = kT2[:, G, (b + 1) * 128:(b + 2) * 128]
                            vprev, vown = vall[:, b, G, 0:65], vall[:, b + 1, G, 0:65]
                            KO = 128
                            rdk = [("kT2", b), ("kT2", b + 1)]
                            rdv = [("vall", b), ("vall", b + 1)]
                        else:
                            NQ, qc = 8, slice(s * 8, s * 8 + 8)
                            kprev = kT2s[:, G, s * 136:s * 136 + 128]
                            kown = kT2s[:, G, s * 136 + 128:s * 136 + 136]
                            vprev, vown = vSc[:, s, G, 0:65], vSn[0:8, s, G, 0:65]
                            KO = 8
                            rdk = [("kT2s", s)]
                            rdv = ["vSc", "vSn"]
                        if part == 1:
                            for hl in range(4):
                                c, half = hl // 2, hl % 2
                                r0 = half * 64
                                S.op("pe", lambda e, c=c, r0=r0, half=half, kprev=kprev, qc=qc, NQ=NQ: e.matmul(
                                    ps[2 + half][:, c * 128:c * 128 + NQ], lhsT=kprev[r0:r0 + 64, :], rhs=qT[r0:r0 + 64, c, qc],
                                    start=True, stop=True), reads=rdk + ["qT"], writes=[PS(2 + half)])
                                S.op("pe", lambda e, c=c, r0=r0, half=half, kown=kown, qc=qc, NQ=NQ, KO=KO: e.matmul(
                                    ps[2 + half][0:KO, 256 + c * 128:256 + c * 128 + NQ], lhsT=kown[r0:r0 + 64, :], rhs=qT[r0:r0 + 64, c, qc],
                                    start=True, stop=True), reads=rdk + ["qT"], writes=[PS(2 + half)])
                            for i, (bnk, off) in enumerate(((2, 0), (3, 0), (2, 256), (3, 256))):
                                KP = 128 if i < 2 else KO
                                Ev = E[e0 + i][0:KP, :, 0:NQ]
                                S.op("act", lambda e, Ev=Ev, bnk=bnk, off=off, KP=KP, NQ=NQ: e.activation(
                                    out=Ev, in_=ps[bnk][0:KP, off:off + 256].rearrange("p (h q) -> p h q", q=128)[:, :, 0:NQ],
                                    func=AF.Exp, scale=0.125), reads=[PS(bnk)], writes=[("E", e0 + i)])
                                m = (mprev if i < 2 else mown)[0:KP, :, 0:NQ]
                                if i < 2 and b == 0:
                                    S.op("dve", lambda e, Ev=Ev, m=m: e.scalar_tensor_tensor(
                                        out=Ev, in0=Ev, scalar=flag[:, 0:1], in1=m, op0=ALU.mult, op1=ALU.mult),
                                        reads=[("E", e0 + i), "flag", "mprev", "mown"], writes=[("E", e0 + i)])
                                else:
                                    S.op("dve", lambda e, Ev=Ev, m=m: e.tensor_tensor(out=Ev, in0=Ev, in1=m, op=ALU.mult),
                                         reads=[("E", e0 + i), "mprev", "mown"], writes=[("E", e0 + i)])
                            if s is None:
                                continue
                        bnk = 6 + ((b if s is None else s) % 2)
                        for hl in range(4):
                            c, half = hl // 2, hl % 2
                            S.op("pe", lambda e, c=c, half=half, hl=hl, bnk=bnk, vprev=vprev, NQ=NQ: e.matmul(
                                ps[bnk][0:NQ, hl * 65:(hl + 1) * 65], lhsT=E[e0 + half][:, c, 0:NQ], rhs=vprev,
                                start=True, stop=False), reads=[("E", e0 + half)] + rdv, writes=[PS(bnk)])
                            S.op("pe", lambda e, c=c, half=half, hl=hl, bnk=bnk, vown=vown, NQ=NQ, KO=KO: e.matmul(
                                ps[bnk][0:NQ, hl * 65:(hl + 1) * 65], lhsT=E[e0 + 2 + half][0:KO, c, 0:NQ], rhs=vown,
                                start=False, stop=True), reads=[("E", e0 + 2 + half)] + rdv, writes=[PS(bnk)])
                        pv = ps[bnk][0:NQ, 0:260].rearrange("p (h e) -> p h e", e=65)
                        es_ = esink[0:NQ, GG * 4:(GG + 1) * 4]
                        dn = den[0:NQ, 0:4]
                        S.op("dve", lambda e, pv=pv, es_=es_, dn=dn: e.tensor_tensor(
                            out=dn, in0=pv[:, :, 64], in1=es_, op=ALU.add), reads=[PS(bnk), "esink"], writes=["den"])
                        S.op("dve", lambda e, dn=dn: e.reciprocal(out=dn, in_=dn), reads=["den"], writes=["den"])
                        if s is None:
                            obv = oball[0:NQ, b, :].rearrange("p (h d) -> p h d", d=64)
                            wr = [("oball", b)]
                        else:
                            obv = obs[0:NQ]
                            wr = ["obs"]
                        S.op("dve", lambda e, pv=pv, dn=dn, obv=obv, NQ=NQ: e.tensor_tensor(
                            out=obv, in0=pv[:, :, 0:64], in1=dn.unsqueeze(2).to_broadcast([NQ, 4, 64]), op=ALU.mult),
                            reads=[PS(bnk), "den"], writes=wr)
                        if s is not None:
                            S.dma("sp", "obs", lambda e, s=s, b=b: [e.dma_start(
                                out=oball[s * 8:(s + 1) * 8, b, :], in_=obs[:, :, :].rearrange("p h d -> p (h d)"))],
                                reads=["obs"], writes=[("oball", b)])

                stP(0)
                stP(1)
                for b in range(8):
                    stA(b, 1)
                    if b + 2 <= 8:
                        stP(b + 2)
                    stA(b, 2)
                stA(8, 1)

                wz, wzres = next_tile()

                def stZP(b, wz=wz, wzres=wzres):
                    P = blkP(b)
                    i2 = b % 2
                    bank = pt_bank()
                    proj_tm(wz, wzres, b, bank)
                    S.op("act", lambda e, P=P, bank=bank, i2=i2: e.activation(out=sz[i2][0:P, :], in_=ps[bank][0:P, 0:256], func=AF.Silu),
                         reads=[PS(bank)], writes=[("sz", i2)])
                    S.op("dve", lambda e, P=P, b=b, i2=i2: e.tensor_tensor(out=gg[i2][0:P, :], in0=sz[i2][0:P, :], in1=oball[0:P, b, :], op=ALU.mult),
                         reads=[("sz", i2), ("oball", b)], writes=[("gg", i2)])

                def stZT(b, GG=GG):
                    P = blkP(b)
                    i2 = b % 2
                    transpose_to(gg[i2], ("gg", i2), 2, P, lambda c, b=b, P=P, GG=GG: goT[:, 2 * GG + c, tok0(b):tok0(b) + P],
                                 lambda c, b=b: ("goT", b))

                stZP(0)
                for b in range(NB):
                    if b + 1 < NB:
                        stZP(b + 1)
                    stZT(b)

        TT = [(0, 512), (512, 512), (1024, 32)]

        def hT_res(T0, N):
            return [("hT", b) for b in range(T0 // 128, (T0 + N + 127) // 128)]

        def goT_res(T0, N):
            return [("goT", b) for b in range(T0 // 128, (T0 + N + 127) // 128)]

        def proj_fm(w, wres, col0, T0, N, bank):
            for k in range(16):
                S.op("pe", lambda e, k=k, w=w, col0=col0, T0=T0, N=N, bank=bank: e.matmul(
                    ps[bank][:, 0:N], lhsT=w[:, k, col0:col0 + 128], rhs=hT[:, k, T0:T0 + N],
                    start=(k == 0), stop=(k == 15)), reads=[wres] + hT_res(T0, N), writes=[PS(bank)])

        def tm_to_fm(src, src_res, R, dst, dst_res):
            bank = pt_bank()
            for j in range(16):
                S.op("pe", lambda e, j=j, bank=bank: e.transpose(out=ps[bank][:, j * R:(j + 1) * R], in_=src[0:R, j * 128:(j + 1) * 128],
                                                             identity=identf[0:R, 0:R]), reads=[src_res, "identf"], writes=[PS(bank)])
            copy("dve", dst, ps[bank][:, 0:16 * R].rearrange("p (j r) -> p j r", r=R), [PS(bank)], [dst_res])

        def fm_to_tm_out(src, src_res, R, stage, dram_out, slot):
            for g0 in range(0, 16, 4):
                bank = pt_bank()
                for jj_ in range(4):
                    j = g0 + jj_
                    S.op("pe", lambda e, j=j, jj_=jj_, bank=bank: e.transpose(
                        out=ps[bank][0:R, jj_ * 128:(jj_ + 1) * 128], in_=src[:, j, :], identity=identf[:]),
                        reads=[src_res, "identf"], writes=[PS(bank)])
                copy("act", stage[0:R, g0 * 128:(g0 + 4) * 128], ps[bank][0:R, 0:512], [PS(bank)], [("stage", slot)])
            S.dma("sp", slot, lambda e: [e.dma_start(out=dram_out, in_=stage[0:R, :])], reads=[("stage", slot)])

        def conv_phase():
            with contextlib.ExitStack() as ph:
                bsb = T("bsb", [128, NTOK], F32, ph)
                csb = T("csb", [128, NTOK], F32, ph)
                uT = T("uT", [128, 1026], F32, ph)
                usT = T("usT", [128, 4, 10], F32, ph)
                szb = T("szb", [128, 512], F32, ph)
                acc = T("acc", [128, 512], F32, ph)
                cwt = T("bufA", [8, D], F32, ph)
                cw = T("cw", [128, 16, 3], F32, ph)
                sct0 = T("bufB", [8, D], F32, ph)
                sct = T("sct", [128, 16, 8], F32, ph)
                ulast = T("ulast", [128, 16, 2], F32, ph)
                uls = T("uls", [128, 16, 8], F32, ph)
                bz01 = T("bz01", [128, 16, 2], F32, ph)
                cv01 = T("cv01", [128, 16, 2], F32, ph)
                uh = T("uh", [128, 16, 2], F32, ph)
                tmp2 = [T(f"tmp2{i}", [128, 16], F32, ph) for i in range(3)]
                stage, stage2 = cwt, sct0

                S.dma("sp", "misc", lambda e: [e.dma_start(out=cwt[0:3, :], in_=b_conv_w[0]),
                                               e.dma_start(out=sct0[:, :], in_=sconv[:, :])], n=2, writes=[("stage", "cvo1"), ("stage", "cvo2")])
                tm_to_fm(cwt, ("stage", "cvo1"), 3, cw[:], "cw")
                tm_to_fm(sct0, ("stage", "cvo2"), 8, sct[:], "sct")
                S.op("pool", lambda e: e.memset(uT[:, 0:2], 0.0), writes=["uT"])
                for j in range(16):
                    w0_, w0res = next_tile()
                    for (T0, N) in TT:
                        bank = pt_bank()
                        proj_fm(w0_, w0res, 0, T0, N, bank)
                        copy("act", bsb[:, T0:T0 + N], ps[bank][:, 0:N], [PS(bank)], [("bsb", T0)])
                        bank = pt_bank()
                        proj_fm(w0_, w0res, 128, T0, N, bank)
                        copy("act", csb[:, T0:T0 + N], ps[bank][:, 0:N], [PS(bank)], [("csb", T0)])
                    w1_, w1res = next_tile()
                    copy("dve", usT[:, :, 0:2], sct[:, j, :].rearrange("p (s i) -> p s i", i=2), ["sct"], ["usT"])
                    for (T0, N) in TT:
                        bank = pt_bank()
                        proj_fm(w1_, w1res, 0, T0, N, bank)
                        if T0 < 1024:
                            S.op("dve", lambda e, T0=T0, N=N, bank=bank: e.tensor_tensor(
                                out=uT[:, 2 + T0:2 + T0 + N], in0=csb[:, T0:T0 + N], in1=ps[bank][:, 0:N], op=ALU.mult),
                                reads=[PS(bank), ("csb", T0)], writes=["uT"])
                        else:
                            S.op("dve", lambda e, T0=T0, N=N, bank=bank: e.tensor_tensor(
                                out=usT[:, :, 2:10], in0=csb[:, T0:T0 + N].rearrange("p (s t) -> p s t", t=8),
                                in1=ps[bank][:, 0:N].rearrange("p (s t) -> p s t", t=8), op=ALU.mult),
                                reads=[PS(bank), ("csb", T0)], writes=["usT"])
                        bank = pt_bank()
                        proj_fm(w1_, w1res, 128, T0, N, bank)
                        S.op("act", lambda e, N=N, bank=bank: e.activation(out=szb[:, 0:N], in_=ps[bank][:, 0:N], func=AF.Silu),
                             reads=[PS(bank)], writes=["szb"])
                        S.op("dve", lambda e, T0=T0, N=N: e.tensor_tensor(out=szb[:, 0:N], in0=szb[:, 0:N], in1=bsb[:, T0:T0 + N], op=ALU.mult),
                             reads=["szb", ("bsb", T0)], writes=["szb"])
                        if T0 < 1024:
                            u0, u1, u2 = uT[:, T0:T0 + N], uT[:, T0 + 1:T0 + 1 + N], uT[:, T0 + 2:T0 + 2 + N]
                            a_, ures = acc[:, 0:N], "uT"
                            sz_ = szb[:, 0:N]
                            gout = goT[:, j, T0:T0 + N]
                        else:
                            u0, u1, u2 = usT[:, :, 0:8], usT[:, :, 1:9], usT[:, :, 2:10]
                            a_, ures = acc[:, 0:32].rearrange("p (s t) -> p s t", t=8), "usT"
                            sz_ = szb[:, 0:32].rearrange("p (s t) -> p s t", t=8)
                            gout = goT[:, j, T0:T0 + N].rearrange("p (s t) -> p s t", t=8)
                        S.op("dve", lambda e, j=j, u0=u0, a_=a_: e.tensor_scalar(out=a_, in0=u0, scalar1=cw[:, j, 0:1], scalar2=None, op0=ALU.mult),
                             reads=[ures, "cw"], writes=["acc"])
                        S.op("dve", lambda e, j=j, u1=u1, a_=a_: e.scalar_tensor_tensor(out=a_, in0=u1, scalar=cw[:, j, 1:2], in1=a_, op0=ALU.mult, op1=ALU.add),
                             reads=[ures, "cw", "acc"], writes=["acc"])
                        S.op("dve", lambda e, j=j, u2=u2, a_=a_: e.scalar_tensor_tensor(out=a_, in0=u2, scalar=cw[:, j, 2:3], in1=a_, op0=ALU.mult, op1=ALU.add),
                             reads=[ures, "cw", "acc"], writes=["acc"])
                        if T0 == 0:
                            copy("act", bz01[:, j, :], szb[:, 0:2], ["szb"], ["bz01"])
                            copy("act", cv01[:, j, :], acc[:, 0:2], ["acc"], ["cv01"])
                        S.op("dve", lambda e, a_=a_, sz_=sz_, gout=gout: e.tensor_tensor(out=gout, in0=a_, in1=sz_, op=ALU.mult),
                             reads=["acc", "szb"], writes=goT_res(T0, N))
                    copy("act", ulast[:, j, :], uT[:, 1024:1026], ["uT"], ["ulast"])
                    copy("act", uls[:, j, :].rearrange("p (s i) -> p s i", i=2), usT[:, :, 8:10], ["usT"], ["uls"])
                S.dma("sp", "cvx", lambda e: [e.dma_start(out=cc_b_in[:, :], in_=ulast[:].rearrange("p j i -> p (j i)"))],
                      reads=["ulast"], writes=["cc_b_in"])
                S.dma("pool", "cc_b", lambda e: [e.collective_compute(
                    "AllGather", ALU.bypass, replica_groups=PAIRS, ins=[cc_b_in.opt()], outs=[cc_b_out.opt()])],
                    inc=1, reads=["cc_b_in"], writes=["cc_b_out"])
                S.dma("sp", "cvx", lambda e: [e.dma_start(out=uh[:].rearrange("p j i -> p (j i)"), in_=cc_b_out[0:128, :])],
                      reads=["cc_b_out"], writes=["uh"])
                S.op("dve", lambda e: e.tensor_scalar(out=uh[:], in0=uh[:], scalar1=flag[:, 0:1], scalar2=None, op0=ALU.mult),
                     reads=["uh", "flag"], writes=["uh"])
                t0_, t1_, t2_ = tmp2[0][:], tmp2[1][:], tmp2[2][:]
                S.op("dve", lambda e: e.tensor_tensor(out=t0_, in0=cw[:, :, 0], in1=uh[:, :, 0], op=ALU.mult), reads=["cw", "uh"], writes=["t0"])
                S.op("dve", lambda e: e.tensor_tensor(out=t1_, in0=cw[:, :, 1], in1=uh[:, :, 1], op=ALU.mult), reads=["cw", "uh"], writes=["t1"])
                S.op("dve", lambda e: e.tensor_tensor(out=t2_, in0=cw[:, :, 0], in1=uh[:, :, 1], op=ALU.mult), reads=["cw", "uh"], writes=["t2"])
                S.op("dve", lambda e: e.tensor_tensor(out=t0_, in0=t0_, in1=t1_, op=ALU.add), reads=["t0", "t1"], writes=["t0"])
                S.op("dve", lambda e: e.tensor_tensor(out=cv01[:, :, 0], in0=cv01[:, :, 0], in1=t0_, op=ALU.add), reads=["t0", "cv01"], writes=["cv01"])
                S.op("dve", lambda e: e.tensor_tensor(out=cv01[:, :, 1], in0=cv01[:, :, 1], in1=t2_, op=ALU.add), reads=["t2", "cv01"], writes=["cv01"])
                S.op("dve", lambda e: e.tensor_tensor(out=goT[:, :, 0:2], in0=cv01[:], in1=bz01[:], op=ALU.mult),
                     reads=["cv01", "bz01"], writes=[("goT", 0)])
                fm_to_tm_out(ulast, "ulast", 2, stage, convp[:, :], "cvo1")
                fm_to_tm_out(uls, "uls", 8, stage2, convs[:, :], "cvo2")
            S.barrier()

        def hgrn_phase(li):
            with contextlib.ExitStack() as ph:
                clb0 = T("clb0", [64, 128], F32, ph)
                clbf = T("clbf", [128, 4, 16], F32, ph)
                lbt = T("lbt", [128, 16], F32, ph)
                omlt = T("omlt", [128, 16], F32, ph)
                dent = T("dent", [128, 16], F32, ph)
                ng = T("ng", [128, 1], F32, ph)
                m64 = T("m64", [128, 512], BF16, ph)
                m8 = T("m8", [128, 32], BF16, ph)
                mone = T("mone", [128, 512], BF16, ph)
                t_sq = T("t_sq", [128, NTOK], F32, ph)
                t_fg = T("t_fg", [128, NTOK], F32, ph)
                t_lf = T("t_lf", [128, NTOK], F32, ph)
                t_b = T("t_b", [128, NTOK], F32, ph)
                t_bg = T("t_bg", [128, NTOK], F32, ph)
                t_sq2 = T("t_sq2", [128, NTOK], BF16, ph)
                qtT = T("qtT", [128, NTOK], BF16, ph)
                kiT = T("kiT", [128, NTOK], BF16, ph)
                ksT = T("ksT", [128, NTOK], BF16, ph)
                qgT = T("qgT", [128, 1024], BF16, ph)
                vT = T("vT", [128, NTOK], BF16, ph)
                szT = T("szT", [128, NTOK], BF16, ph)
                oTs = t_sq
                ebl = T("ebl", [128, 20], F32, ph)
                ebt = T("ebt", [128, 1], F32, ph)
                bgl = T("bgl", [128, 1], F32, ph)
                Sloc = T("Sloc", [128, 16, 128], F32, ph)
                Sball = t_lf[:].bitcast(BF16)[:, 0:1920].rearrange("p (c d) -> p c d", d=128)
                SA = T("SA", [128, 128], F32, ph)
                SAb = T("SAb", [128, 128], BF16, ph)
                Sf = T("Sf", [128, 128], F32, ph)
                SfP = [Sf, T("Sf1", [128, 128], F32, ph)]
                SbP = [SAb, T("SAb1", [128, 128], BF16, ph)]
                kvtm = [T(f"kvtm{i}", [64, 256], BF16, ph) for i in range(2)]
                att = [T(f"att{i}", [64, 64], BF16, ph) for i in range(2)]

                S.dma("sp", "misc", lambda e: [e.dma_start(out=clb0[:, :], in_=c_lb.rearrange("r (j p) -> (r j) p", p=128)),
                                               e.dma_start(out=ng[:, :], in_=c_norm_g[0].rearrange("(p o) -> p o", o=1))],
                      n=2, writes=["clb0", "ng"])
                bank = pt_bank()
                S.op("pe", lambda e, bank=bank: e.transpose(out=ps[bank][:, 0:64], in_=clb0[:, :], identity=identf[0:64, 0:64]),
                     reads=["clb0", "identf"], writes=[PS(bank)])
                S.op("act", lambda e, bank=bank: e.activation(out=clbf[:].rearrange("p r j -> p (r j)"), in_=ps[bank][:, 0:64], func=AF.Exp),
                     reads=[PS(bank)], writes=["clbf"])
                assert li == 2
                S.op("dve", lambda e: e.tensor_tensor(out=dent[:], in0=clbf[:, 0, :], in1=clbf[:, 1, :], op=ALU.add), reads=["clbf"], writes=["dent"])
                S.op("dve", lambda e: e.tensor_tensor(out=lbt[:], in0=clbf[:, 2, :], in1=clbf[:, 3, :], op=ALU.add), reads=["clbf"], writes=["lbt"])
                S.op("dve", lambda e: e.tensor_tensor(out=dent[:], in0=dent[:], in1=lbt[:], op=ALU.add), reads=["dent", "lbt"], writes=["dent"])
                S.op("dve", lambda e: e.reciprocal(out=dent[:], in_=dent[:]), reads=["dent"], writes=["dent"])
                S.op("dve", lambda e: e.tensor_tensor(out=lbt[:], in0=clbf[:, 1, :], in1=clbf[:, 2, :], op=ALU.add), reads=["clbf", "lbt"], writes=["lbt"])
                S.op("dve", lambda e: e.tensor_tensor(out=lbt[:], in0=lbt[:], in1=dent[:], op=ALU.mult), reads=["lbt", "dent"], writes=["lbt"])
                S.op("dve", lambda e: e.tensor_scalar(out=omlt[:], in0=lbt[:], scalar1=-1.0, scalar2=1.0, op0=ALU.mult, op1=ALU.add),
                     reads=["lbt"], writes=["omlt"])
                S.op("pool", lambda e: e.memset(mone[:], 1.0), writes=["mone"])
                S.op("pool", lambda e: e.memset(m64[:], 1.0), writes=["m64"])
                S.op("pool", lambda e: e.memset(m64[:].rearrange("p (c t) -> p c t", t=64)[:, :, 0:1], 0.0), reads=["m64"], writes=["m64"])
                S.op("pool", lambda e: e.memset(m8[:], 1.0), writes=["m8"])
                S.op("pool", lambda e: e.memset(m8[:].rearrange("p (c t) -> p c t", t=8)[:, :, 0:1], 0.0), reads=["m8"], writes=["m8"])

                for j in range(16):
                    def hb_bank():
                        rr["hb"] = (rr.get("hb", -1) + 1) % 7
                        return rr["hb"]
                    w0_, w0res = next_tile(prefetch=False)
                    issue_load()
                    for ti, (T0, N) in enumerate(TT):
                        bank = hb_bank()
                        proj_fm(w0_, w0res, 128, T0, N, bank)
                        S.op("act", lambda e, T0=T0, N=N, bank=bank: e.activation(out=t_fg[:, T0:T0 + N], in_=ps[bank][:, 0:N], func=AF.Sigmoid),
                             reads=[PS(bank)], writes=[("t_fg", ti)])
                    for ti, (T0, N) in enumerate(TT):
                        bank = hb_bank()
                        proj_fm(w0_, w0res, 0, T0, N, bank)
                        S.op("act", lambda e, T0=T0, N=N, bank=bank: e.activation(out=t_sq[:, T0:T0 + N], in_=ps[bank][:, 0:N], func=AF.Silu),
                             reads=[PS(bank)], writes=[("t_sq", ti)])
                    w1_, w1res = next_tile(prefetch=False)
                    issue_load()

                    def chain(ti, T0, N, j=j):
                        C = 64 if T0 < 1024 else 8
                        nch = N // C
                        c0 = T0 // 64
                        msk = m64 if T0 < 1024 else m8
                        sl = slice(T0, T0 + N)
                        fg, lf, bb, bg, sq_ = t_fg[:, sl], t_lf[:, sl], t_b[:, sl], t_bg[:, sl], t_sq[:, sl]
                        R = lambda n: (n, ti)
                        st = []
                        st.append(lambda: S.op("dve", lambda e: e.tensor_scalar(out=fg, in0=fg, scalar1=omlt[:, j:j + 1], scalar2=lbt[:, j:j + 1],
                                                                                op0=ALU.mult, op1=ALU.add),
                                               reads=[R("t_fg"), "omlt", "lbt"], writes=[R("t_fg")]))
                        st.append(lambda: S.op("act", lambda e: e.activation(out=lf, in_=fg, func=AF.Ln), reads=[R("t_fg")], writes=[R("t_lf")]))
                        st.append(lambda: S.op("dve", lambda e: e.tensor_scalar(out=fg, in0=fg, scalar1=-1.0, scalar2=1.0, op0=ALU.mult, op1=ALU.add),
                                               reads=[R("t_fg"), R("t_lf")], writes=[R("t_fg")]))
                        st.append(lambda: S.op("dve", lambda e: e.tensor_tensor_scan(out=bb, data0=msk[:, 0:N], data1=lf, initial=0.0,
                                                                                     op0=ALU.mult, op1=ALU.add),
                                               reads=[R("t_lf"), "m64", "m8"], writes=[R("t_b")]))
                        if T0 < 1024:
                            init = 0.0 if ti == 0 else bgl[:, 0:1]

                            def gsc():
                                S.op("dve", lambda e: e.tensor_tensor_scan(out=bg, data0=mone[:, 0:N], data1=lf, initial=init,
                                                                           op0=ALU.mult, op1=ALU.add),
                                     reads=[R("t_lf"), "mone", "bgl"], writes=[R("t_bg")])
                                copy("dve", bgl[:, 0:1], t_bg[:, T0 + N - 1:T0 + N], [R("t_bg")], ["bgl"])
                                if ti == 1:
                                    S.op("act", lambda e: e.activation(out=ebt[:, 0:1], in_=bgl[:, 0:1], func=AF.Exp), reads=["bgl"], writes=["ebt"])
                            st.append(gsc)
                        else:
                            st.append(lambda: None)
                        st.append(lambda: S.op("act", lambda e: e.activation(out=lf, in_=bb, func=AF.Exp), reads=[R("t_b"), R("t_bg")], writes=[R("t_lf")]))
                        st.append(lambda: S.op("dve", lambda e: e.tensor_tensor(out=qtT[:, sl], in0=sq_, in1=lf, op=ALU.mult),
                                               reads=[R("t_sq"), R("t_lf")], writes=[("qtT", ti)]))
                        st.append(lambda: S.op("act", lambda e: e.activation(out=lf, in_=bb, func=AF.Exp, scale=-1.0),
                                               reads=[R("t_b"), ("qtT", ti)], writes=[R("t_lf")]))
                        st.append(lambda: S.op("dve", lambda e: e.tensor_tensor(out=fg, in0=fg, in1=lf, op=ALU.mult),
                                               reads=[R("t_fg"), R("t_lf")], writes=[R("t_fg")]))
                        st.append(lambda: S.op("act", lambda e: e.activation(
                            out=ebl[:, c0:c0 + nch], in_=bb.rearrange("p (c t) -> p c t", t=C)[:, :, C - 1], func=AF.Exp),
                            reads=[R("t_b")], writes=[("ebl", ti)]))
                        st.append(lambda: copy("pool", kiT[:, sl], fg, [R("t_fg")], [("kiT", ti)]))
                        st.append(lambda: S.op("dve", lambda e: e.tensor_tensor(
                            out=ksT[:, sl].rearrange("p (c t) -> p c t", t=C), in0=fg.rearrange("p (c t) -> p c t", t=C),
                            in1=ebl[:, c0:c0 + nch].unsqueeze(2).to_broadcast([128, nch, C]), op=ALU.mult),
                            reads=[R("t_fg"), ("ebl", ti)], writes=[("ksT", ti)]))
                        if T0 < 1024:
                            st.append(lambda: S.op("act", lambda e: e.activation(out=bg, in_=bg, func=AF.Exp), reads=[R("t_bg")], writes=[R("t_bg")]))
                            st.append(lambda: S.op("dve", lambda e: e.tensor_tensor(out=qgT[:, sl], in0=sq_, in1=bg, op=ALU.mult),
                                                   reads=[R("t_sq"), R("t_bg")], writes=[("qgT", ti)]))
                        return st

                    chains = [chain(ti, T0, N) for ti, (T0, N) in enumerate(TT)]
                    for k in range(max(len(c_) for c_ in chains)):
                        for c_ in chains:
                            if k < len(c_):
                                c_[k]()

                    for ti, (T0, N) in enumerate(TT):
                        bank = hb_bank()
                        proj_fm(w1_, w1res, 128, T0, N, bank)
                        S.op("act", lambda e, N=N, T0=T0, bank=bank: e.activation(out=szT[:, T0:T0 + N], in_=ps[bank][:, 0:N], func=AF.Silu),
                             reads=[PS(bank)], writes=[("szT", ti)])
                    for ti, (T0, N) in enumerate(TT):
                        bank = hb_bank()
                        proj_fm(w1_, w1res, 0, T0, N, bank)
                        copy("act", vT[:, T0:T0 + N], ps[bank][:, 0:N], [PS(bank)], [("vT", ti)])

                    def chunk_step(t0, C, ci, Sf32, Sbf, sres, par):
                        ti = 0 if t0 < 512 else (1 if t0 < 1024 else 2)
                        kv = kvtm[par]
                        at = att[par]
                        tb = pt_bank()
                        S.op("pe", lambda e: e.transpose(out=psb[tb][0:C, 0:128], in_=ksT[:, t0:t0 + C], identity=ident[:]),
                             reads=[("ksT", ti), "ident"], writes=[PS(tb)])
                        S.op("pe", lambda e: e.transpose(out=psb[tb][0:C, 128:256], in_=vT[:, t0:t0 + C], identity=ident[:]),
                             reads=[("vT", ti), "ident"], writes=[PS(tb)])
                        copy("act", kv[0:C, :], psb[tb][0:C, 0:256], [PS(tb)], [("kvtm", par)])
                        S.op("pe", lambda e: e.matmul(ps[2][0:C, 0:C], lhsT=kiT[:, t0:t0 + C], rhs=qtT[:, t0:t0 + C], start=True, stop=True),
                             reads=[("kiT", ti), ("qtT", ti)], writes=[PS(2)])
                        S.op("dve", lambda e: e.tensor_tensor(out=at[0:C, 0:C], in0=ps[2][0:C, 0:C], in1=mown[0:C, 0, 0:C], op=ALU.mult),
                             reads=[PS(2), "mown"], writes=[("att", par)])
                        ob_ = 3 + par
                        S.op("pe", lambda e: e.matmul(ps[ob_][:, 0:C], lhsT=Sbf[:, :], rhs=qtT[:, t0:t0 + C], start=True, stop=False),
                             reads=[sres + "b", ("qtT", ti)], writes=[PS(ob_)])
                        S.op("pe", lambda e: e.matmul(ps[ob_][:, 0:C], lhsT=kv[0:C, 128:256], rhs=at[0:C, 0:C], start=False, stop=True),
                             reads=[("kvtm", par), ("att", par)], writes=[PS(ob_)])
                        copy("act", oTs[:, t0:t0 + C], ps[ob_][:, 0:C], [PS(ob_)], [("t_sq", ti)])
                        sb_ = 5 + par
                        S.op("pe", lambda e: e.matmul(ps[sb_][:, 0:128], lhsT=kv[0:C, 0:128], rhs=kv[0:C, 128:256], start=True, stop=True),
                             reads=[("kvtm", par)], writes=[PS(sb_)])
                        S.op("dve", lambda e: e.scalar_tensor_tensor(out=Sf32[:, :], in0=Sf32[:, :], scalar=ebl[:, ci:ci + 1], in1=ps[sb_][:, 0:128],
                                                                     op0=ALU.mult, op1=ALU.add),
                             reads=[PS(sb_), sres, ("ebl", ti)], writes=[sres])
                        copy("act", Sbf[:, :], Sf32[:, :], [sres], [sres + "b"])

                    TLR = [("t_lf", 0), ("t_lf", 1), ("t_lf", 2)]

                    def l_tr(c):
                        t0, ti, par = c * 64, (0 if c < 8 else 1), c % 2
                        kv = kvtm[par]
                        S.op("pe", lambda e: e.transpose(out=psb[7][0:64, 0:128], in_=ksT[:, t0:t0 + 64], identity=ident[:]),
                             reads=[("ksT", ti), "ident"], writes=[PS(7)])
                        S.op("pe", lambda e: e.transpose(out=psb[7][0:64, 128:256], in_=vT[:, t0:t0 + 64], identity=ident[:]),
                             reads=[("vT", ti), "ident"], writes=[PS(7)])
                        copy("act", kv[0:64, :], psb[7][0:64, 0:256], [PS(7)], [("kvtm", par)])

                    def l_att(c):
                        t0, ti, par = c * 64, (0 if c < 8 else 1), c % 2
                        at = att[par]
                        S.op("pe", lambda e: e.matmul(ps[2][0:64, 0:64], lhsT=kiT[:, t0:t0 + 64], rhs=qtT[:, t0:t0 + 64], start=True, stop=True),
                             reads=[("kiT", ti), ("qtT", ti)], writes=[PS(2)])
                        S.op("dve", lambda e: e.tensor_tensor(out=at[0:64, 0:64], in0=ps[2][0:64, 0:64], in1=mown[0:64, 0, 0:64], op=ALU.mult),
                             reads=[PS(2), "mown"], writes=[("att", par)])

                    def q_half(half):
                        bank = 3 + half
                        for c in range(8 * half, 8 * half + 8):
                            if c == 0:
                                continue
                            col = (c % 8) * 64
                            S.op("pe", lambda e, c=c, col=col: e.matmul(
                                ps[bank][:, col:col + 64], lhsT=Sball[:, c - 1, :], rhs=qtT[:, c * 64:(c + 1) * 64], start=True, stop=True),
                                reads=[("Sball", (c - 1) // 8), ("qtT", half)] + TLR, writes=[PS(bank)])
                        lo = 64 if half == 0 else 0
                        T0_ = half * 512
                        S.op("dve", lambda e: e.tensor_tensor(
                            out=oTs[:, T0_ + lo:T0_ + 512], in0=oTs[:, T0_ + lo:T0_ + 512], in1=ps[bank][:, lo:512], op=ALU.add),
                            reads=[PS(bank), ("t_sq", half)], writes=[("t_sq", half)])

                    def l_mm(c):
                        ti, par = (0 if c < 8 else 1), c % 2
                        kv, at = kvtm[par], att[par]
                        ob_ = 3 + c // 8
                        col = (c % 8) * 64
                        S.op("pe", lambda e: e.matmul(ps[ob_][:, col:col + 64], lhsT=kv[0:64, 128:256], rhs=at[0:64, 0:64], start=True, stop=True),
                             reads=[("kvtm", par), ("att", par)], writes=[PS(ob_)])
                        sb_ = 5 + (c // 4) % 2
                        scol = (c % 4) * 128
                        S.op("pe", lambda e: e.matmul(ps[sb_][:, scol:scol + 128], lhsT=kv[0:64, 0:128], rhs=kv[0:64, 128:256], start=True, stop=True),
                             reads=[("kvtm", par)], writes=[PS(sb_)])
                        if c % 4 == 3:
                            copy("dve", Sloc[:, c - 3:c + 1, :], ps[sb_][:, 0:512].rearrange("p (c d) -> p c d", d=128),
                                 [PS(sb_)], [("Sl", cc_) for cc_ in range(c - 3, c + 1)])
                        if c % 8 == 7:
                            T0_ = (c // 8) * 512
                            copy("act", oTs[:, T0_:T0_ + 512], ps[ob_][:, 0:512], [PS(ob_)], [("t_sq", ti)])
                        if c % 4 == 3:
                            for c2 in range(max(c - 3, 1), c + 1):
                                ti2 = 0 if c2 < 8 else 1
                                S.op("dve", lambda e, c2=c2: e.scalar_tensor_tensor(out=Sloc[:, c2, :], in0=Sloc[:, c2 - 1, :], scalar=ebl[:, c2:c2 + 1],
                                                                                  in1=Sloc[:, c2, :], op0=ALU.mult, op1=ALU.add),
                                     reads=[("Sl", c2), ("Sl", c2 - 1), ("ebl", ti2)], writes=[("Sl", c2)])
                        if c == 7 or c == 15:
                            half = c // 8
                            lo_c, hi_c = (0, 8) if half == 0 else (8, 15)
                            copy("act", Sball[:, lo_c:hi_c, :], Sloc[:, lo_c:hi_c, :], [("Sl", cc_) for cc_ in range(lo_c, hi_c)],
                                 [("Sball", half)] + TLR)

                    l_tr(0)
                    l_att(0)
                    for c in range(16):
                        if c + 1 < 16:
                            l_tr(c + 1)
                            l_att(c + 1)
                        l_mm(c)
                        if c == 11:
                            q_half(0)
                    q_half(1)
                    S.dma("sp", "hgx", lambda e, j=j: [e.dma_start(out=cc_h_in[j][:, :], in_=Sloc[:, 15, :])], reads=[("Sl", 15)], writes=["cc_h_in"])
                    S.dma("pool", "cc_h", lambda e, j=j: [e.collective_compute(
                        "AllGather", ALU.bypass, replica_groups=PAIRS, ins=[cc_h_in[j].opt()], outs=[cc_h_out[j].opt()])],
                        inc=1, reads=["cc_h_in"], writes=["cc_h_out"])
                    for s_ in range(4):
                        pq = s_ % 2
                        Sfq, Sbq, nmq = SfP[pq], SbP[pq], f"Sf{pq}"
                        S.dma("sp", ("hgs_in", pq), lambda e, s_=s_, j=j, Sfq=Sfq: [e.dma_start(out=Sfq[:, :], in_=shg[s_, j])], writes=[nmq])
                        copy("act", Sbq[:, :], Sfq[:, :], [nmq], [nmq + "b"])
                        chunk_step(1024 + s_ * 8, 8, 16 + s_, Sfq, Sbq, nmq, pq)
                        S.dma("sp", ("hgs_out", pq), lambda e, s_=s_, j=j, Sfq=Sfq: [e.dma_start(out=hgs[s_, j], in_=Sfq[:, :])], reads=[nmq])
                    S.dma("sp", "hgx", lambda e, j=j: [e.dma_start(out=SA[:, :], in_=cc_h_out[j][0:128, :])], reads=["cc_h_out"], writes=["SA"])
                    S.op("dve", lambda e: e.tensor_scalar(out=SA[:, :], in0=SA[:, :], scalar1=flag[:, 0:1], scalar2=None, op0=ALU.mult),
                         reads=["SA", "flag"], writes=["SA"])
                    copy("act", SAb[:, :], SA[:, :], ["SA", "Sf0b"], ["SAb", "Sf0b"])
                    for ti, (T0, N) in enumerate(TT[:2]):
                        bank = pt_bank()
                        S.op("pe", lambda e, T0=T0, N=N, bank=bank: e.matmul(ps[bank][:, 0:N], lhsT=SAb[:, :], rhs=qgT[:, T0:T0 + N],
                                                                           start=True, stop=True),
                             reads=["SAb", ("qgT", ti)], writes=[PS(bank)])
                        S.op("dve", lambda e, T0=T0, N=N, bank=bank: e.tensor_tensor(out=oTs[:, T0:T0 + N], in0=oTs[:, T0:T0 + N],
                                                                                  in1=ps[bank][:, 0:N], op=ALU.add),
                             reads=[PS(bank), ("t_sq", ti)], writes=[("t_sq", ti)])
                    S.op("dve", lambda e: e.scalar_tensor_tensor(out=Sf[:, :], in0=SA[:, :], scalar=ebt[:, 0:1], in1=Sloc[:, 15, :],
                                                                 op0=ALU.mult, op1=ALU.add), reads=["SA", "ebt", ("Sl", 15), "Sf0"], writes=["Sf0"])
                    S.dma("sp", "hgp_out", lambda e, j=j: [e.dma_start(out=hgp[j], in_=Sf[:, :])], reads=["Sf0"])
                    def tail(ti, T0, N, j=j):
                        sl = slice(T0, T0 + N)
                        bank = (0, 1, 7)[ti]
                        R = lambda n: (n, ti)
                        return [
                            lambda: S.op("pool", lambda e: e.tensor_tensor(out=t_sq2[:, sl], in0=oTs[:, sl], in1=oTs[:, sl], op=ALU.mult),
                                         reads=[("t_sq", ti)], writes=[R("t_sq2")]),
                            lambda: S.op("pe", lambda e: e.matmul(ps[bank][:, 0:N], lhsT=onesb[:, :], rhs=t_sq2[:, sl], start=True, stop=True),
                                         reads=[R("t_sq2"), "onesb"], writes=[PS(bank)]),
                            lambda: S.op("act", lambda e: e.activation(out=t_b[:, sl], in_=ps[bank][:, 0:N], func=AF.Ln,
                                                                       scale=1.0 / 128.0, bias=EPS), reads=[PS(bank)], writes=[R("t_b")]),
                            lambda: S.op("act", lambda e: e.activation(out=t_b[:, sl], in_=t_b[:, sl], func=AF.Exp, scale=-0.5),
                                         reads=[R("t_b")], writes=[R("t_b")]),
                            lambda: S.op("dve", lambda e: e.tensor_tensor(out=t_b[:, sl], in0=t_b[:, sl], in1=oTs[:, sl], op=ALU.mult),
                                         reads=[R("t_b"), ("t_sq", ti)], writes=[R("t_b")]),
                            lambda: S.op("dve", lambda e: e.scalar_tensor_tensor(out=goT[:, j, sl], in0=t_b[:, sl], scalar=ng[:, 0:1],
                                                                                 in1=szT[:, sl], op0=ALU.mult, op1=ALU.mult),
                                         reads=[R("t_b"), "ng", ("szT", ti)], writes=goT_res(T0, N)),
                        ]
                    tails = [tail(ti, T0, N) for ti, (T0, N) in enumerate(TT)]
                    for k in range(6):
                        for t_ in tails:
                            t_[k]()
            S.barrier()

        if True:
            for li, kind in enumerate(layer_kinds):
                norm_phase(ln_g[li])
                if kind == "a":
                    attn_phase(li // 3)
                elif kind == "b":
                    conv_phase()
                else:
                    hgrn_phase(li)
                wout_phase()
            norm_phase(final_g if not dbg else final_g, final=True)
        S.emit()
    nc._sched_stats = S.stats
    return nc


def rope_tables(hf):
    half = 8
    inv = 500000.0 ** (-np.arange(half, dtype=np.float64) * 2.0 / 16.0)
    pos = np.zeros((128, NB), np.float64)
    for b in range(8):
        pos[:, b] = hf * 1024 + b * 128 + np.arange(128)
    pos[:, 8] = 16384 + (np.arange(128) % 8)
    ang = pos[:, :, None].astype(np.float32).astype(np.float64) * inv.astype(np.float32).astype(np.float64)[None, None, :]
    ang = ang.astype(np.float32).astype(np.float64)
    return np.cos(ang).astype(np.float32), np.sin(ang).astype(np.float32)


_NC_CACHE = {}


def make_in_maps(inp):
    f = lambda a: np.ascontiguousarray(np.asarray(a, dtype=np.float32))
    shared = dict(
        ln_g=f(inp["ln_g"]), final_g=f(inp["final_g"]), a_w_in=f(inp["a_w_in"]), a_w_out=f(inp["a_w_out"]),
        a_sinks=f(inp["a_sinks"]), b_w_in=f(inp["b_w_in"]), b_conv_w=f(inp["b_conv_w"]), b_w_out=f(inp["b_w_out"]),
        c_w_in=f(inp["c_w_in"]), c_norm_g=f(inp["c_norm_g"]), c_w_out=f(inp["c_w_out"]), c_lb=f(inp["c_lb_logits"]))
    x_prompt, x_sample = f(inp["x_prompt"]), f(inp["x_sample"])
    cache_k, cache_v = f(inp["cache_k"]), f(inp["cache_v"])
    state_conv, state_hgrn = f(inp["state_conv"]), f(inp["state_hgrn"])
    maps = []
    for c in range(8):
        p, hf = c // 2, c % 2
        cs, sn = rope_tables(hf)
        m = dict(shared)
        m.update(
            xp=np.ascontiguousarray(x_prompt[p, hf * 1024:(hf + 1) * 1024]),
            xsm=np.ascontiguousarray(x_sample[4 * c:4 * c + 4].reshape(32, D)),
            ck=np.ascontiguousarray(cache_k[:, 4 * c:4 * c + 4].reshape(2, 4, 128, 256)),
            cv=np.ascontiguousarray(cache_v[:, 4 * c:4 * c + 4].reshape(2, 4, 128, 256)),
            sconv=np.ascontiguousarray(state_conv[0, 4 * c:4 * c + 4].reshape(8, D)),
            shg=np.ascontiguousarray(state_hgrn[0, 4 * c:4 * c + 4]),
            ropec=cs, ropes=sn, flag=np.full((128, 1), float(hf), np.float32))
        maps.append(m)
    return maps


def assemble(res):
    y_prompt = np.zeros((4, 2048, D), np.float32)
    y_sample = np.zeros((32, 8, D), np.float32)
    kpo = np.zeros((2, 4, 128, 4, 64), np.float32)
    vpo = np.zeros_like(kpo)
    kso = np.zeros((2, 32, 128, 4, 64), np.float32)
    vso = np.zeros_like(kso)
    cpo = np.zeros((1, 4, 2, D), np.float32)
    cso = np.zeros((1, 32, 2, D), np.float32)
    hpo = np.zeros((1, 4, 16, 128, 128), np.float32)
    hso = np.zeros((1, 32, 16, 128, 128), np.float32)
    for c in range(8):
        r = res[c]
        p, hf = c // 2, c % 2
        y_prompt[p, hf * 1024:(hf + 1) * 1024] = r["y"][:1024]
        y_sample[4 * c:4 * c + 4] = r["y"][1024:1056].reshape(4, 8, D)
        kso[:, 4 * c:4 * c + 4] = r["ks"].reshape(2, 4, 128, 4, 64)
        vso[:, 4 * c:4 * c + 4] = r["vs"].reshape(2, 4, 128, 4, 64)
        cso[0, 4 * c:4 * c + 4] = r["convs"].reshape(4, 2, D)
        hso[0, 4 * c:4 * c + 4] = r["hgs"]
        if hf == 1:
            kpo[:, p] = r["kp"].reshape(2, 128, 4, 64)
            vpo[:, p] = r["vp"].reshape(2, 128, 4, 64)
            cpo[0, p] = r["convp"]
            hpo[0, p] = r["hgp"]
    return (y_prompt, y_sample, kpo, vpo, kso, vso, cpo, cso, hpo, hso)


def kernel(**inputs):
    if "nc" not in _NC_CACHE:
        _NC_CACHE["nc"] = build()
    nc = _NC_CACHE["nc"]
    maps = make_in_maps(inputs)
    res = run_bass_kernel_spmd(nc, maps, core_ids=list(range(8)))
    return assemble(res.results)
```

```python
import contextlib
import os
import numpy as np
import concourse.bass as bass
import concourse.mybir as mybir
from concourse.bass_utils import run_bass_kernel_spmd

F32 = mybir.dt.float32
BF16 = mybir.dt.bfloat16
AF = mybir.ActivationFunctionType
ALU = mybir.AluOpType

D = 2048
NB = 9
NTOK = 1056
EPS = 1e-6
PAIRS = [[0, 1], [2, 3], [4, 5], [6, 7]]
COMPUTE = ("pe", "act", "dve", "pool")


class Sched:
    def __init__(self, nc):
        self.nc = nc
        self.ops = []
        self.res_w = {}
        self.res_r = {}
        self.dma_last = {}
        self.streams = {e: [] for e in ("pe", "act", "dve", "pool", "sp")}

    def _deps(self, reads, writes, idx, key):
        raw, war = set(), set()
        for r in reads:
            w = self.res_w.get(r)
            if w is not None:
                raw.add(w)
            if isinstance(r, tuple) and r[0] == "ps":
                for k2, rd in self.res_r.get(r, {}).items():
                    if k2 != key:
                        raw.add(rd)
        for r in writes:
            w = self.res_w.get(r)
            if w is not None:
                war.add(w)
            for rd in self.res_r.get(r, {}).values():
                war.add(rd)
        for r in writes:
            self.res_w[r] = idx
            self.res_r[r] = {}
        for r in reads:
            self.res_r.setdefault(r, {})[key if key is not None else ("dma", idx)] = idx
        raw.discard(idx)
        war.discard(idx)
        return raw, war

    def op(self, eng, fn, reads=(), writes=()):
        idx = len(self.ops)
        raw, war = self._deps(reads, writes, idx, eng)
        self.ops.append(dict(eng=eng, fn=fn, raw=raw, war=war, dma=None, idx=idx))
        self.streams[eng].append(idx)
        return idx

    def dma(self, queue, slot, fn, n=1, inc=16, reads=(), writes=()):
        idx = len(self.ops)
        raw, war = self._deps(reads, writes, idx, None)
        last = self.dma_last.get(slot)
        if last is not None:
            raw.add(last)
        self.dma_last[slot] = idx
        self.ops.append(dict(eng=queue, fn=fn, raw=raw, war=war, dma=slot, idx=idx, n=n, inc=inc))
        self.streams[queue].append(idx)
        return idx

    def barrier(self):
        last = set()
        for e, st in self.streams.items():
            if st:
                last.add(st[-1])
        for v in self.dma_last.values():
            last.add(v)
        for e in COMPUTE + ("sp",):
            idx = len(self.ops)
            self.ops.append(dict(eng=e, fn=None, raw=set(last), war=set(), dma=None, idx=idx))
            self.streams[e].append(idx)

    def emit(self):
        nc, ops = self.nc, self.ops

        def same_eng(p, o):
            return p["dma"] is None and o["dma"] is None and p["eng"] == "pe" and o["eng"] == "pe"

        needed = set()
        for o in ops:
            needed |= o["raw"]
            for d in o["war"]:
                if not same_eng(ops[d], o):
                    needed.add(d)
        with contextlib.ExitStack() as es:
            sem_eng = {e: es.enter_context(nc.semaphore(f"s_{e}")) for e in COMPUTE}
            slots = list(self.dma_last.keys())
            sem_dma = {k: es.enter_context(nc.semaphore(f"d_{i}")) for i, k in enumerate(slots)}
            cnt = {e: 0 for e in COMPUTE}
            dcnt = {k: 0 for k in slots}
            ev = {}
            for o in ops:
                if o["dma"] is not None:
                    dcnt[o["dma"]] += o["inc"] * o["n"]
                    ev[o["idx"]] = (sem_dma[o["dma"]], dcnt[o["dma"]])
                elif o["idx"] in needed and o["fn"] is not None:
                    cnt[o["eng"]] += 1
                    ev[o["idx"]] = (sem_eng[o["eng"]], cnt[o["eng"]])
            self.stats = (dict(cnt), {str(k): v for k, v in dcnt.items()}, {e: len(v) for e, v in self.streams.items()})
            blk = es.enter_context(nc.Block())

            def replay(engname):
                def body(eng):
                    waited = {}
                    for idx in self.streams[engname]:
                        o = ops[idx]
                        deps = set(o["raw"])
                        for d in o["war"]:
                            if not same_eng(ops[d], o):
                                deps.add(d)
                        wl = {}
                        for d in deps:
                            if d not in ev:
                                continue
                            s, v = ev[d]
                            key = id(s)
                            if waited.get(key, 0) >= v:
                                continue
                            if key not in wl or wl[key][1] < v:
                                wl[key] = (s, v)
                        for key, (s, v) in wl.items():
                            eng.wait_ge(s, v)
                            waited[key] = v
                        if o["fn"] is None:
                            continue
                        r = o["fn"](eng)
                        if o["dma"] is not None:
                            s, _ = ev[idx]
                            assert len(r) == o["n"], (len(r), o["n"])
                            for ins in r:
                                ins.then_inc(s, o["inc"])
                        elif idx in ev:
                            ins = r[-1] if isinstance(r, (list, tuple)) else r
                            ins.then_inc(ev[idx][0], 1)
                    if engname == "sp":
                        for k, s in sem_dma.items():
                            if dcnt[k]:
                                eng.wait_ge(s, dcnt[k])
                return body

            blk.tensor(replay("pe"))
            blk.scalar(replay("act"))
            blk.vector(replay("dve"))
            blk.gpsimd(replay("pool"))
            blk.sync(replay("sp"))


def build(n_layers=4, dbg=False, stop=99, small=False):
    nc = bass.Bass("TRN2", target_bir_lowering=False)
    din = lambda name, shape, dt=F32: nc.dram_tensor(name, list(shape), dt, kind="ExternalInput").ap()
    dout = lambda name, shape, dt=F32: nc.dram_tensor(name, list(shape), dt, kind="ExternalOutput").ap()
    dint = lambda name, shape, dt=F32: nc.dram_tensor(name, list(shape), dt, kind="Internal").ap()

    xp = din("xp", [1024, D])
    xsm = din("xsm", [32, D])
    ck = din("ck", [2, 4, 128, 256])
    cv = din("cv", [2, 4, 128, 256])
    sconv = din("sconv", [8, D])
    shg = din("shg", [4, 16, 128, 128])
    ln_g = din("ln_g", [4, D])
    final_g = din("final_g", [D])
    na = 2 if (n_layers >= 4 or not small) else 1
    a_w_in = din("a_w_in", [na, D, 4608])
    a_w_out = din("a_w_out", [na, D, D])
    a_sinks = din("a_sinks", [2, 32])
    b_w_in = din("b_w_in", [1, D, 8192] if (n_layers >= 2 or not small) else [1, 1, 8192])
    b_conv_w = din("b_conv_w", [1, 3, D])
    b_w_out = din("b_w_out", [1, D, D] if (n_layers >= 2 or not small) else [1, 1, D])
    c_w_in = din("c_w_in", [1, D, 8192] if (n_layers >= 3 or not small) else [1, 1, 8192])
    c_norm_g = din("c_norm_g", [1, 128])
    c_w_out = din("c_w_out", [1, D, D] if (n_layers >= 3 or not small) else [1, 1, D])
    c_lb = din("c_lb", [4, D])
    ropec = din("ropec", [128, NB, 8])
    ropes = din("ropes", [128, NB, 8])
    flag_d = din("flag", [128, 1])

    y = dout("y", [NTOK, D])
    kp = dout("kp", [2, 128, 256])
    vp = dout("vp", [2, 128, 256])
    ks = dout("ks", [2, 4, 128, 256])
    vs = dout("vs", [2, 4, 128, 256])
    convp = dout("convp", [2, D])
    convs = dout("convs", [8, D])
    hgp = dout("hgp", [16, 128, 128])
    hgs = dout("hgs", [4, 16, 128, 128])

    cc_a_in = [dint(f"cc_a_in{j}", [128, 512], BF16) for j in range(2)]
    cc_a_out = [dint(f"cc_a_out{j}", [256, 512], BF16) for j in range(2)]
    cc_b_in = dint("cc_b_in", [128, 32])
    cc_b_out = dint("cc_b_out", [256, 32])
    cc_h_in = [dint(f"cc_h_in{h}", [128, 128]) for h in range(16)]
    cc_h_out = [dint(f"cc_h_out{h}", [256, 128]) for h in range(16)]

    S = Sched(nc)
    with contextlib.ExitStack() as es:
        uniq = [0]

        def T(name, shape, dt, stack=es):
            uniq[0] += 1
            return stack.enter_context(nc.sbuf_tensor(f"{name}_{uniq[0]}", list(shape), dt))

        xs = T("xs", [128, NB, D], F32)
        hT = T("hT", [128, 16, NTOK], BF16)
        goT = T("goT", [128, 16, NTOK], BF16)
        wt = [T(f"wt{i}", [128, 16, 256], BF16) for i in range(2)]
        rt = [None] * 4
        ident = T("ident", [128, 128], BF16)
        identf = T("identf", [128, 128], F32)
        onesb = T("onesb", [128, 128], BF16)
        onesf = T("onesf", [128, 128], F32)
        mprev = T("mprev", [128, 2, 128], BF16)
        mown = T("mown", [128, 2, 128], BF16)
        ones4 = T("ones4", [128, 2, 128], BF16)
        flag = T("flag_sb", [128, 1], F32)
        ss = T("ss", [128, NB], F32)
        rstd = T("rstd", [128, NB], F32)
        cosT = T("cosT", [128, NB, 8], F32)
        sinT = T("sinT", [128, NB, 8], F32)
        ps = [es.enter_context(nc.psum_tensor(f"ps{i}", [128, 512], F32)) for i in range(8)]
        psb = [p.bitcast(BF16) for p in ps]

        PS = lambda i: ("ps", i)

        S.op("pool", lambda e: e.memset(onesb[:], 1.0), writes=["onesb"])
        S.op("pool", lambda e: e.memset(onesf[:], 1.0), writes=["onesf"])
        S.op("pool", lambda e: e.memset(ones4[:], 1.0), writes=["ones4"])
        S.op("pool", lambda e: e.affine_select(out=ident[:], in_=onesb[:], pattern=[[-1, 128]], compare_op=ALU.is_equal,
                                               fill=0.0, base=0, channel_multiplier=1), reads=["onesb"], writes=["ident"])
        S.op("pool", lambda e: e.affine_select(out=identf[:], in_=onesf[:], pattern=[[-1, 128]], compare_op=ALU.is_equal,
                                               fill=0.0, base=0, channel_multiplier=1), reads=["onesf"], writes=["identf"])
        S.op("pool", lambda e: e.affine_select(out=mprev[:], in_=ones4[:], pattern=[[0, 2], [-1, 128]], compare_op=ALU.is_ge,
                                               fill=0.0, base=-1, channel_multiplier=1), reads=["ones4"], writes=["mprev"])
        S.op("pool", lambda e: e.affine_select(out=mown[:], in_=ones4[:], pattern=[[0, 2], [1, 128]], compare_op=ALU.is_ge,
                                               fill=0.0, base=0, channel_multiplier=-1), reads=["ones4"], writes=["mown"])
        S.op("pool", lambda e: e.memset(xs[:, 8, :], 0.0), writes=[("x", 8)])
        S.dma("sp", "misc", lambda e: [e.dma_start(out=flag[:], in_=flag_d[:, :]),
                                       e.dma_start(out=cosT[:], in_=ropec[:, :, :]),
                                       e.dma_start(out=sinT[:], in_=ropes[:, :, :])], n=3, writes=["flag", "rope"])
        for b in range(8):
            S.dma("sp", ("xin", b % 4), lambda e, b=b: [e.dma_start(out=xs[:, b, :], in_=xp[b * 128:(b + 1) * 128, :])],
                  writes=[("x", b)])
        S.dma("sp", ("xin", 0), lambda e: [e.dma_start(out=xs[0:32, 8, :], in_=xsm[:, :])], writes=[("x", 8)])

        tiles = []

        def wsrc(w2d, c0, width=256):
            return [(w2d[:, c0:c0 + width], 0)]

        def group_src(w2d, j, hh):
            return [(w2d[:, g * 2048 + j * 128: g * 2048 + (j + 1) * 128], (g % 2) * 128) for g in (2 * hh, 2 * hh + 1)]

        layer_kinds = ["a", "b", "c", "a"][:n_layers]
        for li, kind in enumerate(layer_kinds):
            jj = li // 3
            if kind == "a":
                w_in, w_out = a_w_in[jj], a_w_out[jj]
                tiles.append(wsrc(w_in, 2048))
                tiles.append(wsrc(w_in, 2304))
                for GG in range(8):
                    tiles.append(wsrc(w_in, GG * 256))
                    tiles.append(wsrc(w_in, 2560 + GG * 256))
            elif kind == "b":
                w_in, w_out = b_w_in[0], b_w_out[0]
                for j in range(16):
                    tiles.append(group_src(w_in, j, 0))
                    tiles.append(group_src(w_in, j, 1))
            else:
                w_in, w_out = c_w_in[0], c_w_out[0]
                for j in range(16):
                    tiles.append(group_src(w_in, j, 0))
                    tiles.append(group_src(w_in, j, 1))
            for t in range(8):
                tiles.append(wsrc(w_out, t * 256))
        tstate = dict(next_load=0, next_use=0)

        def issue_load():
            i = tstate["next_load"]
            if i >= len(tiles):
                return
            tstate["next_load"] += 1
            slot = i % 2
            srcs = tiles[i]

            def fn(e, srcs=srcs, slot=slot):
                return [e.dma_start(out=wt[slot][:, :, off:off + a.shape[1]],
                                    in_=a.rearrange("(k p) n -> p k n", p=128)) for a, off in srcs]
            S.dma("pool", ("wt", slot), fn, n=len(srcs), writes=[("wt", slot)])

        def next_tile(prefetch=True):
            i = tstate["next_use"]
            tstate["next_use"] += 1
            while tstate["next_load"] <= (min(i + 1, len(tiles) - 1) if prefetch else i):
                issue_load()
            return wt[i % 2], ("wt", i % 2)

        issue_load()

        def blkP(b):
            return 128 if b < 8 else 32

        def tok0(b):
            return b * 128

        rr = dict(pt=0, ev=0)

        def pt_bank():
            rr["pt"] ^= 1
            return rr["pt"]

        def evac_eng():
            rr["ev"] ^= 1
            return "act" if rr["ev"] else "dve"

        def copy(engname, out, in_, reads, writes):
            if engname == "act":
                S.op("act", lambda e: e.copy(out=out, in_=in_), reads=reads, writes=writes)
            else:
                S.op(engname, lambda e: e.tensor_copy(out=out, in_=in_), reads=reads, writes=writes)

        def transpose_to(src_tile, src_res, ncols_blocks, P, dst_fn, dst_res_fn):
            for g0 in range(0, ncols_blocks, 4):
                rr["tr"] = rr.get("tr", 0) ^ 1
                bank = 4 + rr["tr"]
                n = min(4, ncols_blocks - g0)
                for j in range(n):
                    c = g0 + j
                    S.op("pe", lambda e, c=c, j=j, bank=bank: e.transpose(
                        out=psb[bank][:, j * 128:j * 128 + P], in_=src_tile[0:P, c * 128:(c + 1) * 128],
                        identity=ident[0:P, 0:P]), reads=[src_res, "ident"], writes=[PS(bank)])
                ee = evac_eng()
                for j in range(n):
                    c = g0 + j
                    copy(ee, dst_fn(c), psb[bank][:, j * 128:j * 128 + P], [PS(bank)], [dst_res_fn(c)])

        def norm_phase(g_row, final=False):
            with contextlib.ExitStack() as ph:
                gbc = T("gbc", [128, D], F32, ph)
                junk = T("junk", [128, D], BF16, ph)
                if final:
                    yb = [T(f"yb{i}", [128, D], F32, ph) for i in range(2)]
                else:
                    hb = [T(f"hb{i}", [128, D], BF16, ph) for i in range(2)]
                S.dma("sp", "gbc", lambda e: [e.dma_start(out=gbc[:], in_=g_row.partition_broadcast(128))], writes=["gbc"])
                for b in range(NB):
                    P = blkP(b)
                    S.op("act", lambda e, b=b, P=P: e.activation(out=junk[0:P, :], in_=xs[0:P, b, :], func=AF.Square,
                                                                 accum_out=ss[0:P, b:b + 1]),
                         reads=[("x", b)], writes=["junk", ("ss", b)])
                    S.op("act", lambda e, b=b, P=P: e.activation(out=rstd[0:P, b:b + 1], in_=ss[0:P, b:b + 1], func=AF.Sqrt,
                                                                 scale=1.0 / D, bias=EPS), reads=[("ss", b)], writes=[("rstd", b)])
                    S.op("dve", lambda e, b=b, P=P: e.reciprocal(out=rstd[0:P, b:b + 1], in_=rstd[0:P, b:b + 1]),
                         reads=[("rstd", b)], writes=[("rstd", b)])
                    if final:
                        ybt = yb[b % 2]
                        S.op("dve", lambda e, b=b, P=P, ybt=ybt: e.scalar_tensor_tensor(
                            out=ybt[0:P, :], in0=xs[0:P, b, :], scalar=rstd[0:P, b:b + 1], in1=gbc[0:P, :],
                            op0=ALU.mult, op1=ALU.mult), reads=[("x", b), ("rstd", b), "gbc"], writes=[("yb", b % 2)])
                        S.dma("sp", ("yout", b % 2), lambda e, b=b, P=P, ybt=ybt: [
                            e.dma_start(out=y[b * 128:b * 128 + P, :], in_=ybt[0:P, :])], reads=[("yb", b % 2)])
                    else:
                        hbt = hb[b % 2]
                        S.op("dve", lambda e, b=b, P=P, hbt=hbt: e.scalar_tensor_tensor(
                            out=hbt[0:P, :], in0=xs[0:P, b, :], scalar=rstd[0:P, b:b + 1], in1=gbc[0:P, :],
                            op0=ALU.mult, op1=ALU.mult), reads=[("x", b), ("rstd", b), "gbc"], writes=[("hb", b % 2)])
                        transpose_to(hbt, ("hb", b % 2), 16, P,
                                     lambda c, b=b, P=P: hT[:, c, tok0(b):tok0(b) + P], lambda c, b=b: ("hT", b))
                S.barrier()

        def wout_phase():
            for t in range(8):
                w, wres = next_tile()
                for b in range(NB):
                    P = blkP(b)
                    bank = pt_bank()
                    for k in range(16):
                        S.op("pe", lambda e, k=k, b=b, P=P, bank=bank, w=w: e.matmul(
                            ps[bank][0:P, 0:256], lhsT=goT[:, k, tok0(b):tok0(b) + P], rhs=w[:, k, :],
                            start=(k == 0), stop=(k == 15)), reads=[wres, ("goT", b)], writes=[PS(bank)])
                    S.op("dve", lambda e, b=b, P=P, bank=bank, t=t: e.tensor_tensor(
                        out=xs[0:P, b, t * 256:(t + 1) * 256], in0=xs[0:P, b, t * 256:(t + 1) * 256],
                        in1=ps[bank][0:P, 0:256], op=ALU.add), reads=[PS(bank), ("x", b)], writes=[("x", b)])

        def proj_tm(w, wres, b, bank, ncols=256):
            P = blkP(b)
            for k in range(16):
                S.op("pe", lambda e, k=k, b=b, P=P, bank=bank, w=w: e.matmul(
                    ps[bank][0:P, 0:ncols], lhsT=hT[:, k, tok0(b):tok0(b) + P], rhs=w[:, k, 0:ncols],
                    start=(k == 0), stop=(k == 15)), reads=[wres, ("hT", b)], writes=[PS(bank)])

        def rope_tm(src3, dst1, dst2, P, b, nh, rd, wr):
            x1, x2 = src3[:, :, 0:8], src3[:, :, 8:16]
            cosb = cosT[0:P, b, :].unsqueeze(1).to_broadcast([P, nh, 8])
            sinb = sinT[0:P, b, :].unsqueeze(1).to_broadcast([P, nh, 8])
            r = [t_[0:P, 0:nh, :] for t_ in rt]
            for i, (xa, tb_) in enumerate(((x1, cosb), (x2, sinb), (x2, cosb), (x1, sinb))):
                S.op("dve", lambda e, i=i, xa=xa, tb_=tb_, r=r: e.tensor_tensor(out=r[i], in0=xa, in1=tb_, op=ALU.mult),
                     reads=rd + ["rope"], writes=[f"rt{i}"])
            S.op("dve", lambda e, r=r: e.tensor_tensor(out=dst1, in0=r[0], in1=r[1], op=ALU.subtract),
                 reads=["rt0", "rt1"], writes=wr)
            S.op("dve", lambda e, r=r: e.tensor_tensor(out=dst2, in0=r[2], in1=r[3], op=ALU.add),
                 reads=["rt2", "rt3"], writes=wr)

        def attn_phase(jl):
            with contextlib.ExitStack() as ph:
                kT2 = T("kT2", [128, 4, 1152], BF16, ph)
                vall = T("vall", [128, 9, 4, 66], BF16, ph)
                kT2s = T("kT2s", [128, 4, 4 * 136], BF16, ph)
                vSc = T("vSc", [128, 4, 4, 66], BF16, ph)
                vSn = T("vSn", [8, 4, 4, 66], BF16, ph)
                vS32 = T("vS32", [32, 4, 64], BF16, ph)
                kcb = T("kcb", [128, 4, 2, 64], BF16, ph)
                kc32 = T("kc32", [128, 256], BF16, ph)
                ktm = T("ktm", [128, NB, 256], F32, ph) if False else None
                ktm1 = T("ktm1", [128, 256], F32, ph)
                k7 = T("k7", [128, 256], F32, ph)
                vtm = T("vtm", [128, 256], F32, ph)
                kb = T("kb", [128, 4, 2, 64], BF16, ph)
                kb2 = T("kb2", [128, 4, 2, 64], BF16, ph)
                halo = T("halo", [128, 512], BF16, ph)
                halo_in = T("halo_in", [128, 512], BF16, ph)
                rt = [T(f"rt{i}", [128, 4, 8], F32, ph) for i in range(4)]
                qf = [T(f"qf{i}", [128, 256], F32, ph) for i in range(3)]
                qb = [T(f"qb{i}", [128, 256], BF16, ph) for i in range(3)]
                qT = T("qT", [128, 2, 128], BF16, ph)
                E = [T(f"E{i}", [128, 2, 128], BF16, ph) for i in range(8)]
                den = T("den", [128, 4], F32, ph)
                ob = T("ob", [128, 4, 64], BF16, ph)
                obs = T("obs", [8, 4, 64], BF16, ph)
                oball = T("oball", [128, NB, 256], BF16, ph)
                sz = [T(f"sz{i}", [128, 256], BF16, ph) for i in range(2)]
                gg = [T(f"gg{i}", [128, 256], BF16, ph) for i in range(2)]
                esink = T("esink", [128, 32], F32, ph)
                attn_body(jl, locals())
            S.barrier()

        def attn_body(jl, L):
            kT2, vall, kT2s, vSc, vSn, vS32, kcb, kc32 = (L[k] for k in "kT2 vall kT2s vSc vSn vS32 kcb kc32".split())
            ktm1, k7, vtm, kb, halo, halo_in, qf, qb, qT, E, den, ob, obs, oball, sz, gg, esink, kb2 = (
                L[k] for k in "ktm1 k7 vtm kb halo halo_in qf qb qT E den ob obs oball sz gg esink kb2".split())
            nonlocal_rt = L["rt"]
            rt[:] = nonlocal_rt

            S.dma("sp", "misc", lambda e: [e.dma_start(out=esink[:], in_=a_sinks[jl].partition_broadcast(128))],
                  writes=["esink"])
            S.op("act", lambda e: e.activation(out=esink[:], in_=esink[:], func=AF.Exp), reads=["esink"], writes=["esink"])
            S.op("pool", lambda e: e.memset(ob[:], 0.0), writes=["ob"])
            S.op("pool", lambda e: e.memset(vall[:, :, :, 64:65], 1.0), writes=["vall_ones"])
            S.op("pool", lambda e: e.memset(vSc[:, :, :, 64:65], 1.0), writes=["vSc_ones"])
            S.op("pool", lambda e: e.memset(vSn[:, :, :, 64:65], 1.0), writes=["vSn_ones"])
            S.dma("pool", "cachev", lambda e: [e.dma_start(
                out=vSc[:, s, :, 0:64], in_=cv[jl, s].rearrange("r (h d) -> r h d", d=64)) for s in range(4)], n=4,
                reads=["vSc_ones"], writes=["vSc"])
            S.dma("sp", "cachecp", lambda e: [e.dma_start(out=ks[jl, :, 0:120, :], in_=ck[jl, :, 8:128, :]),
                                              e.dma_start(out=vs[jl, :, 0:120, :], in_=cv[jl, :, 8:128, :])], n=2)

            border = [7, 6, 5, 4, 3, 2, 1, 0, 8]
            if stop <= 1:
                return
            w, wres = next_tile()
            kbX = [kb, kb2]

            def k_stage1(i, b):
                P = blkP(b)
                kbi, kbn = kbX[i % 2], ("kb" if i % 2 == 0 else "kb1")
                bank = pt_bank()
                proj_tm(w, wres, b, bank)
                kt = k7 if b >= 7 else ktm1
                kres = "k7" if b >= 7 else "ktm1"
                copy("act", kt[0:P, :], ps[bank][0:P, 0:256], [PS(bank)], [kres])
                kv3 = kt[0:P, :].rearrange("p (h d) -> p h d", d=64)
                rope_tm(kv3, kv3[:, :, 0:8], kv3[:, :, 8:16], P, b, 4, [kres], [kres])
                for half in range(2):
                    copy("act" if half else "dve", kbi[0:P, :, half, :], kv3, [kres], [(kbn, half)])
                if b == 7:
                    S.dma("sp", "kvout", lambda e: [e.dma_start(out=kp[jl], in_=k7[:, :])], reads=["k7"])
                    copy("dve", halo[:, 0:256], k7[:, :], ["k7"], ["halo"])
                if b == 8:
                    S.dma("sp", "kvout", lambda e: [e.dma_start(out=ks[jl, s, 120:128, :], in_=k7[s * 8:(s + 1) * 8, :])
                                                    for s in range(4)], n=4, reads=["k7"])

            def k_stage2(i, b):
                P = blkP(b)
                kbi, kbn = kbX[i % 2], ("kb" if i % 2 == 0 else "kb1")
                tb = pt_bank()
                for h in range(4):
                    S.op("pe", lambda e, h=h, P=P, tb=tb, kbi=kbi: e.transpose(
                        out=psb[tb][:, h * 128:h * 128 + P], in_=kbi[0:P, h].rearrange("p t d -> p (t d)"),
                        identity=ident[0:P, 0:P]), reads=[(kbn, 0), (kbn, 1), "ident"], writes=[PS(tb)])
                pv3 = psb[tb][:, 0:512].rearrange("p (h t) -> p h t", t=128)
                if b < 8:
                    copy(evac_eng(), kT2[:, :, 128 + b * 128:256 + b * 128], pv3, [PS(tb)], [("kT2", 1 + b)])
                else:
                    for s in range(4):
                        copy(evac_eng(), kT2s[:, :, s * 136 + 128:s * 136 + 136], pv3[:, :, s * 8:(s + 1) * 8],
                             [PS(tb)], [("kT2s", s)])

            k_stage1(0, border[0])
            for i, b in enumerate(border):
                if i + 1 < len(border):
                    k_stage1(i + 1, border[i + 1])
                k_stage2(i, b)
            if stop <= 2:
                return
            w, wres = next_tile()
            for b in border:
                P = blkP(b)
                bank = pt_bank()
                proj_tm(w, wres, b, bank)
                if stop < 2.1:
                    continue
                if b < 8:
                    copy("dve", vall[:, 1 + b, :, 0:64], ps[bank][:, 0:256].rearrange("p (h d) -> p h d", d=64),
                         [PS(bank), "vall_ones"], [("vall", 1 + b)])
                else:
                    copy("dve", vS32[:, :, :], ps[bank][0:32, 0:256].rearrange("p (h d) -> p h d", d=64), [PS(bank)], ["vS32"])
                    if stop >= 2.3:
                        S.dma("sp", "vsn", lambda e: [e.dma_start(out=vSn[:, s, :, 0:64], in_=vS32[s * 8:(s + 1) * 8, :, :])
                                                      for s in range(4)], n=4, reads=["vS32", "vSn_ones"], writes=["vSn"])
                if stop < 2.15:
                    continue
                if b >= 7:
                    copy("act", vtm[0:P, :], ps[bank][0:P, 0:256], [PS(bank), ("vall", 1 + b) if b < 8 else "vS32"], ["vtm"])
                if b == 8:
                    S.dma("sp", "kvout", lambda e: [e.dma_start(out=vs[jl, s, 120:128, :], in_=vtm[s * 8:(s + 1) * 8, :])
                                                    for s in range(4)], n=4, reads=["vtm"])
                if b == 7:
                    S.dma("sp", "kvout", lambda e: [e.dma_start(out=vp[jl], in_=vtm[:, :])], reads=["vtm"])
                    if stop < 2.2:
                        continue
                    copy("dve", halo[:, 256:512], vtm[:, :], ["vtm"], ["halo"])
                    S.dma("sp", "halo", lambda e: [e.dma_start(out=cc_a_in[jl][:, :], in_=halo[:, :])],
                          reads=["halo"], writes=["cc_a_in"])
                    if stop >= 2.6:
                      S.dma("pool", "cc_a", lambda e: [e.collective_compute(
                        "AllGather", ALU.bypass, replica_groups=PAIRS, ins=[cc_a_in[jl].opt()],
                        outs=[cc_a_out[jl].opt()])], inc=1, reads=["cc_a_in"], writes=["cc_a_out"])
                    S.dma("sp", "halo", lambda e: [e.dma_start(out=halo_in[:, :], in_=cc_a_out[jl][0:128, :])],
                          reads=["cc_a_out"], writes=["halo_in"])
            if stop <= 3:
                return
            for s in range(4):
                S.dma("pool", "cachek", lambda e, s=s: [e.dma_start(out=kc32[:], in_=ck[jl, s])], writes=["kc32"])
                for half in range(2):
                    copy("dve" if half else "act", kcb[:, :, half, :], kc32[:].rearrange("r (h d) -> r h d", d=64),
                         ["kc32"], [("kcb", half)])
                bank = pt_bank()
                for h in range(4):
                    S.op("pe", lambda e, h=h, bank=bank: e.transpose(
                        out=psb[bank][:, h * 128:(h + 1) * 128], in_=kcb[:, h].rearrange("r t d -> r (t d)"), identity=ident[:]),
                        reads=[("kcb", 0), ("kcb", 1), "ident"], writes=[PS(bank)])
                copy(evac_eng(), kT2s[:, :, s * 136:s * 136 + 128], psb[bank][:, 0:512].rearrange("p (h t) -> p h t", t=128),
                     [PS(bank)], [("kT2s", s)])
            for half in range(2):
                copy("dve", kb[:, :, half, :], halo_in[:, 0:256].rearrange("p (h d) -> p h d", d=64),
                     ["halo_in"], [("kb", half)])
            copy("act", vall[:, 0, :, 0:64], halo_in[:, 256:512].rearrange("p (h d) -> p h d", d=64),
                 ["halo_in", "vall_ones"], [("vall", 0)])
            tb = pt_bank()
            for h in range(4):
                S.op("pe", lambda e, h=h, tb=tb: e.transpose(
                    out=psb[tb][:, h * 128:(h + 1) * 128], in_=kb[:, h].rearrange("p t d -> p (t d)"),
                    identity=ident[:]), reads=[("kb", 0), ("kb", 1), "ident"], writes=[PS(tb)])
            copy("dve", kT2[:, :, 0:128], psb[tb][:, 0:512].rearrange("p (h t) -> p h t", t=128), [PS(tb)], [("kT2", 0)])

            if stop <= 4:
                return
            for GG in range(8 if stop >= 10 else stop - 4):
                G = GG // 2
                wq, wqres = next_tile()

                def stP(b, wq=wq, wqres=wqres):
                    P = blkP(b)
                    i3 = b % 3
                    bank = pt_bank()
                    proj_tm(wq, wqres, b, bank)
                    copy("act", qf[i3][0:P, :], ps[bank][0:P, 0:256], [PS(bank)], [("qf", i3)])
                    q3 = qf[i3][0:P, :].rearrange("p (h d) -> p h d", d=64)
                    rope_tm(q3, q3[:, :, 0:8], q3[:, :, 8:16], P, b, 4, [("qf", i3)], [("qf", i3)])
                    copy("act", qb[i3][0:P, :], qf[i3][0:P, :], [("qf", i3)], [("qb", i3)])

                def stA(b, part, G=G, GG=GG):
                    P = blkP(b)
                    i3 = b % 3
                    e0 = (b % 2) * 4
                    if part == 1:
                        transpose_to(qb[i3], ("qb", i3), 2, P, lambda c, P=P: qT[:, c, 0:P], lambda c: "qT")
                    seqs = [None] if b < 8 else list(range(4))
                    if b == 8 and part == 2:
                        return
                    for s in seqs:
                        if s is None:
                            NQ, qc = 128, slice(0, 128)
                            kprev = kT2[:, G, b * 128:(b + 1) * 128]
                            kown = kT2[:, G, (b + 1) * 128:(b + 2) * 128]
                            vprev, vown = vall[:, b, G, 0:65], vall[:, b + 1, G, 0:65]
                            KO = 128
                            rdk = [("kT2", b), ("kT2", b + 1)]
                            rdv = [("vall", b), ("vall", b + 1)]
                        else:
                            NQ, qc = 8, slice(s * 8, s * 8 + 8)
                            kprev = kT2s[:, G, s * 136:s * 136 + 128]
                            kown = kT2s[:, G, s * 136 + 128:s * 136 + 136]
                            vprev, vown = vSc[:, s, G, 0:65], vSn[0:8, s, G, 0:65]
                            KO = 8
                            rdk = [("kT2s", s)]
                            rdv = ["vSc", "vSn"]
                        if part == 1:
                            for hl in range(4):
                                c, half = hl // 2, hl % 2
                                r0 = half * 64
                                S.op("pe", lambda e, c=c, r0=r0, half=half, kprev=kprev, qc=qc, NQ=NQ: e.matmul(
                                    ps[2 + half][:, c * 128:c * 128 + NQ], lhsT=kprev[r0:r0 + 64, :], rhs=qT[r0:r0 + 64, c, qc],
                                    start=True, stop=True), reads=rdk + ["qT"], writes=[PS(2 + half)])
                                S.op("pe", lambda e, c=c, r0=r0, half=half, kown=kown, qc=qc, NQ=NQ, KO=KO: e.matmul(
                                    ps[2 + half][0:KO, 256 + c * 128:256 + c * 128 + NQ], lhsT=kown[r0:r0 + 64, :], rhs=qT[r0:r0 + 64, c, qc],
                                    start=True, stop=True), reads=rdk + ["qT"], writes=[PS(2 + half)])
                            for i, (bnk, off) in enumerate(((2, 0), (3, 0), (2, 256), (3, 256))):
                                KP = 128 if i < 2 else KO
                                Ev = E[e0 + i][0:KP, :, 0:NQ]
                                S.op("act", lambda e, Ev=Ev, bnk=bnk, off=off, KP=KP, NQ=NQ: e.activation(
                                    out=Ev, in_=ps[bnk][0:KP, off:off + 256].rearrange("p (h q) -> p h q", q=128)[:, :, 0:NQ],
                                    func=AF.Exp, scale=0.125), reads=[PS(bnk)], writes=[("E", e0 + i)])
                                m = (mprev if i < 2 else mown)[0:KP, :, 0:NQ]
                                if i < 2 and b == 0:
                                    S.op("dve", lambda e, Ev=Ev, m=m: e.scalar_tensor_tensor(
                                        out=Ev, in0=Ev, scalar=flag[:, 0:1], in1=m, op0=ALU.mult, op1=ALU.mult),
                                        reads=[("E", e0 + i), "flag", "mprev", "mown"], writes=[("E", e0 + i)])
                                else:
                                    S.op("dve", lambda e, Ev=Ev, m=m: e.tensor_tensor(out=Ev, in0=Ev, in1=m, op=ALU.mult),
                                         reads=[("E", e0 + i), "mprev", "mown"], writes=[("E", e0 + i)])
                            if s is None:
                                continue
                        bnk = 6 + ((b if s is None else s) % 2)
                        for hl in range(4):
                            c, half = hl // 2, hl % 2
                            S.op("pe", lambda e, c=c, half=half, hl=hl, bnk=bnk, vprev=vprev, NQ=NQ: e.matmul(
                                ps[bnk][0:NQ, hl * 65:(hl + 1) * 65], lhsT=E[e0 + half][:, c, 0:NQ], rhs=vprev,
                                start=True, stop=False), reads=[("E", e0 + half)] + rdv, writes=[PS(bnk)])
                            S.op("pe", lambda e, c=c, half=half, hl=hl, bnk=bnk, vown=vown, NQ=NQ, KO=KO: e.matmul(
                                ps[bnk][0:NQ, hl * 65:(hl + 1) * 65], lhsT=E[e0 + 2 + half][0:KO, c, 0:NQ], rhs=vown,
                                start=False, stop=True), reads=[("E", e0 + 2 + half)] + rdv, writes=[PS(bnk)])
                        pv = ps[bnk][0:NQ, 0:260].rearrange("p (h e) -> p h e", e=65)
                        es_ = esink[0:NQ, GG * 4:(GG + 1) * 4]
                        dn = den[0:NQ, 0:4]
                        S.op("dve", lambda e, pv=pv, es_=es_, dn=dn: e.tensor_tensor(
                            out=dn, in0=pv[:, :, 64], in1=es_, op=ALU.add), reads=[PS(bnk), "esink"], writes=["den"])
                        S.op("dve", lambda e, dn=dn: e.reciprocal(out=dn, in_=dn), reads=["den"], writes=["den"])
                        if s is None:
                            obv = oball[0:NQ, b, :].rearrange("p (h d) -> p h d", d=64)
                            wr = [("oball", b)]
                        else:
                            obv = obs[0:NQ]
                            wr = ["obs"]
                        S.op("dve", lambda e, pv=pv, dn=dn, obv=obv, NQ=NQ: e.tensor_tensor(
                            out=obv, in0=pv[:, :, 0:64], in1=dn.unsqueeze(2).to_broadcast([NQ, 4, 64]), op=ALU.mult),
                            reads=[PS(bnk), "den"], writes=wr)
                        if s is not None:
                            S.dma("sp", "obs", lambda e, s=s, b=b: [e.dma_start(
                                out=oball[s * 8:(s + 1) * 8, b, :], in_=obs[:, :, :].rearrange("p h d -> p (h d)"))],
                                reads=["obs"], writes=[("oball", b)])

                stP(0)
                stP(1)
                for b in range(8):
                    stA(b, 1)
                    if b + 2 <= 8:
                        stP(b + 2)
                    stA(b, 2)
                stA(8, 1)

                wz, wzres = next_tile()

                def stZP(b, wz=wz, wzres=wzres):
                    P = blkP(b)
                    i2 = b % 2
                    bank = pt_bank()
                    proj_tm(wz, wzres, b, bank)
                    S.op("act", lambda e, P=P, bank=bank, i2=i2: e.activation(out=sz[i2][0:P, :], in_=ps[bank][0:P, 0:256], func=AF.Silu),
                         reads=[PS(bank)], writes=[("sz", i2)])
                    S.op("dve", lambda e, P=P, b=b, i2=i2: e.tensor_tensor(out=gg[i2][0:P, :], in0=sz[i2][0:P, :], in1=oball[0:P, b, :], op=ALU.mult),
                         reads=[("sz", i2), ("oball", b)], writes=[("gg", i2)])

                def stZT(b, GG=GG):
                    P = blkP(b)
                    i2 = b % 2
                    transpose_to(gg[i2], ("gg", i2), 2, P, lambda c, b=b, P=P, GG=GG: goT[:, 2 * GG + c, tok0(b):tok0(b) + P],
                                 lambda c, b=b: ("goT", b))

                stZP(0)
                for b in range(NB):
                    if b + 1 < NB:
                        stZP(b + 1)
                    stZT(b)

        TT = [(0, 512), (512, 512), (1024, 32)]

        def hT_res(T0, N):
            return [("hT", b) for b in range(T0 // 128, (T0 + N + 127) // 128)]

        def goT_res(T0, N):
            return [("goT", b) for b in range(T0 // 128, (T0 + N + 127) // 128)]

        def proj_fm(w, wres, col0, T0, N, bank):
            for k in range(16):
                S.op("pe", lambda e, k=k, w=w, col0=col0, T0=T0, N=N, bank=bank: e.matmul(
                    ps[bank][:, 0:N], lhsT=w[:, k, col0:col0 + 128], rhs=hT[:, k, T0:T0 + N],
                    start=(k == 0), stop=(k == 15)), reads=[wres] + hT_res(T0, N), writes=[PS(bank)])

        def tm_to_fm(src, src_res, R, dst, dst_res):
            bank = pt_bank()
            for j in range(16):
                S.op("pe", lambda e, j=j, bank=bank: e.transpose(out=ps[bank][:, j * R:(j + 1) * R], in_=src[0:R, j * 128:(j + 1) * 128],
                                                             identity=identf[0:R, 0:R]), reads=[src_res, "identf"], writes=[PS(bank)])
            copy("dve", dst, ps[bank][:, 0:16 * R].rearrange("p (j r) -> p j r", r=R), [PS(bank)], [dst_res])

        def fm_to_tm_out(src, src_res, R, stage, dram_out, slot):
            for g0 in range(0, 16, 4):
                bank = pt_bank()
                for jj_ in range(4):
                    j = g0 + jj_
                    S.op("pe", lambda e, j=j, jj_=jj_, bank=bank: e.transpose(
                        out=ps[bank][0:R, jj_ * 128:(jj_ + 1) * 128], in_=src[:, j, :], identity=identf[:]),
                        reads=[src_res, "identf"], writes=[PS(bank)])
                copy("act", stage[0:R, g0 * 128:(g0 + 4) * 128], ps[bank][0:R, 0:512], [PS(bank)], [("stage", slot)])
            S.dma("sp", slot, lambda e: [e.dma_start(out=dram_out, in_=stage[0:R, :])], reads=[("stage", slot)])

        def conv_phase():
            with contextlib.ExitStack() as ph:
                bsb = T("bsb", [128, NTOK], F32, ph)
                csb = T("csb", [128, NTOK], F32, ph)
                uT = T("uT", [128, 1026], F32, ph)
                usT = T("usT", [128, 4, 10], F32, ph)
                szb = T("szb", [128, 512], F32, ph)
                acc = T("acc", [128, 512], F32, ph)
                cwt = T("bufA", [8, D], F32, ph)
                cw = T("cw", [128, 16, 3], F32, ph)
                sct0 = T("bufB", [8, D], F32, ph)
                sct = T("sct", [128, 16, 8], F32, ph)
                ulast = T("ulast", [128, 16, 2], F32, ph)
                uls = T("uls", [128, 16, 8], F32, ph)
                bz01 = T("bz01", [128, 16, 2], F32, ph)
                cv01 = T("cv01", [128, 16, 2], F32, ph)
                uh = T("uh", [128, 16, 2], F32, ph)
                tmp2 = [T(f"tmp2{i}", [128, 16], F32, ph) for i in range(3)]
                stage, stage2 = cwt, sct0

                S.dma("sp", "misc", lambda e: [e.dma_start(out=cwt[0:3, :], in_=b_conv_w[0]),
                                               e.dma_start(out=sct0[:, :], in_=sconv[:, :])], n=2, writes=[("stage", "cvo1"), ("stage", "cvo2")])
                tm_to_fm(cwt, ("stage", "cvo1"), 3, cw[:], "cw")
                tm_to_fm(sct0, ("stage", "cvo2"), 8, sct[:], "sct")
                S.op("pool", lambda e: e.memset(uT[:, 0:2], 0.0), writes=["uT"])
                for j in range(16):
                    w0_, w0res = next_tile()
                    for (T0, N) in TT:
                        bank = pt_bank()
                        proj_fm(w0_, w0res, 0, T0, N, bank)
                        copy("act", bsb[:, T0:T0 + N], ps[bank][:, 0:N], [PS(bank)], [("bsb", T0)])
                        bank = pt_bank()
                        proj_fm(w0_, w0res, 128, T0, N, bank)
                        copy("act", csb[:, T0:T0 + N], ps[bank][:, 0:N], [PS(bank)], [("csb", T0)])
                    w1_, w1res = next_tile()
                    copy("dve", usT[:, :, 0:2], sct[:, j, :].rearrange("p (s i) -> p s i", i=2), ["sct"], ["usT"])
                    for (T0, N) in TT:
                        bank = pt_bank()
                        proj_fm(w1_, w1res, 0, T0, N, bank)
                        if T0 < 1024:
                            S.op("dve", lambda e, T0=T0, N=N, bank=bank: e.tensor_tensor(
                                out=uT[:, 2 + T0:2 + T0 + N], in0=csb[:, T0:T0 + N], in1=ps[bank][:, 0:N], op=ALU.mult),
                                reads=[PS(bank), ("csb", T0)], writes=["uT"])
                        else:
                            S.op("dve", lambda e, T0=T0, N=N, bank=bank: e.tensor_tensor(
                                out=usT[:, :, 2:10], in0=csb[:, T0:T0 + N].rearrange("p (s t) -> p s t", t=8),
                                in1=ps[bank][:, 0:N].rearrange("p (s t) -> p s t", t=8), op=ALU.mult),
                                reads=[PS(bank), ("csb", T0)], writes=["usT"])
                        bank = pt_bank()
                        proj_fm(w1_, w1res, 128, T0, N, bank)
                        S.op("act", lambda e, N=N, bank=bank: e.activation(out=szb[:, 0:N], in_=ps[bank][:, 0:N], func=AF.Silu),
                             reads=[PS(bank)], writes=["szb"])
                        S.op("dve", lambda e, T0=T0, N=N: e.tensor_tensor(out=szb[:, 0:N], in0=szb[:, 0:N], in1=bsb[:, T0:T0 + N], op=ALU.mult),
                             reads=["szb", ("bsb", T0)], writes=["szb"])
                        if T0 < 1024:
                            u0, u1, u2 = uT[:, T0:T0 + N], uT[:, T0 + 1:T0 + 1 + N], uT[:, T0 + 2:T0 + 2 + N]
                            a_, ures = acc[:, 0:N], "uT"
                            sz_ = szb[:, 0:N]
                            gout = goT[:, j, T0:T0 + N]
                        else:
                            u0, u1, u2 = usT[:, :, 0:8], usT[:, :, 1:9], usT[:, :, 2:10]
                            a_, ures = acc[:, 0:32].rearrange("p (s t) -> p s t", t=8), "usT"
                            sz_ = szb[:, 0:32].rearrange("p (s t) -> p s t", t=8)
                            gout = goT[:, j, T0:T0 + N].rearrange("p (s t) -> p s t", t=8)
                        S.op("dve", lambda e, j=j, u0=u0, a_=a_: e.tensor_scalar(out=a_, in0=u0, scalar1=cw[:, j, 0:1], scalar2=None, op0=ALU.mult),
                             reads=[ures, "cw"], writes=["acc"])
                        S.op("dve", lambda e, j=j, u1=u1, a_=a_: e.scalar_tensor_tensor(out=a_, in0=u1, scalar=cw[:, j, 1:2], in1=a_, op0=ALU.mult, op1=ALU.add),
                             reads=[ures, "cw", "acc"], writes=["acc"])
                        S.op("dve", lambda e, j=j, u2=u2, a_=a_: e.scalar_tensor_tensor(out=a_, in0=u2, scalar=cw[:, j, 2:3], in1=a_, op0=ALU.mult, op1=ALU.add),
                             reads=[ures, "cw", "acc"], writes=["acc"])
                        if T0 == 0:
                            copy("act", bz01[:, j, :], szb[:, 0:2], ["szb"], ["bz01"])
                            copy("act", cv01[:, j, :], acc[:, 0:2], ["acc"], ["cv01"])
                        S.op("dve", lambda e, a_=a_, sz_=sz_, gout=gout: e.tensor_tensor(out=gout, in0=a_, in1=sz_, op=ALU.mult),
                             reads=["acc", "szb"], writes=goT_res(T0, N))
                    copy("act", ulast[:, j, :], uT[:, 1024:1026], ["uT"], ["ulast"])
                    copy("act", uls[:, j, :].rearrange("p (s i) -> p s i", i=2), usT[:, :, 8:10], ["usT"], ["uls"])
                S.dma("sp", "cvx", lambda e: [e.dma_start(out=cc_b_in[:, :], in_=ulast[:].rearrange("p j i -> p (j i)"))],
                      reads=["ulast"], writes=["cc_b_in"])
                S.dma("pool", "cc_b", lambda e: [e.collective_compute(
                    "AllGather", ALU.bypass, replica_groups=PAIRS, ins=[cc_b_in.opt()], outs=[cc_b_out.opt()])],
                    inc=1, reads=["cc_b_in"], writes=["cc_b_out"])
                S.dma("sp", "cvx", lambda e: [e.dma_start(out=uh[:].rearrange("p j i -> p (j i)"), in_=cc_b_out[0:128, :])],
                      reads=["cc_b_out"], writes=["uh"])
                S.op("dve", lambda e: e.tensor_scalar(out=uh[:], in0=uh[:], scalar1=flag[:, 0:1], scalar2=None, op0=ALU.mult),
                     reads=["uh", "flag"], writes=["uh"])
                t0_, t1_, t2_ = tmp2[0][:], tmp2[1][:], tmp2[2][:]
                S.op("dve", lambda e: e.tensor_tensor(out=t0_, in0=cw[:, :, 0], in1=uh[:, :, 0], op=ALU.mult), reads=["cw", "uh"], writes=["t0"])
                S.op("dve", lambda e: e.tensor_tensor(out=t1_, in0=cw[:, :, 1], in1=uh[:, :, 1], op=ALU.mult), reads=["cw", "uh"], writes=["t1"])
                S.op("dve", lambda e: e.tensor_tensor(out=t2_, in0=cw[:, :, 0], in1=uh[:, :, 1], op=ALU.mult), reads=["cw", "uh"], writes=["t2"])
                S.op("dve", lambda e: e.tensor_tensor(out=t0_, in0=t0_, in1=t1_, op=ALU.add), reads=["t0", "t1"], writes=["t0"])
                S.op("dve", lambda e: e.tensor_tensor(out=cv01[:, :, 0], in0=cv01[:, :, 0], in1=t0_, op=ALU.add), reads=["t0", "cv01"], writes=["cv01"])
                S.op("dve", lambda e: e.tensor_tensor(out=cv01[:, :, 1], in0=cv01[:, :, 1], in1=t2_, op=ALU.add), reads=["t2", "cv01"], writes=["cv01"])
                S.op("dve", lambda e: e.tensor_tensor(out=goT[:, :, 0:2], in0=cv01[:], in1=bz01[:], op=ALU.mult),
                     reads=["cv01", "bz01"], writes=[("goT", 0)])
                fm_to_tm_out(ulast, "ulast", 2, stage, convp[:, :], "cvo1")
                fm_to_tm_out(uls, "uls", 8, stage2, convs[:, :], "cvo2")
            S.barrier()

        def hgrn_phase(li):
            with contextlib.ExitStack() as ph:
                clb0 = T("clb0", [64, 128], F32, ph)
                clbf = T("clbf", [128, 4, 16], F32, ph)
                lbt = T("lbt", [128, 16], F32, ph)
                omlt = T("omlt", [128, 16], F32, ph)
                dent = T("dent", [128, 16], F32, ph)
                ng = T("ng", [128, 1], F32, ph)
                m64 = T("m64", [128, 512], BF16, ph)
                m8 = T("m8", [128, 32], BF16, ph)
                mone = T("mone", [128, 512], BF16, ph)
                t_sq = T("t_sq", [128, NTOK], F32, ph)
                t_fg = T("t_fg", [128, NTOK], F32, ph)
                t_lf = T("t_lf", [128, NTOK], F32, ph)
                t_b = T("t_b", [128, NTOK], F32, ph)
                t_bg = T("t_bg", [128, NTOK], F32, ph)
                t_sq2 = T("t_sq2", [128, NTOK], BF16, ph)
                qtT = T("qtT", [128, NTOK], BF16, ph)
                kiT = T("kiT", [128, NTOK], BF16, ph)
                ksT = T("ksT", [128, NTOK], BF16, ph)
                qgT = T("qgT", [128, 1024], BF16, ph)
                vT = T("vT", [128, NTOK], BF16, ph)
                szT = T("szT", [128, NTOK], BF16, ph)
                oTs = t_sq
                ebl = T("ebl", [128, 20], F32, ph)
                ebt = T("ebt", [128, 1], F32, ph)
                bgl = T("bgl", [128, 1], F32, ph)
                Sloc = T("Sloc", [128, 16, 128], F32, ph)
                Sball = t_lf[:].bitcast(BF16)[:, 0:1920].rearrange("p (c d) -> p c d", d=128)
                SA = T("SA", [128, 128], F32, ph)
                SAb = T("SAb", [128, 128], BF16, ph)
                Sf = T("Sf", [128, 128], F32, ph)
                SfP = [Sf, T("Sf1", [128, 128], F32, ph)]
                SbP = [SAb, T("SAb1", [128, 128], BF16, ph)]
                kvtm = [T(f"kvtm{i}", [64, 256], BF16, ph) for i in range(2)]
                att = [T(f"att{i}", [64, 64], BF16, ph) for i in range(2)]

                S.dma("sp", "misc", lambda e: [e.dma_start(out=clb0[:, :], in_=c_lb.rearrange("r (j p) -> (r j) p", p=128)),
                                               e.dma_start(out=ng[:, :], in_=c_norm_g[0].rearrange("(p o) -> p o", o=1))],
                      n=2, writes=["clb0", "ng"])
                bank = pt_bank()
                S.op("pe", lambda e, bank=bank: e.transpose(out=ps[bank][:, 0:64], in_=clb0[:, :], identity=identf[0:64, 0:64]),
                     reads=["clb0", "identf"], writes=[PS(bank)])
                S.op("act", lambda e, bank=bank: e.activation(out=clbf[:].rearrange("p r j -> p (r j)"), in_=ps[bank][:, 0:64], func=AF.Exp),
                     reads=[PS(bank)], writes=["clbf"])
                assert li == 2
                S.op("dve", lambda e: e.tensor_tensor(out=dent[:], in0=clbf[:, 0, :], in1=clbf[:, 1, :], op=ALU.add), reads=["clbf"], writes=["dent"])
                S.op("dve", lambda e: e.tensor_tensor(out=lbt[:], in0=clbf[:, 2, :], in1=clbf[:, 3, :], op=ALU.add), reads=["clbf"], writes=["lbt"])
                S.op("dve", lambda e: e.tensor_tensor(out=dent[:], in0=dent[:], in1=lbt[:], op=ALU.add), reads=["dent", "lbt"], writes=["dent"])
                S.op("dve", lambda e: e.reciprocal(out=dent[:], in_=dent[:]), reads=["dent"], writes=["dent"])
                S.op("dve", lambda e: e.tensor_tensor(out=lbt[:], in0=clbf[:, 1, :], in1=clbf[:, 2, :], op=ALU.add), reads=["clbf", "lbt"], writes=["lbt"])
                S.op("dve", lambda e: e.tensor_tensor(out=lbt[:], in0=lbt[:], in1=dent[:], op=ALU.mult), reads=["lbt", "dent"], writes=["lbt"])
                S.op("dve", lambda e: e.tensor_scalar(out=omlt[:], in0=lbt[:], scalar1=-1.0, scalar2=1.0, op0=ALU.mult, op1=ALU.add),
                     reads=["lbt"], writes=["omlt"])
                S.op("pool", lambda e: e.memset(mone[:], 1.0), writes=["mone"])
                S.op("pool", lambda e: e.memset(m64[:], 1.0), writes=["m64"])
                S.op("pool", lambda e: e.memset(m64[:].rearrange("p (c t) -> p c t", t=64)[:, :, 0:1], 0.0), reads=["m64"], writes=["m64"])
                S.op("pool", lambda e: e.memset(m8[:], 1.0), writes=["m8"])
                S.op("pool", lambda e: e.memset(m8[:].rearrange("p (c t) -> p c t", t=8)[:, :, 0:1], 0.0), reads=["m8"], writes=["m8"])

                for j in range(16):
                    def hb_bank():
                        rr["hb"] = (rr.get("hb", -1) + 1) % 7
                        return rr["hb"]
                    w0_, w0res = next_tile(prefetch=False)
                    issue_load()
                    for ti, (T0, N) in enumerate(TT):
                        bank = hb_bank()
                        proj_fm(w0_, w0res, 128, T0, N, bank)
                        S.op("act", lambda e, T0=T0, N=N, bank=bank: e.activation(out=t_fg[:, T0:T0 + N], in_=ps[bank][:, 0:N], func=AF.Sigmoid),
                             reads=[PS(bank)], writes=[("t_fg", ti)])
                    for ti, (T0, N) in enumerate(TT):
                        bank = hb_bank()
                        proj_fm(w0_, w0res, 0, T0, N, bank)
                        S.op("act", lambda e, T0=T0, N=N, bank=bank: e.activation(out=t_sq[:, T0:T0 + N], in_=ps[bank][:, 0:N], func=AF.Silu),
                             reads=[PS(bank)], writes=[("t_sq", ti)])
                    w1_, w1res = next_tile(prefetch=False)
                    issue_load()

                    def chain(ti, T0, N, j=j):
                        C = 64 if T0 < 1024 else 8
                        nch = N // C
                        c0 = T0 // 64
                        msk = m64 if T0 < 1024 else m8
                        sl = slice(T0, T0 + N)
                        fg, lf, bb, bg, sq_ = t_fg[:, sl], t_lf[:, sl], t_b[:, sl], t_bg[:, sl], t_sq[:, sl]
                        R = lambda n: (n, ti)
                        st = []
                        st.append(lambda: S.op("dve", lambda e: e.tensor_scalar(out=fg, in0=fg, scalar1=omlt[:, j:j + 1], scalar2=lbt[:, j:j + 1],
                                                                                op0=ALU.mult, op1=ALU.add),
                                               reads=[R("t_fg"), "omlt", "lbt"], writes=[R("t_fg")]))
                        st.append(lambda: S.op("act", lambda e: e.activation(out=lf, in_=fg, func=AF.Ln), reads=[R("t_fg")], writes=[R("t_lf")]))
                        st.append(lambda: S.op("dve", lambda e: e.tensor_scalar(out=fg, in0=fg, scalar1=-1.0, scalar2=1.0, op0=ALU.mult, op1=ALU.add),
                                               reads=[R("t_fg"), R("t_lf")], writes=[R("t_fg")]))
                        st.append(lambda: S.op("dve", lambda e: e.tensor_tensor_scan(out=bb, data0=msk[:, 0:N], data1=lf, initial=0.0,
                                                                                     op0=ALU.mult, op1=ALU.add),
                                               reads=[R("t_lf"), "m64", "m8"], writes=[R("t_b")]))
                        if T0 < 1024:
                            init = 0.0 if ti == 0 else bgl[:, 0:1]

                            def gsc():
                                S.op("dve", lambda e: e.tensor_tensor_scan(out=bg, data0=mone[:, 0:N], data1=lf, initial=init,
                                                                           op0=ALU.mult, op1=ALU.add),
                                     reads=[R("t_lf"), "mone", "bgl"], writes=[R("t_bg")])
                                copy("dve", bgl[:, 0:1], t_bg[:, T0 + N - 1:T0 + N], [R("t_bg")], ["bgl"])
                                if ti == 1:
                                    S.op("act", lambda e: e.activation(out=ebt[:, 0:1], in_=bgl[:, 0:1], func=AF.Exp), reads=["bgl"], writes=["ebt"])
                            st.append(gsc)
                        else:
                            st.append(lambda: None)
                        st.append(lambda: S.op("act", lambda e: e.activation(out=lf, in_=bb, func=AF.Exp), reads=[R("t_b"), R("t_bg")], writes=[R("t_lf")]))
                        st.append(lambda: S.op("dve", lambda e: e.tensor_tensor(out=qtT[:, sl], in0=sq_, in1=lf, op=ALU.mult),
                                               reads=[R("t_sq"), R("t_lf")], writes=[("qtT", ti)]))
                        st.append(lambda: S.op("act", lambda e: e.activation(out=lf, in_=bb, func=AF.Exp, scale=-1.0),
                                               reads=[R("t_b"), ("qtT", ti)], writes=[R("t_lf")]))
                        st.append(lambda: S.op("dve", lambda e: e.tensor_tensor(out=fg, in0=fg, in1=lf, op=ALU.mult),
                                               reads=[R("t_fg"), R("t_lf")], writes=[R("t_fg")]))
                        st.append(lambda: S.op("act", lambda e: e.activation(
                            out=ebl[:, c0:c0 + nch], in_=bb.rearrange("p (c t) -> p c t", t=C)[:, :, C - 1], func=AF.Exp),
                            reads=[R("t_b")], writes=[("ebl", ti)]))
                        st.append(lambda: copy("pool", kiT[:, sl], fg, [R("t_fg")], [("kiT", ti)]))
                        st.append(lambda: S.op("dve", lambda e: e.tensor_tensor(
                            out=ksT[:, sl].rearrange("p (c t) -> p c t", t=C), in0=fg.rearrange("p (c t) -> p c t", t=C),
                            in1=ebl[:, c0:c0 + nch].unsqueeze(2).to_broadcast([128, nch, C]), op=ALU.mult),
                            reads=[R("t_fg"), ("ebl", ti)], writes=[("ksT", ti)]))
                        if T0 < 1024:
                            st.append(lambda: S.op("act", lambda e: e.activation(out=bg, in_=bg, func=AF.Exp), reads=[R("t_bg")], writes=[R("t_bg")]))
                            st.append(lambda: S.op("dve", lambda e: e.tensor_tensor(out=qgT[:, sl], in0=sq_, in1=bg, op=ALU.mult),
                                                   reads=[R("t_sq"), R("t_bg")], writes=[("qgT", ti)]))
                        return st

                    chains = [chain(ti, T0, N) for ti, (T0, N) in enumerate(TT)]
                    for k in range(max(len(c_) for c_ in chains)):
                        for c_ in chains:
                            if k < len(c_):
                                c_[k]()

                    for ti, (T0, N) in enumerate(TT):
                        bank = hb_bank()
                        proj_fm(w1_, w1res, 128, T0, N, bank)
                        S.op("act", lambda e, N=N, T0=T0, bank=bank: e.activation(out=szT[:, T0:T0 + N], in_=ps[bank][:, 0:N], func=AF.Silu),
                             reads=[PS(bank)], writes=[("szT", ti)])
                    for ti, (T0, N) in enumerate(TT):
                        bank = hb_bank()
                        proj_fm(w1_, w1res, 0, T0, N, bank)
                        copy("act", vT[:, T0:T0 + N], ps[bank][:, 0:N], [PS(bank)], [("vT", ti)])

                    def chunk_step(t0, C, ci, Sf32, Sbf, sres, par):
                        ti = 0 if t0 < 512 else (1 if t0 < 1024 else 2)
                        kv = kvtm[par]
                        at = att[par]
                        tb = pt_bank()
                        S.op("pe", lambda e: e.transpose(out=psb[tb][0:C, 0:128], in_=ksT[:, t0:t0 + C], identity=ident[:]),
                             reads=[("ksT", ti), "ident"], writes=[PS(tb)])
                        S.op("pe", lambda e: e.transpose(out=psb[tb][0:C, 128:256], in_=vT[:, t0:t0 + C], identity=ident[:]),
                             reads=[("vT", ti), "ident"], writes=[PS(tb)])
                        copy("act", kv[0:C, :], psb[tb][0:C, 0:256], [PS(tb)], [("kvtm", par)])
                        S.op("pe", lambda e: e.matmul(ps[2][0:C, 0:C], lhsT=kiT[:, t0:t0 + C], rhs=qtT[:, t0:t0 + C], start=True, stop=True),
                             reads=[("kiT", ti), ("qtT", ti)], writes=[PS(2)])
                        S.op("dve", lambda e: e.tensor_tensor(out=at[0:C, 0:C], in0=ps[2][0:C, 0:C], in1=mown[0:C, 0, 0:C], op=ALU.mult),
                             reads=[PS(2), "mown"], writes=[("att", par)])
                        ob_ = 3 + par
                        S.op("pe", lambda e: e.matmul(ps[ob_][:, 0:C], lhsT=Sbf[:, :], rhs=qtT[:, t0:t0 + C], start=True, stop=False),
                             reads=[sres + "b", ("qtT", ti)], writes=[PS(ob_)])
                        S.op("pe", lambda e: e.matmul(ps[ob_][:, 0:C], lhsT=kv[0:C, 128:256], rhs=at[0:C, 0:C], start=False, stop=True),
                             reads=[("kvtm", par), ("att", par)], writes=[PS(ob_)])
                        copy("act", oTs[:, t0:t0 + C], ps[ob_][:, 0:C], [PS(ob_)], [("t_sq", ti)])
                        sb_ = 5 + par
                        S.op("pe", lambda e: e.matmul(ps[sb_][:, 0:128], lhsT=kv[0:C, 0:128], rhs=kv[0:C, 128:256], start=True, stop=True),
                             reads=[("kvtm", par)], writes=[PS(sb_)])
                        S.op("dve", lambda e: e.scalar_tensor_tensor(out=Sf32[:, :], in0=Sf32[:, :], scalar=ebl[:, ci:ci + 1], in1=ps[sb_][:, 0:128],
                                                                     op0=ALU.mult, op1=ALU.add),
                             reads=[PS(sb_), sres, ("ebl", ti)], writes=[sres])
                        copy("act", Sbf[:, :], Sf32[:, :], [sres], [sres + "b"])

                    TLR = [("t_lf", 0), ("t_lf", 1), ("t_lf", 2)]

                    def l_tr(c):
                        t0, ti, par = c * 64, (0 if c < 8 else 1), c % 2
                        kv = kvtm[par]
                        S.op("pe", lambda e: e.transpose(out=psb[7][0:64, 0:128], in_=ksT[:, t0:t0 + 64], identity=ident[:]),
                             reads=[("ksT", ti), "ident"], writes=[PS(7)])
                        S.op("pe", lambda e: e.transpose(out=psb[7][0:64, 128:256], in_=vT[:, t0:t0 + 64], identity=ident[:]),
                             reads=[("vT", ti), "ident"], writes=[PS(7)])
                        copy("act", kv[0:64, :], psb[7][0:64, 0:256], [PS(7)], [("kvtm", par)])

                    def l_att(c):
                        t0, ti, par = c * 64, (0 if c < 8 else 1), c % 2
                        at = att[par]
                        S.op("pe", lambda e: e.matmul(ps[2][0:64, 0:64], lhsT=kiT[:, t0:t0 + 64], rhs=qtT[:, t0:t0 + 64], start=True, stop=True),
                             reads=[("kiT", ti), ("qtT", ti)], writes=[PS(2)])
                        S.op("dve", lambda e: e.tensor_tensor(out=at[0:64, 0:64], in0=ps[2][0:64, 0:64], in1=mown[0:64, 0, 0:64], op=ALU.mult),
                             reads=[PS(2), "mown"], writes=[("att", par)])

                    def q_half(half):
                        bank = 3 + half
                        for c in range(8 * half, 8 * half + 8):
                            if c == 0:
                                continue
                            col = (c % 8) * 64
                            S.op("pe", lambda e, c=c, col=col: e.matmul(
                                ps[bank][:, col:col + 64], lhsT=Sball[:, c - 1, :], rhs=qtT[:, c * 64:(c + 1) * 64], start=True, stop=True),
                                reads=[("Sball", (c - 1) // 8), ("qtT", half)] + TLR, writes=[PS(bank)])
                        lo = 64 if half == 0 else 0
                        T0_ = half * 512
                        S.op("dve", lambda e: e.tensor_tensor(
                            out=oTs[:, T0_ + lo:T0_ + 512], in0=oTs[:, T0_ + lo:T0_ + 512], in1=ps[bank][:, lo:512], op=ALU.add),
                            reads=[PS(bank), ("t_sq", half)], writes=[("t_sq", half)])

                    def l_mm(c):
                        ti, par = (0 if c < 8 else 1), c % 2
                        kv, at = kvtm[par], att[par]
                        ob_ = 3 + c // 8
                        col = (c % 8) * 64
                        S.op("pe", lambda e: e.matmul(ps[ob_][:, col:col + 64], lhsT=kv[0:64, 128:256], rhs=at[0:64, 0:64], start=True, stop=True),
                             reads=[("kvtm", par), ("att", par)], writes=[PS(ob_)])
                        sb_ = 5 + (c // 4) % 2
                        scol = (c % 4) * 128
                        S.op("pe", lambda e: e.matmul(ps[sb_][:, scol:scol + 128], lhsT=kv[0:64, 0:128], rhs=kv[0:64, 128:256], start=True, stop=True),
                             reads=[("kvtm", par)], writes=[PS(sb_)])
                        if c % 4 == 3:
                            copy("dve", Sloc[:, c - 3:c + 1, :], ps[sb_][:, 0:512].rearrange("p (c d) -> p c d", d=128),
                                 [PS(sb_)], [("Sl", cc_) for cc_ in range(c - 3, c + 1)])
                        if c % 8 == 7:
                            T0_ = (c // 8) * 512
                            copy("act", oTs[:, T0_:T0_ + 512], ps[ob_][:, 0:512], [PS(ob_)], [("t_sq", ti)])
                        if c % 4 == 3:
                            for c2 in range(max(c - 3, 1), c + 1):
                                ti2 = 0 if c2 < 8 else 1
                                S.op("dve", lambda e, c2=c2: e.scalar_tensor_tensor(out=Sloc[:, c2, :], in0=Sloc[:, c2 - 1, :], scalar=ebl[:, c2:c2 + 1],
                                                                                  in1=Sloc[:, c2, :], op0=ALU.mult, op1=ALU.add),
                                     reads=[("Sl", c2), ("Sl", c2 - 1), ("ebl", ti2)], writes=[("Sl", c2)])
                        if c == 7 or c == 15:
                            half = c // 8
                            lo_c, hi_c = (0, 8) if half == 0 else (8, 15)
                            copy("act", Sball[:, lo_c:hi_c, :], Sloc[:, lo_c:hi_c, :], [("Sl", cc_) for cc_ in range(lo_c, hi_c)],
                                 [("Sball", half)] + TLR)

                    l_tr(0)
                    l_att(0)
                    for c in range(16):
                        if c + 1 < 16:
                            l_tr(c + 1)
                            l_att(c + 1)
                        l_mm(c)
                        if c == 11:
                            q_half(0)
                    q_half(1)
                    S.dma("sp", "hgx", lambda e, j=j: [e.dma_start(out=cc_h_in[j][:, :], in_=Sloc[:, 15, :])], reads=[("Sl", 15)], writes=["cc_h_in"])
                    S.dma("pool", "cc_h", lambda e, j=j: [e.collective_compute(
                        "AllGather", ALU.bypass, replica_groups=PAIRS, ins=[cc_h_in[j].opt()], outs=[cc_h_out[j].opt()])],
                        inc=1, reads=["cc_h_in"], writes=["cc_h_out"])
                    for s_ in range(4):
                        pq = s_ % 2
                        Sfq, Sbq, nmq = SfP[pq], SbP[pq], f"Sf{pq}"
                        S.dma("sp", ("hgs_in", pq), lambda e, s_=s_, j=j, Sfq=Sfq: [e.dma_start(out=Sfq[:, :], in_=shg[s_, j])], writes=[nmq])
                        copy("act", Sbq[:, :], Sfq[:, :], [nmq], [nmq + "b"])
                        chunk_step(1024 + s_ * 8, 8, 16 + s_, Sfq, Sbq, nmq, pq)
                        S.dma("sp", ("hgs_out", pq), lambda e, s_=s_, j=j, Sfq=Sfq: [e.dma_start(out=hgs[s_, j], in_=Sfq[:, :])], reads=[nmq])
                    S.dma("sp", "hgx", lambda e, j=j: [e.dma_start(out=SA[:, :], in_=cc_h_out[j][0:128, :])], reads=["cc_h_out"], writes=["SA"])
                    S.op("dve", lambda e: e.tensor_scalar(out=SA[:, :], in0=SA[:, :], scalar1=flag[:, 0:1], scalar2=None, op0=ALU.mult),
                         reads=["SA", "flag"], writes=["SA"])
                    copy("act", SAb[:, :], SA[:, :], ["SA", "Sf0b"], ["SAb", "Sf0b"])
                    for ti, (T0, N) in enumerate(TT[:2]):
                        bank = pt_bank()
                        S.op("pe", lambda e, T0=T0, N=N, bank=bank: e.matmul(ps[bank][:, 0:N], lhsT=SAb[:, :], rhs=qgT[:, T0:T0 + N],
                                                                           start=True, stop=True),
                             reads=["SAb", ("qgT", ti)], writes=[PS(bank)])
                        S.op("dve", lambda e, T0=T0, N=N, bank=bank: e.tensor_tensor(out=oTs[:, T0:T0 + N], in0=oTs[:, T0:T0 + N],
                                                                                  in1=ps[bank][:, 0:N], op=ALU.add),
                             reads=[PS(bank), ("t_sq", ti)], writes=[("t_sq", ti)])
                    S.op("dve", lambda e: e.scalar_tensor_tensor(out=Sf[:, :], in0=SA[:, :], scalar=ebt[:, 0:1], in1=Sloc[:, 15, :],
                                                                 op0=ALU.mult, op1=ALU.add), reads=["SA", "ebt", ("Sl", 15), "Sf0"], writes=["Sf0"])
                    S.dma("sp", "hgp_out", lambda e, j=j: [e.dma_start(out=hgp[j], in_=Sf[:, :])], reads=["Sf0"])
                    def tail(ti, T0, N, j=j):
                        sl = slice(T0, T0 + N)
                        bank = (0, 1, 7)[ti]
                        R = lambda n: (n, ti)
                        return [
                            lambda: S.op("pool", lambda e: e.tensor_tensor(out=t_sq2[:, sl], in0=oTs[:, sl], in1=oTs[:, sl], op=ALU.mult),
                                         reads=[("t_sq", ti)], writes=[R("t_sq2")]),
                            lambda: S.op("pe", lambda e: e.matmul(ps[bank][:, 0:N], lhsT=onesb[:, :], rhs=t_sq2[:, sl], start=True, stop=True),
                                         reads=[R("t_sq2"), "onesb"], writes=[PS(bank)]),
                            lambda: S.op("act", lambda e: e.activation(out=t_b[:, sl], in_=ps[bank][:, 0:N], func=AF.Ln,
                                                                       scale=1.0 / 128.0, bias=EPS), reads=[PS(bank)], writes=[R("t_b")]),
                            lambda: S.op("act", lambda e: e.activation(out=t_b[:, sl], in_=t_b[:, sl], func=AF.Exp, scale=-0.5),
                                         reads=[R("t_b")], writes=[R("t_b")]),
                            lambda: S.op("dve", lambda e: e.tensor_tensor(out=t_b[:, sl], in0=t_b[:, sl], in1=oTs[:, sl], op=ALU.mult),
                                         reads=[R("t_b"), ("t_sq", ti)], writes=[R("t_b")]),
                            lambda: S.op("dve", lambda e: e.scalar_tensor_tensor(out=goT[:, j, sl], in0=t_b[:, sl], scalar=ng[:, 0:1],
                                                                                 in1=szT[:, sl], op0=ALU.mult, op1=ALU.mult),
                                         reads=[R("t_b"), "ng", ("szT", ti)], writes=goT_res(T0, N)),
                        ]
                    tails = [tail(ti, T0, N) for ti, (T0, N) in enumerate(TT)]
                    for k in range(6):
                        for t_ in tails:
                            t_[k]()
            S.barrier()

        if True:
            for li, kind in enumerate(layer_kinds):
                norm_phase(ln_g[li])
                if kind == "a":
                    attn_phase(li // 3)
                elif kind == "b":
                    conv_phase()
                else:
                    hgrn_phase(li)
                wout_phase()
            norm_phase(final_g if not dbg else final_g, final=True)
        S.emit()
    nc._sched_stats = S.stats
    return nc


def rope_tables(hf):
    half = 8
    inv = 500000.0 ** (-np.arange(half, dtype=np.float64) * 2.0 / 16.0)
    pos = np.zeros((128, NB), np.float64)
    for b in range(8):
        pos[:, b] = hf * 1024 + b * 128 + np.arange(128)
    pos[:, 8] = 16384 + (np.arange(128) % 8)
    ang = pos[:, :, None].astype(np.float32).astype(np.float64) * inv.astype(np.float32).astype(np.float64)[None, None, :]
    ang = ang.astype(np.float32).astype(np.float64)
    return np.cos(ang).astype(np.float32), np.sin(ang).astype(np.float32)


_NC_CACHE = {}


def make_in_maps(inp):
    f = lambda a: np.ascontiguousarray(np.asarray(a, dtype=np.float32))
    shared = dict(
        ln_g=f(inp["ln_g"]), final_g=f(inp["final_g"]), a_w_in=f(inp["a_w_in"]), a_w_out=f(inp["a_w_out"]),
        a_sinks=f(inp["a_sinks"]), b_w_in=f(inp["b_w_in"]), b_conv_w=f(inp["b_conv_w"]), b_w_out=f(inp["b_w_out"]),
        c_w_in=f(inp["c_w_in"]), c_norm_g=f(inp["c_norm_g"]), c_w_out=f(inp["c_w_out"]), c_lb=f(inp["c_lb_logits"]))
    x_prompt, x_sample = f(inp["x_prompt"]), f(inp["x_sample"])
    cache_k, cache_v = f(inp["cache_k"]), f(inp["cache_v"])
    state_conv, state_hgrn = f(inp["state_conv"]), f(inp["state_hgrn"])
    maps = []
    for c in range(8):
        p, hf = c // 2, c % 2
        cs, sn = rope_tables(hf)
        m = dict(shared)
        m.update(
            xp=np.ascontiguousarray(x_prompt[p, hf * 1024:(hf + 1) * 1024]),
            xsm=np.ascontiguousarray(x_sample[4 * c:4 * c + 4].reshape(32, D)),
            ck=np.ascontiguousarray(cache_k[:, 4 * c:4 * c + 4].reshape(2, 4, 128, 256)),
            cv=np.ascontiguousarray(cache_v[:, 4 * c:4 * c + 4].reshape(2, 4, 128, 256)),
            sconv=np.ascontiguousarray(state_conv[0, 4 * c:4 * c + 4].reshape(8, D)),
            shg=np.ascontiguousarray(state_hgrn[0, 4 * c:4 * c + 4]),
            ropec=cs, ropes=sn, flag=np.full((128, 1), float(hf), np.float32))
        maps.append(m)
    return maps


def assemble(res):
    y_prompt = np.zeros((4, 2048, D), np.float32)
    y_sample = np.zeros((32, 8, D), np.float32)
    kpo = np.zeros((2, 4, 128, 4, 64), np.float32)
    vpo = np.zeros_like(kpo)
    kso = np.zeros((2, 32, 128, 4, 64), np.float32)
    vso = np.zeros_like(kso)
    cpo = np.zeros((1, 4, 2, D), np.float32)
    cso = np.zeros((1, 32, 2, D), np.float32)
    hpo = np.zeros((1, 4, 16, 128, 128), np.float32)
    hso = np.zeros((1, 32, 16, 128, 128), np.float32)
    for c in range(8):
        r = res[c]
        p, hf = c // 2, c % 2
        y_prompt[p, hf * 1024:(hf + 1) * 1024] = r["y"][:1024]
        y_sample[4 * c:4 * c + 4] = r["y"][1024:1056].reshape(4, 8, D)
        kso[:, 4 * c:4 * c + 4] = r["ks"].reshape(2, 4, 128, 4, 64)
        vso[:, 4 * c:4 * c + 4] = r["vs"].reshape(2, 4, 128, 4, 64)
        cso[0, 4 * c:4 * c + 4] = r["convs"].reshape(4, 2, D)
        hso[0, 4 * c:4 * c + 4] = r["hgs"]
        if hf == 1:
            kpo[:, p] = r["kp"].reshape(2, 128, 4, 64)
            vpo[:, p] = r["vp"].reshape(2, 128, 4, 64)
            cpo[0, p] = r["convp"]
            hpo[0, p] = r["hgp"]
    return (y_prompt, y_sample, kpo, vpo, kso, vso, cpo, cso, hpo, hso)


def kernel(**inputs):
    if "nc" not in _NC_CACHE:
        _NC_CACHE["nc"] = build()
    nc = _NC_CACHE["nc"]
    maps = make_in_maps(inputs)
    res = run_bass_kernel_spmd(nc, maps, core_ids=list(range(8)))
    return assemble(res.results)
```

```python
import contextlib
import os
import numpy as np
import concourse.bass as bass
import concourse.mybir as mybir
from concourse.bass_utils import run_bass_kernel_spmd

F32 = mybir.dt.float32
BF16 = mybir.dt.bfloat16
AF = mybir.ActivationFunctionType
ALU = mybir.AluOpType

D = 2048
NB = 9
NTOK = 1056
EPS = 1e-6
PAIRS = [[0, 1], [2, 3], [4, 5], [6, 7]]
COMPUTE = ("pe", "act", "dve", "pool")


class Sched:
    def __init__(self, nc):
        self.nc = nc
        self.ops = []
        self.res_w = {}
        self.res_r = {}
        self.dma_last = {}
        self.streams = {e: [] for e in ("pe", "act", "dve", "pool", "sp")}

    def _deps(self, reads, writes, idx, key):
        raw, war = set(), set()
        for r in reads:
            w = self.res_w.get(r)
            if w is not None:
                raw.add(w)
            if isinstance(r, tuple) and r[0] == "ps":
                for k2, rd in self.res_r.get(r, {}).items():
                    if k2 != key:
                        raw.add(rd)
        for r in writes:
            w = self.res_w.get(r)
            if w is not None:
                war.add(w)
            for rd in self.res_r.get(r, {}).values():
                war.add(rd)
        for r in writes:
            self.res_w[r] = idx
            self.res_r[r] = {}
        for r in reads:
            self.res_r.setdefault(r, {})[key if key is not None else ("dma", idx)] = idx
        raw.discard(idx)
        war.discard(idx)
        return raw, war

    def op(self, eng, fn, reads=(), writes=()):
        idx = len(self.ops)
        raw, war = self._deps(reads, writes, idx, eng)
        self.ops.append(dict(eng=eng, fn=fn, raw=raw, war=war, dma=None, idx=idx))
        self.streams[eng].append(idx)
        return idx

    def dma(self, queue, slot, fn, n=1, inc=16, reads=(), writes=()):
        idx = len(self.ops)
        raw, war = self._deps(reads, writes, idx, None)
        last = self.dma_last.get(slot)
        if last is not None:
            raw.add(last)
        self.dma_last[slot] = idx
        self.ops.append(dict(eng=queue, fn=fn, raw=raw, war=war, dma=slot, idx=idx, n=n, inc=inc))
        self.streams[queue].append(idx)
        return idx

    def barrier(self):
        last = set()
        for e, st in self.streams.items():
            if st:
                last.add(st[-1])
        for v in self.dma_last.values():
            last.add(v)
        for e in COMPUTE + ("sp",):
            idx = len(self.ops)
            self.ops.append(dict(eng=e, fn=None, raw=set(last), war=set(), dma=None, idx=idx))
            self.streams[e].append(idx)

    def emit(self):
        nc, ops = self.nc, self.ops

        def same_eng(p, o):
            return p["dma"] is None and o["dma"] is None and p["eng"] == "pe" and o["eng"] == "pe"

        needed = set()
        for o in ops:
            needed |= o["raw"]
            for d in o["war"]:
                if not same_eng(ops[d], o):
                    needed.add(d)
        with contextlib.ExitStack() as es:
            sem_eng = {e: es.enter_context(nc.semaphore(f"s_{e}")) for e in COMPUTE}
            slots = list(self.dma_last.keys())
            sem_dma = {k: es.enter_context(nc.semaphore(f"d_{i}")) for i, k in enumerate(slots)}
            cnt = {e: 0 for e in COMPUTE}
            dcnt = {k: 0 for k in slots}
            ev = {}
            for o in ops:
                if o["dma"] is not None:
                    dcnt[o["dma"]] += o["inc"] * o["n"]
                    ev[o["idx"]] = (sem_dma[o["dma"]], dcnt[o["dma"]])
                elif o["idx"] in needed and o["fn"] is not None:
                    cnt[o["eng"]] += 1
                    ev[o["idx"]] = (sem_eng[o["eng"]], cnt[o["eng"]])
            self.stats = (dict(cnt), {str(k): v for k, v in dcnt.items()}, {e: len(v) for e, v in self.streams.items()})
            blk = es.enter_context(nc.Block())

            def replay(engname):
                def body(eng):
                    waited = {}
                    for idx in self.streams[engname]:
                        o = ops[idx]
                        deps = set(o["raw"])
                        for d in o["war"]:
                            if not same_eng(ops[d], o):
                                deps.add(d)
                        wl = {}
                        for d in deps:
                            if d not in ev:
                                continue
                            s, v = ev[d]
                            key = id(s)
                            if waited.get(key, 0) >= v:
                                continue
                            if key not in wl or wl[key][1] < v:
                                wl[key] = (s, v)
                        for key, (s, v) in wl.items():
                            eng.wait_ge(s, v)
                            waited[key] = v
                        if o["fn"] is None:
                            continue
                        r = o["fn"](eng)
                        if o["dma"] is not None:
                            s, _ = ev[idx]
                            assert len(r) == o["n"], (len(r), o["n"])
                            for ins in r:
                                ins.then_inc(s, o["inc"])
                        elif idx in ev:
                            ins = r[-1] if isinstance(r, (list, tuple)) else r
                            ins.then_inc(ev[idx][0], 1)
                    if engname == "sp":
                        for k, s in sem_dma.items():
                            if dcnt[k]:
                                eng.wait_ge(s, dcnt[k])
                return body

            blk.tensor(replay("pe"))
            blk.scalar(replay("act"))
            blk.vector(replay("dve"))
            blk.gpsimd(replay("pool"))
            blk.sync(replay("sp"))


def build(n_layers=4, dbg=False, stop=99, small=False):
    nc = bass.Bass("TRN2", target_bir_lowering=False)
    din = lambda name, shape, dt=F32: nc.dram_tensor(name, list(shape), dt, kind="ExternalInput").ap()
    dout = lambda name, shape, dt=F32: nc.dram_tensor(name, list(shape), dt, kind="ExternalOutput").ap()
    dint = lambda name, shape, dt=F32: nc.dram_tensor(name, list(shape), dt, kind="Internal").ap()

    xp = din("xp", [1024, D])
    xsm = din("xsm", [32, D])
    ck = din("ck", [2, 4, 128, 256])
    cv = din("cv", [2, 4, 128, 256])
    sconv = din("sconv", [8, D])
    shg = din("shg", [4, 16, 128, 128])
    ln_g = din("ln_g", [4, D])
    final_g = din("final_g", [D])
    na = 2 if (n_layers >= 4 or not small) else 1
    a_w_in = din("a_w_in", [na, D, 4608])
    a_w_out = din("a_w_out", [na, D, D])
    a_sinks = din("a_sinks", [2, 32])
    b_w_in = din("b_w_in", [1, D, 8192] if (n_layers >= 2 or not small) else [1, 1, 8192])
    b_conv_w = din("b_conv_w", [1, 3, D])
    b_w_out = din("b_w_out", [1, D, D] if (n_layers >= 2 or not small) else [1, 1, D])
    c_w_in = din("c_w_in", [1, D, 8192] if (n_layers >= 3 or not small) else [1, 1, 8192])
    c_norm_g = din("c_norm_g", [1, 128])
    c_w_out = din("c_w_out", [1, D, D] if (n_layers >= 3 or not small) else [1, 1, D])
    c_lb = din("c_lb", [4, D])
    ropec = din("ropec", [128, NB, 8])
    ropes = din("ropes", [128, NB, 8])
    flag_d = din("flag", [128, 1])

    y = dout("y", [NTOK, D])
    kp = dout("kp", [2, 128, 256])
    vp = dout("vp", [2, 128, 256])
    ks = dout("ks", [2, 4, 128, 256])
    vs = dout("vs", [2, 4, 128, 256])
    convp = dout("convp", [2, D])
    convs = dout("convs", [8, D])
    hgp = dout("hgp", [16, 128, 128])
    hgs = dout("hgs", [4, 16, 128, 128])

    cc_a_in = [dint(f"cc_a_in{j}", [128, 512], BF16) for j in range(2)]
    cc_a_out = [dint(f"cc_a_out{j}", [256, 512], BF16) for j in range(2)]
    cc_b_in = dint("cc_b_in", [128, 32])
    cc_b_out = dint("cc_b_out", [256, 32])
    cc_h_in = [dint(f"cc_h_in{h}", [128, 128]) for h in range(16)]
    cc_h_out = [dint(f"cc_h_out{h}", [256, 128]) for h in range(16)]

    S = Sched(nc)
    with contextlib.ExitStack() as es:
        uniq = [0]

        def T(name, shape, dt, stack=es):
            uniq[0] += 1
            return stack.enter_context(nc.sbuf_tensor(f"{name}_{uniq[0]}", list(shape), dt))

        xs = T("xs", [128, NB, D], F32)
        hT = T("hT", [128, 16, NTOK], BF16)
        goT = T("goT", [128, 16, NTOK], BF16)
        wt = [T(f"wt{i}", [128, 16, 256], BF16) for i in range(2)]
        rt = [None] * 4
        ident = T("ident", [128, 128], BF16)
        identf = T("identf", [128, 128], F32)
        onesb = T("onesb", [128, 128], BF16)
        onesf = T("onesf", [128, 128], F32)
        mprev = T("mprev", [128, 2, 128], BF16)
        mown = T("mown", [128, 2, 128], BF16)
        ones4 = T("ones4", [128, 2, 128], BF16)
        flag = T("flag_sb", [128, 1], F32)
        ss = T("ss", [128, NB], F32)
        rstd = T("rstd", [128, NB], F32)
        cosT = T("cosT", [128, NB, 8], F32)
        sinT = T("sinT", [128, NB, 8], F32)
        ps = [es.enter_context(nc.psum_tensor(f"ps{i}", [128, 512], F32)) for i in range(8)]
        psb = [p.bitcast(BF16) for p in ps]

        PS = lambda i: ("ps", i)

        S.op("pool", lambda e: e.memset(onesb[:], 1.0), writes=["onesb"])
        S.op("pool", lambda e: e.memset(onesf[:], 1.0), writes=["onesf"])
        S.op("pool", lambda e: e.memset(ones4[:], 1.0), writes=["ones4"])
        S.op("pool", lambda e: e.affine_select(out=ident[:], in_=onesb[:], pattern=[[-1, 128]], compare_op=ALU.is_equal,
                                               fill=0.0, base=0, channel_multiplier=1), reads=["onesb"], writes=["ident"])
        S.op("pool", lambda e: e.affine_select(out=identf[:], in_=onesf[:], pattern=[[-1, 128]], compare_op=ALU.is_equal,
                                               fill=0.0, base=0, channel_multiplier=1), reads=["onesf"], writes=["identf"])
        S.op("pool", lambda e: e.affine_select(out=mprev[:], in_=ones4[:], pattern=[[0, 2], [-1, 128]], compare_op=ALU.is_ge,
                                               fill=0.0, base=-1, channel_multiplier=1), reads=["ones4"], writes=["mprev"])
        S.op("pool", lambda e: e.affine_select(out=mown[:], in_=ones4[:], pattern=[[0, 2], [1, 128]], compare_op=ALU.is_ge,
                                               fill=0.0, base=0, channel_multiplier=-1), reads=["ones4"], writes=["mown"])
        S.op("pool", lambda e: e.memset(xs[:, 8, :], 0.0), writes=[("x", 8)])
        S.dma("sp", "misc", lambda e: [e.dma_start(out=flag[:], in_=flag_d[:, :]),
                                       e.dma_start(out=cosT[:], in_=ropec[:, :, :]),
                                       e.dma_start(out=sinT[:], in_=ropes[:, :, :])], n=3, writes=["flag", "rope"])
        for b in range(8):
            S.dma("sp", ("xin", b % 4), lambda e, b=b: [e.dma_start(out=xs[:, b, :], in_=xp[b * 128:(b + 1) * 128, :])],
                  writes=[("x", b)])
        S.dma("sp", ("xin", 0), lambda e: [e.dma_start(out=xs[0:32, 8, :], in_=xsm[:, :])], writes=[("x", 8)])

        tiles = []

        def wsrc(w2d, c0, width=256):
            return [(w2d[:, c0:c0 + width], 0)]

        def group_src(w2d, j, hh):
            return [(w2d[:, g * 2048 + j * 128: g * 2048 + (j + 1) * 128], (g % 2) * 128) for g in (2 * hh, 2 * hh + 1)]

        layer_kinds = ["a", "b", "c", "a"][:n_layers]
        for li, kind in enumerate(layer_kinds):
            jj = li // 3
            if kind == "a":
                w_in, w_out = a_w_in[jj], a_w_out[jj]
                tiles.append(wsrc(w_in, 2048))
                tiles.append(wsrc(w_in, 2304))
                for GG in range(8):
                    tiles.append(wsrc(w_in, GG * 256))
                    tiles.append(wsrc(w_in, 2560 + GG * 256))
            elif kind == "b":
                w_in, w_out = b_w_in[0], b_w_out[0]
                for j in range(16):
                    tiles.append(group_src(w_in, j, 0))
                    tiles.append(group_src(w_in, j, 1))
            else:
                w_in, w_out = c_w_in[0], c_w_out[0]
                for j in range(16):
                    tiles.append(group_src(w_in, j, 0))
                    tiles.append(group_src(w_in, j, 1))
            for t in range(8):
                tiles.append(wsrc(w_out, t * 256))
        tstate = dict(next_load=0, next_use=0)

        def issue_load():
            i = tstate["next_load"]
            if i >= len(tiles):
                return
            tstate["next_load"] += 1
            slot = i % 2
            srcs = tiles[i]

            def fn(e, srcs=srcs, slot=slot):
                return [e.dma_start(out=wt[slot][:, :, off:off + a.shape[1]],
                                    in_=a.rearrange("(k p) n -> p k n", p=128)) for a, off in srcs]
            S.dma("pool", ("wt", slot), fn, n=len(srcs), writes=[("wt", slot)])

        def next_tile(prefetch=True):
            i = tstate["next_use"]
            tstate["next_use"] += 1
            while tstate["next_load"] <= (min(i + 1, len(tiles) - 1) if prefetch else i):
                issue_load()
            return wt[i % 2], ("wt", i % 2)

        issue_load()

        def blkP(b):
            return 128 if b < 8 else 32

        def tok0(b):
            return b * 128

        rr = dict(pt=0, ev=0)

        def pt_bank():
            rr["pt"] ^= 1
            return rr["pt"]

        def evac_eng():
            rr["ev"] ^= 1
            return "act" if rr["ev"] else "dve"

        def copy(engname, out, in_, reads, writes):
            if engname == "act":
                S.op("act", lambda e: e.copy(out=out, in_=in_), reads=reads, writes=writes)
            else:
                S.op(engname, lambda e: e.tensor_copy(out=out, in_=in_), reads=reads, writes=writes)

        def transpose_to(src_tile, src_res, ncols_blocks, P, dst_fn, dst_res_fn):
            for g0 in range(0, ncols_blocks, 4):
                rr["tr"] = rr.get("tr", 0) ^ 1
                bank = 4 + rr["tr"]
                n = min(4, ncols_blocks - g0)
                for j in range(n):
                    c = g0 + j
                    S.op("pe", lambda e, c=c, j=j, bank=bank: e.transpose(
                        out=psb[bank][:, j * 128:j * 128 + P], in_=src_tile[0:P, c * 128:(c + 1) * 128],
                        identity=ident[0:P, 0:P]), reads=[src_res, "ident"], writes=[PS(bank)])
                ee = evac_eng()
                for j in range(n):
                    c = g0 + j
                    copy(ee, dst_fn(c), psb[bank][:, j * 128:j * 128 + P], [PS(bank)], [dst_res_fn(c)])

        def norm_phase(g_row, final=False):
            with contextlib.ExitStack() as ph:
                gbc = T("gbc", [128, D], F32, ph)
                junk = T("junk", [128, D], BF16, ph)
                if final:
                    yb = [T(f"yb{i}", [128, D], F32, ph) for i in range(2)]
                else:
                    hb = [T(f"hb{i}", [128, D], BF16, ph) for i in range(2)]
                S.dma("sp", "gbc", lambda e: [e.dma_start(out=gbc[:], in_=g_row.partition_broadcast(128))], writes=["gbc"])
                pend = []
                for b in range(NB):
                    P = blkP(b)
                    S.op("act", lambda e, b=b, P=P: e.activation(out=junk[0:P, :], in_=xs[0:P, b, :], func=AF.Square,
                                                                 accum_out=ss[0:P, b:b + 1]),
                         reads=[("x", b)], writes=["junk", ("ss", b)])
                    S.op("act", lambda e, b=b, P=P: e.activation(out=rstd[0:P, b:b + 1], in_=ss[0:P, b:b + 1], func=AF.Sqrt,
                                                                 scale=1.0 / D, bias=EPS), reads=[("ss", b)], writes=[("rstd", b)])
                    S.op("dve", lambda e, b=b, P=P: e.reciprocal(out=rstd[0:P, b:b + 1], in_=rstd[0:P, b:b + 1]),
                         reads=[("rstd", b)], writes=[("rstd", b)])
                    if final:
                        ybt = yb[b % 2]
                        S.op("dve", lambda e, b=b, P=P, ybt=ybt: e.scalar_tensor_tensor(
                            out=ybt[0:P, :], in0=xs[0:P, b, :], scalar=rstd[0:P, b:b + 1], in1=gbc[0:P, :],
                            op0=ALU.mult, op1=ALU.mult), reads=[("x", b), ("rstd", b), "gbc"], writes=[("yb", b % 2)])
                        S.dma("sp", ("yout", b % 2), lambda e, b=b, P=P, ybt=ybt: [
                            e.dma_start(out=y[b * 128:b * 128 + P, :], in_=ybt[0:P, :])], reads=[("yb", b % 2)])
                    else:
                        hbt = hb[b % 2]
                        S.op("dve", lambda e, b=b, P=P, hbt=hbt: e.scalar_tensor_tensor(
                            out=hbt[0:P, :], in0=xs[0:P, b, :], scalar=rstd[0:P, b:b + 1], in1=gbc[0:P, :],
                            op0=ALU.mult, op1=ALU.mult), reads=[("x", b), ("rstd", b), "gbc"], writes=[("hb", b % 2)])
                        pend.append((hbt, b, P))
                        if len(pend) == 2:
                            hb_, b_, P_ = pend.pop(0)
                            transpose_to(hb_, ("hb", b_ % 2), 16, P_,
                                         lambda c, b_=b_, P_=P_: hT[:, c, tok0(b_):tok0(b_) + P_], lambda c, b_=b_: ("hT", b_))
                for hb_, b_, P_ in pend:
                    transpose_to(hb_, ("hb", b_ % 2), 16, P_,
                                 lambda c, b_=b_, P_=P_: hT[:, c, tok0(b_):tok0(b_) + P_], lambda c, b_=b_: ("hT", b_))
                S.barrier()

        def wout_phase():
            for t in range(8):
                w, wres = next_tile()
                for b in range(NB):
                    P = blkP(b)
                    bank = pt_bank()
                    for k in range(16):
                        S.op("pe", lambda e, k=k, b=b, P=P, bank=bank, w=w: e.matmul(
                            ps[bank][0:P, 0:256], lhsT=goT[:, k, tok0(b):tok0(b) + P], rhs=w[:, k, :],
                            start=(k == 0), stop=(k == 15)), reads=[wres, ("goT", b)], writes=[PS(bank)])
                    S.op("dve", lambda e, b=b, P=P, bank=bank, t=t: e.tensor_tensor(
                        out=xs[0:P, b, t * 256:(t + 1) * 256], in0=xs[0:P, b, t * 256:(t + 1) * 256],
                        in1=ps[bank][0:P, 0:256], op=ALU.add), reads=[PS(bank), ("x", b)], writes=[("x", b)])

        def proj_tm(w, wres, b, bank, ncols=256):
            P = blkP(b)
            for k in range(16):
                S.op("pe", lambda e, k=k, b=b, P=P, bank=bank, w=w: e.matmul(
                    ps[bank][0:P, 0:ncols], lhsT=hT[:, k, tok0(b):tok0(b) + P], rhs=w[:, k, 0:ncols],
                    start=(k == 0), stop=(k == 15)), reads=[wres, ("hT", b)], writes=[PS(bank)])

        def rope_tm(src3, dst1, dst2, P, b, nh, rd, wr):
            x1, x2 = src3[:, :, 0:8], src3[:, :, 8:16]
            cosb = cosT[0:P, b, :].unsqueeze(1).to_broadcast([P, nh, 8])
            sinb = sinT[0:P, b, :].unsqueeze(1).to_broadcast([P, nh, 8])
            r = [t_[0:P, 0:nh, :] for t_ in rt]
            for i, (xa, tb_) in enumerate(((x1, cosb), (x2, sinb), (x2, cosb), (x1, sinb))):
                S.op("dve", lambda e, i=i, xa=xa, tb_=tb_, r=r: e.tensor_tensor(out=r[i], in0=xa, in1=tb_, op=ALU.mult),
                     reads=rd + ["rope"], writes=[f"rt{i}"])
            S.op("dve", lambda e, r=r: e.tensor_tensor(out=dst1, in0=r[0], in1=r[1], op=ALU.subtract),
                 reads=["rt0", "rt1"], writes=wr)
            S.op("dve", lambda e, r=r: e.tensor_tensor(out=dst2, in0=r[2], in1=r[3], op=ALU.add),
                 reads=["rt2", "rt3"], writes=wr)

        def attn_phase(jl):
            with contextlib.ExitStack() as ph:
                kT2 = T("kT2", [128, 4, 1152], BF16, ph)
                vall = T("vall", [128, 9, 4, 66], BF16, ph)
                kT2s = T("kT2s", [128, 4, 4 * 136], BF16, ph)
                vSc = T("vSc", [128, 4, 4, 66], BF16, ph)
                vSn = T("vSn", [8, 4, 4, 66], BF16, ph)
                vS32 = T("vS32", [32, 4, 64], BF16, ph)
                kcb = T("kcb", [128, 4, 2, 64], BF16, ph)
                kc32 = T("kc32", [128, 256], BF16, ph)
                ktm = T("ktm", [128, NB, 256], F32, ph) if False else None
                ktm1 = T("ktm1", [128, 256], F32, ph)
                k7 = T("k7", [128, 256], F32, ph)
                vtm = T("vtm", [128, 256], F32, ph)
                kb = T("kb", [128, 4, 2, 64], BF16, ph)
                kb2 = T("kb2", [128, 4, 2, 64], BF16, ph)
                halo = T("halo", [128, 512], BF16, ph)
                halo_in = T("halo_in", [128, 512], BF16, ph)
                rt = [T(f"rt{i}", [128, 4, 8], F32, ph) for i in range(4)]
                qf = [T(f"qf{i}", [128, 256], F32, ph) for i in range(3)]
                qb = [T(f"qb{i}", [128, 256], BF16, ph) for i in range(3)]
                qT = T("qT", [128, 2, 128], BF16, ph)
                E = [T(f"E{i}", [128, 2, 128], BF16, ph) for i in range(8)]
                den = T("den", [128, 4], F32, ph)
                ob = T("ob", [128, 4, 64], BF16, ph)
                obs = T("obs", [8, 4, 64], BF16, ph)
                oball = T("oball", [128, NB, 256], BF16, ph)
                sz = [T(f"sz{i}", [128, 256], BF16, ph) for i in range(2)]
                gg = [T(f"gg{i}", [128, 256], BF16, ph) for i in range(2)]
                esink = T("esink", [128, 32], F32, ph)
                attn_body(jl, locals())
            S.barrier()

        def attn_body(jl, L):
            kT2, vall, kT2s, vSc, vSn, vS32, kcb, kc32 = (L[k] for k in "kT2 vall kT2s vSc vSn vS32 kcb kc32".split())
            ktm1, k7, vtm, kb, halo, halo_in, qf, qb, qT, E, den, ob, obs, oball, sz, gg, esink, kb2 = (
                L[k] for k in "ktm1 k7 vtm kb halo halo_in qf qb qT E den ob obs oball sz gg esink kb2".split())
            nonlocal_rt = L["rt"]
            rt[:] = nonlocal_rt

            S.dma("sp", "misc", lambda e: [e.dma_start(out=esink[:], in_=a_sinks[jl].partition_broadcast(128))],
                  writes=["esink"])
            S.op("act", lambda e: e.activation(out=esink[:], in_=esink[:], func=AF.Exp), reads=["esink"], writes=["esink"])
            S.op("pool", lambda e: e.memset(ob[:], 0.0), writes=["ob"])
            S.op("pool", lambda e: e.memset(vall[:, :, :, 64:65], 1.0), writes=["vall_ones"])
            S.op("pool", lambda e: e.memset(vSc[:, :, :, 64:65], 1.0), writes=["vSc_ones"])
            S.op("pool", lambda e: e.memset(vSn[:, :, :, 64:65], 1.0), writes=["vSn_ones"])
            S.dma("pool", "cachev", lambda e: [e.dma_start(
                out=vSc[:, s, :, 0:64], in_=cv[jl, s].rearrange("r (h d) -> r h d", d=64)) for s in range(4)], n=4,
                reads=["vSc_ones"], writes=["vSc"])
            S.dma("sp", "cachecp", lambda e: [e.dma_start(out=ks[jl, :, 0:120, :], in_=ck[jl, :, 8:128, :]),
                                              e.dma_start(out=vs[jl, :, 0:120, :], in_=cv[jl, :, 8:128, :])], n=2)

            border = [7, 6, 5, 4, 3, 2, 1, 0, 8]
            if stop <= 1:
                return
            w, wres = next_tile()
            kbX = [kb, kb2]

            def k_stage1(i, b):
                P = blkP(b)
                kbi, kbn = kbX[i % 2], ("kb" if i % 2 == 0 else "kb1")
                bank = pt_bank()
                proj_tm(w, wres, b, bank)
                kt = k7 if b >= 7 else ktm1
                kres = "k7" if b >= 7 else "ktm1"
                copy("act", kt[0:P, :], ps[bank][0:P, 0:256], [PS(bank)], [kres])
                kv3 = kt[0:P, :].rearrange("p (h d) -> p h d", d=64)
                rope_tm(kv3, kv3[:, :, 0:8], kv3[:, :, 8:16], P, b, 4, [kres], [kres])
                for half in range(2):
                    copy("act" if half else "dve", kbi[0:P, :, half, :], kv3, [kres], [(kbn, half)])
                if b == 7:
                    S.dma("sp", "kvout", lambda e: [e.dma_start(out=kp[jl], in_=k7[:, :])], reads=["k7"])
                    copy("dve", halo[:, 0:256], k7[:, :], ["k7"], ["halo"])
                if b == 8:
                    S.dma("sp", "kvout", lambda e: [e.dma_start(out=ks[jl, s, 120:128, :], in_=k7[s * 8:(s + 1) * 8, :])
                                                    for s in range(4)], n=4, reads=["k7"])

            def k_stage2(i, b):
                P = blkP(b)
                kbi, kbn = kbX[i % 2], ("kb" if i % 2 == 0 else "kb1")
                tb = pt_bank()
                for h in range(4):
                    S.op("pe", lambda e, h=h, P=P, tb=tb, kbi=kbi: e.transpose(
                        out=psb[tb][:, h * 128:h * 128 + P], in_=kbi[0:P, h].rearrange("p t d -> p (t d)"),
                        identity=ident[0:P, 0:P]), reads=[(kbn, 0), (kbn, 1), "ident"], writes=[PS(tb)])
                pv3 = psb[tb][:, 0:512].rearrange("p (h t) -> p h t", t=128)
                if b < 8:
                    copy(evac_eng(), kT2[:, :, 128 + b * 128:256 + b * 128], pv3, [PS(tb)], [("kT2", 1 + b)])
                else:
                    for s in range(4):
                        copy(evac_eng(), kT2s[:, :, s * 136 + 128:s * 136 + 136], pv3[:, :, s * 8:(s + 1) * 8],
                             [PS(tb)], [("kT2s", s)])

            k_stage1(0, border[0])
            for i, b in enumerate(border):
                if i + 1 < len(border):
                    k_stage1(i + 1, border[i + 1])
                k_stage2(i, b)
            if stop <= 2:
                return
            w, wres = next_tile()
            for b in border:
                P = blkP(b)
                bank = pt_bank()
                proj_tm(w, wres, b, bank)
                if stop < 2.1:
                    continue
                if b < 8:
                    copy("dve", vall[:, 1 + b, :, 0:64], ps[bank][:, 0:256].rearrange("p (h d) -> p h d", d=64),
                         [PS(bank), "vall_ones"], [("vall", 1 + b)])
                else:
                    copy("dve", vS32[:, :, :], ps[bank][0:32, 0:256].rearrange("p (h d) -> p h d", d=64), [PS(bank)], ["vS32"])
                    if stop >= 2.3:
                        S.dma("sp", "vsn", lambda e: [e.dma_start(out=vSn[:, s, :, 0:64], in_=vS32[s * 8:(s + 1) * 8, :, :])
                                                      for s in range(4)], n=4, reads=["vS32", "vSn_ones"], writes=["vSn"])
                if stop < 2.15:
                    continue
                if b >= 7:
                    copy("act", vtm[0:P, :], ps[bank][0:P, 0:256], [PS(bank), ("vall", 1 + b) if b < 8 else "vS32"], ["vtm"])
                if b == 8:
                    S.dma("sp", "kvout", lambda e: [e.dma_start(out=vs[jl, s, 120:128, :], in_=vtm[s * 8:(s + 1) * 8, :])
                                                    for s in range(4)], n=4, reads=["vtm"])
                if b == 7:
                    S.dma("sp", "kvout", lambda e: [e.dma_start(out=vp[jl], in_=vtm[:, :])], reads=["vtm"])
                    if stop < 2.2:
                        continue
                    copy("dve", halo[:, 256:512], vtm[:, :], ["vtm"], ["halo"])
                    S.dma("sp", "halo", lambda e: [e.dma_start(out=cc_a_in[jl][:, :], in_=halo[:, :])],
                          reads=["halo"], writes=["cc_a_in"])
                    if stop >= 2.6:
                      S.dma("pool", "cc_a", lambda e: [e.collective_compute(
                        "AllGather", ALU.bypass, replica_groups=PAIRS, ins=[cc_a_in[jl].opt()],
                        outs=[cc_a_out[jl].opt()])], inc=1, reads=["cc_a_in"], writes=["cc_a_out"])
                    S.dma("sp", "halo", lambda e: [e.dma_start(out=halo_in[:, :], in_=cc_a_out[jl][0:128, :])],
                          reads=["cc_a_out"], writes=["halo_in"])
            if stop <= 3:
                return
            for s in range(4):
                S.dma("pool", "cachek", lambda e, s=s: [e.dma_start(out=kc32[:], in_=ck[jl, s])], writes=["kc32"])
                for half in range(2):
                    copy("dve" if half else "act", kcb[:, :, half, :], kc32[:].rearrange("r (h d) -> r h d", d=64),
                         ["kc32"], [("kcb", half)])
                bank = pt_bank()
                for h in range(4):
                    S.op("pe", lambda e, h=h, bank=bank: e.transpose(
                        out=psb[bank][:, h * 128:(h + 1) * 128], in_=kcb[:, h].rearrange("r t d -> r (t d)"), identity=ident[:]),
                        reads=[("kcb", 0), ("kcb", 1), "ident"], writes=[PS(bank)])
                copy(evac_eng(), kT2s[:, :, s * 136:s * 136 + 128], psb[bank][:, 0:512].rearrange("p (h t) -> p h t", t=128),
                     [PS(bank)], [("kT2s", s)])
            for half in range(2):
                copy("dve", kb[:, :, half, :], halo_in[:, 0:256].rearrange("p (h d) -> p h d", d=64),
                     ["halo_in"], [("kb", half)])
            copy("act", vall[:, 0, :, 0:64], halo_in[:, 256:512].rearrange("p (h d) -> p h d", d=64),
                 ["halo_in", "vall_ones"], [("vall", 0)])
            tb = pt_bank()
            for h in range(4):
                S.op("pe", lambda e, h=h, tb=tb: e.transpose(
                    out=psb[tb][:, h * 128:(h + 1) * 128], in_=kb[:, h].rearrange("p t d -> p (t d)"),
                    identity=ident[:]), reads=[("kb", 0), ("kb", 1), "ident"], writes=[PS(tb)])
            copy("dve", kT2[:, :, 0:128], psb[tb][:, 0:512].rearrange("p (h t) -> p h t", t=128), [PS(tb)], [("kT2", 0)])

            if stop <= 4:
                return
            for GG in range(8 if stop >= 10 else stop - 4):
                G = GG // 2
                wq, wqres = next_tile()

                def stP(b, wq=wq, wqres=wqres):
                    P = blkP(b)
                    i3 = b % 3
                    bank = pt_bank()
                    proj_tm(wq, wqres, b, bank)
                    copy("act", qf[i3][0:P, :], ps[bank][0:P, 0:256], [PS(bank)], [("qf", i3)])
                    q3 = qf[i3][0:P, :].rearrange("p (h d) -> p h d", d=64)
                    rope_tm(q3, q3[:, :, 0:8], q3[:, :, 8:16], P, b, 4, [("qf", i3)], [("qf", i3)])
                    copy("act", qb[i3][0:P, :], qf[i3][0:P, :], [("qf", i3)], [("qb", i3)])

                def stA(b, part, G=G, GG=GG):
                    P = blkP(b)
                    i3 = b % 3
                    e0 = (b % 2) * 4
                    if part == 1:
                        transpose_to(qb[i3], ("qb", i3), 2, P, lambda c, P=P: qT[:, c, 0:P], lambda c: "qT")
                    seqs = [None] if b < 8 else list(range(4))
                    if b == 8 and part == 2:
                        return
                    for s in seqs:
                        if s is None:
                            NQ, qc = 128, slice(0, 128)
                            kprev = kT2[:, G, b * 128:(b + 1) * 128]
                            kown = kT2[:, G, (b + 1) * 128:(b + 2) * 128]
                            vprev, vown = vall[:, b, G, 0:65], vall[:, b + 1, G, 0:65]
                            KO = 128
                            rdk = [("kT2", b), ("kT2", b + 1)]
                            rdv = [("vall", b), ("vall", b + 1)]
                        else:
                            NQ, qc = 8, slice(s * 8, s * 8 + 8)
                            kprev = kT2s[:, G, s * 136:s * 136 + 128]
                            kown = kT2s[:, G, s * 136 + 128:s * 136 + 136]
                            vprev, vown = vSc[:, s, G, 0:65], vSn[0:8, s, G, 0:65]
                            KO = 8
                            rdk = [("kT2s", s)]
                            rdv = ["vSc", "vSn"]
                        if part == 1:
                            for hl in range(4):
                                c, half = hl // 2, hl % 2
                                r0 = half * 64
                                S.op("pe", lambda e, c=c, r0=r0, half=half, kprev=kprev, qc=qc, NQ=NQ: e.matmul(
                                    ps[2 + half][:, c * 128:c * 128 + NQ], lhsT=kprev[r0:r0 + 64, :], rhs=qT[r0:r0 + 64, c, qc],
                                    start=True, stop=True), reads=rdk + ["qT"], writes=[PS(2 + half)])
                                S.op("pe", lambda e, c=c, r0=r0, half=half, kown=kown, qc=qc, NQ=NQ, KO=KO: e.matmul(
                                    ps[2 + half][0:KO, 256 + c * 128:256 + c * 128 + NQ], lhsT=kown[r0:r0 + 64, :], rhs=qT[r0:r0 + 64, c, qc],
                                    start=True, stop=True), reads=rdk + ["qT"], writes=[PS(2 + half)])
                            for i, (bnk, off) in enumerate(((2, 0), (3, 0), (2, 256), (3, 256))):
                                KP = 128 if i < 2 else KO
                                Ev = E[e0 + i][0:KP, :, 0:NQ]
                                S.op("act", lambda e, Ev=Ev, bnk=bnk, off=off, KP=KP, NQ=NQ: e.activation(
                                    out=Ev, in_=ps[bnk][0:KP, off:off + 256].rearrange("p (h q) -> p h q", q=128)[:, :, 0:NQ],
                                    func=AF.Exp, scale=0.125), reads=[PS(bnk)], writes=[("E", e0 + i)])
                                m = (mprev if i < 2 else mown)[0:KP, :, 0:NQ]
                                if i < 2 and b == 0:
                                    S.op("dve", lambda e, Ev=Ev, m=m: e.scalar_tensor_tensor(
                                        out=Ev, in0=Ev, scalar=flag[:, 0:1], in1=m, op0=ALU.mult, op1=ALU.mult),
                                        reads=[("E", e0 + i), "flag", "mprev", "mown"], writes=[("E", e0 + i)])
                                else:
                                    S.op("dve", lambda e, Ev=Ev, m=m: e.tensor_tensor(out=Ev, in0=Ev, in1=m, op=ALU.mult),
                                         reads=[("E", e0 + i), "mprev", "mown"], writes=[("E", e0 + i)])
                            if s is None:
                                continue
                        bnk = 6 + ((b if s is None else s) % 2)
                        for hl in range(4):
                            c, half = hl // 2, hl % 2
                            S.op("pe", lambda e, c=c, half=half, hl=hl, bnk=bnk, vprev=vprev, NQ=NQ: e.matmul(
                                ps[bnk][0:NQ, hl * 65:(hl + 1) * 65], lhsT=E[e0 + half][:, c, 0:NQ], rhs=vprev,
                                start=True, stop=False), reads=[("E", e0 + half)] + rdv, writes=[PS(bnk)])
                            S.op("pe", lambda e, c=c, half=half, hl=hl, bnk=bnk, vown=vown, NQ=NQ, KO=KO: e.matmul(
                                ps[bnk][0:NQ, hl * 65:(hl + 1) * 65], lhsT=E[e0 + 2 + half][0:KO, c, 0:NQ], rhs=vown,
                                start=False, stop=True), reads=[("E", e0 + 2 + half)] + rdv, writes=[PS(bnk)])
                        pv = ps[bnk][0:NQ, 0:260].rearrange("p (h e) -> p h e", e=65)
                        es_ = esink[0:NQ, GG * 4:(GG + 1) * 4]
                        dn = den[0:NQ, 0:4]
                        S.op("dve", lambda e, pv=pv, es_=es_, dn=dn: e.tensor_tensor(
                            out=dn, in0=pv[:, :, 64], in1=es_, op=ALU.add), reads=[PS(bnk), "esink"], writes=["den"])
                        S.op("dve", lambda e, dn=dn: e.reciprocal(out=dn, in_=dn), reads=["den"], writes=["den"])
                        if s is None:
                            obv = oball[0:NQ, b, :].rearrange("p (h d) -> p h d", d=64)
                            wr = [("oball", b)]
                        else:
                            obv = obs[0:NQ]
                            wr = ["obs"]
                        S.op("dve", lambda e, pv=pv, dn=dn, obv=obv, NQ=NQ: e.tensor_tensor(
                            out=obv, in0=pv[:, :, 0:64], in1=dn.unsqueeze(2).to_broadcast([NQ, 4, 64]), op=ALU.mult),
                            reads=[PS(bnk), "den"], writes=wr)
                        if s is not None:
                            S.dma("sp", "obs", lambda e, s=s, b=b: [e.dma_start(
                                out=oball[s * 8:(s + 1) * 8, b, :], in_=obs[:, :, :].rearrange("p h d -> p (h d)"))],
                                reads=["obs"], writes=[("oball", b)])

                stP(0)
                stP(1)
                for b in range(8):
                    stA(b, 1)
                    if b + 2 <= 8:
                        stP(b + 2)
                    stA(b, 2)
                stA(8, 1)

                wz, wzres = next_tile()

                def stZP(b, wz=wz, wzres=wzres):
                    P = blkP(b)
                    i2 = b % 2
                    bank = pt_bank()
                    proj_tm(wz, wzres, b, bank)
                    S.op("act", lambda e, P=P, bank=bank, i2=i2: e.activation(out=sz[i2][0:P, :], in_=ps[bank][0:P, 0:256], func=AF.Silu),
                         reads=[PS(bank)], writes=[("sz", i2)])
                    S.op("dve", lambda e, P=P, b=b, i2=i2: e.tensor_tensor(out=gg[i2][0:P, :], in0=sz[i2][0:P, :], in1=oball[0:P, b, :], op=ALU.mult),
                         reads=[("sz", i2), ("oball", b)], writes=[("gg", i2)])

                def stZT(b, GG=GG):
                    P = blkP(b)
                    i2 = b % 2
                    transpose_to(gg[i2], ("gg", i2), 2, P, lambda c, b=b, P=P, GG=GG: goT[:, 2 * GG + c, tok0(b):tok0(b) + P],
                                 lambda c, b=b: ("goT", b))

                stZP(0)
                for b in range(NB):
                    if b + 1 < NB:
                        stZP(b + 1)
                    stZT(b)

        TT = [(0, 512), (512, 512), (1024, 32)]

        def hT_res(T0, N):
            return [("hT", b) for b in range(T0 // 128, (T0 + N + 127) // 128)]

        def goT_res(T0, N):
            return [("goT", b) for b in range(T0 // 128, (T0 + N + 127) // 128)]

        def proj_fm(w, wres, col0, T0, N, bank):
            for k in range(16):
                S.op("pe", lambda e, k=k, w=w, col0=col0, T0=T0, N=N, bank=bank: e.matmul(
                    ps[bank][:, 0:N], lhsT=w[:, k, col0:col0 + 128], rhs=hT[:, k, T0:T0 + N],
                    start=(k == 0), stop=(k == 15)), reads=[wres] + hT_res(T0, N), writes=[PS(bank)])

        def tm_to_fm(src, src_res, R, dst, dst_res):
            bank = pt_bank()
            for j in range(16):
                S.op("pe", lambda e, j=j, bank=bank: e.transpose(out=ps[bank][:, j * R:(j + 1) * R], in_=src[0:R, j * 128:(j + 1) * 128],
                                                             identity=identf[0:R, 0:R]), reads=[src_res, "identf"], writes=[PS(bank)])
            copy("dve", dst, ps[bank][:, 0:16 * R].rearrange("p (j r) -> p j r", r=R), [PS(bank)], [dst_res])

        def fm_to_tm_out(src, src_res, R, stage, dram_out, slot):
            for g0 in range(0, 16, 4):
                bank = pt_bank()
                for jj_ in range(4):
                    j = g0 + jj_
                    S.op("pe", lambda e, j=j, jj_=jj_, bank=bank: e.transpose(
                        out=ps[bank][0:R, jj_ * 128:(jj_ + 1) * 128], in_=src[:, j, :], identity=identf[:]),
                        reads=[src_res, "identf"], writes=[PS(bank)])
                copy("act", stage[0:R, g0 * 128:(g0 + 4) * 128], ps[bank][0:R, 0:512], [PS(bank)], [("stage", slot)])
            S.dma("sp", slot, lambda e: [e.dma_start(out=dram_out, in_=stage[0:R, :])], reads=[("stage", slot)])

        def conv_phase():
            with contextlib.ExitStack() as ph:
                bsb = T("bsb", [128, NTOK], F32, ph)
                csb = T("csb", [128, NTOK], F32, ph)
                uT = T("uT", [128, 1026], F32, ph)
                usT = T("usT", [128, 4, 10], F32, ph)
                szb = T("szb", [128, 512], F32, ph)
                acc = T("acc", [128, 512], F32, ph)
                cwt = T("bufA", [8, D], F32, ph)
                cw = T("cw", [128, 16, 3], F32, ph)
                sct0 = T("bufB", [8, D], F32, ph)
                sct = T("sct", [128, 16, 8], F32, ph)
                ulast = T("ulast", [128, 16, 2], F32, ph)
                uls = T("uls", [128, 16, 8], F32, ph)
                bz01 = T("bz01", [128, 16, 2], F32, ph)
                cv01 = T("cv01", [128, 16, 2], F32, ph)
                uh = T("uh", [128, 16, 2], F32, ph)
                tmp2 = [T(f"tmp2{i}", [128, 16], F32, ph) for i in range(3)]
                stage, stage2 = cwt, sct0

                S.dma("sp", "misc", lambda e: [e.dma_start(out=cwt[0:3, :], in_=b_conv_w[0]),
                                               e.dma_start(out=sct0[:, :], in_=sconv[:, :])], n=2, writes=[("stage", "cvo1"), ("stage", "cvo2")])
                tm_to_fm(cwt, ("stage", "cvo1"), 3, cw[:], "cw")
                tm_to_fm(sct0, ("stage", "cvo2"), 8, sct[:], "sct")
                S.op("pool", lambda e: e.memset(uT[:, 0:2], 0.0), writes=["uT"])
                for j in range(16):
                    w0_, w0res = next_tile()
                    for (T0, N) in TT:
                        bank = pt_bank()
                        proj_fm(w0_, w0res, 0, T0, N, bank)
                        copy("act", bsb[:, T0:T0 + N], ps[bank][:, 0:N], [PS(bank)], [("bsb", T0)])
                        bank = pt_bank()
                        proj_fm(w0_, w0res, 128, T0, N, bank)
                        copy("act", csb[:, T0:T0 + N], ps[bank][:, 0:N], [PS(bank)], [("csb", T0)])
                    w1_, w1res = next_tile()
                    copy("dve", usT[:, :, 0:2], sct[:, j, :].rearrange("p (s i) -> p s i", i=2), ["sct"], ["usT"])
                    for (T0, N) in TT:
                        bank = pt_bank()
                        proj_fm(w1_, w1res, 0, T0, N, bank)
                        if T0 < 1024:
                            S.op("dve", lambda e, T0=T0, N=N, bank=bank: e.tensor_tensor(
                                out=uT[:, 2 + T0:2 + T0 + N], in0=csb[:, T0:T0 + N], in1=ps[bank][:, 0:N], op=ALU.mult),
                                reads=[PS(bank), ("csb", T0)], writes=["uT"])
                        else:
                            S.op("dve", lambda e, T0=T0, N=N, bank=bank: e.tensor_tensor(
                                out=usT[:, :, 2:10], in0=csb[:, T0:T0 + N].rearrange("p (s t) -> p s t", t=8),
                                in1=ps[bank][:, 0:N].rearrange("p (s t) -> p s t", t=8), op=ALU.mult),
                                reads=[PS(bank), ("csb", T0)], writes=["usT"])
                        bank = pt_bank()
                        proj_fm(w1_, w1res, 128, T0, N, bank)
                        S.op("act", lambda e, N=N, bank=bank: e.activation(out=szb[:, 0:N], in_=ps[bank][:, 0:N], func=AF.Silu),
                             reads=[PS(bank)], writes=["szb"])
                        S.op("dve", lambda e, T0=T0, N=N: e.tensor_tensor(out=szb[:, 0:N], in0=szb[:, 0:N], in1=bsb[:, T0:T0 + N], op=ALU.mult),
                             reads=["szb", ("bsb", T0)], writes=["szb"])
                        if T0 < 1024:
                            u0, u1, u2 = uT[:, T0:T0 + N], uT[:, T0 + 1:T0 + 1 + N], uT[:, T0 + 2:T0 + 2 + N]
                            a_, ures = acc[:, 0:N], "uT"
                            sz_ = szb[:, 0:N]
                            gout = goT[:, j, T0:T0 + N]
                        else:
                            u0, u1, u2 = usT[:, :, 0:8], usT[:, :, 1:9], usT[:, :, 2:10]
                            a_, ures = acc[:, 0:32].rearrange("p (s t) -> p s t", t=8), "usT"
                            sz_ = szb[:, 0:32].rearrange("p (s t) -> p s t", t=8)
                            gout = goT[:, j, T0:T0 + N].rearrange("p (s t) -> p s t", t=8)
                        S.op("dve", lambda e, j=j, u0=u0, a_=a_: e.tensor_scalar(out=a_, in0=u0, scalar1=cw[:, j, 0:1], scalar2=None, op0=ALU.mult),
                             reads=[ures, "cw"], writes=["acc"])
                        S.op("dve", lambda e, j=j, u1=u1, a_=a_: e.scalar_tensor_tensor(out=a_, in0=u1, scalar=cw[:, j, 1:2], in1=a_, op0=ALU.mult, op1=ALU.add),
                             reads=[ures, "cw", "acc"], writes=["acc"])
                        S.op("dve", lambda e, j=j, u2=u2, a_=a_: e.scalar_tensor_tensor(out=a_, in0=u2, scalar=cw[:, j, 2:3], in1=a_, op0=ALU.mult, op1=ALU.add),
                             reads=[ures, "cw", "acc"], writes=["acc"])
                        if T0 == 0:
                            copy("act", bz01[:, j, :], szb[:, 0:2], ["szb"], ["bz01"])
                            copy("act", cv01[:, j, :], acc[:, 0:2], ["acc"], ["cv01"])
                        S.op("dve", lambda e, a_=a_, sz_=sz_, gout=gout: e.tensor_tensor(out=gout, in0=a_, in1=sz_, op=ALU.mult),
                             reads=["acc", "szb"], writes=goT_res(T0, N))
                    copy("act", ulast[:, j, :], uT[:, 1024:1026], ["uT"], ["ulast"])
                    copy("act", uls[:, j, :].rearrange("p (s i) -> p s i", i=2), usT[:, :, 8:10], ["usT"], ["uls"])
                S.dma("sp", "cvx", lambda e: [e.dma_start(out=cc_b_in[:, :], in_=ulast[:].rearrange("p j i -> p (j i)"))],
                      reads=["ulast"], writes=["cc_b_in"])
                S.dma("pool", "cc_b", lambda e: [e.collective_compute(
                    "AllGather", ALU.bypass, replica_groups=PAIRS, ins=[cc_b_in.opt()], outs=[cc_b_out.opt()])],
                    inc=1, reads=["cc_b_in"], writes=["cc_b_out"])
                S.dma("sp", "cvx", lambda e: [e.dma_start(out=uh[:].rearrange("p j i -> p (j i)"), in_=cc_b_out[0:128, :])],
                      reads=["cc_b_out"], writes=["uh"])
                S.op("dve", lambda e: e.tensor_scalar(out=uh[:], in0=uh[:], scalar1=flag[:, 0:1], scalar2=None, op0=ALU.mult),
                     reads=["uh", "flag"], writes=["uh"])
                t0_, t1_, t2_ = tmp2[0][:], tmp2[1][:], tmp2[2][:]
                S.op("dve", lambda e: e.tensor_tensor(out=t0_, in0=cw[:, :, 0], in1=uh[:, :, 0], op=ALU.mult), reads=["cw", "uh"], writes=["t0"])
                S.op("dve", lambda e: e.tensor_tensor(out=t1_, in0=cw[:, :, 1], in1=uh[:, :, 1], op=ALU.mult), reads=["cw", "uh"], writes=["t1"])
                S.op("dve", lambda e: e.tensor_tensor(out=t2_, in0=cw[:, :, 0], in1=uh[:, :, 1], op=ALU.mult), reads=["cw", "uh"], writes=["t2"])
                S.op("dve", lambda e: e.tensor_tensor(out=t0_, in0=t0_, in1=t1_, op=ALU.add), reads=["t0", "t1"], writes=["t0"])
                S.op("dve", lambda e: e.tensor_tensor(out=cv01[:, :, 0], in0=cv01[:, :, 0], in1=t0_, op=ALU.add), reads=["t0", "cv01"], writes=["cv01"])
                S.op("dve", lambda e: e.tensor_tensor(out=cv01[:, :, 1], in0=cv01[:, :, 1], in1=t2_, op=ALU.add), reads=["t2", "cv01"], writes=["cv01"])
                S.op("dve", lambda e: e.tensor_tensor(out=goT[:, :, 0:2], in0=cv01[:], in1=bz01[:], op=ALU.mult),
                     reads=["cv01", "bz01"], writes=[("goT", 0)])
                fm_to_tm_out(ulast, "ulast", 2, stage, convp[:, :], "cvo1")
                fm_to_tm_out(uls, "uls", 8, stage2, convs[:, :], "cvo2")
            S.barrier()

        def hgrn_phase(li):
            with contextlib.ExitStack() as ph:
                clb0 = T("clb0", [64, 128], F32, ph)
                clbf = T("clbf", [128, 4, 16], F32, ph)
                lbt = T("lbt", [128, 16], F32, ph)
                omlt = T("omlt", [128, 16], F32, ph)
                dent = T("dent", [128, 16], F32, ph)
                ng = T("ng", [128, 1], F32, ph)
                m64 = T("m64", [128, 512], BF16, ph)
                m8 = T("m8", [128, 32], BF16, ph)
                mone = T("mone", [128, 512], BF16, ph)
                t_sq = T("t_sq", [128, NTOK], F32, ph)
                t_fg = T("t_fg", [128, NTOK], F32, ph)
                t_lf = T("t_lf", [128, NTOK], F32, ph)
                t_b = T("t_b", [128, NTOK], F32, ph)
                t_bg = T("t_bg", [128, NTOK], F32, ph)
                t_sq2 = T("t_sq2", [128, NTOK], BF16, ph)
                qtT = T("qtT", [128, NTOK], BF16, ph)
                kiT = T("kiT", [128, NTOK], BF16, ph)
                ksT = T("ksT", [128, NTOK], BF16, ph)
                qgT = T("qgT", [128, 1024], BF16, ph)
                vT = T("vT", [128, NTOK], BF16, ph)
                szT = T("szT", [128, NTOK], BF16, ph)
                oTs = t_sq
                ebl = T("ebl", [128, 20], F32, ph)
                ebt = T("ebt", [128, 1], F32, ph)
                bgl = T("bgl", [128, 1], F32, ph)
                Sloc = T("Sloc", [128, 16, 128], F32, ph)
                Sball = t_lf[:].bitcast(BF16)[:, 0:1920].rearrange("p (c d) -> p c d", d=128)
                SA = T("SA", [128, 128], F32, ph)
                SAb = T("SAb", [128, 128], BF16, ph)
                Sf = T("Sf", [128, 128], F32, ph)
                SfP = [Sf, T("Sf1", [128, 128], F32, ph)]
                SbP = [SAb, T("SAb1", [128, 128], BF16, ph)]
                kvtm = [T(f"kvtm{i}", [64, 256], BF16, ph) for i in range(2)]
                att = [T(f"att{i}", [64, 64], BF16, ph) for i in range(2)]

                S.dma("sp", "misc", lambda e: [e.dma_start(out=clb0[:, :], in_=c_lb.rearrange("r (j p) -> (r j) p", p=128)),
                                               e.dma_start(out=ng[:, :], in_=c_norm_g[0].rearrange("(p o) -> p o", o=1))],
                      n=2, writes=["clb0", "ng"])
                bank = pt_bank()
                S.op("pe", lambda e, bank=bank: e.transpose(out=ps[bank][:, 0:64], in_=clb0[:, :], identity=identf[0:64, 0:64]),
                     reads=["clb0", "identf"], writes=[PS(bank)])
                S.op("act", lambda e, bank=bank: e.activation(out=clbf[:].rearrange("p r j -> p (r j)"), in_=ps[bank][:, 0:64], func=AF.Exp),
                     reads=[PS(bank)], writes=["clbf"])
                assert li == 2
                S.op("dve", lambda e: e.tensor_tensor(out=dent[:], in0=clbf[:, 0, :], in1=clbf[:, 1, :], op=ALU.add), reads=["clbf"], writes=["dent"])
                S.op("dve", lambda e: e.tensor_tensor(out=lbt[:], in0=clbf[:, 2, :], in1=clbf[:, 3, :], op=ALU.add), reads=["clbf"], writes=["lbt"])
                S.op("dve", lambda e: e.tensor_tensor(out=dent[:], in0=dent[:], in1=lbt[:], op=ALU.add), reads=["dent", "lbt"], writes=["dent"])
                S.op("dve", lambda e: e.reciprocal(out=dent[:], in_=dent[:]), reads=["dent"], writes=["dent"])
                S.op("dve", lambda e: e.tensor_tensor(out=lbt[:], in0=clbf[:, 1, :], in1=clbf[:, 2, :], op=ALU.add), reads=["clbf", "lbt"], writes=["lbt"])
                S.op("dve", lambda e: e.tensor_tensor(out=lbt[:], in0=lbt[:], in1=dent[:], op=ALU.mult), reads=["lbt", "dent"], writes=["lbt"])
                S.op("dve", lambda e: e.tensor_scalar(out=omlt[:], in0=lbt[:], scalar1=-1.0, scalar2=1.0, op0=ALU.mult, op1=ALU.add),
                     reads=["lbt"], writes=["omlt"])
                S.op("pool", lambda e: e.memset(mone[:], 1.0), writes=["mone"])
                S.op("pool", lambda e: e.memset(m64[:], 1.0), writes=["m64"])
                S.op("pool", lambda e: e.memset(m64[:].rearrange("p (c t) -> p c t", t=64)[:, :, 0:1], 0.0), reads=["m64"], writes=["m64"])
                S.op("pool", lambda e: e.memset(m8[:], 1.0), writes=["m8"])
                S.op("pool", lambda e: e.memset(m8[:].rearrange("p (c t) -> p c t", t=8)[:, :, 0:1], 0.0), reads=["m8"], writes=["m8"])

                for j in range(16):
                    def hb_bank():
                        rr["hb"] = (rr.get("hb", -1) + 1) % 7
                        return rr["hb"]
                    w0_, w0res = next_tile(prefetch=False)
                    issue_load()
                    for ti, (T0, N) in enumerate(TT):
                        bank = hb_bank()
                        proj_fm(w0_, w0res, 128, T0, N, bank)
                        S.op("act", lambda e, T0=T0, N=N, bank=bank: e.activation(out=t_fg[:, T0:T0 + N], in_=ps[bank][:, 0:N], func=AF.Sigmoid),
                             reads=[PS(bank)], writes=[("t_fg", ti)])
                    for ti, (T0, N) in enumerate(TT):
                        bank = hb_bank()
                        proj_fm(w0_, w0res, 0, T0, N, bank)
                        S.op("act", lambda e, T0=T0, N=N, bank=bank: e.activation(out=t_sq[:, T0:T0 + N], in_=ps[bank][:, 0:N], func=AF.Silu),
                             reads=[PS(bank)], writes=[("t_sq", ti)])
                    w1_, w1res = next_tile(prefetch=False)
                    issue_load()

                    def chain(ti, T0, N, j=j):
                        C = 64 if T0 < 1024 else 8
                        nch = N // C
                        c0 = T0 // 64
                        msk = m64 if T0 < 1024 else m8
                        sl = slice(T0, T0 + N)
                        fg, lf, bb, bg, sq_ = t_fg[:, sl], t_lf[:, sl], t_b[:, sl], t_bg[:, sl], t_sq[:, sl]
                        R = lambda n: (n, ti)
                        st = []
                        st.append(lambda: S.op("dve", lambda e: e.tensor_scalar(out=fg, in0=fg, scalar1=omlt[:, j:j + 1], scalar2=lbt[:, j:j + 1],
                                                                                op0=ALU.mult, op1=ALU.add),
                                               reads=[R("t_fg"), "omlt", "lbt"], writes=[R("t_fg")]))
                        st.append(lambda: S.op("act", lambda e: e.activation(out=lf, in_=fg, func=AF.Ln), reads=[R("t_fg")], writes=[R("t_lf")]))
                        st.append(lambda: S.op("dve", lambda e: e.tensor_scalar(out=fg, in0=fg, scalar1=-1.0, scalar2=1.0, op0=ALU.mult, op1=ALU.add),
                                               reads=[R("t_fg"), R("t_lf")], writes=[R("t_fg")]))
                        st.append(lambda: S.op("dve", lambda e: e.tensor_tensor_scan(out=bb, data0=msk[:, 0:N], data1=lf, initial=0.0,
                                                                                     op0=ALU.mult, op1=ALU.add),
                                               reads=[R("t_lf"), "m64", "m8"], writes=[R("t_b")]))
                        if T0 < 1024:
                            init = 0.0 if ti == 0 else bgl[:, 0:1]

                            def gsc():
                                S.op("dve", lambda e: e.tensor_tensor_scan(out=bg, data0=mone[:, 0:N], data1=lf, initial=init,
                                                                           op0=ALU.mult, op1=ALU.add),
                                     reads=[R("t_lf"), "mone", "bgl"], writes=[R("t_bg")])
                                copy("dve", bgl[:, 0:1], t_bg[:, T0 + N - 1:T0 + N], [R("t_bg")], ["bgl"])
                                if ti == 1:
                                    S.op("act", lambda e: e.activation(out=ebt[:, 0:1], in_=bgl[:, 0:1], func=AF.Exp), reads=["bgl"], writes=["ebt"])
                            st.append(gsc)
                        else:
                            st.append(lambda: None)
                        st.append(lambda: S.op("act", lambda e: e.activation(out=lf, in_=bb, func=AF.Exp), reads=[R("t_b"), R("t_bg")], writes=[R("t_lf")]))
                        st.append(lambda: S.op("dve", lambda e: e.tensor_tensor(out=qtT[:, sl], in0=sq_, in1=lf, op=ALU.mult),
                                               reads=[R("t_sq"), R("t_lf")], writes=[("qtT", ti)]))
                        st.append(lambda: S.op("act", lambda e: e.activation(out=lf, in_=bb, func=AF.Exp, scale=-1.0),
                                               reads=[R("t_b"), ("qtT", ti)], writes=[R("t_lf")]))
                        st.append(lambda: S.op("dve", lambda e: e.tensor_tensor(out=fg, in0=fg, in1=lf, op=ALU.mult),
                                               reads=[R("t_fg"), R("t_lf")], writes=[R("t_fg")]))
                        st.append(lambda: S.op("act", lambda e: e.activation(
                            out=ebl[:, c0:c0 + nch], in_=bb.rearrange("p (c t) -> p c t", t=C)[:, :, C - 1], func=AF.Exp),
                            reads=[R("t_b")], writes=[("ebl", ti)]))
                        st.append(lambda: copy("pool", kiT[:, sl], fg, [R("t_fg")], [("kiT", ti)]))
                        st.append(lambda: S.op("dve", lambda e: e.tensor_tensor(
                            out=ksT[:, sl].rearrange("p (c t) -> p c t", t=C), in0=fg.rearrange("p (c t) -> p c t", t=C),
                            in1=ebl[:, c0:c0 + nch].unsqueeze(2).to_broadcast([128, nch, C]), op=ALU.mult),
                            reads=[R("t_fg"), ("ebl", ti)], writes=[("ksT", ti)]))
                        if T0 < 1024:
                            st.append(lambda: S.op("act", lambda e: e.activation(out=bg, in_=bg, func=AF.Exp), reads=[R("t_bg")], writes=[R("t_bg")]))
                            st.append(lambda: S.op("dve", lambda e: e.tensor_tensor(out=qgT[:, sl], in0=sq_, in1=bg, op=ALU.mult),
                                                   reads=[R("t_sq"), R("t_bg")], writes=[("qgT", ti)]))
                        return st

                    chains = [chain(ti, T0, N) for ti, (T0, N) in enumerate(TT)]
                    for k in range(max(len(c_) for c_ in chains)):
                        for c_ in chains:
                            if k < len(c_):
                                c_[k]()

                    for ti, (T0, N) in enumerate(TT):
                        bank = hb_bank()
                        proj_fm(w1_, w1res, 128, T0, N, bank)
                        S.op("act", lambda e, N=N, T0=T0, bank=bank: e.activation(out=szT[:, T0:T0 + N], in_=ps[bank][:, 0:N], func=AF.Silu),
                             reads=[PS(bank)], writes=[("szT", ti)])
                    for ti, (T0, N) in enumerate(TT):
                        bank = hb_bank()
                        proj_fm(w1_, w1res, 0, T0, N, bank)
                        copy("act", vT[:, T0:T0 + N], ps[bank][:, 0:N], [PS(bank)], [("vT", ti)])

                    def chunk_step(t0, C, ci, Sf32, Sbf, sres, par):
                        ti = 0 if t0 < 512 else (1 if t0 < 1024 else 2)
                        kv = kvtm[par]
                        at = att[par]
                        tb = pt_bank()
                        S.op("pe", lambda e: e.transpose(out=psb[tb][0:C, 0:128], in_=ksT[:, t0:t0 + C], identity=ident[:]),
                             reads=[("ksT", ti), "ident"], writes=[PS(tb)])
                        S.op("pe", lambda e: e.transpose(out=psb[tb][0:C, 128:256], in_=vT[:, t0:t0 + C], identity=ident[:]),
                             reads=[("vT", ti), "ident"], writes=[PS(tb)])
                        copy("act", kv[0:C, :], psb[tb][0:C, 0:256], [PS(tb)], [("kvtm", par)])
                        S.op("pe", lambda e: e.matmul(ps[2][0:C, 0:C], lhsT=kiT[:, t0:t0 + C], rhs=qtT[:, t0:t0 + C], start=True, stop=True),
                             reads=[("kiT", ti), ("qtT", ti)], writes=[PS(2)])
                        S.op("dve", lambda e: e.tensor_tensor(out=at[0:C, 0:C], in0=ps[2][0:C, 0:C], in1=mown[0:C, 0, 0:C], op=ALU.mult),
                             reads=[PS(2), "mown"], writes=[("att", par)])
                        ob_ = 3 + par
                        S.op("pe", lambda e: e.matmul(ps[ob_][:, 0:C], lhsT=Sbf[:, :], rhs=qtT[:, t0:t0 + C], start=True, stop=False),
                             reads=[sres + "b", ("qtT", ti)], writes=[PS(ob_)])
                        S.op("pe", lambda e: e.matmul(ps[ob_][:, 0:C], lhsT=kv[0:C, 128:256], rhs=at[0:C, 0:C], start=False, stop=True),
                             reads=[("kvtm", par), ("att", par)], writes=[PS(ob_)])
                        copy("act", oTs[:, t0:t0 + C], ps[ob_][:, 0:C], [PS(ob_)], [("t_sq", ti)])
                        sb_ = 5 + par
                        S.op("pe", lambda e: e.matmul(ps[sb_][:, 0:128], lhsT=kv[0:C, 0:128], rhs=kv[0:C, 128:256], start=True, stop=True),
                             reads=[("kvtm", par)], writes=[PS(sb_)])
                        S.op("dve", lambda e: e.scalar_tensor_tensor(out=Sf32[:, :], in0=Sf32[:, :], scalar=ebl[:, ci:ci + 1], in1=ps[sb_][:, 0:128],
                                                                     op0=ALU.mult, op1=ALU.add),
                             reads=[PS(sb_), sres, ("ebl", ti)], writes=[sres])
                        copy("act", Sbf[:, :], Sf32[:, :], [sres], [sres + "b"])

                    TLR = [("t_lf", 0), ("t_lf", 1), ("t_lf", 2)]

                    def l_tr(c):
                        t0, ti, par = c * 64, (0 if c < 8 else 1), c % 2
                        kv = kvtm[par]
                        S.op("pe", lambda e: e.transpose(out=psb[7][0:64, 0:128], in_=ksT[:, t0:t0 + 64], identity=ident[:]),
                             reads=[("ksT", ti), "ident"], writes=[PS(7)])
                        S.op("pe", lambda e: e.transpose(out=psb[7][0:64, 128:256], in_=vT[:, t0:t0 + 64], identity=ident[:]),
                             reads=[("vT", ti), "ident"], writes=[PS(7)])
                        copy("act", kv[0:64, :], psb[7][0:64, 0:256], [PS(7)], [("kvtm", par)])

                    def l_att(c):
                        t0, ti, par = c * 64, (0 if c < 8 else 1), c % 2
                        at = att[par]
                        S.op("pe", lambda e: e.matmul(ps[2][0:64, 0:64], lhsT=kiT[:, t0:t0 + 64], rhs=qtT[:, t0:t0 + 64], start=True, stop=True),
                             reads=[("kiT", ti), ("qtT", ti)], writes=[PS(2)])
                        S.op("dve", lambda e: e.tensor_tensor(out=at[0:64, 0:64], in0=ps[2][0:64, 0:64], in1=mown[0:64, 0, 0:64], op=ALU.mult),
                             reads=[PS(2), "mown"], writes=[("att", par)])

                    def q_half(half):
                        bank = 3 + half
                        for c in range(8 * half, 8 * half + 8):
                            if c == 0:
                                continue
                            col = (c % 8) * 64
                            S.op("pe", lambda e, c=c, col=col: e.matmul(
                                ps[bank][:, col:col + 64], lhsT=Sball[:, c - 1, :], rhs=qtT[:, c * 64:(c + 1) * 64], start=True, stop=True),
                                reads=[("Sball", (c - 1) // 8), ("qtT", half)] + TLR, writes=[PS(bank)])
                        lo = 64 if half == 0 else 0
                        T0_ = half * 512
                        S.op("dve", lambda e: e.tensor_tensor(
                            out=oTs[:, T0_ + lo:T0_ + 512], in0=oTs[:, T0_ + lo:T0_ + 512], in1=ps[bank][:, lo:512], op=ALU.add),
                            reads=[PS(bank), ("t_sq", half)], writes=[("t_sq", half)])

                    def l_mm(c):
                        ti, par = (0 if c < 8 else 1), c % 2
                        kv, at = kvtm[par], att[par]
                        ob_ = 3 + c // 8
                        col = (c % 8) * 64
                        S.op("pe", lambda e: e.matmul(ps[ob_][:, col:col + 64], lhsT=kv[0:64, 128:256], rhs=at[0:64, 0:64], start=True, stop=True),
                             reads=[("kvtm", par), ("att", par)], writes=[PS(ob_)])
                        sb_ = 5 + (c // 4) % 2
                        scol = (c % 4) * 128
                        S.op("pe", lambda e: e.matmul(ps[sb_][:, scol:scol + 128], lhsT=kv[0:64, 0:128], rhs=kv[0:64, 128:256], start=True, stop=True),
                             reads=[("kvtm", par)], writes=[PS(sb_)])
                        if c % 4 == 3:
                            copy("dve", Sloc[:, c - 3:c + 1, :], ps[sb_][:, 0:512].rearrange("p (c d) -> p c d", d=128),
                                 [PS(sb_)], [("Sl", cc_) for cc_ in range(c - 3, c + 1)])
                        if c % 8 == 7:
                            T0_ = (c // 8) * 512
                            copy("act", oTs[:, T0_:T0_ + 512], ps[ob_][:, 0:512], [PS(ob_)], [("t_sq", ti)])
                        if c % 4 == 3:
                            for c2 in range(max(c - 3, 1), c + 1):
                                ti2 = 0 if c2 < 8 else 1
                                S.op("dve", lambda e, c2=c2: e.scalar_tensor_tensor(out=Sloc[:, c2, :], in0=Sloc[:, c2 - 1, :], scalar=ebl[:, c2:c2 + 1],
                                                                                  in1=Sloc[:, c2, :], op0=ALU.mult, op1=ALU.add),
                                     reads=[("Sl", c2), ("Sl", c2 - 1), ("ebl", ti2)], writes=[("Sl", c2)])
                        if c == 7 or c == 15:
                            half = c // 8
                            lo_c, hi_c = (0, 8) if half == 0 else (8, 15)
                            copy("act", Sball[:, lo_c:hi_c, :], Sloc[:, lo_c:hi_c, :], [("Sl", cc_) for cc_ in range(lo_c, hi_c)],
                                 [("Sball", half)] + TLR)

                    l_tr(0)
                    l_att(0)
                    for c in range(16):
                        if c + 1 < 16:
                            l_tr(c + 1)
                            l_att(c + 1)
                        l_mm(c)
                        if c == 11:
                            q_half(0)
                    q_half(1)
                    S.dma("sp", "hgx", lambda e, j=j: [e.dma_start(out=cc_h_in[j][:, :], in_=Sloc[:, 15, :])], reads=[("Sl", 15)], writes=["cc_h_in"])
                    S.dma("pool", "cc_h", lambda e, j=j: [e.collective_compute(
                        "AllGather", ALU.bypass, replica_groups=PAIRS, ins=[cc_h_in[j].opt()], outs=[cc_h_out[j].opt()])],
                        inc=1, reads=["cc_h_in"], writes=["cc_h_out"])
                    for s_ in range(4):
                        pq = s_ % 2
                        Sfq, Sbq, nmq = SfP[pq], SbP[pq], f"Sf{pq}"
                        S.dma("sp", ("hgs_in", pq), lambda e, s_=s_, j=j, Sfq=Sfq: [e.dma_start(out=Sfq[:, :], in_=shg[s_, j])], writes=[nmq])
                        copy("act", Sbq[:, :], Sfq[:, :], [nmq], [nmq + "b"])
                        chunk_step(1024 + s_ * 8, 8, 16 + s_, Sfq, Sbq, nmq, pq)
                        S.dma("sp", ("hgs_out", pq), lambda e, s_=s_, j=j, Sfq=Sfq: [e.dma_start(out=hgs[s_, j], in_=Sfq[:, :])], reads=[nmq])
                    S.dma("sp", "hgx", lambda e, j=j: [e.dma_start(out=SA[:, :], in_=cc_h_out[j][0:128, :])], reads=["cc_h_out"], writes=["SA"])
                    S.op("dve", lambda e: e.tensor_scalar(out=SA[:, :], in0=SA[:, :], scalar1=flag[:, 0:1], scalar2=None, op0=ALU.mult),
                         reads=["SA", "flag"], writes=["SA"])
                    copy("act", SAb[:, :], SA[:, :], ["SA", "Sf0b"], ["SAb", "Sf0b"])
                    for ti, (T0, N) in enumerate(TT[:2]):
                        bank = pt_bank()
                        S.op("pe", lambda e, T0=T0, N=N, bank=bank: e.matmul(ps[bank][:, 0:N], lhsT=SAb[:, :], rhs=qgT[:, T0:T0 + N],
                                                                           start=True, stop=True),
                             reads=["SAb", ("qgT", ti)], writes=[PS(bank)])
                        S.op("dve", lambda e, T0=T0, N=N, bank=bank: e.tensor_tensor(out=oTs[:, T0:T0 + N], in0=oTs[:, T0:T0 + N],
                                                                                  in1=ps[bank][:, 0:N], op=ALU.add),
                             reads=[PS(bank), ("t_sq", ti)], writes=[("t_sq", ti)])
                    S.op("dve", lambda e: e.scalar_tensor_tensor(out=Sf[:, :], in0=SA[:, :], scalar=ebt[:, 0:1], in1=Sloc[:, 15, :],
                                                                 op0=ALU.mult, op1=ALU.add), reads=["SA", "ebt", ("Sl", 15), "Sf0"], writes=["Sf0"])
                    S.dma("sp", "hgp_out", lambda e, j=j: [e.dma_start(out=hgp[j], in_=Sf[:, :])], reads=["Sf0"])
                    def tail(ti, T0, N, j=j):
                        sl = slice(T0, T0 + N)
                        bank = (0, 1, 7)[ti]
                        R = lambda n: (n, ti)
                        return [
                            lambda: S.op("pool", lambda e: e.tensor_tensor(out=t_sq2[:, sl], in0=oTs[:, sl], in1=oTs[:, sl], op=ALU.mult),
                                         reads=[("t_sq", ti)], writes=[R("t_sq2")]),
                            lambda: S.op("pe", lambda e: e.matmul(ps[bank][:, 0:N], lhsT=onesb[:, :], rhs=t_sq2[:, sl], start=True, stop=True),
                                         reads=[R("t_sq2"), "onesb"], writes=[PS(bank)]),
                            lambda: S.op("act", lambda e: e.activation(out=t_b[:, sl], in_=ps[bank][:, 0:N], func=AF.Ln,
                                                                       scale=1.0 / 128.0, bias=EPS), reads=[PS(bank)], writes=[R("t_b")]),
                            lambda: S.op("act", lambda e: e.activation(out=t_b[:, sl], in_=t_b[:, sl], func=AF.Exp, scale=-0.5),
                                         reads=[R("t_b")], writes=[R("t_b")]),
                            lambda: S.op("dve", lambda e: e.tensor_tensor(out=t_b[:, sl], in0=t_b[:, sl], in1=oTs[:, sl], op=ALU.mult),
                                         reads=[R("t_b"), ("t_sq", ti)], writes=[R("t_b")]),
                            lambda: S.op("dve", lambda e: e.scalar_tensor_tensor(out=goT[:, j, sl], in0=t_b[:, sl], scalar=ng[:, 0:1],
                                                                                 in1=szT[:, sl], op0=ALU.mult, op1=ALU.mult),
                                         reads=[R("t_b"), "ng", ("szT", ti)], writes=goT_res(T0, N)),
                        ]
                    tails = [tail(ti, T0, N) for ti, (T0, N) in enumerate(TT)]
                    for k in range(6):
                        for t_ in tails:
                            t_[k]()
            S.barrier()

        if True:
            for li, kind in enumerate(layer_kinds):
                norm_phase(ln_g[li])
                if kind == "a":
                    attn_phase(li // 3)
                elif kind == "b":
                    conv_phase()
                else:
                    hgrn_phase(li)
                wout_phase()
            norm_phase(final_g if not dbg else final_g, final=True)
        S.emit()
    nc._sched_stats = S.stats
    return nc


def rope_tables(hf):
    half = 8
    inv = 500000.0 ** (-np.arange(half, dtype=np.float64) * 2.0 / 16.0)
    pos = np.zeros((128, NB), np.float64)
    for b in range(8):
        pos[:, b] = hf * 1024 + b * 128 + np.arange(128)
    pos[:, 8] = 16384 + (np.arange(128) % 8)
    ang = pos[:, :, None].astype(np.float32).astype(np.float64) * inv.astype(np.float32).astype(np.float64)[None, None, :]
    ang = ang.astype(np.float32).astype(np.float64)
    return np.cos(ang).astype(np.float32), np.sin(ang).astype(np.float32)


_NC_CACHE = {}


def make_in_maps(inp):
    f = lambda a: np.ascontiguousarray(np.asarray(a, dtype=np.float32))
    shared = dict(
        ln_g=f(inp["ln_g"]), final_g=f(inp["final_g"]), a_w_in=f(inp["a_w_in"]), a_w_out=f(inp["a_w_out"]),
        a_sinks=f(inp["a_sinks"]), b_w_in=f(inp["b_w_in"]), b_conv_w=f(inp["b_conv_w"]), b_w_out=f(inp["b_w_out"]),
        c_w_in=f(inp["c_w_in"]), c_norm_g=f(inp["c_norm_g"]), c_w_out=f(inp["c_w_out"]), c_lb=f(inp["c_lb_logits"]))
    x_prompt, x_sample = f(inp["x_prompt"]), f(inp["x_sample"])
    cache_k, cache_v = f(inp["cache_k"]), f(inp["cache_v"])
    state_conv, state_hgrn = f(inp["state_conv"]), f(inp["state_hgrn"])
    maps = []
    for c in range(8):
        p, hf = c // 2, c % 2
        cs, sn = rope_tables(hf)
        m = dict(shared)
        m.update(
            xp=np.ascontiguousarray(x_prompt[p, hf * 1024:(hf + 1) * 1024]),
            xsm=np.ascontiguousarray(x_sample[4 * c:4 * c + 4].reshape(32, D)),
            ck=np.ascontiguousarray(cache_k[:, 4 * c:4 * c + 4].reshape(2, 4, 128, 256)),
            cv=np.ascontiguousarray(cache_v[:, 4 * c:4 * c + 4].reshape(2, 4, 128, 256)),
            sconv=np.ascontiguousarray(state_conv[0, 4 * c:4 * c + 4].reshape(8, D)),
            shg=np.ascontiguousarray(state_hgrn[0, 4 * c:4 * c + 4]),
            ropec=cs, ropes=sn, flag=np.full((128, 1), float(hf), np.float32))
        maps.append(m)
    return maps


def assemble(res):
    y_prompt = np.zeros((4, 2048, D), np.float32)
    y_sample = np.zeros((32, 8, D), np.float32)
    kpo = np.zeros((2, 4, 128, 4, 64), np.float32)
    vpo = np.zeros_like(kpo)
    kso = np.zeros((2, 32, 128, 4, 64), np.float32)
    vso = np.zeros_like(kso)
    cpo = np.zeros((1, 4, 2, D), np.float32)
    cso = np.zeros((1, 32, 2, D), np.float32)
    hpo = np.zeros((1, 4, 16, 128, 128), np.float32)
    hso = np.zeros((1, 32, 16, 128, 128), np.float32)
    for c in range(8):
        r = res[c]
        p, hf = c // 2, c % 2
        y_prompt[p, hf * 1024:(hf + 1) * 1024] = r["y"][:1024]
        y_sample[4 * c:4 * c + 4] = r["y"][1024:1056].reshape(4, 8, D)
        kso[:, 4 * c:4 * c + 4] = r["ks"].reshape(2, 4, 128, 4, 64)
        vso[:, 4 * c:4 * c + 4] = r["vs"].reshape(2, 4, 128, 4, 64)
        cso[0, 4 * c:4 * c + 4] = r["convs"].reshape(4, 2, D)
        hso[0, 4 * c:4 * c + 4] = r["hgs"]
        if hf == 1:
            kpo[:, p] = r["kp"].reshape(2, 128, 4, 64)
            vpo[:, p] = r["vp"].reshape(2, 128, 4, 64)
            cpo[0, p] = r["convp"]
            hpo[0, p] = r["hgp"]
    return (y_prompt, y_sample, kpo, vpo, kso, vso, cpo, cso, hpo, hso)


def kernel(**inputs):
    if "nc" not in _NC_CACHE:
        _NC_CACHE["nc"] = build()
    nc = _NC_CACHE["nc"]
    maps = make_in_maps(inputs)
    res = run_bass_kernel_spmd(nc, maps, core_ids=list(range(8)))
    return assemble(res.results)
```
